# Optimizing a Trainium2 kernel written in Bass

```python
import jax, jax.numpy as jnp
from jax import lax
import numpy as np

D_MODEL = 1024
BATCH = 8
SEQ = 2048
DEPTH = 1

GRID_W = 64
CTX_LEN = 256
D_POOL = 1024
POOL_GROUPS = 4
POOL_WINDOWS = (2, 4, 8, 16)
POOL_GW = D_POOL // POOL_GROUPS
D_LRU = 1024
LRU_BLOCKS = 8
LRU_BW = D_LRU // LRU_BLOCKS
LRU_CONV_W = 4
LRU_C = 8.0
D_FF = 3 * D_MODEL
FFN_CONV_W = 3
N_MOD = 6
EPS = 1e-6
D_IN = D_POOL + 2 * D_LRU + 2 * D_MODEL
SPLITS = (D_POOL, D_POOL + D_LRU, D_POOL + 2 * D_LRU, D_POOL + 2 * D_LRU + D_MODEL)

kernel_name = 'hybrid_pool_rglru_convffn_dit_block'


def rmsnorm(x, g):
    xf = x.astype(jnp.float32)
    y = xf * lax.rsqrt(jnp.mean(xf * xf, axis=-1, keepdims=True) + EPS)
    return (y * g.astype(jnp.float32)).astype(x.dtype)


def modulation(cond, w_mod, b_mod):
    m = jax.nn.silu(cond) @ w_mod + b_mod
    return jnp.split(m[..., None, :], N_MOD, axis=-1)


def window_bounds(n, w):
    pos = jnp.arange(n)
    return jnp.clip(pos - w // 2, 0, n), jnp.clip(pos + w - w // 2, 0, n)


def multiscale_pool_minus_identity(u):
    B, R, W, C = u.shape
    uf = u.astype(jnp.float32)
    s = jnp.cumsum(jnp.cumsum(uf, axis=1), axis=2)
    s = jnp.pad(s, ((0, 0), (1, 0), (1, 0), (0, 0)))
    outs = []
    for gi, w in enumerate(POOL_WINDOWS):
        sg = s[..., gi * POOL_GW:(gi + 1) * POOL_GW]
        r_lo, r_hi = window_bounds(R, w)
        c_lo, c_hi = window_bounds(W, w)
        total = (sg[:, r_hi][:, :, c_hi] - sg[:, r_lo][:, :, c_hi]
                 - sg[:, r_hi][:, :, c_lo] + sg[:, r_lo][:, :, c_lo])
        cnt = ((r_hi - r_lo)[:, None] * (c_hi - c_lo)[None, :]).astype(jnp.float32)
        outs.append(total / cnt[None, :, :, None])
    pooled = jnp.concatenate(outs, axis=-1)
    return (pooled - uf).astype(u.dtype)


def pool_mixer(u, pool_w, pool_scale):
    B, R, W, C = u.shape
    d = multiscale_pool_minus_identity(u).reshape(B, R * W, POOL_GROUPS, POOL_GW)
    y = jnp.einsum('blgc,gcd->blgd', d, pool_w).reshape(B, R * W, C)
    return y * pool_scale


def centred_dwconv1d(u, w, b):
    K = w.shape[0]
    L = u.shape[1]
    left = K // 2
    up = jnp.pad(u, ((0, 0), (left, K - 1 - left), (0, 0)))
    acc = b + up[:, 0:L] * w[0]
    for k in range(1, K):
        acc = acc + up[:, k:k + L] * w[k]
    return acc


def rglru_coeffs(xc, w_a, b_a, w_x, b_x, lam):
    B, L, C = xc.shape
    xb = xc.reshape(B, L, LRU_BLOCKS, LRU_BW)
    r = jax.nn.sigmoid((jnp.einsum('blnc,ncd->blnd', xb, w_a).reshape(B, L, C) + b_a).astype(jnp.float32))
    i = jax.nn.sigmoid((jnp.einsum('blnc,ncd->blnd', xb, w_x).reshape(B, L, C) + b_x).astype(jnp.float32))
    log_a = -LRU_C * r * jax.nn.softplus(-lam.astype(jnp.float32))
    a = jnp.exp(log_a)
    mult = jnp.sqrt(-jnp.expm1(2.0 * log_a))
    return a, mult * i * xc.astype(jnp.float32)


def linear_scan(a, b, h0):
    b = b.at[:, 0].add(a[:, 0] * h0)

    def combine(left, right):
        a_l, b_l = left
        a_r, b_r = right
        return a_l * a_r, a_r * b_l + b_r

    _, h = lax.associative_scan(combine, (a, b), axis=1)
    return h


def project_and_scan(h, p, h0_f, h0_b):
    z = h @ p['w_in']
    u_pool, u_lru, u_gate, g_pool, g_lru = jnp.split(z, SPLITS, axis=-1)
    xc = centred_dwconv1d(u_lru, p['lru_conv_w'], p['lru_conv_b'])
    a_f, b_f = rglru_coeffs(xc, p['lru_wa'][0], p['lru_ba'][0], p['lru_wx'][0], p['lru_bx'][0], p['lru_lambda'][0])
    a_b, b_b = rglru_coeffs(xc, p['lru_wa'][1], p['lru_ba'][1], p['lru_wx'][1], p['lru_bx'][1], p['lru_lambda'][1])
    h_f = linear_scan(a_f, b_f, h0_f)
    h_b = jnp.flip(linear_scan(jnp.flip(a_b, 1), jnp.flip(b_b, 1), h0_b), 1)
    return (u_pool, u_gate, g_pool, g_lru), h_f, h_b


def finish_mixer(parts, h_f, h_b, rows, cols, p):
    u_pool, u_gate, g_pool, g_lru = parts
    B, L, _ = u_pool.shape
    y_pool = pool_mixer(u_pool.reshape(B, rows, cols, D_POOL), p['pool_w'], p['pool_scale'])
    y_lru = ((h_f + h_b) * jax.nn.gelu(u_gate.astype(jnp.float32))).astype(u_pool.dtype)
    m = (jax.nn.sigmoid(g_pool) * (y_pool @ p['w_proj_pool'])
         + jax.nn.sigmoid(g_lru) * (y_lru @ p['w_proj_lru']))
    return m @ p['w_out']


def conv_ffn(h, rows, cols, p):
    B, L, _ = h.shape
    g, u = jnp.split(h @ p['w_up'], 2, axis=-1)
    g = lax.conv_general_dilated(g.reshape(B, rows, cols, D_FF), p['ffn_conv_w'], (1, 1), 'SAME',
                                 dimension_numbers=('NHWC', 'HWIO', 'NHWC'),
                                 feature_group_count=D_FF).reshape(B, L, D_FF) + p['ffn_conv_b']
    return (jax.nn.gelu(g) * u) @ p['w_down']


def setup_inputs(seed: int = 0) -> dict:
    key = jax.random.key(seed)
    ks = jax.random.split(key, 32)
    f32 = jnp.float32

    def nrm(k, shape, scale):
        return jax.random.normal(k, shape, f32) * scale

    p_lam = jax.random.uniform(ks[19], (DEPTH, 2, D_LRU), f32, 0.9, 0.999)
    return {
        'x': nrm(ks[0], (BATCH, SEQ, D_MODEL), 1.0),
        'c': nrm(ks[1], (BATCH, D_MODEL), 1.0),
        'ctx': nrm(ks[2], (BATCH, CTX_LEN, D_MODEL), 1.0),
        'c_ctx': nrm(ks[3], (D_MODEL,), 1.0),
        'w_mod': nrm(ks[4], (DEPTH, D_MODEL, N_MOD * D_MODEL), D_MODEL ** -0.5),
        'b_mod': nrm(ks[5], (DEPTH, N_MOD * D_MODEL), 0.01),
        'g_pre_mix': 1.0 + nrm(ks[6], (DEPTH, D_MODEL), 0.05),
        'g_post_mix': 1.0 + nrm(ks[7], (DEPTH, D_MODEL), 0.05),
        'g_pre_ffn': 1.0 + nrm(ks[8], (DEPTH, D_MODEL), 0.05),
        'g_post_ffn': 1.0 + nrm(ks[9], (DEPTH, D_MODEL), 0.05),
        'w_in': nrm(ks[10], (DEPTH, D_MODEL, D_IN), D_MODEL ** -0.5),
        'pool_w': nrm(ks[11], (DEPTH, POOL_GROUPS, POOL_GW, POOL_GW), POOL_GW ** -0.5),
        'pool_scale': 1.0 + nrm(ks[12], (DEPTH, D_POOL), 0.1),
        'lru_conv_w': nrm(ks[13], (DEPTH, LRU_CONV_W, D_LRU), LRU_CONV_W ** -0.5),
        'lru_conv_b': nrm(ks[14], (DEPTH, D_LRU), 0.01),
        'lru_wa': nrm(ks[15], (DEPTH, 2, LRU_BLOCKS, LRU_BW, LRU_BW), LRU_BW ** -0.5),
        'lru_ba': nrm(ks[16], (DEPTH, 2, D_LRU), 0.01),
        'lru_wx': nrm(ks[17], (DEPTH, 2, LRU_BLOCKS, LRU_BW, LRU_BW), LRU_BW ** -0.5),
        'lru_bx': nrm(ks[18], (DEPTH, 2, D_LRU), 0.01),
        'lru_lambda': jnp.log(p_lam) - jnp.log1p(-p_lam),
        'w_proj_pool': nrm(ks[20], (DEPTH, D_POOL, D_MODEL), D_POOL ** -0.5),
        'w_proj_lru': nrm(ks[21], (DEPTH, D_LRU, D_MODEL), D_LRU ** -0.5),
        'w_out': nrm(ks[22], (DEPTH, D_MODEL, D_MODEL), D_MODEL ** -0.5),
        'w_up': nrm(ks[23], (DEPTH, D_MODEL, 2 * D_FF), D_MODEL ** -0.5),
        'ffn_conv_w': nrm(ks[24], (DEPTH, FFN_CONV_W, FFN_CONV_W, 1, D_FF), 1.0 / FFN_CONV_W),
        'ffn_conv_b': nrm(ks[25], (DEPTH, D_FF), 0.01),
        'w_down': nrm(ks[26], (DEPTH, D_FF, D_MODEL), D_FF ** -0.5),
    }


def reference(x, c, ctx, c_ctx, w_mod, b_mod, g_pre_mix, g_post_mix, g_pre_ffn, g_post_ffn,
              w_in, pool_w, pool_scale, lru_conv_w, lru_conv_b, lru_wa, lru_ba, lru_wx, lru_bx,
              lru_lambda, w_proj_pool, w_proj_lru, w_out, w_up, ffn_conv_w, ffn_conv_b, w_down):
    B, L, _ = x.shape
    rows = L // GRID_W
    ctx_len = ctx.shape[1]
    zeros_state = jnp.zeros((B, D_LRU), jnp.float32)
    for i in range(DEPTH):
        p = {
            'w_in': w_in[i], 'pool_w': pool_w[i], 'pool_scale': pool_scale[i],
            'lru_conv_w': lru_conv_w[i], 'lru_conv_b': lru_conv_b[i],
            'lru_wa': lru_wa[i], 'lru_ba': lru_ba[i], 'lru_wx': lru_wx[i], 'lru_bx': lru_bx[i],
            'lru_lambda': lru_lambda[i], 'w_proj_pool': w_proj_pool[i], 'w_proj_lru': w_proj_lru[i],
            'w_out': w_out[i], 'w_up': w_up[i], 'ffn_conv_w': ffn_conv_w[i],
            'ffn_conv_b': ffn_conv_b[i], 'w_down': w_down[i],
        }
        last = i == DEPTH - 1
        sh1_x, sc1_x, ga1_x, sh2_x, sc2_x, ga2_x = modulation(c, w_mod[i], b_mod[i])
        sh1_c, sc1_c, ga1_c, sh2_c, sc2_c, ga2_c = modulation(c_ctx, w_mod[i], b_mod[i])

        hc = rmsnorm(ctx, g_pre_mix[i]) * (1.0 + sc1_c) + sh1_c
        parts_c, hf_c, hb_c = project_and_scan(hc, p, zeros_state, zeros_state)

        hx = rmsnorm(x, g_pre_mix[i]) * (1.0 + sc1_x) + sh1_x
        parts_x, hf_x, hb_x = project_and_scan(hx, p, hf_c[:, -1], hb_c[:, 0])
        mix_x = finish_mixer(parts_x, hf_x, hb_x, rows, GRID_W, p)
        x = x + ga1_x * rmsnorm(mix_x, g_post_mix[i])
        fx = conv_ffn(rmsnorm(x, g_pre_ffn[i]) * (1.0 + sc2_x) + sh2_x, rows, GRID_W, p)
        x = x + ga2_x * rmsnorm(fx, g_post_ffn[i])

        if not last:
            mix_c = finish_mixer(parts_c, hf_c, hb_c, 1, ctx_len, p)
            ctx = ctx + ga1_c * rmsnorm(mix_c, g_post_mix[i])
            fc = conv_ffn(rmsnorm(ctx, g_pre_ffn[i]) * (1.0 + sc2_c) + sh2_c, 1, ctx_len, p)
            ctx = ctx + ga2_c * rmsnorm(fc, g_post_ffn[i])
    return x
```

```python
import numpy as np
from contextlib import ExitStack
import concourse.bass as bass
import concourse.mybir as mybir
from concourse.bass_utils import run_bass_kernel_spmd

F32 = mybir.dt.float32
BF16 = mybir.dt.bfloat16
RELAX_DVE = False
AF = mybir.ActivationFunctionType
ALU = mybir.AluOpType

NCORES = 8
D = 1024
T = 2048
CT = 256
TT = T + CT
NCH = 8
EPS = 1e-6
POOL_WINDOWS = (2, 4, 8, 16)
SEGS = [(0, 256)] + [(256 + 512 * i, 512) for i in range(4)]

_VEC_SPEC = [("g_pre_mix", 8), ("g_post_mix", 8), ("g_pre_ffn", 8), ("g_post_ffn", 8), ("pool_scale", 8),
             ("lru_conv_w", 32), ("lru_conv_b", 8), ("lru_ba", 16), ("lru_bx", 16), ("lru_lambda", 16),
             ("ffn_conv_w", 216), ("ffn_conv_b", 24), ("b_mod", 48)]
VOFF = {}
_o = 0
for _n, _c in _VEC_SPEC:
    VOFF[_n] = _o
    _o += _c
NV = _o


class _Op:
    __slots__ = ("eng", "fn", "deps", "needs_inc", "sem", "val", "is_dma", "slot", "pos")

    def __init__(self, eng, fn, is_dma=False, slot=None):
        self.eng = eng
        self.fn = fn
        self.deps = []
        self.needs_inc = False
        self.sem = None
        self.val = None
        self.is_dma = is_dma
        self.slot = slot
        self.pos = -1


class Prog:
    ENGS = ("pe", "act", "dve", "pool", "sp")

    def __init__(self, nc):
        self.nc = nc
        self.q = {e: [] for e in self.ENGS}
        self.last_w = {}
        self.readers = {}
        self.all_ops = []

    def _track(self, op, reads, writes):
        deps = {}
        for t in reads:
            w = self.last_w.get(t)
            if w is not None:
                deps[id(w)] = w
        for t in writes:
            w = self.last_w.get(t)
            if w is not None:
                deps[id(w)] = w
            for r in self.readers.get(t, {}).values():
                deps[id(r)] = r
        for d in deps.values():
            if d is op:
                continue
            if (not d.is_dma) and (not op.is_dma) and d.eng == "pe" and op.eng == "pe":
                continue
            if RELAX_DVE and (not d.is_dma) and (not op.is_dma) and d.eng == "dve" and op.eng == "dve" and op.pos - d.pos >= 2:
                continue
            d.needs_inc = True
            op.deps.append(d)
        for t in writes:
            self.last_w[t] = op
            self.readers[t] = {}
        for t in reads:
            key = ("dma", op.slot) if op.is_dma else op.eng
            self.readers.setdefault(t, {})[key] = op

    def op(self, eng, fn, reads=(), writes=()):
        o = _Op(eng, fn)
        o.pos = len(self.q[eng])
        self._track(o, reads, writes)
        self.q[eng].append(o)
        self.all_ops.append(o)
        return o

    def dma(self, eng, out, in_, reads=(), writes=(), slot=None, **kw):
        o = _Op(eng, None, is_dma=True, slot=slot)
        o.fn = lambda e, s, o_=out, i_=in_, kw_=kw: e.dma_start(out=o_, in_=i_, **kw_).then_inc(s, 16)
        self._track(o, reads, writes)
        o.needs_inc = True
        self.q[eng].append(o)
        self.all_ops.append(o)
        return o

    def emit(self, final_wait_eng="sp"):
        nc = self.nc
        with ExitStack() as es:
            esem = {e: es.enter_context(nc.semaphore("sem_" + e)) for e in self.ENGS}
            slot_names = sorted({o.slot for o in self.all_ops if o.is_dma})
            ssem = {s: es.enter_context(nc.semaphore("dsem_" + s)) for s in slot_names}
            cnt = {e: 0 for e in self.ENGS}
            for e in self.ENGS:
                for o in self.q[e]:
                    if (not o.is_dma) and o.needs_inc:
                        cnt[e] += 1
                        o.sem, o.val = esem[e], cnt[e]
            scnt = {s: 0 for s in slot_names}
            for o in self.all_ops:
                if o.is_dma:
                    scnt[o.slot] += 16
                    o.sem, o.val = ssem[o.slot], scnt[o.slot]
            block = es.enter_context(nc.Block())
            final = [(ssem[s], scnt[s]) for s in slot_names]

            def run(e):
                def body(eng):
                    waited = {}
                    for o in self.q[e]:
                        for d in o.deps:
                            k = id(d.sem)
                            if waited.get(k, 0) < d.val:
                                eng.wait_ge(d.sem, d.val)
                                waited[k] = d.val
                        if o.is_dma:
                            o.fn(eng, o.sem)
                        else:
                            ins = o.fn(eng)
                            if o.needs_inc:
                                ins.then_inc(o.sem, 1)
                    if e == final_wait_eng:
                        for s, v in final:
                            if v > 0:
                                eng.wait_ge(s, v)
                return body

            block.tensor(run("pe"))
            block.scalar(run("act"))
            block.vector(run("dve"))
            block.gpsimd(run("pool"))
            block.sync(run("sp"))


def build_program(stop=None, dbg=()):
    nc = bass.Bass("TRN2", target_bir_lowering=False)
    dr = {}

    def din(name, shape, dt=F32):
        dr[name] = nc.dram_tensor(name, shape, dt, kind="ExternalInput").ap()

    din("xT", [D, T])
    din("ctxT", [D, CT])
    din("cc", [128, 16])
    din("vecs", [128, NV])
    din("ident", [128, 128])
    din("cnt", [1, 384])
    din("wmod", [12, 128, 4096])
    din("win", [40, 128, 1024])
    din("wpp", [8, 128, 1024])
    din("wpl", [8, 128, 1024])
    din("wout", [8, 128, 1024])
    din("wup", [24, 128, 2048])
    din("wdown", [8, 128, 3072])
    din("poolw", [128, 2048])
    din("lruw", [128, 4096])
    outT = nc.dram_tensor("outT", [D, T], F32, kind="ExternalOutput").ap()
    wupS = nc.dram_tensor("wupS", [24, 128, 2048], BF16, kind="Internal").ap()
    wdnS = nc.dram_tensor("wdnS", [8, 128, 3072], BF16, kind="Internal").ap()
    woutS = nc.dram_tensor("woutS", [8, 128, 1024], BF16, kind="Internal").ap()
    dbg_out = {}

    es = ExitStack()
    with es:
        AW = 53200
        arena = es.enter_context(nc.sbuf_tensor("arena", [128, AW], F32))
        ps = [es.enter_context(nc.psum_tensor("ps%d" % i, [128, 512], F32)) for i in range(8)]
        P = Prog(nc)

        def fv(off, n):
            assert off % 4 == 0 and off + 4 * n <= AW * 4, (off, n)
            return arena[:, off // 4: off // 4 + n]

        def bv(off, n):
            assert off % 4 == 0 and n % 2 == 0 and off + 2 * n <= AW * 4, (off, n)
            return arena[:, off // 4: off // 4 + n // 2].bitcast(BF16)

        class Carve:
            def __init__(self, base, size):
                self.base, self.end, self.cur = base, base + size, base

            def f(self, n):
                a = fv(self.cur, n)
                self.cur += 4 * n
                self.cur = (self.cur + 63) // 64 * 64
                assert self.cur <= self.end, ("carve overflow", self.cur, self.end)
                return a

            def b(self, n):
                a = bv(self.cur, n)
                self.cur += 2 * n
                self.cur = (self.cur + 63) // 64 * 64
                assert self.cur <= self.end, ("carve overflow", self.cur, self.end)
                return a

        KB = 1024
        R_H = (0, 36 * KB)
        R_Y = (36 * KB, 64 * KB)
        R_S = (100 * KB, 84 * KB)
        R_W = (184 * KB, AW * 4 - 184 * KB)

        bank_ctr = [0]

        def nextbank():
            b = bank_ctr[0] % 8
            bank_ctr[0] += 1
            return b

        def mm(out, lhsT, rhs, start, stop, reads, writes):
            P.op("pe", lambda e: e.matmul(out, lhsT, rhs, start=start, stop=stop), reads, writes)

        def act(out, in_, func, reads, writes, bias=None, scale=None):
            kw = {}
            if bias is not None:
                kw["bias"] = bias
            if scale is not None:
                kw["scale"] = scale
            P.op("act", lambda e: e.activation(out=out, in_=in_, func=func, **kw), reads, writes)

        def tt(eng, out, in0, in1, op, reads, writes):
            P.op(eng, lambda e: e.tensor_tensor(out=out, in0=in0, in1=in1, op=op), reads, writes)

        def ts(eng, out, in0, s1, op0, reads, writes, s2=None, op1=None):
            if op1 is None:
                P.op(eng, lambda e: e.tensor_scalar(out=out, in0=in0, scalar1=s1, scalar2=None, op0=op0), reads, writes)
            else:
                P.op(eng, lambda e: e.tensor_scalar(out=out, in0=in0, scalar1=s1, scalar2=s2, op0=op0, op1=op1), reads, writes)

        def stt(out, in0, scalar, in1, op0, op1, reads, writes):
            P.op("dve", lambda e: e.scalar_tensor_tensor(out=out, in0=in0, scalar=scalar, in1=in1, op0=op0, op1=op1), reads, writes)

        def memset(ap, val, writes):
            P.op("pool", lambda e: e.memset(ap, val), (), writes)

        def dump(name, ap, dt=F32):
            shape = [ap.shape[0], int(np.prod(ap.shape[1:]))]
            t = nc.dram_tensor("dbg_" + name, shape, dt, kind="ExternalOutput").ap()
            dbg_out[name] = t
            return t

        cw = Carve(*R_W)
        vecs = cw.f(NV)
        ident = cw.f(128)
        ones = cw.f(128)
        identb = cw.b(128)
        ccs = cw.f(16)
        ssil = cw.f(16)
        modfm = cw.f(96)
        der = cw.f(64)
        lrud = cw.f(64)
        cw2 = cw.f(32)
        cn1 = cw.f(384)
        W_DYN = cw.cur

        def V(name, i=0):
            o = VOFF[name] + i
            return vecs[:, o:o + 1]

        def Vs(name, i0, n):
            o = VOFF[name] + i0
            return vecs[:, o:o + n]

        P.dma("sp", vecs, dr["vecs"][:, :], writes=["vecs"], slot="c_vecs")
        P.dma("sp", ident, dr["ident"][:, :], writes=["ident"], slot="c_ident")
        P.dma("sp", ccs, dr["cc"][:, :], writes=["cc"], slot="c_cc")
        memset(ones, 1.0, ["ones"])
        P.op("dve", lambda e: e.tensor_copy(out=identb, in_=ident), ["ident"], ["identb"])
        act(ssil, ccs, AF.Silu, ["cc"], ["ssil"])
        s3 = ssil.rearrange("p (k r) -> p k r", r=2)

        cs = Carve(*R_S)
        wmb = [cs.f(4096), cs.f(4096)]
        modrow = cs.f(6144)
        xa4 = cs.f(8 * 512)
        rstdA = cs.f(TT)
        cy = Carve(*R_Y)
        xa = [cy.f(8 * 256), cy.f(8 * 512), cy.f(8 * 512), cy.f(8 * 512), xa4]
        sqb = [cy.f(512), cy.f(512)]
        ssumA = cy.f(512)
        sdb = cy.f(512)
        xT3 = dr["xT"].rearrange("(c p) t -> p c t", p=128)
        cT3 = dr["ctxT"].rearrange("(c p) t -> p c t", p=128)
        xa3 = [xa[si].rearrange("p (c t) -> p c t", t=SEGS[si][1]) for si in range(5)]

        def load_x(si):
            o, n = SEGS[si]
            src = cT3[:, :, :] if si == 0 else xT3[:, :, o - 256:o - 256 + n]
            P.dma("sp", xa3[si], src, writes=["xa%d" % si], slot="xa%d" % si)

        def stats(si):
            o, n = SEGS[si]
            tk = "xa%d" % si
            for c in range(8):
                if c == 0:
                    act(ssumA[:, 0:n], xa3[si][:, c, :], AF.Square, [tk], ["ssumA"])
                else:
                    sq = sqb[c % 2]
                    act(sq[:, 0:n], xa3[si][:, c, :], AF.Square, [tk], ["sq%d" % (c % 2)])
                    tt("dve", ssumA[:, 0:n], ssumA[:, 0:n], sq[:, 0:n], ALU.add, ["ssumA", "sq%d" % (c % 2)], ["ssumA"])
            b = nextbank()
            mm(ps[b][:, 0:n], ones, ssumA[:, 0:n], True, True, ["ones", "ssumA"], ["ps%d" % b])
            act(sdb[:, 0:n], ps[b][:, 0:n], AF.Ln, ["ps%d" % b], ["sd"], scale=1.0 / D, bias=EPS)
            act(rstdA[:, o:o + n], sdb[:, 0:n], AF.Exp, ["sd"], ["rstdA%d" % si], scale=-0.5)
            for c in range(8):
                tt("dve", xa3[si][:, c, :], xa3[si][:, c, :], rstdA[:, o:o + n], ALU.mult, [tk, "rstdA%d" % si], ["xn%d_%d" % (si, c)])

        xsched = {1: 0, 2: 1, 4: 2, 6: 3, 8: 4}
        ssched = {3: 0, 5: 1, 7: 2, 9: 3, 11: 4}
        for blk in range(12):
            buf = wmb[blk % 2]
            tk = "wm%d" % (blk % 2)
            P.dma("sp", buf, dr["wmod"][blk], writes=[tk], slot=tk)
            if blk in xsched:
                load_x(xsched[blk])
            b = nextbank()
            for k in range(8):
                mm(ps[b][0:2, 0:512], s3[:, k, :], buf[:, k * 512:(k + 1) * 512], k == 0, k == 7,
                   ["ssil", tk], ["ps%d" % b])
            act(modrow[0:2, blk * 512:(blk + 1) * 512], ps[b][0:2, 0:512], AF.Identity, ["ps%d" % b], ["modrow%d" % blk])
            if blk in ssched:
                stats(ssched[blk])
        b = nextbank()
        for oc in range(48):
            mm(ps[b][:, 2 * oc:2 * oc + 2], modrow[0:2, oc * 128:(oc + 1) * 128], ident[0:2, 0:2], True, True,
               ["modrow%d" % (oc // 4), "ident"], ["ps%d" % b])
        act(modfm, ps[b][:, 0:96], AF.Identity, ["ps%d" % b], ["modfm"])
        mod3 = modfm.rearrange("p (c r) -> p c r", r=2)
        for r in range(2):
            tt("dve", mod3[:, :, r], mod3[:, :, r], Vs("b_mod", 0, 48), ALU.add, ["modfm", "vecs"], ["modfm"])

        def MOD(which, c, r=0):
            return mod3[:, which * 8 + c, r:r + 1]

        A1 = der[:, 0:8]
        A1c = der[:, 8:16]
        GG1 = der[:, 16:24]
        A2 = der[:, 24:32]
        GG2 = der[:, 32:40]
        stt(A1, mod3[:, 8:16, 0], 1.0, Vs("g_pre_mix", 0, 8), ALU.add, ALU.mult, ["modfm", "vecs"], ["der"])
        stt(A1c, mod3[:, 8:16, 1], 1.0, Vs("g_pre_mix", 0, 8), ALU.add, ALU.mult, ["modfm", "vecs"], ["der"])
        tt("dve", GG1, mod3[:, 16:24, 0], Vs("g_post_mix", 0, 8), ALU.mult, ["modfm", "vecs"], ["der"])
        stt(A2, mod3[:, 32:40, 0], 1.0, Vs("g_pre_ffn", 0, 8), ALU.add, ALU.mult, ["modfm", "vecs"], ["der"])
        tt("dve", GG2, mod3[:, 40:48, 0], Vs("g_post_ffn", 0, 8), ALU.mult, ["modfm", "vecs"], ["der"])
        lam = Vs("lru_lambda", 0, 16)
        le = lrud[:, 0:16]
        lsp = lrud[:, 16:32]
        ls1 = lrud[:, 32:48]
        act(le, lam, AF.Exp, ["vecs"], ["le"], scale=-1.0)
        act(lsp, le, AF.Ln, ["le"], ["lsp"], bias=1.0)
        ts("dve", ls1, lsp, -4.0, ALU.mult, ["lsp"], ["hs1"])
        hs1 = ls1
        hbias = cw2
        ts("dve", hbias, Vs("lru_ba", 0, 32), 0.5, ALU.mult, ["vecs"], ["hbias"])

        ch = Carve(*R_H)
        h = ch.b(8 * TT)
        h3 = h.rearrange("p (c t) -> p c t", t=TT)
        for si, (o, n) in enumerate(SEGS):
            for c in range(8):
                if si == 0:
                    sc_, bi_ = A1c[:, c:c + 1], MOD(0, c, 1)
                else:
                    sc_, bi_ = A1[:, c:c + 1], MOD(0, c, 0)
                act(h3[:, c, o:o + n], xa3[si][:, c, :], AF.Identity, ["xn%d_%d" % (si, c), "der", "modfm"], ["h%d_%d" % (c, si)],
                    bias=bi_, scale=sc_)
        HSEG = lambda si: ["h%d_%d" % (c, si) for c in range(8)]

        if "h" in dbg:
            P.dma("sp", dump("h", h, BF16)[:, :], h, reads=[t for si in range(5) for t in HSEG(si)], slot="dbg_h")
            P.dma("sp", dump("modfm", modfm)[:, :], modfm, reads=["modfm"], slot="dbg_m")
        if stop == "A":
            P.emit()
            return nc, dbg_out

        cy = Carve(*R_Y)
        ypool = cy.b(8 * T)
        ylru = cy.b(8 * T)
        ypool3 = ypool.rearrange("p (c t) -> p c t", t=T)
        ylru3 = ylru.rearrange("p (c t) -> p c t", t=T)
        YTOK = ["xa%d" % si for si in range(4)] + ["xn%d_%d" % (si, c) for si in range(4) for c in range(8)] + ["sq0", "sq1", "sd", "ssumA"]
        y_first = [True]

        cwd = Carve(W_DYN, R_W[0] + R_W[1] - W_DYN)
        lruw = cwd.b(4096)
        poolw = cwd.b(2048)
        NWIN = 3
        winb = [cwd.b(1024) for _ in range(NWIN)]
        win_ctr = [0]
        P.dma("pool", lruw[:, 0:2048], dr["lruw"][:, 0:2048], writes=["lruw"], slot="w_lruw")
        P.dma("pool", lruw[:, 2048:4096], dr["lruw"][:, 2048:4096], writes=["lruw"], slot="w_lruw")
        P.dma("pool", poolw, dr["poolw"][:, :], writes=["poolw"], slot="w_poolw")

        def load_win(oc):
            i = win_ctr[0] % NWIN
            win_ctr[0] += 1
            P.dma("pool", winb[i], dr["win"][oc], writes=["win%d" % i], slot="win%d" % i)
            return winb[i], "win%d" % i

        def proj_seg(wt, wtk, si, b):
            o, n = SEGS[si]
            for k in range(8):
                mm(ps[b][:, 0:n], wt[:, k * 128:(k + 1) * 128], h3[:, k, o:o + n], k == 0, k == 7,
                   [wtk] + HSEG(si), ["ps%d" % b])

        cast_plan = []
        for oc in range(8):
            cast_plan.append((woutS[oc], dr["wout"][oc]))
        for j in range(24):
            cast_plan.append((wupS[j], dr["wup"][j]))
        for oc in range(8):
            cast_plan.append((wdnS[oc][:, 0:2048], dr["wdown"][oc][:, 0:2048]))
            cast_plan.append((wdnS[oc][:, 2048:3072], dr["wdown"][oc][:, 2048:3072]))
        cast_ctr = [0]

        def issue_casts(k):
            for _ in range(k):
                i = cast_ctr[0]
                if i >= len(cast_plan):
                    return
                cast_ctr[0] += 1
                dst, src = cast_plan[i]
                wr = ["wc%d" % i] + (["wc_all"] if i == len(cast_plan) - 1 else [])
                P.dma("pool", dst, src, writes=wr, slot="wcast")

        STOK_A = (["wm0", "wm1"] + ["modrow%d" % i for i in range(12)] + ["xa4"] + ["xn4_%d" % c for c in range(8)]
                  + ["rstdA%d" % si for si in range(5)])
        cs = Carve(*R_S)
        cntb = cs.f(T)
        PQ = [[cs.f(47 * 79), cs.f(47 * 79)] for _ in range(2)]
        dbf = [cs.b(T), cs.b(T)]
        PQTOK = ["pq%d_%d" % (cl, i) for cl in range(2) for i in range(2)]
        cnt3 = cntb.rearrange("p (r c) -> p r c", c=64)
        first_S = [True]
        nb_ctr = [0]

        def nextB():
            b = 4 + nb_ctr[0] % 4
            nb_ctr[0] += 1
            return b

        def proj_cl0(g_):
            wt_, wtk_ = load_win(2 * g_)
            for t in range(4):
                proj_seg(wt_, wtk_, 1 + t, t)
            return (wt_, wtk_)

        pre_cl0 = proj_cl0(0)
        P.dma("sp", cn1, dr["cnt"][0:1, :].partition_broadcast(128), writes=["cn1"], slot="c_cn1")
        P.op("dve", lambda e: e.reciprocal(out=cn1, in_=cn1), ["cn1"], ["cn1"])
        for g, w in enumerate(POOL_WINDOWS):
            hw = w // 2
            Hp, Wp = 32 + w - 1, 64 + w - 1
            extra = STOK_A if first_S[0] else []
            first_S[0] = False
            ir = cn1[:, g * 96: g * 96 + 32].unsqueeze(2).broadcast_to([128, 32, 64])
            ic = cn1[:, g * 96 + 32: g * 96 + 96].unsqueeze(1).broadcast_to([128, 32, 64])
            tt("dve", cnt3, ir, ic, ALU.mult, ["cn1"], ["cnt"] + extra)
            views = []
            for cl in range(2):
                v = [PQ[cl][i][:, 0:Hp * Wp].rearrange("p (r c) -> p r c", c=Wp) for i in range(2)]
                views.append(v)
                Pv, Qv = v
                memset(Pv[:, 0:hw, :], 0.0, ["pq%d_0" % cl] + extra)
                if hw > 1:
                    memset(Pv[:, hw + 32:Hp, :], 0.0, ["pq%d_0" % cl])
                memset(Pv[:, hw:hw + 32, 0:hw], 0.0, ["pq%d_0" % cl])
                if hw > 1:
                    memset(Pv[:, hw:hw + 32, hw + 64:Wp], 0.0, ["pq%d_0" % cl])
                if w in (2, 8):
                    memset(Qv[:, 0:hw, 0:64], 0.0, ["pq%d_1" % cl])
                    if hw > 1:
                        memset(Qv[:, hw + 32:Hp, 0:64], 0.0, ["pq%d_1" % cl])
            wts = [pre_cl0]
            for t in range(4):
                act(views[0][0][:, hw + 8 * t: hw + 8 * t + 8, hw:hw + 64], ps[t][:, :].rearrange("p (r c) -> p r c", c=64),
                    AF.Identity, ["ps%d" % t], ["pq0_0"])
            wt, wtk = load_win(2 * g + 1)
            wts.append((wt, wtk))
            for t in range(4):
                b = nextB()
                proj_seg(wt, wtk, 1 + t, b)
                act(views[1][0][:, hw + 8 * t: hw + 8 * t + 8, hw:hw + 64], ps[b][:, :].rearrange("p (r c) -> p r c", c=64),
                    AF.Identity, ["ps%d" % b], ["pq1_0"])
            if g + 1 < 4:
                pre_cl0 = proj_cl0(g + 1)
            state = [dict(cur=0, ln=Wp, rows=Hp) for _ in range(2)]
            k = 1
            while k < w:
                for cl in range(2):
                    st = state[cl]
                    src, dst = views[cl][st["cur"]], views[cl][1 - st["cur"]]
                    nl = st["ln"] - k
                    tt("dve", dst[:, hw:hw + 32, 0:nl], src[:, hw:hw + 32, 0:nl], src[:, hw:hw + 32, k:k + nl], ALU.add,
                       ["pq%d_%d" % (cl, st["cur"])], ["pq%d_%d" % (cl, 1 - st["cur"])])
                    st["cur"], st["ln"] = 1 - st["cur"], nl
                k *= 2
            k = 1
            while k < w:
                for cl in range(2):
                    st = state[cl]
                    src, dst = views[cl][st["cur"]], views[cl][1 - st["cur"]]
                    nr = st["rows"] - k
                    tt("dve", dst[:, 0:nr, 0:64], src[:, 0:nr, 0:64], src[:, k:k + nr, 0:64], ALU.add,
                       ["pq%d_%d" % (cl, st["cur"])], ["pq%d_%d" % (cl, 1 - st["cur"])])
                    st["cur"], st["rows"] = 1 - st["cur"], nr
                k *= 2
            for cl in range(2):
                st = state[cl]
                assert st["ln"] == 64 and st["rows"] == 32
                src, oth = views[cl][st["cur"]], views[cl][1 - st["cur"]]
                tt("dve", oth[:, 0:32, 0:64], src[:, 0:32, 0:64], cnt3, ALU.mult, ["pq%d_%d" % (cl, st["cur"]), "cnt"],
                   ["pq%d_%d" % (cl, 1 - st["cur"])])
                wt, wtk = wts[cl]
                for t in range(4):
                    b = nextB()
                    proj_seg(wt, wtk, 1 + t, b)
                    tt("dve", dbf[cl][:, 512 * t:512 * t + 512].rearrange("p (r c) -> p r c", c=64), oth[:, 8 * t:8 * t + 8, 0:64],
                       ps[b][:, :].rearrange("p (r c) -> p r c", c=64), ALU.subtract, ["pq%d_%d" % (cl, 1 - st["cur"]), "ps%d" % b],
                       ["dbf%d_%d" % (cl, t)])
            for ocl in range(2):
                oc = 2 * g + ocl
                for t in range(4):
                    b = nextB()
                    for k in range(2):
                        idx = ((g * 2 + ocl) * 2 + k) * 128
                        mm(ps[b][:, :], poolw[:, idx:idx + 128], dbf[k][:, 512 * t:512 * t + 512], k == 0, k == 1,
                           ["poolw", "dbf%d_%d" % (k, t)], ["ps%d" % b])
                    extra = YTOK if y_first[0] else []
                    y_first[0] = False
                    act(ypool3[:, oc, 512 * t:512 * t + 512], ps[b][:, :], AF.Identity, ["ps%d" % b, "vecs"],
                        ["ypool%d_%d" % (oc, t)] + extra, scale=V("pool_scale", oc))
            issue_casts(5)
        if "ypool" in dbg:
            P.dma("sp", dump("ypool", ypool, BF16)[:, :], ypool,
                  reads=["ypool%d_%d" % (oc, t) for oc in range(8) for t in range(4)], slot="dbg_yp")
        if stop == "Bp":
            P.emit()
            return nc, dbg_out

        cs = Carve(*R_S)
        UPW = 2320
        LOFF = 264
        upad = cs.b(UPW)
        dgw = cs.b(32 * 128)
        o_m2b = cs.cur
        m2b = cs.f(TT)
        xcb = bv(o_m2b, TT)
        xc = cs.f(TT)
        m2f = cs.f(TT)
        ra = [cs.f(TT), cs.f(TT)]
        ib = [cs.f(TT), cs.f(TT)]
        gel = cs.f(T)
        m2 = [m2f, m2b]
        m2tok = ["m2f", "m2b"]
        POOLTOK = ["cnt"] + PQTOK + ["dbf%d_%d" % (cl, t) for cl in range(2) for t in range(4)]
        memset(upad, 0.0, ["upad"] + POOLTOK)
        for k in range(4):
            for n in range(8):
                i = k * 8 + n
                ts("dve", dgw[:, i * 128:(i + 1) * 128], ident, V("lru_conv_w", i), ALU.mult, ["ident", "vecs"], ["dgw"] + (POOLTOK if i == 0 else []))
        RA = lambda d: ["ra%d_%d" % (d, si) for si in range(5)]
        IB = lambda d: ["ib%d_%d" % (d, si) for si in range(5)]
        win_next = load_win(8 + 0)
        for n in range(8):
            wt, wtk = win_next
            for si, (o, nn) in enumerate(SEGS):
                b = nextbank()
                proj_seg(wt, wtk, si, b)
                po = 2 + o if si == 0 else LOFF + (o - 256)
                act(upad[:, po:po + nn], ps[b][:, 0:nn], AF.Identity, ["ps%d" % b], ["upad"])
            for si, (o, nn) in enumerate(SEGS):
                base = o if si == 0 else LOFF - 2 + (o - 256)
                b = nextbank()
                for k in range(4):
                    i = k * 8 + n
                    mm(ps[b][:, 0:nn], dgw[:, i * 128:(i + 1) * 128], upad[:, base + k:base + k + nn], k == 0, k == 3,
                       ["dgw", "upad"], ["ps%d" % b])
                act(xc[:, o:o + nn], ps[b][:, 0:nn], AF.Identity, ["ps%d" % b, "vecs"], ["xc"], bias=V("lru_conv_b", n))
            P.op("dve", lambda e: e.tensor_copy(out=xcb, in_=xc), ["xc"], ["m2b"])
            for dr_ in range(2):
                for kind, dst, tkf in ((0, ra[dr_], RA(dr_)), (1, ib[dr_], IB(dr_))):
                    widx = ((kind * 2 + dr_) * 8 + n) * 128
                    hb_ = hbias[:, kind * 16 + dr_ * 8 + n: kind * 16 + dr_ * 8 + n + 1]
                    for si, (o, nn) in enumerate(SEGS):
                        b = nextbank()
                        mm(ps[b][:, 0:nn], lruw[:, widx:widx + 128], xcb[:, o:o + nn], True, True, ["lruw", "m2b"], ["ps%d" % b])
                        act(dst[:, o:o + nn], ps[b][:, 0:nn], AF.Tanh, ["ps%d" % b, "hbias"], [tkf[si]], bias=hb_, scale=0.5)
                hs = hs1[:, dr_ * 8 + n: dr_ * 8 + n + 1]
                act(ra[dr_], ra[dr_], AF.Exp, RA(dr_) + ["hs1"], RA(dr_), bias=hs, scale=hs)
                act(m2[dr_], ra[dr_], AF.Square, RA(dr_), [m2tok[dr_]])
                act(m2[dr_], m2[dr_], AF.Sqrt, [m2tok[dr_]], [m2tok[dr_]], scale=-0.25, bias=0.25)
            for dr_ in range(2):
                stt(ib[dr_], ib[dr_], 1.0, xc, ALU.add, ALU.mult, IB(dr_) + ["xc"], IB(dr_))
                tt("dve", ib[dr_], ib[dr_], m2[dr_], ALU.mult, IB(dr_) + [m2tok[dr_]], IB(dr_))
                if dr_ == 0:
                    P.op("dve", lambda e: e.tensor_tensor_scan(out=ib[0], data0=ra[0], data1=ib[0], initial=0.0, op0=ALU.mult, op1=ALU.add),
                         RA(0) + IB(0), IB(0))
                else:
                    P.op("dve", lambda e: e.tensor_tensor_scan(out=ib[1][:, 0:256][:, ::-1], data0=ra[1][:, 0:256][:, ::-1],
                                                                data1=ib[1][:, 0:256][:, ::-1], initial=0.0, op0=ALU.mult, op1=ALU.add),
                         RA(1) + IB(1), IB(1))
                    P.op("dve", lambda e: e.tensor_tensor_scan(out=ib[1][:, 256:TT][:, ::-1], data0=ra[1][:, 256:TT][:, ::-1],
                                                                data1=ib[1][:, 256:TT][:, ::-1], initial=ib[1][:, 0:1], op0=ALU.mult, op1=ALU.add),
                         RA(1) + IB(1), IB(1))
            wt, wtk = load_win(16 + n)
            if n + 1 < 8:
                win_next = load_win(8 + n + 1)
            issue_casts(5)
            for t in range(4):
                b = nextbank()
                proj_seg(wt, wtk, 1 + t, b)
                act(gel[:, 512 * t:512 * t + 512], ps[b][:, :], AF.Gelu_apprx_tanh, ["ps%d" % b], ["gel%d" % t])
            GEL = ["gel%d" % t for t in range(4)]
            tt("dve", ib[0][:, 256:TT], ib[0][:, 256:TT], ib[1][:, 256:TT], ALU.add, IB(0) + IB(1), IB(0))
            tt("dve", ylru3[:, n, :], ib[0][:, 256:TT], gel, ALU.mult, IB(0) + GEL, ["ylru%d" % n] + (YTOK if n == 0 else []))
        if "ylru" in dbg:
            P.dma("sp", dump("ylru", ylru, BF16)[:, :], ylru, reads=["ylru%d" % n for n in range(8)], slot="dbg_yl")
        if stop == "B":
            P.emit()
            return nc, dbg_out

        LRUTOK = ["upad", "xc", "m2f", "m2b", "dgw"] + RA(0) + RA(1) + IB(0) + IB(1) + GEL
        cs = Carve(*R_S)
        mbuf = cs.b(8 * T)
        m3 = mbuf.rearrange("p (c t) -> p c t", t=T)
        S_C2 = cs.cur
        sgb = [[cs.f(512), cs.f(512)] for _ in range(2)]
        t12 = [[cs.f(512), cs.f(512)] for _ in range(2)]
        cwd = Carve(W_DYN, R_W[0] + R_W[1] - W_DYN)
        c1w = [[cwd.b(1024) for _ in range(4)] for _ in range(2)]
        W_OLD = ["lruw", "poolw"] + ["win%d" % i for i in range(NWIN)]
        first_c1 = [True]
        YP = lambda t: ["ypool%d_%d" % (k, t) for k in range(8)]
        YL = ["ylru%d" % k for k in range(8)]
        it = 0
        for oc in range(8):
            sl = oc % 2
            srcs = [dr["wpp"][oc], dr["win"][24 + oc], dr["wpl"][oc], dr["win"][32 + oc]]
            for i in range(4):
                extra = W_OLD if first_c1[0] else []
                first_c1[0] = False
                P.dma("pool", c1w[sl][i], srcs[i], writes=["c1w%d_%d" % (sl, i)] + extra, slot="c1w%d_%d" % (sl, i))
            for t in range(4):
                bb = [nextbank() for _ in range(4)]
                for k in range(8):
                    mm(ps[bb[0]][:, :], c1w[sl][0][:, k * 128:(k + 1) * 128], ypool3[:, k, 512 * t:512 * t + 512], k == 0, k == 7,
                       ["c1w%d_0" % sl] + YP(t), ["ps%d" % bb[0]])
                for k in range(8):
                    mm(ps[bb[1]][:, :], c1w[sl][1][:, k * 128:(k + 1) * 128], h3[:, k, 256 + 512 * t:256 + 512 * t + 512], k == 0, k == 7,
                       ["c1w%d_1" % sl] + HSEG(1 + t), ["ps%d" % bb[1]])
                for k in range(8):
                    mm(ps[bb[2]][:, :], c1w[sl][2][:, k * 128:(k + 1) * 128], ylru3[:, k, 512 * t:512 * t + 512], k == 0, k == 7,
                       ["c1w%d_2" % sl] + YL, ["ps%d" % bb[2]])
                for k in range(8):
                    mm(ps[bb[3]][:, :], c1w[sl][3][:, k * 128:(k + 1) * 128], h3[:, k, 256 + 512 * t:256 + 512 * t + 512], k == 0, k == 7,
                       ["c1w%d_3" % sl] + HSEG(1 + t), ["ps%d" % bb[3]])
                p = it % 2
                it += 1
                extra = LRUTOK if (oc == 0 and t == 0) else []
                act(sgb[p][0], ps[bb[1]][:, :], AF.Sigmoid, ["ps%d" % bb[1]], ["sg%d_0" % p] + extra)
                act(sgb[p][1], ps[bb[3]][:, :], AF.Sigmoid, ["ps%d" % bb[3]], ["sg%d_1" % p])
                tt("dve", t12[p][0], ps[bb[0]][:, :], sgb[p][0], ALU.mult, ["ps%d" % bb[0], "sg%d_0" % p], ["t12%d_0" % p])
                tt("dve", t12[p][1], ps[bb[2]][:, :], sgb[p][1], ALU.mult, ["ps%d" % bb[2], "sg%d_1" % p], ["t12%d_1" % p])
                tt("dve", m3[:, oc, 512 * t:512 * t + 512], t12[p][0], t12[p][1], ALU.add, ["t12%d_0" % p, "t12%d_1" % p],
                   ["m%d_%d" % (oc, t)])
        MT = lambda tl: ["m%d_%d" % (k, t) for k in range(8) for t in tl]
        if "m" in dbg:
            P.dma("sp", dump("m", mbuf, BF16)[:, :], mbuf, reads=MT(range(4)), slot="dbg_mm")
        if stop == "C1":
            P.emit()
            return nc, dbg_out

        cs = Carve(S_C2, R_S[0] + R_S[1] - S_C2)
        sq2 = [cs.f(640), cs.f(640)]
        tm2 = [cs.f(640), cs.f(640)]
        sd2 = cs.f(640)
        rstd2 = cs.f(640)
        gpad = [cs.f(660) for _ in range(4)]
        ssum2 = cs.f(640)
        hfB = cs.b(8 * 640)
        HY_OLD = [t for si in range(5) for t in HSEG(si)] + [t for tl in range(4) for t in YP(tl)] + YL
        C1S_OLD = ["sg%d_%d" % (p, i) for p in range(2) for i in range(2)] + ["t12%d_%d" % (p, i) for p in range(2) for i in range(2)]
        cb_ = Carve(R_H[0], R_H[1] + R_Y[1])
        xt1 = cb_.f(8 * 640)
        mixb = cb_.f(8 * 640)
        hfA = cb_.b(8 * 640)
        abuf = cb_.b(24 * 512)
        wupb = [cb_.b(2048) for _ in range(3)]
        wdnb = [cb_.b(3072) for _ in range(2)]
        xt13 = xt1.rearrange("p (c t) -> p c t", t=640)
        mix3 = mixb.rearrange("p (c t) -> p c t", t=640)
        f3 = mixb[:, 0:8 * 512].rearrange("p (c t) -> p c t", t=512)
        hf3s = [hfA.rearrange("p (c t) -> p c t", t=640), hfB.rearrange("p (c t) -> p c t", t=640)]
        a3 = abuf.rearrange("p (c t) -> p c t", t=512)
        cwd = Carve(W_DYN, R_W[0] + R_W[1] - W_DYN)
        accb = [cwd.f(512) for _ in range(4)]
        wupb.append(cwd.b(2048))
        woutb = [cwd.b(1024) for _ in range(3)]
        wo_ctr = [0]
        wo_cur = [0]
        C1W_OLD = ["c1w%d_%d" % (s_, i) for s_ in range(2) for i in range(4)]
        oT3 = outT.rearrange("(c p) t -> p c t", p=128)
        memset(gpad[0], 0.0, ["gpad0"] + C1S_OLD)
        for gi in range(1, 4):
            memset(gpad[gi], 0.0, ["gpad%d" % gi])
        wup_ctr = [0]
        wdn_ctr = [0]
        nbf_ctr = [0]
        nbc_ctr = [0]

        def nbF():
            b_ = nbf_ctr[0] % 6
            nbf_ctr[0] += 1
            return b_

        def nbC():
            b_ = 6 + nbc_ctr[0] % 2
            nbc_ctr[0] += 1
            return b_

        def geom(tl):
            r0 = max(8 * tl - 1, 0)
            r1 = min(8 * tl + 9, 32)
            ntok = (r1 - r0) * 64
            return r0, ntok, r0 * 64, [(0, 512)] + [(512, ntok - 512)]

        MIXT = lambda c: "mix_%d" % c
        FT = lambda c: "f_%d" % c
        first_mix = [True]

        def mix_units(tl):
            r0, ntok, tok0, subs = geom(tl)
            mtl = sorted({min(3, (tok0 + so) // 512) for so, sn in subs} | {min(3, (tok0 + so + sn - 1) // 512) for so, sn in subs})
            units = []
            for oc in range(8):
                for (so, sn) in subs:
                    st_ = {}

                    def pe_fn(oc=oc, so=so, sn=sn, st_=st_):
                        if so == 0:
                            wi = wo_ctr[0] % 3
                            wo_ctr[0] += 1
                            extra = (LRUTOK + C1W_OLD) if first_mix[0] else []
                            P.dma("sp", woutb[wi], woutS[oc], reads=["wc_all"], writes=["wo%d" % wi] + extra, slot="wo%d" % wi)
                            wo_cur[0] = wi
                        wi = wo_cur[0]
                        b_ = nbC()
                        st_["b"] = b_
                        for k in range(8):
                            mm(ps[b_][:, 0:sn], woutb[wi][:, k * 128:(k + 1) * 128],
                               m3[:, k, tok0 + so:tok0 + so + sn], k == 0, k == 7, ["wo%d" % wi] + MT(mtl), ["ps%d" % b_])

                    def act_fn(oc=oc, so=so, sn=sn, st_=st_):
                        b_ = st_["b"]
                        extra2 = []
                        if oc == 0 and so == 0:
                            extra2 = [FT(q) for q in range(8)]
                            if first_mix[0]:
                                extra2 = extra2 + C1S_OLD + HY_OLD
                        first_mix[0] = False
                        act(mix3[:, oc, so:so + sn], ps[b_][:, 0:sn], AF.Identity, ["ps%d" % b_], [MIXT(oc)] + extra2)
                    units.append((pe_fn, act_fn))
            return units

        class MixSched:
            def __init__(self, units):
                self.u, self.ipe, self.iact = units, 0, 0

            def step(self):
                n = len(self.u)
                if self.iact < self.ipe and (self.ipe - self.iact >= 2 or self.ipe == n):
                    self.u[self.iact][1]()
                    self.iact += 1
                if self.ipe < n and self.ipe - self.iact < 2:
                    self.u[self.ipe][0]()
                    self.ipe += 1

            def done(self):
                return self.iact == len(self.u)

        first_hy = [True]

        def chain_items(tn):
            r0, ntok, tok0, subs = geom(tn)
            par = tn % 2
            co = (8 * tn - r0) * 64
            hf3 = hf3s[par]
            items = []

            def xloads():
                for c in range(8):
                    extra = HY_OLD if (first_hy[0] and c == 0) else []
                    P.dma("sp", xt13[:, c, 0:ntok], xT3[:, c, tok0:tok0 + ntok], writes=["xt1_%d" % c, "x1_%d" % c] + extra,
                          slot="xt1_%d" % c)
                first_hy[0] = False
            items.append(xloads)
            ms = MixSched(mix_units(tn))
            for _ in range(20):
                items.append(ms.step)

            def norm_chunk(src_fn, rd_fn, c):
                def f():
                    if c == 0:
                        act(ssum2[:, 0:ntok], src_fn(c), AF.Square, rd_fn(c), ["ssum2"])
                    else:
                        sq = sq2[c % 2]
                        act(sq[:, 0:ntok], src_fn(c), AF.Square, rd_fn(c), ["sq2_%d" % (c % 2)])
                        tt("dve", ssum2[:, 0:ntok], ssum2[:, 0:ntok], sq[:, 0:ntok], ALU.add, ["ssum2", "sq2_%d" % (c % 2)], ["ssum2"])
                return f

            def norm_fin():
                assert ms.done()
                for (so, sn) in ([(0, 512)] + ([(512, ntok - 512)] if ntok > 512 else [])):
                    b_ = nbC()
                    mm(ps[b_][:, 0:sn], ones, ssum2[:, so:so + sn], True, True, ["ones", "ssum2"], ["ps%d" % b_])
                    act(sd2[:, so:so + sn], ps[b_][:, 0:sn], AF.Ln, ["ps%d" % b_], ["sd2"], scale=1.0 / D, bias=EPS)
                act(rstd2[:, 0:ntok], sd2[:, 0:ntok], AF.Exp, ["sd2"], ["rstd2"], scale=-0.5)

            for c in range(8):
                items.append(norm_chunk(lambda c_: mix3[:, c_, 0:ntok], lambda c_: [MIXT(c_)], c))
            items.append(norm_fin)

            def x1_chunk(c):
                def f():
                    tm = tm2[c % 2]
                    tt("dve", tm[:, 0:ntok], mix3[:, c, 0:ntok], rstd2[:, 0:ntok], ALU.mult, [MIXT(c), "rstd2"], ["tm2_%d" % (c % 2)])
                    stt(xt13[:, c, 0:ntok], tm[:, 0:ntok], GG1[:, c:c + 1], xt13[:, c, 0:ntok], ALU.mult, ALU.add,
                        ["tm2_%d" % (c % 2), "der", "xt1_%d" % c], ["x1_%d" % c])
                return f
            for c in range(8):
                items.append(x1_chunk(c))

            def dma1():
                P.dma("pool", oT3[:, :, 512 * tn:512 * tn + 512], xt13[:, :, co:co + 512], reads=["x1_%d" % c for c in range(8)],
                      writes=["outt%d" % tn], slot="out1")
            items.append(dma1)
            for c in range(8):
                items.append(norm_chunk(lambda c_: xt13[:, c_, 0:ntok], lambda c_: ["x1_%d" % c_], c))
            items.append(norm_fin)

            def hf_chunk(c):
                def f():
                    tm = tm2[c % 2]
                    tt("dve", tm[:, 0:ntok], xt13[:, c, 0:ntok], rstd2[:, 0:ntok], ALU.mult, ["x1_%d" % c, "rstd2"], ["tm2_%d" % (c % 2)])
                    extra = C1S_OLD if (par == 1 and tn == 1 and c == 0) else []
                    act(hf3[:, c, 0:ntok], tm[:, 0:ntok], AF.Identity, ["tm2_%d" % (c % 2), "der", "modfm"], ["hf%d_%d" % (par, c)] + extra,
                        bias=MOD(3, c, 0), scale=A2[:, c:c + 1])
                return f
            for c in range(8):
                items.append(hf_chunk(c))
            return items

        for it_ in chain_items(0):
            it_()

        for tl in range(4):
            r0, ntok, tok0, subs = geom(tl)
            par = tl % 2
            hf3 = hf3s[par]
            HF = ["hf%d_%d" % (par, c) for c in range(8)]
            co = (8 * tl - r0) * 64
            s0 = r0 - (8 * tl - 1)
            if tl == 3:
                for gi in range(4):
                    memset(gpad[gi].rearrange("p (r c) -> p r c", c=66)[:, 9, :], 0.0, ["gpad%d" % gi])
            pend = chain_items(tl + 1) if tl + 1 < 4 else []

            def pump(k):
                for _ in range(k):
                    if pend:
                        pend.pop(0)()

            def ffn_front(p):
                st = []
                for i in range(2):
                    j = 2 * p + i
                    q = j % 4
                    wi = wup_ctr[0] % 4
                    wup_ctr[0] += 1
                    P.dma("sp", wupb[wi], wupS[j], reads=["wc_all"], writes=["wup%d" % wi], slot="wup%d" % wi)
                    gp3 = gpad[q].rearrange("p (r c) -> p r c", c=66)
                    wg = wupb[wi][:, 0:1024]
                    row = s0
                    for (so, sn) in subs:
                        b = nbF()
                        for k in range(8):
                            mm(ps[b][:, 0:sn], wg[:, k * 128:(k + 1) * 128], hf3[:, k, so:so + sn], k == 0, k == 7,
                               ["wup%d" % wi] + HF, ["ps%d" % b])
                        nr = sn // 64
                        act(gp3[:, row:row + nr, 1:65], ps[b][:, 0:sn].rearrange("p (r c) -> p r c", c=64), AF.Identity,
                            ["ps%d" % b], ["gpad%d" % q])
                        row += nr
                    acc3 = accb[q].rearrange("p (r c) -> p r c", c=64)
                    act(acc3, gp3[:, 0:8, 0:64], AF.Identity, ["gpad%d" % q, "vecs"], ["acc%d" % q],
                        bias=V("ffn_conv_b", j), scale=V("ffn_conv_w", 0 * 24 + j))
                    st.append((j, q, wi, gp3, acc3))
                return st

            def ffn_taps(st, taps):
                for tap in taps:
                    dy, dx = tap // 3, tap % 3
                    for (j, q, wi, gp3, acc3) in st:
                        stt(acc3, gp3[:, dy:dy + 8, dx:dx + 64], V("ffn_conv_w", tap * 24 + j), acc3, ALU.mult, ALU.add,
                            ["gpad%d" % q, "acc%d" % q, "vecs"], ["acc%d" % q])

            def ffn_u(st):
                out = []
                for (j, q, wi, gp3, acc3) in st:
                    wu = wupb[wi][:, 1024:2048]
                    bu = nbF()
                    for k in range(8):
                        mm(ps[bu][:, :], wu[:, k * 128:(k + 1) * 128], hf3[:, k, co:co + 512], k == 0, k == 7,
                           ["wup%d" % wi] + HF, ["ps%d" % bu])
                    out.append(bu)
                return out

            def ffn_gelu(st):
                for (j, q, wi, gp3, acc3) in st:
                    act(accb[q], accb[q], AF.Gelu_apprx_tanh, ["acc%d" % q], ["acc%d" % q])

            def ffn_mult(st, bus):
                for (j, q, wi, gp3, acc3), bu in zip(st, bus):
                    tt("dve", a3[:, j, :], accb[q], ps[bu][:, :], ALU.mult, ["acc%d" % q, "ps%d" % bu], ["a%d" % j])

            prev = None
            for p in range(12):
                per = -(-len(pend) // (12 - p)) if pend else 0
                k1 = per // 3
                k2 = per // 3
                k3 = per - k1 - k2
                st = ffn_front(p)
                pump(k1)
                if prev is not None:
                    ffn_gelu(prev[0])
                ffn_taps(st, range(1, 2))
                if prev is not None:
                    ffn_mult(*prev)
                ffn_taps(st, range(2, 5))
                pump(k2)
                ffn_taps(st, range(5, 9))
                bus = ffn_u(st)
                pump(k3)
                prev = (st, bus)
            ffn_gelu(prev[0])
            ffn_mult(*prev)
            pump(len(pend))
            AT = ["a%d" % j for j in range(24)]
            for oc in range(8):
                wi = wdn_ctr[0] % 2
                wdn_ctr[0] += 1
                P.dma("sp", wdnb[wi], wdnS[oc], reads=["wc_all"], writes=["wdn%d" % wi], slot="wdn%d" % wi)
                b = nbC()
                for k in range(24):
                    mm(ps[b][:, :], wdnb[wi][:, k * 128:(k + 1) * 128], a3[:, k, :], k == 0, k == 23, ["wdn%d" % wi] + AT, ["ps%d" % b])
                act(f3[:, oc, :], ps[b][:, :], AF.Identity, ["ps%d" % b], [FT(oc)] + ([MIXT(q) for q in range(8)] if oc == 0 else []))
                if oc == 0:
                    act(ssum2[:, 0:512], f3[:, oc, :], AF.Square, [FT(oc)], ["ssum2"])
                else:
                    sq = sq2[oc % 2]
                    act(sq[:, 0:512], f3[:, oc, :], AF.Square, [FT(oc)], ["sq2_%d" % (oc % 2)])
                    tt("dve", ssum2[:, 0:512], ssum2[:, 0:512], sq[:, 0:512], ALU.add, ["ssum2", "sq2_%d" % (oc % 2)], ["ssum2"])
            bss = nbC()
            mm(ps[bss][:, :], ones, ssum2[:, 0:512], True, True, ["ones", "ssum2"], ["ps%d" % bss])
            act(sd2[:, 0:512], ps[bss][:, :], AF.Ln, ["ps%d" % bss], ["sd2"], scale=1.0 / D, bias=EPS)
            act(rstd2[:, 0:512], sd2[:, 0:512], AF.Exp, ["sd2"], ["rstd2"], scale=-0.5)
            for oc in range(8):
                P.op("dve", lambda e, oc=oc: e.scalar_tensor_tensor(out=f3[:, oc, :], in0=f3[:, oc, :], scalar=GG2[:, oc:oc + 1],
                                                                    in1=rstd2[:, 0:512], op0=ALU.mult, op1=ALU.mult),
                     [FT(oc), "rstd2", "der"], [FT(oc)])
            P.dma("pool", oT3[:, :, 512 * tl:512 * tl + 512], f3, reads=[FT(oc) for oc in range(8)], writes=["outt%d" % tl],
                  slot="out2", accum_op=ALU.add)
        P.emit()
    return nc, dbg_out


def _fm(v, nch):
    v = np.asarray(v, np.float32)
    lead = v.shape[:-1]
    r = v.reshape(lead + (nch, 128))
    r = np.moveaxis(r, -1, 0)
    return np.ascontiguousarray(r.reshape(128, -1))


def _wtile(w, kch, och):
    w = np.asarray(w, np.float32)
    r = w.reshape(kch, 128, och, 128).transpose(2, 1, 0, 3)
    return np.ascontiguousarray(r.reshape(och, 128, kch * 128))


def _window_counts():
    out = np.zeros((4, 96), np.float32)
    for gi, w in enumerate(POOL_WINDOWS):
        def cnt1(n):
            pos = np.arange(n)
            lo = np.clip(pos - w // 2, 0, n)
            hi = np.clip(pos + w - w // 2, 0, n)
            return (hi - lo).astype(np.float32)
        out[gi, 0:32] = cnt1(32)
        out[gi, 32:96] = cnt1(64)
    return out.reshape(1, 384)


def prep_inputs(x, c, ctx, c_ctx, w_mod, b_mod, g_pre_mix, g_post_mix, g_pre_ffn, g_post_ffn,
                w_in, pool_w, pool_scale, lru_conv_w, lru_conv_b, lru_wa, lru_ba, lru_wx, lru_bx,
                lru_lambda, w_proj_pool, w_proj_lru, w_out, w_up, ffn_conv_w, ffn_conv_b, w_down):
    f = lambda a: np.asarray(a, np.float32)
    vec_parts = {
        "g_pre_mix": _fm(f(g_pre_mix)[0], 8), "g_post_mix": _fm(f(g_post_mix)[0], 8),
        "g_pre_ffn": _fm(f(g_pre_ffn)[0], 8), "g_post_ffn": _fm(f(g_post_ffn)[0], 8),
        "pool_scale": _fm(f(pool_scale)[0], 8), "lru_conv_w": _fm(f(lru_conv_w)[0], 8),
        "lru_conv_b": _fm(f(lru_conv_b)[0], 8), "lru_ba": _fm(f(lru_ba)[0], 8), "lru_bx": _fm(f(lru_bx)[0], 8),
        "lru_lambda": _fm(f(lru_lambda)[0], 8), "ffn_conv_w": _fm(f(ffn_conv_w)[0].reshape(9, 3072), 24),
        "ffn_conv_b": _fm(f(ffn_conv_b)[0], 24), "b_mod": _fm(f(b_mod)[0], 48),
    }
    vecs = np.ascontiguousarray(np.concatenate([vec_parts[n] for n, _ in _VEC_SPEC], axis=1))
    assert vecs.shape == (128, NV)
    wmod_h = np.ascontiguousarray(f(w_mod)[0].reshape(8, 128, 12, 512).transpose(2, 1, 0, 3).reshape(12, 128, 4096))
    win_h = _wtile(f(w_in)[0], 8, 40)
    wpp_h = _wtile(f(w_proj_pool)[0], 8, 8)
    wpl_h = _wtile(f(w_proj_lru)[0], 8, 8)
    wout_h = _wtile(f(w_out)[0], 8, 8)
    wup_t = _wtile(f(w_up)[0], 8, 48)
    wup_h = np.ascontiguousarray(np.concatenate([wup_t[0:24], wup_t[24:48]], axis=2))
    wdown_h = _wtile(f(w_down)[0], 24, 8)
    pw = f(pool_w)[0].reshape(4, 2, 128, 2, 128).transpose(2, 0, 3, 1, 4)
    poolw_h = np.ascontiguousarray(pw.reshape(128, 2048))
    lw = np.stack([f(lru_wa)[0], f(lru_wx)[0]], axis=0)
    lruw_h = np.ascontiguousarray(lw.transpose(3, 0, 1, 2, 4).reshape(128, 4096))
    shared = {"vecs": vecs, "ident": np.eye(128, dtype=np.float32), "cnt": _window_counts(), "wmod": wmod_h,
              "win": win_h, "wpp": wpp_h, "wpl": wpl_h, "wout": wout_h, "wup": wup_h, "wdown": wdown_h,
              "poolw": poolw_h, "lruw": lruw_h}
    xf, cf, ctxf, ccf = f(x), f(c), f(ctx), f(c_ctx)
    in_maps = []
    for b in range(NCORES):
        m = dict(shared)
        m["xT"] = np.ascontiguousarray(xf[b].T)
        m["ctxT"] = np.ascontiguousarray(ctxf[b].T)
        cc2 = np.stack([cf[b], ccf], axis=0)
        m["cc"] = np.ascontiguousarray(cc2.reshape(2, 8, 128).transpose(2, 1, 0).reshape(128, 16))
        in_maps.append(m)
    return in_maps


_CACHE = {}


def kernel(**inputs):
    in_maps = prep_inputs(**inputs)
    if "nc" not in _CACHE:
        _CACHE["nc"] = build_program()[0]
    nc = _CACHE["nc"]
    res = run_bass_kernel_spmd(nc, in_maps, core_ids=list(range(NCORES)))
    out = np.stack([np.asarray(r["outT"], np.float32).T for r in res.results], axis=0)
    return np.ascontiguousarray(out.astype(np.float32))
```

```python
import numpy as np
from contextlib import ExitStack
import concourse.bass as bass
import concourse.mybir as mybir
from concourse.bass_utils import run_bass_kernel_spmd

F32 = mybir.dt.float32
BF16 = mybir.dt.bfloat16
RELAX_DVE = False
AF = mybir.ActivationFunctionType
ALU = mybir.AluOpType

NCORES = 8
D = 1024
T = 2048
CT = 256
TT = T + CT
NCH = 8
EPS = 1e-6
POOL_WINDOWS = (2, 4, 8, 16)
SEGS = [(0, 256)] + [(256 + 512 * i, 512) for i in range(4)]

_VEC_SPEC = [("g_pre_mix", 8), ("g_post_mix", 8), ("g_pre_ffn", 8), ("g_post_ffn", 8), ("pool_scale", 8),
             ("lru_conv_w", 32), ("lru_conv_b", 8), ("lru_ba", 16), ("lru_bx", 16), ("lru_lambda", 16),
             ("ffn_conv_w", 216), ("ffn_conv_b", 24), ("b_mod", 48)]
VOFF = {}
_o = 0
for _n, _c in _VEC_SPEC:
    VOFF[_n] = _o
    _o += _c
NV = _o


class _Op:
    __slots__ = ("eng", "fn", "deps", "needs_inc", "sem", "val", "is_dma", "slot", "pos")

    def __init__(self, eng, fn, is_dma=False, slot=None):
        self.eng = eng
        self.fn = fn
        self.deps = []
        self.needs_inc = False
        self.sem = None
        self.val = None
        self.is_dma = is_dma
        self.slot = slot
        self.pos = -1


class Prog:
    ENGS = ("pe", "act", "dve", "pool", "sp")

    def __init__(self, nc):
        self.nc = nc
        self.q = {e: [] for e in self.ENGS}
        self.last_w = {}
        self.readers = {}
        self.all_ops = []

    def _track(self, op, reads, writes):
        deps = {}
        for t in reads:
            w = self.last_w.get(t)
            if w is not None:
                deps[id(w)] = w
        for t in writes:
            w = self.last_w.get(t)
            if w is not None:
                deps[id(w)] = w
            for r in self.readers.get(t, {}).values():
                deps[id(r)] = r
        for d in deps.values():
            if d is op:
                continue
            if (not d.is_dma) and (not op.is_dma) and d.eng == "pe" and op.eng == "pe":
                continue
            if RELAX_DVE and (not d.is_dma) and (not op.is_dma) and d.eng == "dve" and op.eng == "dve" and op.pos - d.pos >= 2:
                continue
            d.needs_inc = True
            op.deps.append(d)
        for t in writes:
            self.last_w[t] = op
            self.readers[t] = {}
        for t in reads:
            key = ("dma", op.slot) if op.is_dma else op.eng
            self.readers.setdefault(t, {})[key] = op

    def op(self, eng, fn, reads=(), writes=()):
        o = _Op(eng, fn)
        o.pos = len(self.q[eng])
        self._track(o, reads, writes)
        self.q[eng].append(o)
        self.all_ops.append(o)
        return o

    def dma(self, eng, out, in_, reads=(), writes=(), slot=None, **kw):
        o = _Op(eng, None, is_dma=True, slot=slot)
        o.fn = lambda e, s, o_=out, i_=in_, kw_=kw: e.dma_start(out=o_, in_=i_, **kw_).then_inc(s, 16)
        self._track(o, reads, writes)
        o.needs_inc = True
        self.q[eng].append(o)
        self.all_ops.append(o)
        return o

    def emit(self, final_wait_eng="sp"):
        nc = self.nc
        with ExitStack() as es:
            esem = {e: es.enter_context(nc.semaphore("sem_" + e)) for e in self.ENGS}
            slot_names = sorted({o.slot for o in self.all_ops if o.is_dma})
            ssem = {s: es.enter_context(nc.semaphore("dsem_" + s)) for s in slot_names}
            cnt = {e: 0 for e in self.ENGS}
            for e in self.ENGS:
                for o in self.q[e]:
                    if (not o.is_dma) and o.needs_inc:
                        cnt[e] += 1
                        o.sem, o.val = esem[e], cnt[e]
            scnt = {s: 0 for s in slot_names}
            for o in self.all_ops:
                if o.is_dma:
                    scnt[o.slot] += 16
                    o.sem, o.val = ssem[o.slot], scnt[o.slot]
            block = es.enter_context(nc.Block())
            final = [(ssem[s], scnt[s]) for s in slot_names]

            def run(e):
                def body(eng):
                    waited = {}
                    for o in self.q[e]:
                        for d in o.deps:
                            k = id(d.sem)
                            if waited.get(k, 0) < d.val:
                                eng.wait_ge(d.sem, d.val)
                                waited[k] = d.val
                        if o.is_dma:
                            o.fn(eng, o.sem)
                        else:
                            ins = o.fn(eng)
                            if o.needs_inc:
                                ins.then_inc(o.sem, 1)
                    if e == final_wait_eng:
                        for s, v in final:
                            if v > 0:
                                eng.wait_ge(s, v)
                return body

            block.tensor(run("pe"))
            block.scalar(run("act"))
            block.vector(run("dve"))
            block.gpsimd(run("pool"))
            block.sync(run("sp"))


def build_program(stop=None, dbg=()):
    nc = bass.Bass("TRN2", target_bir_lowering=False)
    dr = {}

    def din(name, shape, dt=F32):
        dr[name] = nc.dram_tensor(name, shape, dt, kind="ExternalInput").ap()

    din("xT", [D, T])
    din("ctxT", [D, CT])
    din("cc", [128, 16])
    din("vecs", [128, NV])
    din("ident", [128, 128])
    din("cnt", [1, 384])
    din("wmod", [12, 128, 4096])
    din("win", [40, 128, 1024])
    din("wpp", [8, 128, 1024])
    din("wpl", [8, 128, 1024])
    din("wout", [8, 128, 1024])
    din("wup", [24, 128, 2048])
    din("wdown", [8, 128, 3072])
    din("poolw", [128, 2048])
    din("lruw", [128, 4096])
    outT = nc.dram_tensor("outT", [D, T], F32, kind="ExternalOutput").ap()
    wupS = nc.dram_tensor("wupS", [24, 128, 2048], BF16, kind="Internal").ap()
    wdnS = nc.dram_tensor("wdnS", [8, 128, 3072], BF16, kind="Internal").ap()
    woutS = nc.dram_tensor("woutS", [8, 128, 1024], BF16, kind="Internal").ap()
    dbg_out = {}

    es = ExitStack()
    with es:
        AW = 53200
        arena = es.enter_context(nc.sbuf_tensor("arena", [128, AW], F32))
        ps = [es.enter_context(nc.psum_tensor("ps%d" % i, [128, 512], F32)) for i in range(8)]
        P = Prog(nc)

        def fv(off, n):
            assert off % 4 == 0 and off + 4 * n <= AW * 4, (off, n)
            return arena[:, off // 4: off // 4 + n]

        def bv(off, n):
            assert off % 4 == 0 and n % 2 == 0 and off + 2 * n <= AW * 4, (off, n)
            return arena[:, off // 4: off // 4 + n // 2].bitcast(BF16)

        class Carve:
            def __init__(self, base, size):
                self.base, self.end, self.cur = base, base + size, base

            def f(self, n):
                a = fv(self.cur, n)
                self.cur += 4 * n
                self.cur = (self.cur + 63) // 64 * 64
                assert self.cur <= self.end, ("carve overflow", self.cur, self.end)
                return a

            def b(self, n):
                a = bv(self.cur, n)
                self.cur += 2 * n
                self.cur = (self.cur + 63) // 64 * 64
                assert self.cur <= self.end, ("carve overflow", self.cur, self.end)
                return a

        KB = 1024
        R_H = (0, 36 * KB)
        R_Y = (36 * KB, 64 * KB)
        R_S = (100 * KB, 84 * KB)
        R_W = (184 * KB, AW * 4 - 184 * KB)

        bank_ctr = [0]

        def nextbank():
            b = bank_ctr[0] % 8
            bank_ctr[0] += 1
            return b

        def mm(out, lhsT, rhs, start, stop, reads, writes):
            P.op("pe", lambda e: e.matmul(out, lhsT, rhs, start=start, stop=stop), reads, writes)

        def act(out, in_, func, reads, writes, bias=None, scale=None):
            kw = {}
            if bias is not None:
                kw["bias"] = bias
            if scale is not None:
                kw["scale"] = scale
            P.op("act", lambda e: e.activation(out=out, in_=in_, func=func, **kw), reads, writes)

        def tt(eng, out, in0, in1, op, reads, writes):
            P.op(eng, lambda e: e.tensor_tensor(out=out, in0=in0, in1=in1, op=op), reads, writes)

        def ts(eng, out, in0, s1, op0, reads, writes, s2=None, op1=None):
            if op1 is None:
                P.op(eng, lambda e: e.tensor_scalar(out=out, in0=in0, scalar1=s1, scalar2=None, op0=op0), reads, writes)
            else:
                P.op(eng, lambda e: e.tensor_scalar(out=out, in0=in0, scalar1=s1, scalar2=s2, op0=op0, op1=op1), reads, writes)

        def stt(out, in0, scalar, in1, op0, op1, reads, writes):
            P.op("dve", lambda e: e.scalar_tensor_tensor(out=out, in0=in0, scalar=scalar, in1=in1, op0=op0, op1=op1), reads, writes)

        def memset(ap, val, writes):
            P.op("pool", lambda e: e.memset(ap, val), (), writes)

        def dump(name, ap, dt=F32):
            shape = [ap.shape[0], int(np.prod(ap.shape[1:]))]
            t = nc.dram_tensor("dbg_" + name, shape, dt, kind="ExternalOutput").ap()
            dbg_out[name] = t
            return t

        cw = Carve(*R_W)
        vecs = cw.f(NV)
        ident = cw.f(128)
        ones = cw.f(128)
        identb = cw.b(128)
        ccs = cw.f(16)
        ssil = cw.f(16)
        modfm = cw.f(96)
        der = cw.f(64)
        lrud = cw.f(64)
        cw2 = cw.f(32)
        cn1 = cw.f(384)
        W_DYN = cw.cur

        def V(name, i=0):
            o = VOFF[name] + i
            return vecs[:, o:o + 1]

        def Vs(name, i0, n):
            o = VOFF[name] + i0
            return vecs[:, o:o + n]

        P.dma("sp", vecs, dr["vecs"][:, :], writes=["vecs"], slot="c_vecs")
        P.dma("sp", ident, dr["ident"][:, :], writes=["ident"], slot="c_ident")
        P.dma("sp", ccs, dr["cc"][:, :], writes=["cc"], slot="c_cc")
        memset(ones, 1.0, ["ones"])
        P.op("dve", lambda e: e.tensor_copy(out=identb, in_=ident), ["ident"], ["identb"])
        act(ssil, ccs, AF.Silu, ["cc"], ["ssil"])
        s3 = ssil.rearrange("p (k r) -> p k r", r=2)

        cs = Carve(*R_S)
        wmb = [cs.f(4096), cs.f(4096)]
        modrow = cs.f(6144)
        xa4 = cs.f(8 * 512)
        rstdA = cs.f(TT)
        cy = Carve(*R_Y)
        xa = [cy.f(8 * 256), cy.f(8 * 512), cy.f(8 * 512), cy.f(8 * 512), xa4]
        sqb = [cy.f(512), cy.f(512)]
        ssumA = cy.f(512)
        sdb = cy.f(512)
        xT3 = dr["xT"].rearrange("(c p) t -> p c t", p=128)
        cT3 = dr["ctxT"].rearrange("(c p) t -> p c t", p=128)
        xa3 = [xa[si].rearrange("p (c t) -> p c t", t=SEGS[si][1]) for si in range(5)]

        def load_x(si):
            o, n = SEGS[si]
            src = cT3[:, :, :] if si == 0 else xT3[:, :, o - 256:o - 256 + n]
            P.dma("sp", xa3[si], src, writes=["xa%d" % si], slot="xa%d" % si)

        def stats(si):
            o, n = SEGS[si]
            tk = "xa%d" % si
            for c in range(8):
                if c == 0:
                    act(ssumA[:, 0:n], xa3[si][:, c, :], AF.Square, [tk], ["ssumA"])
                else:
                    sq = sqb[c % 2]
                    act(sq[:, 0:n], xa3[si][:, c, :], AF.Square, [tk], ["sq%d" % (c % 2)])
                    tt("dve", ssumA[:, 0:n], ssumA[:, 0:n], sq[:, 0:n], ALU.add, ["ssumA", "sq%d" % (c % 2)], ["ssumA"])
            b = nextbank()
            mm(ps[b][:, 0:n], ones, ssumA[:, 0:n], True, True, ["ones", "ssumA"], ["ps%d" % b])
            act(sdb[:, 0:n], ps[b][:, 0:n], AF.Ln, ["ps%d" % b], ["sd"], scale=1.0 / D, bias=EPS)
            act(rstdA[:, o:o + n], sdb[:, 0:n], AF.Exp, ["sd"], ["rstdA%d" % si], scale=-0.5)
            for c in range(8):
                tt("dve", xa3[si][:, c, :], xa3[si][:, c, :], rstdA[:, o:o + n], ALU.mult, [tk, "rstdA%d" % si], ["xn%d_%d" % (si, c)])

        xsched = {1: 0, 2: 1, 4: 2, 6: 3, 8: 4}
        ssched = {3: 0, 5: 1, 7: 2, 9: 3, 11: 4}
        for blk in range(12):
            buf = wmb[blk % 2]
            tk = "wm%d" % (blk % 2)
            P.dma("sp", buf, dr["wmod"][blk], writes=[tk], slot=tk)
            if blk in xsched:
                load_x(xsched[blk])
            b = nextbank()
            for k in range(8):
                mm(ps[b][0:2, 0:512], s3[:, k, :], buf[:, k * 512:(k + 1) * 512], k == 0, k == 7,
                   ["ssil", tk], ["ps%d" % b])
            act(modrow[0:2, blk * 512:(blk + 1) * 512], ps[b][0:2, 0:512], AF.Identity, ["ps%d" % b], ["modrow%d" % blk])
            if blk in ssched:
                stats(ssched[blk])
        b = nextbank()
        for oc in range(48):
            mm(ps[b][:, 2 * oc:2 * oc + 2], modrow[0:2, oc * 128:(oc + 1) * 128], ident[0:2, 0:2], True, True,
               ["modrow%d" % (oc // 4), "ident"], ["ps%d" % b])
        act(modfm, ps[b][:, 0:96], AF.Identity, ["ps%d" % b], ["modfm"])
        mod3 = modfm.rearrange("p (c r) -> p c r", r=2)
        for r in range(2):
            tt("dve", mod3[:, :, r], mod3[:, :, r], Vs("b_mod", 0, 48), ALU.add, ["modfm", "vecs"], ["modfm"])

        def MOD(which, c, r=0):
            return mod3[:, which * 8 + c, r:r + 1]

        A1 = der[:, 0:8]
        A1c = der[:, 8:16]
        GG1 = der[:, 16:24]
        A2 = der[:, 24:32]
        GG2 = der[:, 32:40]
        stt(A1, mod3[:, 8:16, 0], 1.0, Vs("g_pre_mix", 0, 8), ALU.add, ALU.mult, ["modfm", "vecs"], ["der"])
        stt(A1c, mod3[:, 8:16, 1], 1.0, Vs("g_pre_mix", 0, 8), ALU.add, ALU.mult, ["modfm", "vecs"], ["der"])
        tt("dve", GG1, mod3[:, 16:24, 0], Vs("g_post_mix", 0, 8), ALU.mult, ["modfm", "vecs"], ["der"])
        stt(A2, mod3[:, 32:40, 0], 1.0, Vs("g_pre_ffn", 0, 8), ALU.add, ALU.mult, ["modfm", "vecs"], ["der"])
        tt("dve", GG2, mod3[:, 40:48, 0], Vs("g_post_ffn", 0, 8), ALU.mult, ["modfm", "vecs"], ["der"])
        lam = Vs("lru_lambda", 0, 16)
        le = lrud[:, 0:16]
        lsp = lrud[:, 16:32]
        ls1 = lrud[:, 32:48]
        act(le, lam, AF.Exp, ["vecs"], ["le"], scale=-1.0)
        act(lsp, le, AF.Ln, ["le"], ["lsp"], bias=1.0)
        ts("dve", ls1, lsp, -4.0, ALU.mult, ["lsp"], ["hs1"])
        hs1 = ls1
        hbias = cw2
        ts("dve", hbias, Vs("lru_ba", 0, 32), 0.5, ALU.mult, ["vecs"], ["hbias"])

        ch = Carve(*R_H)
        h = ch.b(8 * TT)
        h3 = h.rearrange("p (c t) -> p c t", t=TT)
        for si, (o, n) in enumerate(SEGS):
            for c in range(8):
                if si == 0:
                    sc_, bi_ = A1c[:, c:c + 1], MOD(0, c, 1)
                else:
                    sc_, bi_ = A1[:, c:c + 1], MOD(0, c, 0)
                act(h3[:, c, o:o + n], xa3[si][:, c, :], AF.Identity, ["xn%d_%d" % (si, c), "der", "modfm"], ["h%d_%d" % (c, si)],
                    bias=bi_, scale=sc_)
        HSEG = lambda si: ["h%d_%d" % (c, si) for c in range(8)]

        if "h" in dbg:
            P.dma("sp", dump("h", h, BF16)[:, :], h, reads=[t for si in range(5) for t in HSEG(si)], slot="dbg_h")
            P.dma("sp", dump("modfm", modfm)[:, :], modfm, reads=["modfm"], slot="dbg_m")
        if stop == "A":
            P.emit()
            return nc, dbg_out

        cy = Carve(*R_Y)
        ypool = cy.b(8 * T)
        ylru = cy.b(8 * T)
        ypool3 = ypool.rearrange("p (c t) -> p c t", t=T)
        ylru3 = ylru.rearrange("p (c t) -> p c t", t=T)
        YTOK = ["xa%d" % si for si in range(4)] + ["xn%d_%d" % (si, c) for si in range(4) for c in range(8)] + ["sq0", "sq1", "sd", "ssumA"]
        y_first = [True]

        cwd = Carve(W_DYN, R_W[0] + R_W[1] - W_DYN)
        lruw = cwd.b(4096)
        poolw = cwd.b(2048)
        NWIN = 3
        winb = [cwd.b(1024) for _ in range(NWIN)]
        win_ctr = [0]
        P.dma("pool", lruw[:, 0:2048], dr["lruw"][:, 0:2048], writes=["lruw"], slot="w_lruw")
        P.dma("pool", lruw[:, 2048:4096], dr["lruw"][:, 2048:4096], writes=["lruw"], slot="w_lruw")
        P.dma("pool", poolw, dr["poolw"][:, :], writes=["poolw"], slot="w_poolw")

        def load_win(oc):
            i = win_ctr[0] % NWIN
            win_ctr[0] += 1
            P.dma("pool", winb[i], dr["win"][oc], writes=["win%d" % i], slot="win%d" % i)
            return winb[i], "win%d" % i

        def proj_seg(wt, wtk, si, b):
            o, n = SEGS[si]
            for k in range(8):
                mm(ps[b][:, 0:n], wt[:, k * 128:(k + 1) * 128], h3[:, k, o:o + n], k == 0, k == 7,
                   [wtk] + HSEG(si), ["ps%d" % b])

        cast_plan = []
        for oc in range(8):
            cast_plan.append((woutS[oc], dr["wout"][oc]))
        for j in range(24):
            cast_plan.append((wupS[j], dr["wup"][j]))
        for oc in range(8):
            cast_plan.append((wdnS[oc][:, 0:2048], dr["wdown"][oc][:, 0:2048]))
            cast_plan.append((wdnS[oc][:, 2048:3072], dr["wdown"][oc][:, 2048:3072]))
        cast_ctr = [0]

        def issue_casts(k):
            for _ in range(k):
                i = cast_ctr[0]
                if i >= len(cast_plan):
                    return
                cast_ctr[0] += 1
                dst, src = cast_plan[i]
                wr = ["wc%d" % i] + (["wc_all"] if i == len(cast_plan) - 1 else [])
                P.dma("pool", dst, src, writes=wr, slot="wcast")

        STOK_A = (["wm0", "wm1"] + ["modrow%d" % i for i in range(12)] + ["xa4"] + ["xn4_%d" % c for c in range(8)]
                  + ["rstdA%d" % si for si in range(5)])
        cs = Carve(*R_S)
        cntb = cs.f(T)
        PQ = [[cs.f(47 * 79), cs.f(47 * 79)] for _ in range(2)]
        dbf = [cs.b(T), cs.b(T)]
        PQTOK = ["pq%d_%d" % (cl, i) for cl in range(2) for i in range(2)]
        cnt3 = cntb.rearrange("p (r c) -> p r c", c=64)
        first_S = [True]
        nb_ctr = [0]

        def nextB():
            b = 4 + nb_ctr[0] % 4
            nb_ctr[0] += 1
            return b

        def proj_cl0(g_):
            wt_, wtk_ = load_win(2 * g_)
            for t in range(4):
                proj_seg(wt_, wtk_, 1 + t, t)
            return (wt_, wtk_)

        pre_cl0 = proj_cl0(0)
        P.dma("sp", cn1, dr["cnt"][0:1, :].partition_broadcast(128), writes=["cn1"], slot="c_cn1")
        P.op("dve", lambda e: e.reciprocal(out=cn1, in_=cn1), ["cn1"], ["cn1"])
        for g, w in enumerate(POOL_WINDOWS):
            hw = w // 2
            Hp, Wp = 32 + w - 1, 64 + w - 1
            extra = STOK_A if first_S[0] else []
            first_S[0] = False
            ir = cn1[:, g * 96: g * 96 + 32].unsqueeze(2).broadcast_to([128, 32, 64])
            ic = cn1[:, g * 96 + 32: g * 96 + 96].unsqueeze(1).broadcast_to([128, 32, 64])
            tt("dve", cnt3, ir, ic, ALU.mult, ["cn1"], ["cnt"] + extra)
            views = []
            for cl in range(2):
                v = [PQ[cl][i][:, 0:Hp * Wp].rearrange("p (r c) -> p r c", c=Wp) for i in range(2)]
                views.append(v)
                Pv, Qv = v
                memset(Pv[:, 0:hw, :], 0.0, ["pq%d_0" % cl] + extra)
                if hw > 1:
                    memset(Pv[:, hw + 32:Hp, :], 0.0, ["pq%d_0" % cl])
                memset(Pv[:, hw:hw + 32, 0:hw], 0.0, ["pq%d_0" % cl])
                if hw > 1:
                    memset(Pv[:, hw:hw + 32, hw + 64:Wp], 0.0, ["pq%d_0" % cl])
                if w in (2, 8):
                    memset(Qv[:, 0:hw, 0:64], 0.0, ["pq%d_1" % cl])
                    if hw > 1:
                        memset(Qv[:, hw + 32:Hp, 0:64], 0.0, ["pq%d_1" % cl])
            wts = [pre_cl0]
            for t in range(4):
                act(views[0][0][:, hw + 8 * t: hw + 8 * t + 8, hw:hw + 64], ps[t][:, :].rearrange("p (r c) -> p r c", c=64),
                    AF.Identity, ["ps%d" % t], ["pq0_0"])
            wt, wtk = load_win(2 * g + 1)
            wts.append((wt, wtk))
            for t in range(4):
                b = nextB()
                proj_seg(wt, wtk, 1 + t, b)
                act(views[1][0][:, hw + 8 * t: hw + 8 * t + 8, hw:hw + 64], ps[b][:, :].rearrange("p (r c) -> p r c", c=64),
                    AF.Identity, ["ps%d" % b], ["pq1_0"])
            if g + 1 < 4:
                pre_cl0 = proj_cl0(g + 1)
            state = [dict(cur=0, ln=Wp, rows=Hp) for _ in range(2)]
            k = 1
            while k < w:
                for cl in range(2):
                    st = state[cl]
                    src, dst = views[cl][st["cur"]], views[cl][1 - st["cur"]]
                    nl = st["ln"] - k
                    tt("dve", dst[:, hw:hw + 32, 0:nl], src[:, hw:hw + 32, 0:nl], src[:, hw:hw + 32, k:k + nl], ALU.add,
                       ["pq%d_%d" % (cl, st["cur"])], ["pq%d_%d" % (cl, 1 - st["cur"])])
                    st["cur"], st["ln"] = 1 - st["cur"], nl
                k *= 2
            k = 1
            while k < w:
                for cl in range(2):
                    st = state[cl]
                    src, dst = views[cl][st["cur"]], views[cl][1 - st["cur"]]
                    nr = st["rows"] - k
                    tt("dve", dst[:, 0:nr, 0:64], src[:, 0:nr, 0:64], src[:, k:k + nr, 0:64], ALU.add,
                       ["pq%d_%d" % (cl, st["cur"])], ["pq%d_%d" % (cl, 1 - st["cur"])])
                    st["cur"], st["rows"] = 1 - st["cur"], nr
                k *= 2
            for cl in range(2):
                st = state[cl]
                assert st["ln"] == 64 and st["rows"] == 32
                src, oth = views[cl][st["cur"]], views[cl][1 - st["cur"]]
                tt("dve", oth[:, 0:32, 0:64], src[:, 0:32, 0:64], cnt3, ALU.mult, ["pq%d_%d" % (cl, st["cur"]), "cnt"],
                   ["pq%d_%d" % (cl, 1 - st["cur"])])
                wt, wtk = wts[cl]
                for t in range(4):
                    b = nextB()
                    proj_seg(wt, wtk, 1 + t, b)
                    tt("dve", dbf[cl][:, 512 * t:512 * t + 512].rearrange("p (r c) -> p r c", c=64), oth[:, 8 * t:8 * t + 8, 0:64],
                       ps[b][:, :].rearrange("p (r c) -> p r c", c=64), ALU.subtract, ["pq%d_%d" % (cl, 1 - st["cur"]), "ps%d" % b],
                       ["dbf%d_%d" % (cl, t)])
            for ocl in range(2):
                oc = 2 * g + ocl
                for t in range(4):
                    b = nextB()
                    for k in range(2):
                        idx = ((g * 2 + ocl) * 2 + k) * 128
                        mm(ps[b][:, :], poolw[:, idx:idx + 128], dbf[k][:, 512 * t:512 * t + 512], k == 0, k == 1,
                           ["poolw", "dbf%d_%d" % (k, t)], ["ps%d" % b])
                    extra = YTOK if y_first[0] else []
                    y_first[0] = False
                    act(ypool3[:, oc, 512 * t:512 * t + 512], ps[b][:, :], AF.Identity, ["ps%d" % b, "vecs"],
                        ["ypool%d_%d" % (oc, t)] + extra, scale=V("pool_scale", oc))
            issue_casts(5)
        if "ypool" in dbg:
            P.dma("sp", dump("ypool", ypool, BF16)[:, :], ypool,
                  reads=["ypool%d_%d" % (oc, t) for oc in range(8) for t in range(4)], slot="dbg_yp")
        if stop == "Bp":
            P.emit()
            return nc, dbg_out

        cs = Carve(*R_S)
        UPW = 2320
        LOFF = 264
        upad = cs.b(UPW)
        dgw = cs.b(32 * 128)
        o_m2b = cs.cur
        m2b = cs.f(TT)
        xcb = bv(o_m2b, TT)
        xc = cs.f(TT)
        m2f = cs.f(TT)
        ra = [cs.f(TT), cs.f(TT)]
        ib = [cs.f(TT), cs.f(TT)]
        gel = cs.f(T)
        m2 = [m2f, m2b]
        m2tok = ["m2f", "m2b"]
        POOLTOK = ["cnt"] + PQTOK + ["dbf%d_%d" % (cl, t) for cl in range(2) for t in range(4)]
        memset(upad, 0.0, ["upad"] + POOLTOK)
        for k in range(4):
            for n in range(8):
                i = k * 8 + n
                ts("dve", dgw[:, i * 128:(i + 1) * 128], ident, V("lru_conv_w", i), ALU.mult, ["ident", "vecs"], ["dgw"] + (POOLTOK if i == 0 else []))
        RA = lambda d: ["ra%d_%d" % (d, si) for si in range(5)]
        IB = lambda d: ["ib%d_%d" % (d, si) for si in range(5)]
        win_next = load_win(8 + 0)
        for n in range(8):
            wt, wtk = win_next
            for si, (o, nn) in enumerate(SEGS):
                b = nextbank()
                proj_seg(wt, wtk, si, b)
                po = 2 + o if si == 0 else LOFF + (o - 256)
                act(upad[:, po:po + nn], ps[b][:, 0:nn], AF.Identity, ["ps%d" % b], ["upad"])
            for si, (o, nn) in enumerate(SEGS):
                base = o if si == 0 else LOFF - 2 + (o - 256)
                b = nextbank()
                for k in range(4):
                    i = k * 8 + n
                    mm(ps[b][:, 0:nn], dgw[:, i * 128:(i + 1) * 128], upad[:, base + k:base + k + nn], k == 0, k == 3,
                       ["dgw", "upad"], ["ps%d" % b])
                act(xc[:, o:o + nn], ps[b][:, 0:nn], AF.Identity, ["ps%d" % b, "vecs"], ["xc"], bias=V("lru_conv_b", n))
            P.op("dve", lambda e: e.tensor_copy(out=xcb, in_=xc), ["xc"], ["m2b"])
            for dr_ in range(2):
                for kind, dst, tkf in ((0, ra[dr_], RA(dr_)), (1, ib[dr_], IB(dr_))):
                    widx = ((kind * 2 + dr_) * 8 + n) * 128
                    hb_ = hbias[:, kind * 16 + dr_ * 8 + n: kind * 16 + dr_ * 8 + n + 1]
                    for si, (o, nn) in enumerate(SEGS):
                        b = nextbank()
                        mm(ps[b][:, 0:nn], lruw[:, widx:widx + 128], xcb[:, o:o + nn], True, True, ["lruw", "m2b"], ["ps%d" % b])
                        act(dst[:, o:o + nn], ps[b][:, 0:nn], AF.Tanh, ["ps%d" % b, "hbias"], [tkf[si]], bias=hb_, scale=0.5)
                hs = hs1[:, dr_ * 8 + n: dr_ * 8 + n + 1]
                act(ra[dr_], ra[dr_], AF.Exp, RA(dr_) + ["hs1"], RA(dr_), bias=hs, scale=hs)
                act(m2[dr_], ra[dr_], AF.Square, RA(dr_), [m2tok[dr_]])
                act(m2[dr_], m2[dr_], AF.Sqrt, [m2tok[dr_]], [m2tok[dr_]], scale=-0.25, bias=0.25)
            for dr_ in range(2):
                stt(ib[dr_], ib[dr_], 1.0, xc, ALU.add, ALU.mult, IB(dr_) + ["xc"], IB(dr_))
                tt("dve", ib[dr_], ib[dr_], m2[dr_], ALU.mult, IB(dr_) + [m2tok[dr_]], IB(dr_))
                if dr_ == 0:
                    P.op("dve", lambda e: e.tensor_tensor_scan(out=ib[0], data0=ra[0], data1=ib[0], initial=0.0, op0=ALU.mult, op1=ALU.add),
                         RA(0) + IB(0), IB(0))
                else:
                    P.op("dve", lambda e: e.tensor_tensor_scan(out=ib[1][:, 0:256][:, ::-1], data0=ra[1][:, 0:256][:, ::-1],
                                                                data1=ib[1][:, 0:256][:, ::-1], initial=0.0, op0=ALU.mult, op1=ALU.add),
                         RA(1) + IB(1), IB(1))
                    P.op("dve", lambda e: e.tensor_tensor_scan(out=ib[1][:, 256:TT][:, ::-1], data0=ra[1][:, 256:TT][:, ::-1],
                                                                data1=ib[1][:, 256:TT][:, ::-1], initial=ib[1][:, 0:1], op0=ALU.mult, op1=ALU.add),
                         RA(1) + IB(1), IB(1))
            wt, wtk = load_win(16 + n)
            if n + 1 < 8:
                win_next = load_win(8 + n + 1)
            issue_casts(5)
            for t in range(4):
                b = nextbank()
                proj_seg(wt, wtk, 1 + t, b)
                act(gel[:, 512 * t:512 * t + 512], ps[b][:, :], AF.Gelu_apprx_tanh, ["ps%d" % b], ["gel%d" % t])
            GEL = ["gel%d" % t for t in range(4)]
            tt("dve", ib[0][:, 256:TT], ib[0][:, 256:TT], ib[1][:, 256:TT], ALU.add, IB(0) + IB(1), IB(0))
            tt("dve", ylru3[:, n, :], ib[0][:, 256:TT], gel, ALU.mult, IB(0) + GEL, ["ylru%d" % n] + (YTOK if n == 0 else []))
        if "ylru" in dbg:
            P.dma("sp", dump("ylru", ylru, BF16)[:, :], ylru, reads=["ylru%d" % n for n in range(8)], slot="dbg_yl")
        if stop == "B":
            P.emit()
            return nc, dbg_out

        LRUTOK = ["upad", "xc", "m2f", "m2b", "dgw"] + RA(0) + RA(1) + IB(0) + IB(1) + GEL
        cs = Carve(*R_S)
        mbuf = cs.b(8 * T)
        m3 = mbuf.rearrange("p (c t) -> p c t", t=T)
        S_C2 = cs.cur
        sgb = [[cs.f(512), cs.f(512)] for _ in range(2)]
        t12 = [[cs.f(512), cs.f(512)] for _ in range(2)]
        cwd = Carve(W_DYN, R_W[0] + R_W[1] - W_DYN)
        c1w = [[cwd.b(1024) for _ in range(4)] for _ in range(2)]
        W_OLD = ["lruw", "poolw"] + ["win%d" % i for i in range(NWIN)]
        first_c1 = [True]
        YP = lambda t: ["ypool%d_%d" % (k, t) for k in range(8)]
        YL = ["ylru%d" % k for k in range(8)]
        it = 0
        for oc in range(8):
            sl = oc % 2
            srcs = [dr["wpp"][oc], dr["win"][24 + oc], dr["wpl"][oc], dr["win"][32 + oc]]
            for i in range(4):
                extra = W_OLD if first_c1[0] else []
                first_c1[0] = False
                P.dma("pool", c1w[sl][i], srcs[i], writes=["c1w%d_%d" % (sl, i)] + extra, slot="c1w%d_%d" % (sl, i))
            for t in range(4):
                bb = [nextbank() for _ in range(4)]
                for k in range(8):
                    mm(ps[bb[0]][:, :], c1w[sl][0][:, k * 128:(k + 1) * 128], ypool3[:, k, 512 * t:512 * t + 512], k == 0, k == 7,
                       ["c1w%d_0" % sl] + YP(t), ["ps%d" % bb[0]])
                for k in range(8):
                    mm(ps[bb[1]][:, :], c1w[sl][1][:, k * 128:(k + 1) * 128], h3[:, k, 256 + 512 * t:256 + 512 * t + 512], k == 0, k == 7,
                       ["c1w%d_1" % sl] + HSEG(1 + t), ["ps%d" % bb[1]])
                for k in range(8):
                    mm(ps[bb[2]][:, :], c1w[sl][2][:, k * 128:(k + 1) * 128], ylru3[:, k, 512 * t:512 * t + 512], k == 0, k == 7,
                       ["c1w%d_2" % sl] + YL, ["ps%d" % bb[2]])
                for k in range(8):
                    mm(ps[bb[3]][:, :], c1w[sl][3][:, k * 128:(k + 1) * 128], h3[:, k, 256 + 512 * t:256 + 512 * t + 512], k == 0, k == 7,
                       ["c1w%d_3" % sl] + HSEG(1 + t), ["ps%d" % bb[3]])
                p = it % 2
                it += 1
                extra = LRUTOK if (oc == 0 and t == 0) else []
                act(sgb[p][0], ps[bb[1]][:, :], AF.Sigmoid, ["ps%d" % bb[1]], ["sg%d_0" % p] + extra)
                act(sgb[p][1], ps[bb[3]][:, :], AF.Sigmoid, ["ps%d" % bb[3]], ["sg%d_1" % p])
                tt("dve", t12[p][0], ps[bb[0]][:, :], sgb[p][0], ALU.mult, ["ps%d" % bb[0], "sg%d_0" % p], ["t12%d_0" % p])
                tt("dve", t12[p][1], ps[bb[2]][:, :], sgb[p][1], ALU.mult, ["ps%d" % bb[2], "sg%d_1" % p], ["t12%d_1" % p])
                tt("dve", m3[:, oc, 512 * t:512 * t + 512], t12[p][0], t12[p][1], ALU.add, ["t12%d_0" % p, "t12%d_1" % p],
                   ["m%d_%d" % (oc, t)])
        MT = lambda tl: ["m%d_%d" % (k, t) for k in range(8) for t in tl]
        if "m" in dbg:
            P.dma("sp", dump("m", mbuf, BF16)[:, :], mbuf, reads=MT(range(4)), slot="dbg_mm")
        if stop == "C1":
            P.emit()
            return nc, dbg_out

        cs = Carve(S_C2, R_S[0] + R_S[1] - S_C2)
        sq2 = [cs.f(640), cs.f(640)]
        tm2 = [cs.f(640), cs.f(640)]
        sd2 = cs.f(640)
        rstd2 = cs.f(640)
        gpad = [cs.f(660) for _ in range(4)]
        ssum2 = cs.f(640)
        hfB = cs.b(8 * 640)
        sq4 = [sq2[0], sq2[1], cs.f(640), cs.f(640)]
        tm4 = [tm2[0], tm2[1], cs.f(640), cs.f(640)]
        HY_OLD = [t for si in range(5) for t in HSEG(si)] + [t for tl in range(4) for t in YP(tl)] + YL
        C1S_OLD = ["sg%d_%d" % (p, i) for p in range(2) for i in range(2)] + ["t12%d_%d" % (p, i) for p in range(2) for i in range(2)]
        cb_ = Carve(R_H[0], R_H[1] + R_Y[1])
        xt1 = cb_.f(8 * 640)
        mixb = cb_.f(8 * 640)
        hfA = cb_.b(8 * 640)
        abuf = cb_.b(24 * 512)
        wupb = [cb_.b(2048) for _ in range(3)]
        wdnb = [cb_.b(3072) for _ in range(2)]
        xt13 = xt1.rearrange("p (c t) -> p c t", t=640)
        mix3 = mixb.rearrange("p (c t) -> p c t", t=640)
        f3 = mixb[:, 0:8 * 512].rearrange("p (c t) -> p c t", t=512)
        hf3s = [hfA.rearrange("p (c t) -> p c t", t=640), hfB.rearrange("p (c t) -> p c t", t=640)]
        a3 = abuf.rearrange("p (c t) -> p c t", t=512)
        cwd = Carve(W_DYN, R_W[0] + R_W[1] - W_DYN)
        accb = [cwd.f(512) for _ in range(4)]
        wupb.append(cwd.b(2048))
        woutb = [cwd.b(1024) for _ in range(3)]
        wo_ctr = [0]
        wo_cur = [0]
        C1W_OLD = ["c1w%d_%d" % (s_, i) for s_ in range(2) for i in range(4)]
        oT3 = outT.rearrange("(c p) t -> p c t", p=128)
        memset(gpad[0], 0.0, ["gpad0"] + C1S_OLD)
        for gi in range(1, 4):
            memset(gpad[gi], 0.0, ["gpad%d" % gi])
        wup_ctr = [0]
        wdn_ctr = [0]
        nbf_ctr = [0]
        nbc_ctr = [0]

        def nbF():
            b_ = nbf_ctr[0] % 6
            nbf_ctr[0] += 1
            return b_

        def nbC():
            b_ = 6 + nbc_ctr[0] % 2
            nbc_ctr[0] += 1
            return b_

        def geom(tl):
            r0 = max(8 * tl - 1, 0)
            r1 = min(8 * tl + 9, 32)
            ntok = (r1 - r0) * 64
            return r0, ntok, r0 * 64, [(0, 512)] + [(512, ntok - 512)]

        MIXT = lambda c: "mix_%d" % c
        FT = lambda c: "f_%d" % c
        first_mix = [True]

        def mix_units(tl):
            r0, ntok, tok0, subs = geom(tl)
            mtl = sorted({min(3, (tok0 + so) // 512) for so, sn in subs} | {min(3, (tok0 + so + sn - 1) // 512) for so, sn in subs})
            units = []
            for oc in range(8):
                for (so, sn) in subs:
                    st_ = {}

                    def pe_fn(oc=oc, so=so, sn=sn, st_=st_):
                        if so == 0:
                            wi = wo_ctr[0] % 3
                            wo_ctr[0] += 1
                            extra = (LRUTOK + C1W_OLD) if first_mix[0] else []
                            P.dma("sp", woutb[wi], woutS[oc], reads=["wc_all"], writes=["wo%d" % wi] + extra, slot="wo%d" % wi)
                            wo_cur[0] = wi
                        wi = wo_cur[0]
                        b_ = nbC()
                        st_["b"] = b_
                        for k in range(8):
                            mm(ps[b_][:, 0:sn], woutb[wi][:, k * 128:(k + 1) * 128],
                               m3[:, k, tok0 + so:tok0 + so + sn], k == 0, k == 7, ["wo%d" % wi] + MT(mtl), ["ps%d" % b_])

                    def act_fn(oc=oc, so=so, sn=sn, st_=st_):
                        b_ = st_["b"]
                        extra2 = []
                        if oc == 0 and so == 0:
                            extra2 = [FT(q) for q in range(8)]
                            if first_mix[0]:
                                extra2 = extra2 + C1S_OLD + HY_OLD
                        first_mix[0] = False
                        act(mix3[:, oc, so:so + sn], ps[b_][:, 0:sn], AF.Identity, ["ps%d" % b_], [MIXT(oc)] + extra2)
                    units.append((pe_fn, act_fn))
            return units

        class MixSched:
            def __init__(self, units):
                self.u, self.ipe, self.iact = units, 0, 0

            def step(self):
                n = len(self.u)
                if self.iact < self.ipe and (self.ipe - self.iact >= 2 or self.ipe == n):
                    self.u[self.iact][1]()
                    self.iact += 1
                if self.ipe < n and self.ipe - self.iact < 2:
                    self.u[self.ipe][0]()
                    self.ipe += 1

            def done(self):
                return self.iact == len(self.u)

        first_hy = [True]

        def chain_sched(tn):
            r0, ntok, tok0, subs = geom(tn)
            par = tn % 2
            co = (8 * tn - r0) * 64
            hf3 = hf3s[par]
            sch = {}

            def at(slot, fn):
                sch.setdefault(slot, []).append(fn)

            def xloads():
                for c in range(8):
                    extra = HY_OLD if (first_hy[0] and c == 0) else []
                    P.dma("sp", xt13[:, c, 0:ntok], xT3[:, c, tok0:tok0 + ntok], writes=["xt1_%d" % c, "x1_%d" % c] + extra,
                          slot="xt1_%d" % c)
                first_hy[0] = False
            at(0, xloads)
            ms = MixSched(mix_units(tn))
            for i in range(20):
                at(i * 12 // 20, ms.step)

            def sq_op(src_fn, rd_fn, c):
                def f():
                    if c == 0:
                        act(ssum2[:, 0:ntok], src_fn(c), AF.Square, rd_fn(c), ["ssum2"])
                    else:
                        act(sq4[c % 4][:, 0:ntok], src_fn(c), AF.Square, rd_fn(c), ["sq4_%d" % (c % 4)])
                return f

            def add_op(c):
                def f():
                    tt("dve", ssum2[:, 0:ntok], ssum2[:, 0:ntok], sq4[c % 4][:, 0:ntok], ALU.add, ["ssum2", "sq4_%d" % (c % 4)], ["ssum2"])
                return f

            def norm_fin():
                assert ms.done()
                for (so, sn) in ([(0, 512)] + ([(512, ntok - 512)] if ntok > 512 else [])):
                    b_ = nbC()
                    mm(ps[b_][:, 0:sn], ones, ssum2[:, so:so + sn], True, True, ["ones", "ssum2"], ["ps%d" % b_])
                    act(sd2[:, so:so + sn], ps[b_][:, 0:sn], AF.Ln, ["ps%d" % b_], ["sd2"], scale=1.0 / D, bias=EPS)
                act(rstd2[:, 0:ntok], sd2[:, 0:ntok], AF.Exp, ["sd2"], ["rstd2"], scale=-0.5)

            for c in range(8):
                at(12 + c // 2, sq_op(lambda c_: mix3[:, c_, 0:ntok], lambda c_: [MIXT(c_)], c))
                if c > 0:
                    at(13 + c // 2, add_op(c))
            at(17, norm_fin)

            def x1_chunk(c):
                def f():
                    tm = tm4[c % 4]
                    tt("dve", tm[:, 0:ntok], mix3[:, c, 0:ntok], rstd2[:, 0:ntok], ALU.mult, [MIXT(c), "rstd2"], ["tm4_%d" % (c % 4)])
                    stt(xt13[:, c, 0:ntok], tm[:, 0:ntok], GG1[:, c:c + 1], xt13[:, c, 0:ntok], ALU.mult, ALU.add,
                        ["tm4_%d" % (c % 4), "der", "xt1_%d" % c], ["x1_%d" % c])
                return f
            for c in range(8):
                at(18 + c // 2, x1_chunk(c))
                at(19 + c // 2, sq_op(lambda c_: xt13[:, c_, 0:ntok], lambda c_: ["x1_%d" % c_], c))
                if c > 0:
                    at(20 + c // 2, add_op(c))

            def dma1():
                P.dma("pool", oT3[:, :, 512 * tn:512 * tn + 512], xt13[:, :, co:co + 512], reads=["x1_%d" % c for c in range(8)],
                      writes=["outt%d" % tn], slot="out1")
            at(22, dma1)
            at(24, norm_fin)

            def hf_mult(c):
                def f():
                    tm = tm4[c % 4]
                    tt("dve", tm[:, 0:ntok], xt13[:, c, 0:ntok], rstd2[:, 0:ntok], ALU.mult, ["x1_%d" % c, "rstd2"], ["tm4_%d" % (c % 4)])
                return f

            def hf_aff(c):
                def f():
                    extra = C1S_OLD if (par == 1 and tn == 1 and c == 0) else []
                    act(hf3[:, c, 0:ntok], tm4[c % 4][:, 0:ntok], AF.Identity, ["tm4_%d" % (c % 4), "der", "modfm"],
                        ["hf%d_%d" % (par, c)] + extra, bias=MOD(3, c, 0), scale=A2[:, c:c + 1])
                return f
            for c in range(8):
                at(26 + c // 2, hf_mult(c))
                at(27 + c // 2, hf_aff(c))
            return sch

        def run_sched(sch, lo, hi):
            for slot in sorted(k for k in sch if lo <= k < hi):
                for fn in sch.pop(slot):
                    fn()

        run_sched(chain_sched(0), 0, 10 ** 6)

        for tl in range(4):
            r0, ntok, tok0, subs = geom(tl)
            par = tl % 2
            hf3 = hf3s[par]
            HF = ["hf%d_%d" % (par, c) for c in range(8)]
            co = (8 * tl - r0) * 64
            s0 = r0 - (8 * tl - 1)
            if tl == 3:
                for gi in range(4):
                    memset(gpad[gi].rearrange("p (r c) -> p r c", c=66)[:, 9, :], 0.0, ["gpad%d" % gi])
            pend = chain_sched(tl + 1) if tl + 1 < 4 else {}

            def ffn_front(p):
                st = []
                for i in range(2):
                    j = 2 * p + i
                    q = j % 4
                    wi = wup_ctr[0] % 4
                    wup_ctr[0] += 1
                    P.dma("sp", wupb[wi], wupS[j], reads=["wc_all"], writes=["wup%d" % wi], slot="wup%d" % wi)
                    gp3 = gpad[q].rearrange("p (r c) -> p r c", c=66)
                    wg = wupb[wi][:, 0:1024]
                    row = s0
                    for (so, sn) in subs:
                        b = nbF()
                        for k in range(8):
                            mm(ps[b][:, 0:sn], wg[:, k * 128:(k + 1) * 128], hf3[:, k, so:so + sn], k == 0, k == 7,
                               ["wup%d" % wi] + HF, ["ps%d" % b])
                        nr = sn // 64
                        act(gp3[:, row:row + nr, 1:65], ps[b][:, 0:sn].rearrange("p (r c) -> p r c", c=64), AF.Identity,
                            ["ps%d" % b], ["gpad%d" % q])
                        row += nr
                    acc3 = accb[q].rearrange("p (r c) -> p r c", c=64)
                    act(acc3, gp3[:, 0:8, 0:64], AF.Identity, ["gpad%d" % q, "vecs"], ["acc%d" % q],
                        bias=V("ffn_conv_b", j), scale=V("ffn_conv_w", 0 * 24 + j))
                    st.append((j, q, wi, gp3, acc3))
                return st

            def ffn_taps(st, taps):
                for tap in taps:
                    dy, dx = tap // 3, tap % 3
                    for (j, q, wi, gp3, acc3) in st:
                        stt(acc3, gp3[:, dy:dy + 8, dx:dx + 64], V("ffn_conv_w", tap * 24 + j), acc3, ALU.mult, ALU.add,
                            ["gpad%d" % q, "acc%d" % q, "vecs"], ["acc%d" % q])

            def ffn_u(st):
                out = []
                for (j, q, wi, gp3, acc3) in st:
                    wu = wupb[wi][:, 1024:2048]
                    bu = nbF()
                    for k in range(8):
                        mm(ps[bu][:, :], wu[:, k * 128:(k + 1) * 128], hf3[:, k, co:co + 512], k == 0, k == 7,
                           ["wup%d" % wi] + HF, ["ps%d" % bu])
                    out.append(bu)
                return out

            def ffn_gelu(st):
                for (j, q, wi, gp3, acc3) in st:
                    act(accb[q], accb[q], AF.Gelu_apprx_tanh, ["acc%d" % q], ["acc%d" % q])

            def ffn_mult(st, bus):
                for (j, q, wi, gp3, acc3), bu in zip(st, bus):
                    tt("dve", a3[:, j, :], accb[q], ps[bu][:, :], ALU.mult, ["acc%d" % q, "ps%d" % bu], ["a%d" % j])

            prev = None
            for p in range(12):
                st = ffn_front(p)
                run_sched(pend, 3 * p, 3 * p + 1)
                if prev is not None:
                    ffn_gelu(prev[0])
                ffn_taps(st, range(1, 2))
                if prev is not None:
                    ffn_mult(*prev)
                ffn_taps(st, range(2, 5))
                run_sched(pend, 3 * p + 1, 3 * p + 2)
                ffn_taps(st, range(5, 9))
                bus = ffn_u(st)
                run_sched(pend, 3 * p + 2, 3 * p + 3)
                prev = (st, bus)
            ffn_gelu(prev[0])
            ffn_mult(*prev)
            run_sched(pend, 0, 10 ** 6)
            AT = ["a%d" % j for j in range(24)]
            for oc in range(8):
                wi = wdn_ctr[0] % 2
                wdn_ctr[0] += 1
                P.dma("sp", wdnb[wi], wdnS[oc], reads=["wc_all"], writes=["wdn%d" % wi], slot="wdn%d" % wi)
                b = nbC()
                for k in range(24):
                    mm(ps[b][:, :], wdnb[wi][:, k * 128:(k + 1) * 128], a3[:, k, :], k == 0, k == 23, ["wdn%d" % wi] + AT, ["ps%d" % b])
                act(f3[:, oc, :], ps[b][:, :], AF.Identity, ["ps%d" % b], [FT(oc)] + ([MIXT(q) for q in range(8)] if oc == 0 else []))
                if oc == 0:
                    act(ssum2[:, 0:512], f3[:, oc, :], AF.Square, [FT(oc)], ["ssum2"])
                else:
                    sq = sq4[oc % 4]
                    act(sq[:, 0:512], f3[:, oc, :], AF.Square, [FT(oc)], ["sq4_%d" % (oc % 4)])
                    tt("dve", ssum2[:, 0:512], ssum2[:, 0:512], sq[:, 0:512], ALU.add, ["ssum2", "sq4_%d" % (oc % 4)], ["ssum2"])
            bss = nbC()
            mm(ps[bss][:, :], ones, ssum2[:, 0:512], True, True, ["ones", "ssum2"], ["ps%d" % bss])
            act(sd2[:, 0:512], ps[bss][:, :], AF.Ln, ["ps%d" % bss], ["sd2"], scale=1.0 / D, bias=EPS)
            act(rstd2[:, 0:512], sd2[:, 0:512], AF.Exp, ["sd2"], ["rstd2"], scale=-0.5)
            for oc in range(8):
                P.op("dve", lambda e, oc=oc: e.scalar_tensor_tensor(out=f3[:, oc, :], in0=f3[:, oc, :], scalar=GG2[:, oc:oc + 1],
                                                                    in1=rstd2[:, 0:512], op0=ALU.mult, op1=ALU.mult),
                     [FT(oc), "rstd2", "der"], [FT(oc)])
            P.dma("pool", oT3[:, :, 512 * tl:512 * tl + 512], f3, reads=[FT(oc) for oc in range(8)], writes=["outt%d" % tl],
                  slot="out2", accum_op=ALU.add)
        P.emit()
    return nc, dbg_out


def _fm(v, nch):
    v = np.asarray(v, np.float32)
    lead = v.shape[:-1]
    r = v.reshape(lead + (nch, 128))
    r = np.moveaxis(r, -1, 0)
    return np.ascontiguousarray(r.reshape(128, -1))


def _wtile(w, kch, och):
    w = np.asarray(w, np.float32)
    r = w.reshape(kch, 128, och, 128).transpose(2, 1, 0, 3)
    return np.ascontiguousarray(r.reshape(och, 128, kch * 128))


def _window_counts():
    out = np.zeros((4, 96), np.float32)
    for gi, w in enumerate(POOL_WINDOWS):
        def cnt1(n):
            pos = np.arange(n)
            lo = np.clip(pos - w // 2, 0, n)
            hi = np.clip(pos + w - w // 2, 0, n)
            return (hi - lo).astype(np.float32)
        out[gi, 0:32] = cnt1(32)
        out[gi, 32:96] = cnt1(64)
    return out.reshape(1, 384)


def prep_inputs(x, c, ctx, c_ctx, w_mod, b_mod, g_pre_mix, g_post_mix, g_pre_ffn, g_post_ffn,
                w_in, pool_w, pool_scale, lru_conv_w, lru_conv_b, lru_wa, lru_ba, lru_wx, lru_bx,
                lru_lambda, w_proj_pool, w_proj_lru, w_out, w_up, ffn_conv_w, ffn_conv_b, w_down):
    f = lambda a: np.asarray(a, np.float32)
    vec_parts = {
        "g_pre_mix": _fm(f(g_pre_mix)[0], 8), "g_post_mix": _fm(f(g_post_mix)[0], 8),
        "g_pre_ffn": _fm(f(g_pre_ffn)[0], 8), "g_post_ffn": _fm(f(g_post_ffn)[0], 8),
        "pool_scale": _fm(f(pool_scale)[0], 8), "lru_conv_w": _fm(f(lru_conv_w)[0], 8),
        "lru_conv_b": _fm(f(lru_conv_b)[0], 8), "lru_ba": _fm(f(lru_ba)[0], 8), "lru_bx": _fm(f(lru_bx)[0], 8),
        "lru_lambda": _fm(f(lru_lambda)[0], 8), "ffn_conv_w": _fm(f(ffn_conv_w)[0].reshape(9, 3072), 24),
        "ffn_conv_b": _fm(f(ffn_conv_b)[0], 24), "b_mod": _fm(f(b_mod)[0], 48),
    }
    vecs = np.ascontiguousarray(np.concatenate([vec_parts[n] for n, _ in _VEC_SPEC], axis=1))
    assert vecs.shape == (128, NV)
    wmod_h = np.ascontiguousarray(f(w_mod)[0].reshape(8, 128, 12, 512).transpose(2, 1, 0, 3).reshape(12, 128, 4096))
    win_h = _wtile(f(w_in)[0], 8, 40)
    wpp_h = _wtile(f(w_proj_pool)[0], 8, 8)
    wpl_h = _wtile(f(w_proj_lru)[0], 8, 8)
    wout_h = _wtile(f(w_out)[0], 8, 8)
    wup_t = _wtile(f(w_up)[0], 8, 48)
    wup_h = np.ascontiguousarray(np.concatenate([wup_t[0:24], wup_t[24:48]], axis=2))
    wdown_h = _wtile(f(w_down)[0], 24, 8)
    pw = f(pool_w)[0].reshape(4, 2, 128, 2, 128).transpose(2, 0, 3, 1, 4)
    poolw_h = np.ascontiguousarray(pw.reshape(128, 2048))
    lw = np.stack([f(lru_wa)[0], f(lru_wx)[0]], axis=0)
    lruw_h = np.ascontiguousarray(lw.transpose(3, 0, 1, 2, 4).reshape(128, 4096))
    shared = {"vecs": vecs, "ident": np.eye(128, dtype=np.float32), "cnt": _window_counts(), "wmod": wmod_h,
              "win": win_h, "wpp": wpp_h, "wpl": wpl_h, "wout": wout_h, "wup": wup_h, "wdown": wdown_h,
              "poolw": poolw_h, "lruw": lruw_h}
    xf, cf, ctxf, ccf = f(x), f(c), f(ctx), f(c_ctx)
    in_maps = []
    for b in range(NCORES):
        m = dict(shared)
        m["xT"] = np.ascontiguousarray(xf[b].T)
        m["ctxT"] = np.ascontiguousarray(ctxf[b].T)
        cc2 = np.stack([cf[b], ccf], axis=0)
        m["cc"] = np.ascontiguousarray(cc2.reshape(2, 8, 128).transpose(2, 1, 0).reshape(128, 16))
        in_maps.append(m)
    return in_maps


_CACHE = {}


def kernel(**inputs):
    in_maps = prep_inputs(**inputs)
    if "nc" not in _CACHE:
        _CACHE["nc"] = build_program()[0]
    nc = _CACHE["nc"]
    res = run_bass_kernel_spmd(nc, in_maps, core_ids=list(range(NCORES)))
    out = np.stack([np.asarray(r["outT"], np.float32).T for r in res.results], axis=0)
    return np.ascontiguousarray(out.astype(np.float32))
```

```python
import numpy as np
from contextlib import ExitStack
import concourse.bass as bass
import concourse.mybir as mybir
from concourse.bass_utils import run_bass_kernel_spmd

F32 = mybir.dt.float32
BF16 = mybir.dt.bfloat16
RELAX_DVE = False
AF = mybir.ActivationFunctionType
ALU = mybir.AluOpType

NCORES = 8
D = 1024
T = 2048
CT = 256
TT = T + CT
NCH = 8
EPS = 1e-6
POOL_WINDOWS = (2, 4, 8, 16)
SEGS = [(0, 256)] + [(256 + 512 * i, 512) for i in range(4)]

_VEC_SPEC = [("g_pre_mix", 8), ("g_post_mix", 8), ("g_pre_ffn", 8), ("g_post_ffn", 8), ("pool_scale", 8),
             ("lru_conv_w", 32), ("lru_conv_b", 8), ("lru_ba", 16), ("lru_bx", 16), ("lru_lambda", 16),
             ("ffn_conv_w", 216), ("ffn_conv_b", 24), ("b_mod", 48)]
VOFF = {}
_o = 0
for _n, _c in _VEC_SPEC:
    VOFF[_n] = _o
    _o += _c
NV = _o


class _Op:
    __slots__ = ("eng", "fn", "deps", "needs_inc", "sem", "val", "is_dma", "slot", "pos")

    def __init__(self, eng, fn, is_dma=False, slot=None):
        self.eng = eng
        self.fn = fn
        self.deps = []
        self.needs_inc = False
        self.sem = None
        self.val = None
        self.is_dma = is_dma
        self.slot = slot
        self.pos = -1


class Prog:
    ENGS = ("pe", "act", "dve", "pool", "sp")

    def __init__(self, nc):
        self.nc = nc
        self.q = {e: [] for e in self.ENGS}
        self.last_w = {}
        self.readers = {}
        self.all_ops = []

    def _track(self, op, reads, writes):
        deps = {}
        for t in reads:
            w = self.last_w.get(t)
            if w is not None:
                deps[id(w)] = w
        for t in writes:
            w = self.last_w.get(t)
            if w is not None:
                deps[id(w)] = w
            for r in self.readers.get(t, {}).values():
                deps[id(r)] = r
        for d in deps.values():
            if d is op:
                continue
            if (not d.is_dma) and (not op.is_dma) and d.eng == "pe" and op.eng == "pe":
                continue
            if RELAX_DVE and (not d.is_dma) and (not op.is_dma) and d.eng == "dve" and op.eng == "dve" and op.pos - d.pos >= 2:
                continue
            d.needs_inc = True
            op.deps.append(d)
        for t in writes:
            self.last_w[t] = op
            self.readers[t] = {}
        for t in reads:
            key = ("dma", op.slot) if op.is_dma else op.eng
            self.readers.setdefault(t, {})[key] = op

    def op(self, eng, fn, reads=(), writes=()):
        o = _Op(eng, fn)
        o.pos = len(self.q[eng])
        self._track(o, reads, writes)
        self.q[eng].append(o)
        self.all_ops.append(o)
        return o

    def dma(self, eng, out, in_, reads=(), writes=(), slot=None, **kw):
        o = _Op(eng, None, is_dma=True, slot=slot)
        o.fn = lambda e, s, o_=out, i_=in_, kw_=kw: e.dma_start(out=o_, in_=i_, **kw_).then_inc(s, 16)
        self._track(o, reads, writes)
        o.needs_inc = True
        self.q[eng].append(o)
        self.all_ops.append(o)
        return o

    def emit(self, final_wait_eng="sp"):
        nc = self.nc
        with ExitStack() as es:
            esem = {e: es.enter_context(nc.semaphore("sem_" + e)) for e in self.ENGS}
            slot_names = sorted({o.slot for o in self.all_ops if o.is_dma})
            ssem = {s: es.enter_context(nc.semaphore("dsem_" + s)) for s in slot_names}
            cnt = {e: 0 for e in self.ENGS}
            for e in self.ENGS:
                for o in self.q[e]:
                    if (not o.is_dma) and o.needs_inc:
                        cnt[e] += 1
                        o.sem, o.val = esem[e], cnt[e]
            scnt = {s: 0 for s in slot_names}
            for o in self.all_ops:
                if o.is_dma:
                    scnt[o.slot] += 16
                    o.sem, o.val = ssem[o.slot], scnt[o.slot]
            block = es.enter_context(nc.Block())
            final = [(ssem[s], scnt[s]) for s in slot_names]

            def run(e):
                def body(eng):
                    waited = {}
                    for o in self.q[e]:
                        for d in o.deps:
                            k = id(d.sem)
                            if waited.get(k, 0) < d.val:
                                eng.wait_ge(d.sem, d.val)
                                waited[k] = d.val
                        if o.is_dma:
                            o.fn(eng, o.sem)
                        else:
                            ins = o.fn(eng)
                            if o.needs_inc:
                                ins.then_inc(o.sem, 1)
                    if e == final_wait_eng:
                        for s, v in final:
                            if v > 0:
                                eng.wait_ge(s, v)
                return body

            block.tensor(run("pe"))
            block.scalar(run("act"))
            block.vector(run("dve"))
            block.gpsimd(run("pool"))
            block.sync(run("sp"))


def build_program(stop=None, dbg=()):
    nc = bass.Bass("TRN2", target_bir_lowering=False)
    dr = {}

    def din(name, shape, dt=F32):
        dr[name] = nc.dram_tensor(name, shape, dt, kind="ExternalInput").ap()

    din("xT", [D, T])
    din("ctxT", [D, CT])
    din("cc", [128, 16])
    din("vecs", [128, NV])
    din("ident", [128, 128])
    din("cnt", [1, 384])
    din("wmod", [12, 128, 4096])
    din("win", [40, 128, 1024])
    din("wpp", [8, 128, 1024])
    din("wpl", [8, 128, 1024])
    din("wout", [8, 128, 1024])
    din("wup", [24, 128, 2048])
    din("wdown", [8, 128, 3072])
    din("poolw", [128, 2048])
    din("lruw", [128, 4096])
    outT = nc.dram_tensor("outT", [D, T], F32, kind="ExternalOutput").ap()
    wupS = nc.dram_tensor("wupS", [24, 128, 2048], BF16, kind="Internal").ap()
    wdnS = nc.dram_tensor("wdnS", [8, 128, 3072], BF16, kind="Internal").ap()
    woutS = nc.dram_tensor("woutS", [8, 128, 1024], BF16, kind="Internal").ap()
    dbg_out = {}

    es = ExitStack()
    with es:
        AW = 53200
        arena = es.enter_context(nc.sbuf_tensor("arena", [128, AW], F32))
        ps = [es.enter_context(nc.psum_tensor("ps%d" % i, [128, 512], F32)) for i in range(8)]
        P = Prog(nc)

        def fv(off, n):
            assert off % 4 == 0 and off + 4 * n <= AW * 4, (off, n)
            return arena[:, off // 4: off // 4 + n]

        def bv(off, n):
            assert off % 4 == 0 and n % 2 == 0 and off + 2 * n <= AW * 4, (off, n)
            return arena[:, off // 4: off // 4 + n // 2].bitcast(BF16)

        class Carve:
            def __init__(self, base, size):
                self.base, self.end, self.cur = base, base + size, base

            def f(self, n):
                a = fv(self.cur, n)
                self.cur += 4 * n
                self.cur = (self.cur + 63) // 64 * 64
                assert self.cur <= self.end, ("carve overflow", self.cur, self.end)
                return a

            def b(self, n):
                a = bv(self.cur, n)
                self.cur += 2 * n
                self.cur = (self.cur + 63) // 64 * 64
                assert self.cur <= self.end, ("carve overflow", self.cur, self.end)
                return a

        KB = 1024
        R_H = (0, 36 * KB)
        R_Y = (36 * KB, 64 * KB)
        R_S = (100 * KB, 84 * KB)
        R_W = (184 * KB, AW * 4 - 184 * KB)

        bank_ctr = [0]

        def nextbank():
            b = bank_ctr[0] % 8
            bank_ctr[0] += 1
            return b

        def mm(out, lhsT, rhs, start, stop, reads, writes):
            P.op("pe", lambda e: e.matmul(out, lhsT, rhs, start=start, stop=stop), reads, writes)

        def act(out, in_, func, reads, writes, bias=None, scale=None):
            kw = {}
            if bias is not None:
                kw["bias"] = bias
            if scale is not None:
                kw["scale"] = scale
            P.op("act", lambda e: e.activation(out=out, in_=in_, func=func, **kw), reads, writes)

        def tt(eng, out, in0, in1, op, reads, writes):
            P.op(eng, lambda e: e.tensor_tensor(out=out, in0=in0, in1=in1, op=op), reads, writes)

        def ts(eng, out, in0, s1, op0, reads, writes, s2=None, op1=None):
            if op1 is None:
                P.op(eng, lambda e: e.tensor_scalar(out=out, in0=in0, scalar1=s1, scalar2=None, op0=op0), reads, writes)
            else:
                P.op(eng, lambda e: e.tensor_scalar(out=out, in0=in0, scalar1=s1, scalar2=s2, op0=op0, op1=op1), reads, writes)

        def stt(out, in0, scalar, in1, op0, op1, reads, writes):
            P.op("dve", lambda e: e.scalar_tensor_tensor(out=out, in0=in0, scalar=scalar, in1=in1, op0=op0, op1=op1), reads, writes)

        def memset(ap, val, writes):
            P.op("pool", lambda e: e.memset(ap, val), (), writes)

        def dump(name, ap, dt=F32):
            shape = [ap.shape[0], int(np.prod(ap.shape[1:]))]
            t = nc.dram_tensor("dbg_" + name, shape, dt, kind="ExternalOutput").ap()
            dbg_out[name] = t
            return t

        cw = Carve(*R_W)
        vecs = cw.f(NV)
        ident = cw.f(128)
        ones = cw.f(128)
        identb = cw.b(128)
        ccs = cw.f(16)
        ssil = cw.f(16)
        modfm = cw.f(96)
        der = cw.f(64)
        lrud = cw.f(64)
        cw2 = cw.f(32)
        cn1 = cw.f(384)
        W_DYN = cw.cur

        def V(name, i=0):
            o = VOFF[name] + i
            return vecs[:, o:o + 1]

        def Vs(name, i0, n):
            o = VOFF[name] + i0
            return vecs[:, o:o + n]

        P.dma("sp", vecs, dr["vecs"][:, :], writes=["vecs"], slot="c_vecs")
        P.dma("sp", ident, dr["ident"][:, :], writes=["ident"], slot="c_ident")
        P.dma("sp", ccs, dr["cc"][:, :], writes=["cc"], slot="c_cc")
        memset(ones, 1.0, ["ones"])
        P.op("dve", lambda e: e.tensor_copy(out=identb, in_=ident), ["ident"], ["identb"])
        act(ssil, ccs, AF.Silu, ["cc"], ["ssil"])
        s3 = ssil.rearrange("p (k r) -> p k r", r=2)

        cs = Carve(*R_S)
        wmb = [cs.f(4096), cs.f(4096)]
        modrow = cs.f(6144)
        xa4 = cs.f(8 * 512)
        rstdA = cs.f(TT)
        cy = Carve(*R_Y)
        xa = [cy.f(8 * 256), cy.f(8 * 512), cy.f(8 * 512), cy.f(8 * 512), xa4]
        sqb = [cy.f(512), cy.f(512)]
        ssumA = cy.f(512)
        sdb = cy.f(512)
        xT3 = dr["xT"].rearrange("(c p) t -> p c t", p=128)
        cT3 = dr["ctxT"].rearrange("(c p) t -> p c t", p=128)
        xa3 = [xa[si].rearrange("p (c t) -> p c t", t=SEGS[si][1]) for si in range(5)]

        def load_x(si):
            o, n = SEGS[si]
            src = cT3[:, :, :] if si == 0 else xT3[:, :, o - 256:o - 256 + n]
            P.dma("sp", xa3[si], src, writes=["xa%d" % si], slot="xa%d" % si)

        def stats(si):
            o, n = SEGS[si]
            tk = "xa%d" % si
            for c in range(8):
                if c == 0:
                    act(ssumA[:, 0:n], xa3[si][:, c, :], AF.Square, [tk], ["ssumA"])
                else:
                    sq = sqb[c % 2]
                    act(sq[:, 0:n], xa3[si][:, c, :], AF.Square, [tk], ["sq%d" % (c % 2)])
                    tt("dve", ssumA[:, 0:n], ssumA[:, 0:n], sq[:, 0:n], ALU.add, ["ssumA", "sq%d" % (c % 2)], ["ssumA"])
            b = nextbank()
            mm(ps[b][:, 0:n], ones, ssumA[:, 0:n], True, True, ["ones", "ssumA"], ["ps%d" % b])
            act(sdb[:, 0:n], ps[b][:, 0:n], AF.Ln, ["ps%d" % b], ["sd"], scale=1.0 / D, bias=EPS)
            act(rstdA[:, o:o + n], sdb[:, 0:n], AF.Exp, ["sd"], ["rstdA%d" % si], scale=-0.5)
            for c in range(8):
                tt("dve", xa3[si][:, c, :], xa3[si][:, c, :], rstdA[:, o:o + n], ALU.mult, [tk, "rstdA%d" % si], ["xn%d_%d" % (si, c)])

        xsched = {1: 0, 2: 1, 4: 2, 6: 3, 8: 4}
        ssched = {3: 0, 5: 1, 7: 2, 9: 3, 11: 4}
        for blk in range(12):
            buf = wmb[blk % 2]
            tk = "wm%d" % (blk % 2)
            P.dma("sp", buf, dr["wmod"][blk], writes=[tk], slot=tk)
            if blk in xsched:
                load_x(xsched[blk])
            b = nextbank()
            for k in range(8):
                mm(ps[b][0:2, 0:512], s3[:, k, :], buf[:, k * 512:(k + 1) * 512], k == 0, k == 7,
                   ["ssil", tk], ["ps%d" % b])
            act(modrow[0:2, blk * 512:(blk + 1) * 512], ps[b][0:2, 0:512], AF.Identity, ["ps%d" % b], ["modrow%d" % blk])
            if blk in ssched:
                stats(ssched[blk])
        b = nextbank()
        for oc in range(48):
            mm(ps[b][:, 2 * oc:2 * oc + 2], modrow[0:2, oc * 128:(oc + 1) * 128], ident[0:2, 0:2], True, True,
               ["modrow%d" % (oc // 4), "ident"], ["ps%d" % b])
        act(modfm, ps[b][:, 0:96], AF.Identity, ["ps%d" % b], ["modfm"])
        mod3 = modfm.rearrange("p (c r) -> p c r", r=2)
        for r in range(2):
            tt("dve", mod3[:, :, r], mod3[:, :, r], Vs("b_mod", 0, 48), ALU.add, ["modfm", "vecs"], ["modfm"])

        def MOD(which, c, r=0):
            return mod3[:, which * 8 + c, r:r + 1]

        A1 = der[:, 0:8]
        A1c = der[:, 8:16]
        GG1 = der[:, 16:24]
        A2 = der[:, 24:32]
        GG2 = der[:, 32:40]
        stt(A1, mod3[:, 8:16, 0], 1.0, Vs("g_pre_mix", 0, 8), ALU.add, ALU.mult, ["modfm", "vecs"], ["der"])
        stt(A1c, mod3[:, 8:16, 1], 1.0, Vs("g_pre_mix", 0, 8), ALU.add, ALU.mult, ["modfm", "vecs"], ["der"])
        tt("dve", GG1, mod3[:, 16:24, 0], Vs("g_post_mix", 0, 8), ALU.mult, ["modfm", "vecs"], ["der"])
        stt(A2, mod3[:, 32:40, 0], 1.0, Vs("g_pre_ffn", 0, 8), ALU.add, ALU.mult, ["modfm", "vecs"], ["der"])
        tt("dve", GG2, mod3[:, 40:48, 0], Vs("g_post_ffn", 0, 8), ALU.mult, ["modfm", "vecs"], ["der"])
        lam = Vs("lru_lambda", 0, 16)
        le = lrud[:, 0:16]
        lsp = lrud[:, 16:32]
        ls1 = lrud[:, 32:48]
        act(le, lam, AF.Exp, ["vecs"], ["le"], scale=-1.0)
        act(lsp, le, AF.Ln, ["le"], ["lsp"], bias=1.0)
        ts("dve", ls1, lsp, -4.0, ALU.mult, ["lsp"], ["hs1"])
        hs1 = ls1
        hbias = cw2
        ts("dve", hbias, Vs("lru_ba", 0, 32), 0.5, ALU.mult, ["vecs"], ["hbias"])

        ch = Carve(*R_H)
        h = ch.b(8 * TT)
        h3 = h.rearrange("p (c t) -> p c t", t=TT)
        for si, (o, n) in enumerate(SEGS):
            for c in range(8):
                if si == 0:
                    sc_, bi_ = A1c[:, c:c + 1], MOD(0, c, 1)
                else:
                    sc_, bi_ = A1[:, c:c + 1], MOD(0, c, 0)
                act(h3[:, c, o:o + n], xa3[si][:, c, :], AF.Identity, ["xn%d_%d" % (si, c), "der", "modfm"], ["h%d_%d" % (c, si)],
                    bias=bi_, scale=sc_)
        HSEG = lambda si: ["h%d_%d" % (c, si) for c in range(8)]

        if "h" in dbg:
            P.dma("sp", dump("h", h, BF16)[:, :], h, reads=[t for si in range(5) for t in HSEG(si)], slot="dbg_h")
            P.dma("sp", dump("modfm", modfm)[:, :], modfm, reads=["modfm"], slot="dbg_m")
        if stop == "A":
            P.emit()
            return nc, dbg_out

        cy = Carve(*R_Y)
        ypool = cy.b(8 * T)
        ylru = cy.b(8 * T)
        ypool3 = ypool.rearrange("p (c t) -> p c t", t=T)
        ylru3 = ylru.rearrange("p (c t) -> p c t", t=T)
        YTOK = ["xa%d" % si for si in range(4)] + ["xn%d_%d" % (si, c) for si in range(4) for c in range(8)] + ["sq0", "sq1", "sd", "ssumA"]
        y_first = [True]

        cwd = Carve(W_DYN, R_W[0] + R_W[1] - W_DYN)
        lruw = cwd.b(4096)
        poolw = cwd.b(2048)
        NWIN = 3
        winb = [cwd.b(1024) for _ in range(NWIN)]
        win_ctr = [0]
        P.dma("pool", lruw[:, 0:2048], dr["lruw"][:, 0:2048], writes=["lruw"], slot="w_lruw")
        P.dma("pool", lruw[:, 2048:4096], dr["lruw"][:, 2048:4096], writes=["lruw"], slot="w_lruw")
        P.dma("pool", poolw, dr["poolw"][:, :], writes=["poolw"], slot="w_poolw")

        def load_win(oc):
            i = win_ctr[0] % NWIN
            win_ctr[0] += 1
            P.dma("pool", winb[i], dr["win"][oc], writes=["win%d" % i], slot="win%d" % i)
            return winb[i], "win%d" % i

        def proj_seg(wt, wtk, si, b):
            o, n = SEGS[si]
            for k in range(8):
                mm(ps[b][:, 0:n], wt[:, k * 128:(k + 1) * 128], h3[:, k, o:o + n], k == 0, k == 7,
                   [wtk] + HSEG(si), ["ps%d" % b])

        cast_plan = []
        for oc in range(8):
            cast_plan.append((woutS[oc], dr["wout"][oc]))
        for j in range(24):
            cast_plan.append((wupS[j], dr["wup"][j]))
        for oc in range(8):
            cast_plan.append((wdnS[oc][:, 0:2048], dr["wdown"][oc][:, 0:2048]))
            cast_plan.append((wdnS[oc][:, 2048:3072], dr["wdown"][oc][:, 2048:3072]))
        cast_ctr = [0]

        def issue_casts(k):
            for _ in range(k):
                i = cast_ctr[0]
                if i >= len(cast_plan):
                    return
                cast_ctr[0] += 1
                dst, src = cast_plan[i]
                wr = ["wc%d" % i] + (["wc_all"] if i == len(cast_plan) - 1 else [])
                P.dma("pool", dst, src, writes=wr, slot="wcast")

        STOK_A = (["wm0", "wm1"] + ["modrow%d" % i for i in range(12)] + ["xa4"] + ["xn4_%d" % c for c in range(8)]
                  + ["rstdA%d" % si for si in range(5)])
        cs = Carve(*R_S)
        cntb = cs.f(T)
        PQ = [[cs.f(47 * 79), cs.f(47 * 79)] for _ in range(2)]
        dbf = [cs.b(T), cs.b(T)]
        PQTOK = ["pq%d_%d" % (cl, i) for cl in range(2) for i in range(2)]
        cnt3 = cntb.rearrange("p (r c) -> p r c", c=64)
        first_S = [True]
        nb_ctr = [0]

        def nextB():
            b = 4 + nb_ctr[0] % 4
            nb_ctr[0] += 1
            return b

        def proj_cl0(g_):
            wt_, wtk_ = load_win(2 * g_)
            for t in range(4):
                proj_seg(wt_, wtk_, 1 + t, t)
            return (wt_, wtk_)

        pre_cl0 = proj_cl0(0)
        P.dma("sp", cn1, dr["cnt"][0:1, :].partition_broadcast(128), writes=["cn1"], slot="c_cn1")
        P.op("dve", lambda e: e.reciprocal(out=cn1, in_=cn1), ["cn1"], ["cn1"])
        for g, w in enumerate(POOL_WINDOWS):
            hw = w // 2
            Hp, Wp = 32 + w - 1, 64 + w - 1
            extra = STOK_A if first_S[0] else []
            first_S[0] = False
            ir = cn1[:, g * 96: g * 96 + 32].unsqueeze(2).broadcast_to([128, 32, 64])
            ic = cn1[:, g * 96 + 32: g * 96 + 96].unsqueeze(1).broadcast_to([128, 32, 64])
            tt("dve", cnt3, ir, ic, ALU.mult, ["cn1"], ["cnt"] + extra)
            views = []
            for cl in range(2):
                v = [PQ[cl][i][:, 0:Hp * Wp].rearrange("p (r c) -> p r c", c=Wp) for i in range(2)]
                views.append(v)
                Pv, Qv = v
                memset(Pv[:, 0:hw, :], 0.0, ["pq%d_0" % cl] + extra)
                if hw > 1:
                    memset(Pv[:, hw + 32:Hp, :], 0.0, ["pq%d_0" % cl])
                memset(Pv[:, hw:hw + 32, 0:hw], 0.0, ["pq%d_0" % cl])
                if hw > 1:
                    memset(Pv[:, hw:hw + 32, hw + 64:Wp], 0.0, ["pq%d_0" % cl])
                if w in (2, 8):
                    memset(Qv[:, 0:hw, 0:64], 0.0, ["pq%d_1" % cl])
                    if hw > 1:
                        memset(Qv[:, hw + 32:Hp, 0:64], 0.0, ["pq%d_1" % cl])
            wts = [pre_cl0]
            for t in range(4):
                act(views[0][0][:, hw + 8 * t: hw + 8 * t + 8, hw:hw + 64], ps[t][:, :].rearrange("p (r c) -> p r c", c=64),
                    AF.Identity, ["ps%d" % t], ["pq0_0"])
            wt, wtk = load_win(2 * g + 1)
            wts.append((wt, wtk))
            for t in range(4):
                b = nextB()
                proj_seg(wt, wtk, 1 + t, b)
                act(views[1][0][:, hw + 8 * t: hw + 8 * t + 8, hw:hw + 64], ps[b][:, :].rearrange("p (r c) -> p r c", c=64),
                    AF.Identity, ["ps%d" % b], ["pq1_0"])
            if g + 1 < 4:
                pre_cl0 = proj_cl0(g + 1)
            state = [dict(cur=0, ln=Wp, rows=Hp) for _ in range(2)]
            k = 1
            while k < w:
                for cl in range(2):
                    st = state[cl]
                    src, dst = views[cl][st["cur"]], views[cl][1 - st["cur"]]
                    nl = st["ln"] - k
                    tt("dve", dst[:, hw:hw + 32, 0:nl], src[:, hw:hw + 32, 0:nl], src[:, hw:hw + 32, k:k + nl], ALU.add,
                       ["pq%d_%d" % (cl, st["cur"])], ["pq%d_%d" % (cl, 1 - st["cur"])])
                    st["cur"], st["ln"] = 1 - st["cur"], nl
                k *= 2
            k = 1
            while k < w:
                for cl in range(2):
                    st = state[cl]
                    src, dst = views[cl][st["cur"]], views[cl][1 - st["cur"]]
                    nr = st["rows"] - k
                    tt("dve", dst[:, 0:nr, 0:64], src[:, 0:nr, 0:64], src[:, k:k + nr, 0:64], ALU.add,
                       ["pq%d_%d" % (cl, st["cur"])], ["pq%d_%d" % (cl, 1 - st["cur"])])
                    st["cur"], st["rows"] = 1 - st["cur"], nr
                k *= 2
            for cl in range(2):
                st = state[cl]
                assert st["ln"] == 64 and st["rows"] == 32
                src, oth = views[cl][st["cur"]], views[cl][1 - st["cur"]]
                tt("dve", oth[:, 0:32, 0:64], src[:, 0:32, 0:64], cnt3, ALU.mult, ["pq%d_%d" % (cl, st["cur"]), "cnt"],
                   ["pq%d_%d" % (cl, 1 - st["cur"])])
                wt, wtk = wts[cl]
                for t in range(4):
                    b = nextB()
                    proj_seg(wt, wtk, 1 + t, b)
                    tt("dve", dbf[cl][:, 512 * t:512 * t + 512].rearrange("p (r c) -> p r c", c=64), oth[:, 8 * t:8 * t + 8, 0:64],
                       ps[b][:, :].rearrange("p (r c) -> p r c", c=64), ALU.subtract, ["pq%d_%d" % (cl, 1 - st["cur"]), "ps%d" % b],
                       ["dbf%d_%d" % (cl, t)])
            for ocl in range(2):
                oc = 2 * g + ocl
                for t in range(4):
                    b = nextB()
                    for k in range(2):
                        idx = ((g * 2 + ocl) * 2 + k) * 128
                        mm(ps[b][:, :], poolw[:, idx:idx + 128], dbf[k][:, 512 * t:512 * t + 512], k == 0, k == 1,
                           ["poolw", "dbf%d_%d" % (k, t)], ["ps%d" % b])
                    extra = YTOK if y_first[0] else []
                    y_first[0] = False
                    act(ypool3[:, oc, 512 * t:512 * t + 512], ps[b][:, :], AF.Identity, ["ps%d" % b, "vecs"],
                        ["ypool%d_%d" % (oc, t)] + extra, scale=V("pool_scale", oc))
            issue_casts(5)
        if "ypool" in dbg:
            P.dma("sp", dump("ypool", ypool, BF16)[:, :], ypool,
                  reads=["ypool%d_%d" % (oc, t) for oc in range(8) for t in range(4)], slot="dbg_yp")
        if stop == "Bp":
            P.emit()
            return nc, dbg_out

        cs = Carve(*R_S)
        UPW = 2320
        LOFF = 264
        upad = cs.b(UPW)
        dgw = cs.b(32 * 128)
        o_m2b = cs.cur
        m2b = cs.f(TT)
        xcb = bv(o_m2b, TT)
        xc = cs.f(TT)
        m2f = cs.f(TT)
        ra = [cs.f(TT), cs.f(TT)]
        ib = [cs.f(TT), cs.f(TT)]
        gel = cs.f(T)
        m2 = [m2f, m2b]
        m2tok = ["m2f", "m2b"]
        POOLTOK = ["cnt"] + PQTOK + ["dbf%d_%d" % (cl, t) for cl in range(2) for t in range(4)]
        memset(upad, 0.0, ["upad"] + POOLTOK)
        for k in range(4):
            for n in range(8):
                i = k * 8 + n
                ts("dve", dgw[:, i * 128:(i + 1) * 128], ident, V("lru_conv_w", i), ALU.mult, ["ident", "vecs"], ["dgw"] + (POOLTOK if i == 0 else []))
        RA = lambda d: ["ra%d_%d" % (d, si) for si in range(5)]
        IB = lambda d: ["ib%d_%d" % (d, si) for si in range(5)]
        win_next = load_win(8 + 0)
        for n in range(8):
            wt, wtk = win_next
            for si, (o, nn) in enumerate(SEGS):
                b = nextbank()
                proj_seg(wt, wtk, si, b)
                po = 2 + o if si == 0 else LOFF + (o - 256)
                act(upad[:, po:po + nn], ps[b][:, 0:nn], AF.Identity, ["ps%d" % b], ["upad"])
            for si, (o, nn) in enumerate(SEGS):
                base = o if si == 0 else LOFF - 2 + (o - 256)
                b = nextbank()
                for k in range(4):
                    i = k * 8 + n
                    mm(ps[b][:, 0:nn], dgw[:, i * 128:(i + 1) * 128], upad[:, base + k:base + k + nn], k == 0, k == 3,
                       ["dgw", "upad"], ["ps%d" % b])
                act(xc[:, o:o + nn], ps[b][:, 0:nn], AF.Identity, ["ps%d" % b, "vecs"], ["xc"], bias=V("lru_conv_b", n))
            P.op("dve", lambda e: e.tensor_copy(out=xcb, in_=xc), ["xc"], ["m2b"])
            for dr_ in range(2):
                for kind, dst, tkf in ((0, ra[dr_], RA(dr_)), (1, ib[dr_], IB(dr_))):
                    widx = ((kind * 2 + dr_) * 8 + n) * 128
                    hb_ = hbias[:, kind * 16 + dr_ * 8 + n: kind * 16 + dr_ * 8 + n + 1]
                    for si, (o, nn) in enumerate(SEGS):
                        b = nextbank()
                        mm(ps[b][:, 0:nn], lruw[:, widx:widx + 128], xcb[:, o:o + nn], True, True, ["lruw", "m2b"], ["ps%d" % b])
                        act(dst[:, o:o + nn], ps[b][:, 0:nn], AF.Tanh, ["ps%d" % b, "hbias"], [tkf[si]], bias=hb_, scale=0.5)
                hs = hs1[:, dr_ * 8 + n: dr_ * 8 + n + 1]
                act(ra[dr_], ra[dr_], AF.Exp, RA(dr_) + ["hs1"], RA(dr_), bias=hs, scale=hs)
                act(m2[dr_], ra[dr_], AF.Square, RA(dr_), [m2tok[dr_]])
                act(m2[dr_], m2[dr_], AF.Sqrt, [m2tok[dr_]], [m2tok[dr_]], scale=-0.25, bias=0.25)
            for dr_ in range(2):
                stt(ib[dr_], ib[dr_], 1.0, xc, ALU.add, ALU.mult, IB(dr_) + ["xc"], IB(dr_))
                tt("dve", ib[dr_], ib[dr_], m2[dr_], ALU.mult, IB(dr_) + [m2tok[dr_]], IB(dr_))
                if dr_ == 0:
                    P.op("dve", lambda e: e.tensor_tensor_scan(out=ib[0], data0=ra[0], data1=ib[0], initial=0.0, op0=ALU.mult, op1=ALU.add),
                         RA(0) + IB(0), IB(0))
                else:
                    P.op("dve", lambda e: e.tensor_tensor_scan(out=ib[1][:, 0:256][:, ::-1], data0=ra[1][:, 0:256][:, ::-1],
                                                                data1=ib[1][:, 0:256][:, ::-1], initial=0.0, op0=ALU.mult, op1=ALU.add),
                         RA(1) + IB(1), IB(1))
                    P.op("dve", lambda e: e.tensor_tensor_scan(out=ib[1][:, 256:TT][:, ::-1], data0=ra[1][:, 256:TT][:, ::-1],
                                                                data1=ib[1][:, 256:TT][:, ::-1], initial=ib[1][:, 0:1], op0=ALU.mult, op1=ALU.add),
                         RA(1) + IB(1), IB(1))
            wt, wtk = load_win(16 + n)
            if n + 1 < 8:
                win_next = load_win(8 + n + 1)
            issue_casts(5)
            for t in range(4):
                b = nextbank()
                proj_seg(wt, wtk, 1 + t, b)
                act(gel[:, 512 * t:512 * t + 512], ps[b][:, :], AF.Gelu_apprx_tanh, ["ps%d" % b], ["gel%d" % t])
            GEL = ["gel%d" % t for t in range(4)]
            tt("dve", ib[0][:, 256:TT], ib[0][:, 256:TT], ib[1][:, 256:TT], ALU.add, IB(0) + IB(1), IB(0))
            tt("dve", ylru3[:, n, :], ib[0][:, 256:TT], gel, ALU.mult, IB(0) + GEL, ["ylru%d" % n] + (YTOK if n == 0 else []))
        if "ylru" in dbg:
            P.dma("sp", dump("ylru", ylru, BF16)[:, :], ylru, reads=["ylru%d" % n for n in range(8)], slot="dbg_yl")
        if stop == "B":
            P.emit()
            return nc, dbg_out

        LRUTOK = ["upad", "xc", "m2f", "m2b", "dgw"] + RA(0) + RA(1) + IB(0) + IB(1) + GEL
        cs = Carve(*R_S)
        mbuf = cs.b(8 * T)
        m3 = mbuf.rearrange("p (c t) -> p c t", t=T)
        S_C2 = cs.cur
        sgb = [[cs.f(512), cs.f(512)] for _ in range(2)]
        t12 = [[cs.f(512), cs.f(512)] for _ in range(2)]
        cwd = Carve(W_DYN, R_W[0] + R_W[1] - W_DYN)
        c1w = [[cwd.b(1024) for _ in range(4)] for _ in range(2)]
        W_OLD = ["lruw", "poolw"] + ["win%d" % i for i in range(NWIN)]
        first_c1 = [True]
        YP = lambda t: ["ypool%d_%d" % (k, t) for k in range(8)]
        YL = ["ylru%d" % k for k in range(8)]
        it = 0
        for oc in range(8):
            sl = oc % 2
            srcs = [dr["wpp"][oc], dr["win"][24 + oc], dr["wpl"][oc], dr["win"][32 + oc]]
            for i in range(4):
                extra = W_OLD if first_c1[0] else []
                first_c1[0] = False
                P.dma("pool", c1w[sl][i], srcs[i], writes=["c1w%d_%d" % (sl, i)] + extra, slot="c1w%d_%d" % (sl, i))
            for t in range(4):
                bb = [nextbank() for _ in range(4)]
                for k in range(8):
                    mm(ps[bb[0]][:, :], c1w[sl][0][:, k * 128:(k + 1) * 128], ypool3[:, k, 512 * t:512 * t + 512], k == 0, k == 7,
                       ["c1w%d_0" % sl] + YP(t), ["ps%d" % bb[0]])
                for k in range(8):
                    mm(ps[bb[1]][:, :], c1w[sl][1][:, k * 128:(k + 1) * 128], h3[:, k, 256 + 512 * t:256 + 512 * t + 512], k == 0, k == 7,
                       ["c1w%d_1" % sl] + HSEG(1 + t), ["ps%d" % bb[1]])
                for k in range(8):
                    mm(ps[bb[2]][:, :], c1w[sl][2][:, k * 128:(k + 1) * 128], ylru3[:, k, 512 * t:512 * t + 512], k == 0, k == 7,
                       ["c1w%d_2" % sl] + YL, ["ps%d" % bb[2]])
                for k in range(8):
                    mm(ps[bb[3]][:, :], c1w[sl][3][:, k * 128:(k + 1) * 128], h3[:, k, 256 + 512 * t:256 + 512 * t + 512], k == 0, k == 7,
                       ["c1w%d_3" % sl] + HSEG(1 + t), ["ps%d" % bb[3]])
                p = it % 2
                it += 1
                extra = LRUTOK if (oc == 0 and t == 0) else []
                act(sgb[p][0], ps[bb[1]][:, :], AF.Sigmoid, ["ps%d" % bb[1]], ["sg%d_0" % p] + extra)
                act(sgb[p][1], ps[bb[3]][:, :], AF.Sigmoid, ["ps%d" % bb[3]], ["sg%d_1" % p])
                tt("dve", t12[p][0], ps[bb[0]][:, :], sgb[p][0], ALU.mult, ["ps%d" % bb[0], "sg%d_0" % p], ["t12%d_0" % p])
                tt("dve", t12[p][1], ps[bb[2]][:, :], sgb[p][1], ALU.mult, ["ps%d" % bb[2], "sg%d_1" % p], ["t12%d_1" % p])
                tt("dve", m3[:, oc, 512 * t:512 * t + 512], t12[p][0], t12[p][1], ALU.add, ["t12%d_0" % p, "t12%d_1" % p],
                   ["m%d_%d" % (oc, t)])
        MT = lambda tl: ["m%d_%d" % (k, t) for k in range(8) for t in tl]
        if "m" in dbg:
            P.dma("sp", dump("m", mbuf, BF16)[:, :], mbuf, reads=MT(range(4)), slot="dbg_mm")
        if stop == "C1":
            P.emit()
            return nc, dbg_out

        cs = Carve(S_C2, R_S[0] + R_S[1] - S_C2)
        sq2 = [cs.f(640), cs.f(640)]
        tm2 = [cs.f(640), cs.f(640)]
        sd2 = cs.f(640)
        rstd2 = cs.f(640)
        gpad = [cs.f(660) for _ in range(4)]
        ssum2 = cs.f(640)
        hfB = cs.b(8 * 640)
        sq3 = [cs.f(512), cs.f(512)]
        ssum3 = cs.f(512)
        rstd3 = cs.f(512)
        sq4 = [sq2[0], sq2[1], cs.f(640), cs.f(640)]
        tm4 = [tm2[0], tm2[1]]
        HY_OLD = [t for si in range(5) for t in HSEG(si)] + [t for tl in range(4) for t in YP(tl)] + YL
        C1S_OLD = ["sg%d_%d" % (p, i) for p in range(2) for i in range(2)] + ["t12%d_%d" % (p, i) for p in range(2) for i in range(2)]
        cb_ = Carve(R_H[0], R_H[1] + R_Y[1])
        xt1 = cb_.f(8 * 640)
        mixb = cb_.f(8 * 640)
        hfA = cb_.b(8 * 640)
        abuf = cb_.b(24 * 512)
        wupb = [cb_.b(2048) for _ in range(3)]
        wdnb = [cb_.b(3072) for _ in range(2)]
        xt13 = xt1.rearrange("p (c t) -> p c t", t=640)
        mix3 = mixb.rearrange("p (c t) -> p c t", t=640)
        f3 = mixb[:, 0:8 * 512].rearrange("p (c t) -> p c t", t=512)
        hf3s = [hfA.rearrange("p (c t) -> p c t", t=640), hfB.rearrange("p (c t) -> p c t", t=640)]
        a3 = abuf.rearrange("p (c t) -> p c t", t=512)
        cwd = Carve(W_DYN, R_W[0] + R_W[1] - W_DYN)
        accb = [cwd.f(512) for _ in range(4)]
        wupb.append(cwd.b(2048))
        woutb = [cwd.b(1024) for _ in range(3)]
        wo_ctr = [0]
        wo_cur = [0]
        C1W_OLD = ["c1w%d_%d" % (s_, i) for s_ in range(2) for i in range(4)]
        oT3 = outT.rearrange("(c p) t -> p c t", p=128)
        memset(gpad[0], 0.0, ["gpad0"] + C1S_OLD)
        for gi in range(1, 4):
            memset(gpad[gi], 0.0, ["gpad%d" % gi])
        wup_ctr = [0]
        wdn_ctr = [0]
        nbf_ctr = [0]
        nbc_ctr = [0]

        def nbF():
            b_ = nbf_ctr[0] % 6
            nbf_ctr[0] += 1
            return b_

        def nbC():
            b_ = 6 + nbc_ctr[0] % 2
            nbc_ctr[0] += 1
            return b_

        def geom(tl):
            r0 = max(8 * tl - 1, 0)
            r1 = min(8 * tl + 9, 32)
            ntok = (r1 - r0) * 64
            return r0, ntok, r0 * 64, [(0, 512)] + [(512, ntok - 512)]

        MIXT = lambda c: "mix_%d" % c
        FT = lambda c: "f_%d" % c
        first_mix = [True]

        def mix_units(tl):
            r0, ntok, tok0, subs = geom(tl)
            mtl = sorted({min(3, (tok0 + so) // 512) for so, sn in subs} | {min(3, (tok0 + so + sn - 1) // 512) for so, sn in subs})
            units = []
            for oc in range(8):
                for (so, sn) in subs:
                    st_ = {}

                    def pe_fn(oc=oc, so=so, sn=sn, st_=st_):
                        if so == 0:
                            wi = wo_ctr[0] % 3
                            wo_ctr[0] += 1
                            extra = (LRUTOK + C1W_OLD) if first_mix[0] else []
                            P.dma("sp", woutb[wi], woutS[oc], reads=["wc_all"], writes=["wo%d" % wi] + extra, slot="wo%d" % wi)
                            wo_cur[0] = wi
                        wi = wo_cur[0]
                        b_ = nbC()
                        st_["b"] = b_
                        for k in range(8):
                            mm(ps[b_][:, 0:sn], woutb[wi][:, k * 128:(k + 1) * 128],
                               m3[:, k, tok0 + so:tok0 + so + sn], k == 0, k == 7, ["wo%d" % wi] + MT(mtl), ["ps%d" % b_])

                    def act_fn(oc=oc, so=so, sn=sn, st_=st_):
                        b_ = st_["b"]
                        extra2 = []
                        if oc == 0 and so == 0:
                            extra2 = [FT(q) for q in range(8)]
                            if first_mix[0]:
                                extra2 = extra2 + C1S_OLD + HY_OLD
                        first_mix[0] = False
                        act(mix3[:, oc, so:so + sn], ps[b_][:, 0:sn], AF.Identity, ["ps%d" % b_], [MIXT(oc)] + extra2)
                    units.append((pe_fn, act_fn))
            return units

        class MixSched:
            def __init__(self, units):
                self.u, self.ipe, self.iact = units, 0, 0

            def step(self):
                n = len(self.u)
                if self.iact < self.ipe and (self.ipe - self.iact >= 2 or self.ipe == n):
                    self.u[self.iact][1]()
                    self.iact += 1
                if self.ipe < n and self.ipe - self.iact < 2:
                    self.u[self.ipe][0]()
                    self.ipe += 1

            def done(self):
                return self.iact == len(self.u)

        first_hy = [True]

        def chain_sched(tn):
            r0, ntok, tok0, subs = geom(tn)
            par = tn % 2
            co = (8 * tn - r0) * 64
            hf3 = hf3s[par]
            sch = {}

            def at(slot, fn):
                sch.setdefault(slot, []).append(fn)

            def xloads():
                for c in range(8):
                    extra = HY_OLD if (first_hy[0] and c == 0) else []
                    P.dma("sp", xt13[:, c, 0:ntok], xT3[:, c, tok0:tok0 + ntok], writes=["xt1_%d" % c, "x1_%d" % c] + extra,
                          slot="xt1_%d" % c)
                first_hy[0] = False
            at(0, xloads)
            ms = MixSched(mix_units(tn))
            for i in range(20):
                at(i * 12 // 20, ms.step)

            def sq_op(src_fn, rd_fn, c):
                def f():
                    if c == 0:
                        act(ssum2[:, 0:ntok], src_fn(c), AF.Square, rd_fn(c), ["ssum2"])
                    else:
                        act(sq4[c % 4][:, 0:ntok], src_fn(c), AF.Square, rd_fn(c), ["sq4_%d" % (c % 4)])
                return f

            def add_op(c):
                def f():
                    tt("dve", ssum2[:, 0:ntok], ssum2[:, 0:ntok], sq4[c % 4][:, 0:ntok], ALU.add, ["ssum2", "sq4_%d" % (c % 4)], ["ssum2"])
                return f

            def norm_fin():
                assert ms.done()
                for (so, sn) in ([(0, 512)] + ([(512, ntok - 512)] if ntok > 512 else [])):
                    b_ = nbC()
                    mm(ps[b_][:, 0:sn], ones, ssum2[:, so:so + sn], True, True, ["ones", "ssum2"], ["ps%d" % b_])
                    act(sd2[:, so:so + sn], ps[b_][:, 0:sn], AF.Ln, ["ps%d" % b_], ["sd2"], scale=1.0 / D, bias=EPS)
                act(rstd2[:, 0:ntok], sd2[:, 0:ntok], AF.Exp, ["sd2"], ["rstd2"], scale=-0.5)

            for c in range(8):
                at(12 + c // 2, sq_op(lambda c_: mix3[:, c_, 0:ntok], lambda c_: [MIXT(c_)], c))
                if c > 0:
                    at(13 + c // 2, add_op(c))
            at(17, norm_fin)

            def x1_chunk(c):
                def f():
                    tm = tm4[c % 2]
                    tt("dve", tm[:, 0:ntok], mix3[:, c, 0:ntok], rstd2[:, 0:ntok], ALU.mult, [MIXT(c), "rstd2"], ["tm4_%d" % (c % 2)])
                    stt(xt13[:, c, 0:ntok], tm[:, 0:ntok], GG1[:, c:c + 1], xt13[:, c, 0:ntok], ALU.mult, ALU.add,
                        ["tm4_%d" % (c % 2), "der", "xt1_%d" % c], ["x1_%d" % c])
                return f
            for c in range(8):
                at(18 + c // 2, x1_chunk(c))
                at(23 + c // 2, sq_op(lambda c_: xt13[:, c_, 0:ntok], lambda c_: ["x1_%d" % c_], c))
                if c > 0:
                    at(24 + c // 2, add_op(c))

            def dma1():
                P.dma("pool", oT3[:, :, 512 * tn:512 * tn + 512], xt13[:, :, co:co + 512], reads=["x1_%d" % c for c in range(8)],
                      writes=["outt%d" % tn], slot="out1")
            at(22, dma1)
            at(28, norm_fin)

            def hf_mult(c):
                def f():
                    tm = tm4[c % 2]
                    tt("dve", tm[:, 0:ntok], xt13[:, c, 0:ntok], rstd2[:, 0:ntok], ALU.mult, ["x1_%d" % c, "rstd2"], ["tm4_%d" % (c % 2)])
                return f

            def hf_aff(c):
                def f():
                    extra = C1S_OLD if (par == 1 and tn == 1 and c == 0) else []
                    act(hf3[:, c, 0:ntok], tm4[c % 2][:, 0:ntok], AF.Identity, ["tm4_%d" % (c % 2), "der", "modfm"],
                        ["hf%d_%d" % (par, c)] + extra, bias=MOD(3, c, 0), scale=A2[:, c:c + 1])
                return f
            for c in range(8):
                at(29 + c // 2, hf_mult(c))
                at(30 + c // 2, hf_aff(c))
            return sch

        def run_sched(sch, lo, hi):
            for slot in sorted(k for k in sch if lo <= k < hi):
                for fn in sch.pop(slot):
                    fn()

        run_sched(chain_sched(0), 0, 10 ** 6)

        for tl in range(4):
            r0, ntok, tok0, subs = geom(tl)
            par = tl % 2
            hf3 = hf3s[par]
            HF = ["hf%d_%d" % (par, c) for c in range(8)]
            co = (8 * tl - r0) * 64
            s0 = r0 - (8 * tl - 1)
            if tl == 3:
                for gi in range(4):
                    memset(gpad[gi].rearrange("p (r c) -> p r c", c=66)[:, 9, :], 0.0, ["gpad%d" % gi])
            pend = chain_sched(tl + 1) if tl + 1 < 4 else {}

            def ffn_front(p):
                st = []
                for i in range(2):
                    j = 2 * p + i
                    q = j % 4
                    wi = wup_ctr[0] % 4
                    wup_ctr[0] += 1
                    P.dma("sp", wupb[wi], wupS[j], reads=["wc_all"], writes=["wup%d" % wi], slot="wup%d" % wi)
                    gp3 = gpad[q].rearrange("p (r c) -> p r c", c=66)
                    wg = wupb[wi][:, 0:1024]
                    row = s0
                    for (so, sn) in subs:
                        b = nbF()
                        for k in range(8):
                            mm(ps[b][:, 0:sn], wg[:, k * 128:(k + 1) * 128], hf3[:, k, so:so + sn], k == 0, k == 7,
                               ["wup%d" % wi] + HF, ["ps%d" % b])
                        nr = sn // 64
                        act(gp3[:, row:row + nr, 1:65], ps[b][:, 0:sn].rearrange("p (r c) -> p r c", c=64), AF.Identity,
                            ["ps%d" % b], ["gpad%d" % q])
                        row += nr
                    acc3 = accb[q].rearrange("p (r c) -> p r c", c=64)
                    act(acc3, gp3[:, 0:8, 0:64], AF.Identity, ["gpad%d" % q, "vecs"], ["acc%d" % q],
                        bias=V("ffn_conv_b", j), scale=V("ffn_conv_w", 0 * 24 + j))
                    st.append((j, q, wi, gp3, acc3))
                return st

            def ffn_taps(st, taps):
                for tap in taps:
                    dy, dx = tap // 3, tap % 3
                    for (j, q, wi, gp3, acc3) in st:
                        stt(acc3, gp3[:, dy:dy + 8, dx:dx + 64], V("ffn_conv_w", tap * 24 + j), acc3, ALU.mult, ALU.add,
                            ["gpad%d" % q, "acc%d" % q, "vecs"], ["acc%d" % q])

            def ffn_u(st):
                out = []
                for (j, q, wi, gp3, acc3) in st:
                    wu = wupb[wi][:, 1024:2048]
                    bu = nbF()
                    for k in range(8):
                        mm(ps[bu][:, :], wu[:, k * 128:(k + 1) * 128], hf3[:, k, co:co + 512], k == 0, k == 7,
                           ["wup%d" % wi] + HF, ["ps%d" % bu])
                    out.append(bu)
                return out

            def ffn_gelu(st):
                for (j, q, wi, gp3, acc3) in st:
                    act(accb[q], accb[q], AF.Gelu_apprx_tanh, ["acc%d" % q], ["acc%d" % q])

            def ffn_mult(st, bus):
                for (j, q, wi, gp3, acc3), bu in zip(st, bus):
                    tt("dve", a3[:, j, :], accb[q], ps[bu][:, :], ALU.mult, ["acc%d" % q, "ps%d" % bu], ["a%d" % j])

            prev = None
            def pump_mix(q):
                if q < 23:
                    run_sched(pend, q, q + 1)

            for p in range(12):
                st = ffn_front(p)
                pump_mix(3 * p)
                if prev is not None:
                    ffn_gelu(prev[0])
                ffn_taps(st, range(1, 2))
                if prev is not None:
                    ffn_mult(*prev)
                ffn_taps(st, range(2, 5))
                pump_mix(3 * p + 1)
                ffn_taps(st, range(5, 9))
                bus = ffn_u(st)
                pump_mix(3 * p + 2)
                prev = (st, bus)
            ffn_gelu(prev[0])
            ffn_mult(*prev)
            run_sched(pend, 0, 23)
            AT = ["a%d" % j for j in range(24)]
            cslot = [23]

            def pump_chain(n):
                run_sched(pend, cslot[0], cslot[0] + n)
                cslot[0] += n

            for oc in range(8):
                wi = wdn_ctr[0] % 2
                wdn_ctr[0] += 1
                P.dma("sp", wdnb[wi], wdnS[oc], reads=["wc_all"], writes=["wdn%d" % wi], slot="wdn%d" % wi)
                b = nbC()
                for k in range(24):
                    mm(ps[b][:, :], wdnb[wi][:, k * 128:(k + 1) * 128], a3[:, k, :], k == 0, k == 23, ["wdn%d" % wi] + AT, ["ps%d" % b])
                act(f3[:, oc, :], ps[b][:, :], AF.Identity, ["ps%d" % b], [FT(oc)] + ([MIXT(q) for q in range(8)] if oc == 0 else []))
                pump_chain(1)
                if oc == 0:
                    act(ssum3, f3[:, oc, :], AF.Square, [FT(oc)], ["ssum3"])
                else:
                    sq = sq3[oc % 2]
                    act(sq, f3[:, oc, :], AF.Square, [FT(oc)], ["sq3_%d" % (oc % 2)])
                    tt("dve", ssum3, ssum3, sq, ALU.add, ["ssum3", "sq3_%d" % (oc % 2)], ["ssum3"])
                pump_chain(1 if oc % 2 == 0 else 0)
            run_sched(pend, 0, 10 ** 6)
            bss = nbC()
            mm(ps[bss][:, :], ones, ssum3, True, True, ["ones", "ssum3"], ["ps%d" % bss])
            act(ssum3, ps[bss][:, :], AF.Ln, ["ps%d" % bss], ["ssum3"], scale=1.0 / D, bias=EPS)
            act(rstd3, ssum3, AF.Exp, ["ssum3"], ["rstd3"], scale=-0.5)
            for oc in range(8):
                P.op("dve", lambda e, oc=oc: e.scalar_tensor_tensor(out=f3[:, oc, :], in0=f3[:, oc, :], scalar=GG2[:, oc:oc + 1],
                                                                    in1=rstd3, op0=ALU.mult, op1=ALU.mult),
                     [FT(oc), "rstd3", "der"], [FT(oc)])
            P.dma("pool", oT3[:, :, 512 * tl:512 * tl + 512], f3, reads=[FT(oc) for oc in range(8)], writes=["outt%d" % tl],
                  slot="out2", accum_op=ALU.add)
        P.emit()
    return nc, dbg_out


def _fm(v, nch):
    v = np.asarray(v, np.float32)
    lead = v.shape[:-1]
    r = v.reshape(lead + (nch, 128))
    r = np.moveaxis(r, -1, 0)
    return np.ascontiguousarray(r.reshape(128, -1))


def _wtile(w, kch, och):
    w = np.asarray(w, np.float32)
    r = w.reshape(kch, 128, och, 128).transpose(2, 1, 0, 3)
    return np.ascontiguousarray(r.reshape(och, 128, kch * 128))


def _window_counts():
    out = np.zeros((4, 96), np.float32)
    for gi, w in enumerate(POOL_WINDOWS):
        def cnt1(n):
            pos = np.arange(n)
            lo = np.clip(pos - w // 2, 0, n)
            hi = np.clip(pos + w - w // 2, 0, n)
            return (hi - lo).astype(np.float32)
        out[gi, 0:32] = cnt1(32)
        out[gi, 32:96] = cnt1(64)
    return out.reshape(1, 384)


def prep_inputs(x, c, ctx, c_ctx, w_mod, b_mod, g_pre_mix, g_post_mix, g_pre_ffn, g_post_ffn,
                w_in, pool_w, pool_scale, lru_conv_w, lru_conv_b, lru_wa, lru_ba, lru_wx, lru_bx,
                lru_lambda, w_proj_pool, w_proj_lru, w_out, w_up, ffn_conv_w, ffn_conv_b, w_down):
    f = lambda a: np.asarray(a, np.float32)
    vec_parts = {
        "g_pre_mix": _fm(f(g_pre_mix)[0], 8), "g_post_mix": _fm(f(g_post_mix)[0], 8),
        "g_pre_ffn": _fm(f(g_pre_ffn)[0], 8), "g_post_ffn": _fm(f(g_post_ffn)[0], 8),
        "pool_scale": _fm(f(pool_scale)[0], 8), "lru_conv_w": _fm(f(lru_conv_w)[0], 8),
        "lru_conv_b": _fm(f(lru_conv_b)[0], 8), "lru_ba": _fm(f(lru_ba)[0], 8), "lru_bx": _fm(f(lru_bx)[0], 8),
        "lru_lambda": _fm(f(lru_lambda)[0], 8), "ffn_conv_w": _fm(f(ffn_conv_w)[0].reshape(9, 3072), 24),
        "ffn_conv_b": _fm(f(ffn_conv_b)[0], 24), "b_mod": _fm(f(b_mod)[0], 48),
    }
    vecs = np.ascontiguousarray(np.concatenate([vec_parts[n] for n, _ in _VEC_SPEC], axis=1))
    assert vecs.shape == (128, NV)
    wmod_h = np.ascontiguousarray(f(w_mod)[0].reshape(8, 128, 12, 512).transpose(2, 1, 0, 3).reshape(12, 128, 4096))
    win_h = _wtile(f(w_in)[0], 8, 40)
    wpp_h = _wtile(f(w_proj_pool)[0], 8, 8)
    wpl_h = _wtile(f(w_proj_lru)[0], 8, 8)
    wout_h = _wtile(f(w_out)[0], 8, 8)
    wup_t = _wtile(f(w_up)[0], 8, 48)
    wup_h = np.ascontiguousarray(np.concatenate([wup_t[0:24], wup_t[24:48]], axis=2))
    wdown_h = _wtile(f(w_down)[0], 24, 8)
    pw = f(pool_w)[0].reshape(4, 2, 128, 2, 128).transpose(2, 0, 3, 1, 4)
    poolw_h = np.ascontiguousarray(pw.reshape(128, 2048))
    lw = np.stack([f(lru_wa)[0], f(lru_wx)[0]], axis=0)
    lruw_h = np.ascontiguousarray(lw.transpose(3, 0, 1, 2, 4).reshape(128, 4096))
    shared = {"vecs": vecs, "ident": np.eye(128, dtype=np.float32), "cnt": _window_counts(), "wmod": wmod_h,
              "win": win_h, "wpp": wpp_h, "wpl": wpl_h, "wout": wout_h, "wup": wup_h, "wdown": wdown_h,
              "poolw": poolw_h, "lruw": lruw_h}
    xf, cf, ctxf, ccf = f(x), f(c), f(ctx), f(c_ctx)
    in_maps = []
    for b in range(NCORES):
        m = dict(shared)
        m["xT"] = np.ascontiguousarray(xf[b].T)
        m["ctxT"] = np.ascontiguousarray(ctxf[b].T)
        cc2 = np.stack([cf[b], ccf], axis=0)
        m["cc"] = np.ascontiguousarray(cc2.reshape(2, 8, 128).transpose(2, 1, 0).reshape(128, 16))
        in_maps.append(m)
    return in_maps


_CACHE = {}


def kernel(**inputs):
    in_maps = prep_inputs(**inputs)
    if "nc" not in _CACHE:
        _CACHE["nc"] = build_program()[0]
    nc = _CACHE["nc"]
    res = run_bass_kernel_spmd(nc, in_maps, core_ids=list(range(NCORES)))
    out = np.stack([np.asarray(r["outT"], np.float32).T for r in res.results], axis=0)
    return np.ascontiguousarray(out.astype(np.float32))
```

```python
import numpy as np
from contextlib import ExitStack
import concourse.bass as bass
import concourse.mybir as mybir
from concourse.bass_utils import run_bass_kernel_spmd

F32 = mybir.dt.float32
BF16 = mybir.dt.bfloat16
RELAX_DVE = False
AF = mybir.ActivationFunctionType
ALU = mybir.AluOpType

NCORES = 8
D = 1024
T = 2048
CT = 256
TT = T + CT
NCH = 8
EPS = 1e-6
POOL_WINDOWS = (2, 4, 8, 16)
SEGS = [(0, 256)] + [(256 + 512 * i, 512) for i in range(4)]

_VEC_SPEC = [("g_pre_mix", 8), ("g_post_mix", 8), ("g_pre_ffn", 8), ("g_post_ffn", 8), ("pool_scale", 8),
             ("lru_conv_w", 32), ("lru_conv_b", 8), ("lru_ba", 16), ("lru_bx", 16), ("lru_lambda", 16),
             ("ffn_conv_w", 216), ("ffn_conv_b", 24), ("b_mod", 48)]
VOFF = {}
_o = 0
for _n, _c in _VEC_SPEC:
    VOFF[_n] = _o
    _o += _c
NV = _o


class _Op:
    __slots__ = ("eng", "fn", "deps", "needs_inc", "sem", "val", "is_dma", "slot", "pos")

    def __init__(self, eng, fn, is_dma=False, slot=None):
        self.eng = eng
        self.fn = fn
        self.deps = []
        self.needs_inc = False
        self.sem = None
        self.val = None
        self.is_dma = is_dma
        self.slot = slot
        self.pos = -1


class Prog:
    ENGS = ("pe", "act", "dve", "pool", "sp")

    def __init__(self, nc):
        self.nc = nc
        self.q = {e: [] for e in self.ENGS}
        self.last_w = {}
        self.readers = {}
        self.all_ops = []

    def _track(self, op, reads, writes):
        deps = {}
        for t in reads:
            w = self.last_w.get(t)
            if w is not None:
                deps[id(w)] = w
        for t in writes:
            w = self.last_w.get(t)
            if w is not None:
                deps[id(w)] = w
            for r in self.readers.get(t, {}).values():
                deps[id(r)] = r
        for d in deps.values():
            if d is op:
                continue
            if (not d.is_dma) and (not op.is_dma) and d.eng == "pe" and op.eng == "pe":
                continue
            if RELAX_DVE and (not d.is_dma) and (not op.is_dma) and d.eng == "dve" and op.eng == "dve" and op.pos - d.pos >= 2:
                continue
            d.needs_inc = True
            op.deps.append(d)
        for t in writes:
            self.last_w[t] = op
            self.readers[t] = {}
        for t in reads:
            key = ("dma", op.slot) if op.is_dma else op.eng
            self.readers.setdefault(t, {})[key] = op

    def op(self, eng, fn, reads=(), writes=()):
        o = _Op(eng, fn)
        o.pos = len(self.q[eng])
        self._track(o, reads, writes)
        self.q[eng].append(o)
        self.all_ops.append(o)
        return o

    def dma(self, eng, out, in_, reads=(), writes=(), slot=None, **kw):
        o = _Op(eng, None, is_dma=True, slot=slot)
        o.fn = lambda e, s, o_=out, i_=in_, kw_=kw: e.dma_start(out=o_, in_=i_, **kw_).then_inc(s, 16)
        self._track(o, reads, writes)
        o.needs_inc = True
        self.q[eng].append(o)
        self.all_ops.append(o)
        return o

    def emit(self, final_wait_eng="sp"):
        nc = self.nc
        with ExitStack() as es:
            esem = {e: es.enter_context(nc.semaphore("sem_" + e)) for e in self.ENGS}
            slot_names = sorted({o.slot for o in self.all_ops if o.is_dma})
            ssem = {s: es.enter_context(nc.semaphore("dsem_" + s)) for s in slot_names}
            cnt = {e: 0 for e in self.ENGS}
            for e in self.ENGS:
                for o in self.q[e]:
                    if (not o.is_dma) and o.needs_inc:
                        cnt[e] += 1
                        o.sem, o.val = esem[e], cnt[e]
            scnt = {s: 0 for s in slot_names}
            for o in self.all_ops:
                if o.is_dma:
                    scnt[o.slot] += 16
                    o.sem, o.val = ssem[o.slot], scnt[o.slot]
            block = es.enter_context(nc.Block())
            final = [(ssem[s], scnt[s]) for s in slot_names]

            def run(e):
                def body(eng):
                    waited = {}
                    for o in self.q[e]:
                        for d in o.deps:
                            k = id(d.sem)
                            if waited.get(k, 0) < d.val:
                                eng.wait_ge(d.sem, d.val)
                                waited[k] = d.val
                        if o.is_dma:
                            o.fn(eng, o.sem)
                        else:
                            ins = o.fn(eng)
                            if o.needs_inc:
                                ins.then_inc(o.sem, 1)
                    if e == final_wait_eng:
                        for s, v in final:
                            if v > 0:
                                eng.wait_ge(s, v)
                return body

            block.tensor(run("pe"))
            block.scalar(run("act"))
            block.vector(run("dve"))
            block.gpsimd(run("pool"))
            block.sync(run("sp"))


def build_program(stop=None, dbg=()):
    nc = bass.Bass("TRN2", target_bir_lowering=False)
    dr = {}

    def din(name, shape, dt=F32):
        dr[name] = nc.dram_tensor(name, shape, dt, kind="ExternalInput").ap()

    din("xT", [D, T])
    din("ctxT", [D, CT])
    din("cc", [128, 16])
    din("vecs", [128, NV])
    din("ident", [128, 128])
    din("cnt", [1, 384])
    din("wmod", [12, 128, 4096])
    din("win", [40, 128, 1024])
    din("wpp", [8, 128, 1024])
    din("wpl", [8, 128, 1024])
    din("wout", [8, 128, 1024])
    din("wup", [24, 128, 2048])
    din("wdown", [8, 128, 3072])
    din("poolw", [128, 2048])
    din("lruw", [128, 4096])
    outT = nc.dram_tensor("outT", [D, T], F32, kind="ExternalOutput").ap()
    wupS = nc.dram_tensor("wupS", [24, 128, 2048], BF16, kind="Internal").ap()
    wdnS = nc.dram_tensor("wdnS", [8, 128, 3072], BF16, kind="Internal").ap()
    woutS = nc.dram_tensor("woutS", [8, 128, 1024], BF16, kind="Internal").ap()
    dbg_out = {}

    es = ExitStack()
    with es:
        AW = 53200
        arena = es.enter_context(nc.sbuf_tensor("arena", [128, AW], F32))
        ps = [es.enter_context(nc.psum_tensor("ps%d" % i, [128, 512], F32)) for i in range(8)]
        P = Prog(nc)

        def fv(off, n):
            assert off % 4 == 0 and off + 4 * n <= AW * 4, (off, n)
            return arena[:, off // 4: off // 4 + n]

        def bv(off, n):
            assert off % 4 == 0 and n % 2 == 0 and off + 2 * n <= AW * 4, (off, n)
            return arena[:, off // 4: off // 4 + n // 2].bitcast(BF16)

        class Carve:
            def __init__(self, base, size):
                self.base, self.end, self.cur = base, base + size, base

            def f(self, n):
                a = fv(self.cur, n)
                self.cur += 4 * n
                self.cur = (self.cur + 63) // 64 * 64
                assert self.cur <= self.end, ("carve overflow", self.cur, self.end)
                return a

            def b(self, n):
                a = bv(self.cur, n)
                self.cur += 2 * n
                self.cur = (self.cur + 63) // 64 * 64
                assert self.cur <= self.end, ("carve overflow", self.cur, self.end)
                return a

        KB = 1024
        R_H = (0, 36 * KB)
        R_Y = (36 * KB, 64 * KB)
        R_S = (100 * KB, 84 * KB)
        R_W = (184 * KB, AW * 4 - 184 * KB)

        bank_ctr = [0]

        def nextbank():
            b = bank_ctr[0] % 8
            bank_ctr[0] += 1
            return b

        def mm(out, lhsT, rhs, start, stop, reads, writes):
            P.op("pe", lambda e: e.matmul(out, lhsT, rhs, start=start, stop=stop), reads, writes)

        def act(out, in_, func, reads, writes, bias=None, scale=None):
            kw = {}
            if bias is not None:
                kw["bias"] = bias
            if scale is not None:
                kw["scale"] = scale
            P.op("act", lambda e: e.activation(out=out, in_=in_, func=func, **kw), reads, writes)

        def tt(eng, out, in0, in1, op, reads, writes):
            P.op(eng, lambda e: e.tensor_tensor(out=out, in0=in0, in1=in1, op=op), reads, writes)

        def ts(eng, out, in0, s1, op0, reads, writes, s2=None, op1=None):
            if op1 is None:
                P.op(eng, lambda e: e.tensor_scalar(out=out, in0=in0, scalar1=s1, scalar2=None, op0=op0), reads, writes)
            else:
                P.op(eng, lambda e: e.tensor_scalar(out=out, in0=in0, scalar1=s1, scalar2=s2, op0=op0, op1=op1), reads, writes)

        def stt(out, in0, scalar, in1, op0, op1, reads, writes):
            P.op("dve", lambda e: e.scalar_tensor_tensor(out=out, in0=in0, scalar=scalar, in1=in1, op0=op0, op1=op1), reads, writes)

        def memset(ap, val, writes):
            P.op("pool", lambda e: e.memset(ap, val), (), writes)

        def dump(name, ap, dt=F32):
            shape = [ap.shape[0], int(np.prod(ap.shape[1:]))]
            t = nc.dram_tensor("dbg_" + name, shape, dt, kind="ExternalOutput").ap()
            dbg_out[name] = t
            return t

        cw = Carve(*R_W)
        vecs = cw.f(NV)
        ident = cw.f(128)
        ones = cw.f(128)
        identb = cw.b(128)
        ccs = cw.f(16)
        ssil = cw.f(16)
        modfm = cw.f(96)
        der = cw.f(64)
        lrud = cw.f(64)
        cw2 = cw.f(32)
        cn1 = cw.f(384)
        W_DYN = cw.cur

        def V(name, i=0):
            o = VOFF[name] + i
            return vecs[:, o:o + 1]

        def Vs(name, i0, n):
            o = VOFF[name] + i0
            return vecs[:, o:o + n]

        P.dma("sp", vecs, dr["vecs"][:, :], writes=["vecs"], slot="c_vecs")
        P.dma("sp", ident, dr["ident"][:, :], writes=["ident"], slot="c_ident")
        P.dma("sp", ccs, dr["cc"][:, :], writes=["cc"], slot="c_cc")
        memset(ones, 1.0, ["ones"])
        P.op("dve", lambda e: e.tensor_copy(out=identb, in_=ident), ["ident"], ["identb"])
        act(ssil, ccs, AF.Silu, ["cc"], ["ssil"])
        s3 = ssil.rearrange("p (k r) -> p k r", r=2)

        cs = Carve(*R_S)
        wmb = [cs.f(4096), cs.f(4096)]
        modrow = cs.f(6144)
        xa4 = cs.f(8 * 512)
        rstdA = cs.f(TT)
        cy = Carve(*R_Y)
        xa = [cy.f(8 * 256), cy.f(8 * 512), cy.f(8 * 512), cy.f(8 * 512), xa4]
        sqb = [cy.f(512), cy.f(512)]
        ssumA = cy.f(512)
        sdb = cy.f(512)
        xT3 = dr["xT"].rearrange("(c p) t -> p c t", p=128)
        cT3 = dr["ctxT"].rearrange("(c p) t -> p c t", p=128)
        xa3 = [xa[si].rearrange("p (c t) -> p c t", t=SEGS[si][1]) for si in range(5)]

        def load_x(si):
            o, n = SEGS[si]
            src = cT3[:, :, :] if si == 0 else xT3[:, :, o - 256:o - 256 + n]
            P.dma("sp", xa3[si], src, writes=["xa%d" % si], slot="xa%d" % si)

        def stats(si):
            o, n = SEGS[si]
            tk = "xa%d" % si
            for c in range(8):
                if c == 0:
                    act(ssumA[:, 0:n], xa3[si][:, c, :], AF.Square, [tk], ["ssumA"])
                else:
                    sq = sqb[c % 2]
                    act(sq[:, 0:n], xa3[si][:, c, :], AF.Square, [tk], ["sq%d" % (c % 2)])
                    tt("dve", ssumA[:, 0:n], ssumA[:, 0:n], sq[:, 0:n], ALU.add, ["ssumA", "sq%d" % (c % 2)], ["ssumA"])
            b = nextbank()
            mm(ps[b][:, 0:n], ones, ssumA[:, 0:n], True, True, ["ones", "ssumA"], ["ps%d" % b])
            act(sdb[:, 0:n], ps[b][:, 0:n], AF.Ln, ["ps%d" % b], ["sd"], scale=1.0 / D, bias=EPS)
            act(rstdA[:, o:o + n], sdb[:, 0:n], AF.Exp, ["sd"], ["rstdA%d" % si], scale=-0.5)
            for c in range(8):
                tt("dve", xa3[si][:, c, :], xa3[si][:, c, :], rstdA[:, o:o + n], ALU.mult, [tk, "rstdA%d" % si], ["xn%d_%d" % (si, c)])

        xsched = {1: 0, 2: 1, 4: 2, 6: 3, 8: 4}
        ssched = {3: 0, 5: 1, 7: 2, 9: 3, 11: 4}
        for blk in range(12):
            buf = wmb[blk % 2]
            tk = "wm%d" % (blk % 2)
            P.dma("sp", buf, dr["wmod"][blk], writes=[tk], slot=tk)
            if blk in xsched:
                load_x(xsched[blk])
            b = nextbank()
            for k in range(8):
                mm(ps[b][0:2, 0:512], s3[:, k, :], buf[:, k * 512:(k + 1) * 512], k == 0, k == 7,
                   ["ssil", tk], ["ps%d" % b])
            act(modrow[0:2, blk * 512:(blk + 1) * 512], ps[b][0:2, 0:512], AF.Identity, ["ps%d" % b], ["modrow%d" % blk])
            if blk in ssched:
                stats(ssched[blk])
        b = nextbank()
        for oc in range(48):
            mm(ps[b][:, 2 * oc:2 * oc + 2], modrow[0:2, oc * 128:(oc + 1) * 128], ident[0:2, 0:2], True, True,
               ["modrow%d" % (oc // 4), "ident"], ["ps%d" % b])
        act(modfm, ps[b][:, 0:96], AF.Identity, ["ps%d" % b], ["modfm"])
        mod3 = modfm.rearrange("p (c r) -> p c r", r=2)
        for r in range(2):
            tt("dve", mod3[:, :, r], mod3[:, :, r], Vs("b_mod", 0, 48), ALU.add, ["modfm", "vecs"], ["modfm"])

        def MOD(which, c, r=0):
            return mod3[:, which * 8 + c, r:r + 1]

        A1 = der[:, 0:8]
        A1c = der[:, 8:16]
        GG1 = der[:, 16:24]
        A2 = der[:, 24:32]
        GG2 = der[:, 32:40]
        stt(A1, mod3[:, 8:16, 0], 1.0, Vs("g_pre_mix", 0, 8), ALU.add, ALU.mult, ["modfm", "vecs"], ["der"])
        stt(A1c, mod3[:, 8:16, 1], 1.0, Vs("g_pre_mix", 0, 8), ALU.add, ALU.mult, ["modfm", "vecs"], ["der"])
        tt("dve", GG1, mod3[:, 16:24, 0], Vs("g_post_mix", 0, 8), ALU.mult, ["modfm", "vecs"], ["der"])
        stt(A2, mod3[:, 32:40, 0], 1.0, Vs("g_pre_ffn", 0, 8), ALU.add, ALU.mult, ["modfm", "vecs"], ["der"])
        tt("dve", GG2, mod3[:, 40:48, 0], Vs("g_post_ffn", 0, 8), ALU.mult, ["modfm", "vecs"], ["der"])
        lam = Vs("lru_lambda", 0, 16)
        le = lrud[:, 0:16]
        lsp = lrud[:, 16:32]
        ls1 = lrud[:, 32:48]
        act(le, lam, AF.Exp, ["vecs"], ["le"], scale=-1.0)
        act(lsp, le, AF.Ln, ["le"], ["lsp"], bias=1.0)
        ts("dve", ls1, lsp, -4.0, ALU.mult, ["lsp"], ["hs1"])
        hs1 = ls1
        hbias = cw2
        ts("dve", hbias, Vs("lru_ba", 0, 32), 0.5, ALU.mult, ["vecs"], ["hbias"])

        ch = Carve(*R_H)
        h = ch.b(8 * TT)
        h3 = h.rearrange("p (c t) -> p c t", t=TT)
        for si, (o, n) in enumerate(SEGS):
            for c in range(8):
                if si == 0:
                    sc_, bi_ = A1c[:, c:c + 1], MOD(0, c, 1)
                else:
                    sc_, bi_ = A1[:, c:c + 1], MOD(0, c, 0)
                act(h3[:, c, o:o + n], xa3[si][:, c, :], AF.Identity, ["xn%d_%d" % (si, c), "der", "modfm"], ["h%d_%d" % (c, si)],
                    bias=bi_, scale=sc_)
        HSEG = lambda si: ["h%d_%d" % (c, si) for c in range(8)]

        if "h" in dbg:
            P.dma("sp", dump("h", h, BF16)[:, :], h, reads=[t for si in range(5) for t in HSEG(si)], slot="dbg_h")
            P.dma("sp", dump("modfm", modfm)[:, :], modfm, reads=["modfm"], slot="dbg_m")
        if stop == "A":
            P.emit()
            return nc, dbg_out

        cy = Carve(*R_Y)
        ypool = cy.b(8 * T)
        ylru = cy.b(8 * T)
        ypool3 = ypool.rearrange("p (c t) -> p c t", t=T)
        ylru3 = ylru.rearrange("p (c t) -> p c t", t=T)
        YTOK = ["xa%d" % si for si in range(4)] + ["xn%d_%d" % (si, c) for si in range(4) for c in range(8)] + ["sq0", "sq1", "sd", "ssumA"]
        y_first = [True]

        cwd = Carve(W_DYN, R_W[0] + R_W[1] - W_DYN)
        lruw = cwd.b(4096)
        poolw = cwd.b(2048)
        NWIN = 3
        winb = [cwd.b(1024) for _ in range(NWIN)]
        win_ctr = [0]
        P.dma("pool", lruw[:, 0:2048], dr["lruw"][:, 0:2048], writes=["lruw"], slot="w_lruw")
        P.dma("pool", lruw[:, 2048:4096], dr["lruw"][:, 2048:4096], writes=["lruw"], slot="w_lruw")
        P.dma("pool", poolw, dr["poolw"][:, :], writes=["poolw"], slot="w_poolw")

        def load_win(oc):
            i = win_ctr[0] % NWIN
            win_ctr[0] += 1
            P.dma("pool", winb[i], dr["win"][oc], writes=["win%d" % i], slot="win%d" % i)
            return winb[i], "win%d" % i

        def proj_seg(wt, wtk, si, b):
            o, n = SEGS[si]
            for k in range(8):
                mm(ps[b][:, 0:n], wt[:, k * 128:(k + 1) * 128], h3[:, k, o:o + n], k == 0, k == 7,
                   [wtk] + HSEG(si), ["ps%d" % b])

        cast_plan = []
        for oc in range(8):
            cast_plan.append((woutS[oc], dr["wout"][oc]))
        for j in range(24):
            cast_plan.append((wupS[j], dr["wup"][j]))
        for oc in range(8):
            cast_plan.append((wdnS[oc][:, 0:2048], dr["wdown"][oc][:, 0:2048]))
            cast_plan.append((wdnS[oc][:, 2048:3072], dr["wdown"][oc][:, 2048:3072]))
        cast_ctr = [0]

        def issue_casts(k):
            for _ in range(k):
                i = cast_ctr[0]
                if i >= len(cast_plan):
                    return
                cast_ctr[0] += 1
                dst, src = cast_plan[i]
                wr = ["wc%d" % i] + (["wc_all"] if i == len(cast_plan) - 1 else [])
                P.dma("pool", dst, src, writes=wr, slot="wcast")

        STOK_A = (["wm0", "wm1"] + ["modrow%d" % i for i in range(12)] + ["xa4"] + ["xn4_%d" % c for c in range(8)]
                  + ["rstdA%d" % si for si in range(5)])
        cs = Carve(*R_S)
        cntb = cs.f(T)
        PQ = [[cs.f(47 * 79), cs.f(47 * 79)] for _ in range(2)]
        dbf = [cs.b(T), cs.b(T)]
        PQTOK = ["pq%d_%d" % (cl, i) for cl in range(2) for i in range(2)]
        cnt3 = cntb.rearrange("p (r c) -> p r c", c=64)
        first_S = [True]
        nb_ctr = [0]

        def nextB():
            b = 4 + nb_ctr[0] % 4
            nb_ctr[0] += 1
            return b

        def proj_cl0(g_):
            wt_, wtk_ = load_win(2 * g_)
            for t in range(4):
                proj_seg(wt_, wtk_, 1 + t, t)
            return (wt_, wtk_)

        pre_cl0 = proj_cl0(0)
        P.dma("sp", cn1, dr["cnt"][0:1, :].partition_broadcast(128), writes=["cn1"], slot="c_cn1")
        P.op("dve", lambda e: e.reciprocal(out=cn1, in_=cn1), ["cn1"], ["cn1"])
        for g, w in enumerate(POOL_WINDOWS):
            hw = w // 2
            Hp, Wp = 32 + w - 1, 64 + w - 1
            extra = STOK_A if first_S[0] else []
            first_S[0] = False
            ir = cn1[:, g * 96: g * 96 + 32].unsqueeze(2).broadcast_to([128, 32, 64])
            ic = cn1[:, g * 96 + 32: g * 96 + 96].unsqueeze(1).broadcast_to([128, 32, 64])
            tt("dve", cnt3, ir, ic, ALU.mult, ["cn1"], ["cnt"] + extra)
            views = []
            for cl in range(2):
                v = [PQ[cl][i][:, 0:Hp * Wp].rearrange("p (r c) -> p r c", c=Wp) for i in range(2)]
                views.append(v)
                Pv, Qv = v
                memset(Pv[:, 0:hw, :], 0.0, ["pq%d_0" % cl] + extra)
                if hw > 1:
                    memset(Pv[:, hw + 32:Hp, :], 0.0, ["pq%d_0" % cl])
                memset(Pv[:, hw:hw + 32, 0:hw], 0.0, ["pq%d_0" % cl])
                if hw > 1:
                    memset(Pv[:, hw:hw + 32, hw + 64:Wp], 0.0, ["pq%d_0" % cl])
                if w in (2, 8):
                    memset(Qv[:, 0:hw, 0:64], 0.0, ["pq%d_1" % cl])
                    if hw > 1:
                        memset(Qv[:, hw + 32:Hp, 0:64], 0.0, ["pq%d_1" % cl])
            wts = [pre_cl0]
            for t in range(4):
                act(views[0][0][:, hw + 8 * t: hw + 8 * t + 8, hw:hw + 64], ps[t][:, :].rearrange("p (r c) -> p r c", c=64),
                    AF.Identity, ["ps%d" % t], ["pq0_0"])
            wt, wtk = load_win(2 * g + 1)
            wts.append((wt, wtk))
            for t in range(4):
                b = nextB()
                proj_seg(wt, wtk, 1 + t, b)
                act(views[1][0][:, hw + 8 * t: hw + 8 * t + 8, hw:hw + 64], ps[b][:, :].rearrange("p (r c) -> p r c", c=64),
                    AF.Identity, ["ps%d" % b], ["pq1_0"])
            if g + 1 < 4:
                pre_cl0 = proj_cl0(g + 1)
            state = [dict(cur=0, ln=Wp, rows=Hp) for _ in range(2)]
            k = 1
            while k < w:
                for cl in range(2):
                    st = state[cl]
                    src, dst = views[cl][st["cur"]], views[cl][1 - st["cur"]]
                    nl = st["ln"] - k
                    tt("dve", dst[:, hw:hw + 32, 0:nl], src[:, hw:hw + 32, 0:nl], src[:, hw:hw + 32, k:k + nl], ALU.add,
                       ["pq%d_%d" % (cl, st["cur"])], ["pq%d_%d" % (cl, 1 - st["cur"])])
                    st["cur"], st["ln"] = 1 - st["cur"], nl
                k *= 2
            k = 1
            while k < w:
                for cl in range(2):
                    st = state[cl]
                    src, dst = views[cl][st["cur"]], views[cl][1 - st["cur"]]
                    nr = st["rows"] - k
                    tt("dve", dst[:, 0:nr, 0:64], src[:, 0:nr, 0:64], src[:, k:k + nr, 0:64], ALU.add,
                       ["pq%d_%d" % (cl, st["cur"])], ["pq%d_%d" % (cl, 1 - st["cur"])])
                    st["cur"], st["rows"] = 1 - st["cur"], nr
                k *= 2
            for cl in range(2):
                st = state[cl]
                assert st["ln"] == 64 and st["rows"] == 32
                src, oth = views[cl][st["cur"]], views[cl][1 - st["cur"]]
                tt("dve", oth[:, 0:32, 0:64], src[:, 0:32, 0:64], cnt3, ALU.mult, ["pq%d_%d" % (cl, st["cur"]), "cnt"],
                   ["pq%d_%d" % (cl, 1 - st["cur"])])
                wt, wtk = wts[cl]
                for t in range(4):
                    b = nextB()
                    proj_seg(wt, wtk, 1 + t, b)
                    tt("dve", dbf[cl][:, 512 * t:512 * t + 512].rearrange("p (r c) -> p r c", c=64), oth[:, 8 * t:8 * t + 8, 0:64],
                       ps[b][:, :].rearrange("p (r c) -> p r c", c=64), ALU.subtract, ["pq%d_%d" % (cl, 1 - st["cur"]), "ps%d" % b],
                       ["dbf%d_%d" % (cl, t)])
            for ocl in range(2):
                oc = 2 * g + ocl
                for t in range(4):
                    b = nextB()
                    for k in range(2):
                        idx = ((g * 2 + ocl) * 2 + k) * 128
                        mm(ps[b][:, :], poolw[:, idx:idx + 128], dbf[k][:, 512 * t:512 * t + 512], k == 0, k == 1,
                           ["poolw", "dbf%d_%d" % (k, t)], ["ps%d" % b])
                    extra = YTOK if y_first[0] else []
                    y_first[0] = False
                    act(ypool3[:, oc, 512 * t:512 * t + 512], ps[b][:, :], AF.Identity, ["ps%d" % b, "vecs"],
                        ["ypool%d_%d" % (oc, t)] + extra, scale=V("pool_scale", oc))
            issue_casts(5)
        if "ypool" in dbg:
            P.dma("sp", dump("ypool", ypool, BF16)[:, :], ypool,
                  reads=["ypool%d_%d" % (oc, t) for oc in range(8) for t in range(4)], slot="dbg_yp")
        if stop == "Bp":
            P.emit()
            return nc, dbg_out

        cs = Carve(*R_S)
        UPW = 2320
        LOFF = 264
        upad = cs.b(UPW)
        dgw = cs.b(32 * 128)
        o_m2b = cs.cur
        m2b = cs.f(TT)
        xcb = bv(o_m2b, TT)
        xc = cs.f(TT)
        m2f = cs.f(TT)
        ra = [cs.f(TT), cs.f(TT)]
        ib = [cs.f(TT), cs.f(TT)]
        gel = cs.f(T)
        m2 = [m2f, m2b]
        m2tok = ["m2f", "m2b"]
        POOLTOK = ["cnt"] + PQTOK + ["dbf%d_%d" % (cl, t) for cl in range(2) for t in range(4)]
        memset(upad, 0.0, ["upad"] + POOLTOK)
        for k in range(4):
            for n in range(8):
                i = k * 8 + n
                ts("dve", dgw[:, i * 128:(i + 1) * 128], ident, V("lru_conv_w", i), ALU.mult, ["ident", "vecs"], ["dgw"] + (POOLTOK if i == 0 else []))
        RA = lambda d: ["ra%d_%d" % (d, si) for si in range(5)]
        IB = lambda d: ["ib%d_%d" % (d, si) for si in range(5)]
        win_next = load_win(8 + 0)
        for n in range(8):
            wt, wtk = win_next
            for si, (o, nn) in enumerate(SEGS):
                b = nextbank()
                proj_seg(wt, wtk, si, b)
                po = 2 + o if si == 0 else LOFF + (o - 256)
                act(upad[:, po:po + nn], ps[b][:, 0:nn], AF.Identity, ["ps%d" % b], ["upad"])
            for si, (o, nn) in enumerate(SEGS):
                base = o if si == 0 else LOFF - 2 + (o - 256)
                b = nextbank()
                for k in range(4):
                    i = k * 8 + n
                    mm(ps[b][:, 0:nn], dgw[:, i * 128:(i + 1) * 128], upad[:, base + k:base + k + nn], k == 0, k == 3,
                       ["dgw", "upad"], ["ps%d" % b])
                act(xc[:, o:o + nn], ps[b][:, 0:nn], AF.Identity, ["ps%d" % b, "vecs"], ["xc"], bias=V("lru_conv_b", n))
            P.op("dve", lambda e: e.tensor_copy(out=xcb, in_=xc), ["xc"], ["m2b"])
            for dr_ in range(2):
                for kind, dst, tkf in ((0, ra[dr_], RA(dr_)), (1, ib[dr_], IB(dr_))):
                    widx = ((kind * 2 + dr_) * 8 + n) * 128
                    hb_ = hbias[:, kind * 16 + dr_ * 8 + n: kind * 16 + dr_ * 8 + n + 1]
                    for si, (o, nn) in enumerate(SEGS):
                        b = nextbank()
                        mm(ps[b][:, 0:nn], lruw[:, widx:widx + 128], xcb[:, o:o + nn], True, True, ["lruw", "m2b"], ["ps%d" % b])
                        act(dst[:, o:o + nn], ps[b][:, 0:nn], AF.Tanh, ["ps%d" % b, "hbias"], [tkf[si]], bias=hb_, scale=0.5)
                hs = hs1[:, dr_ * 8 + n: dr_ * 8 + n + 1]
                act(ra[dr_], ra[dr_], AF.Exp, RA(dr_) + ["hs1"], RA(dr_), bias=hs, scale=hs)
                act(m2[dr_], ra[dr_], AF.Square, RA(dr_), [m2tok[dr_]])
                act(m2[dr_], m2[dr_], AF.Sqrt, [m2tok[dr_]], [m2tok[dr_]], scale=-0.25, bias=0.25)
            for dr_ in range(2):
                stt(ib[dr_], ib[dr_], 1.0, xc, ALU.add, ALU.mult, IB(dr_) + ["xc"], IB(dr_))
                tt("dve", ib[dr_], ib[dr_], m2[dr_], ALU.mult, IB(dr_) + [m2tok[dr_]], IB(dr_))
                if dr_ == 0:
                    P.op("dve", lambda e: e.tensor_tensor_scan(out=ib[0], data0=ra[0], data1=ib[0], initial=0.0, op0=ALU.mult, op1=ALU.add),
                         RA(0) + IB(0), IB(0))
                else:
                    P.op("dve", lambda e: e.tensor_tensor_scan(out=ib[1][:, 0:256][:, ::-1], data0=ra[1][:, 0:256][:, ::-1],
                                                                data1=ib[1][:, 0:256][:, ::-1], initial=0.0, op0=ALU.mult, op1=ALU.add),
                         RA(1) + IB(1), IB(1))
                    P.op("dve", lambda e: e.tensor_tensor_scan(out=ib[1][:, 256:TT][:, ::-1], data0=ra[1][:, 256:TT][:, ::-1],
                                                                data1=ib[1][:, 256:TT][:, ::-1], initial=ib[1][:, 0:1], op0=ALU.mult, op1=ALU.add),
                         RA(1) + IB(1), IB(1))
            wt, wtk = load_win(16 + n)
            if n + 1 < 8:
                win_next = load_win(8 + n + 1)
            issue_casts(5)
            for t in range(4):
                b = nextbank()
                proj_seg(wt, wtk, 1 + t, b)
                act(gel[:, 512 * t:512 * t + 512], ps[b][:, :], AF.Gelu_apprx_tanh, ["ps%d" % b], ["gel%d" % t])
            GEL = ["gel%d" % t for t in range(4)]
            tt("dve", ib[0][:, 256:TT], ib[0][:, 256:TT], ib[1][:, 256:TT], ALU.add, IB(0) + IB(1), IB(0))
            tt("dve", ylru3[:, n, :], ib[0][:, 256:TT], gel, ALU.mult, IB(0) + GEL, ["ylru%d" % n] + (YTOK if n == 0 else []))
        if "ylru" in dbg:
            P.dma("sp", dump("ylru", ylru, BF16)[:, :], ylru, reads=["ylru%d" % n for n in range(8)], slot="dbg_yl")
        if stop == "B":
            P.emit()
            return nc, dbg_out

        LRUTOK = ["upad", "xc", "m2f", "m2b", "dgw"] + RA(0) + RA(1) + IB(0) + IB(1) + GEL
        cs = Carve(*R_S)
        mbuf = cs.b(8 * T)
        m3 = mbuf.rearrange("p (c t) -> p c t", t=T)
        S_C2 = cs.cur
        sgb = [[cs.f(512), cs.f(512)] for _ in range(2)]
        t12 = [[cs.f(512), cs.f(512)] for _ in range(2)]
        cwd = Carve(W_DYN, R_W[0] + R_W[1] - W_DYN)
        c1w = [[cwd.b(1024) for _ in range(4)] for _ in range(2)]
        W_OLD = ["lruw", "poolw"] + ["win%d" % i for i in range(NWIN)]
        first_c1 = [True]
        YP = lambda t: ["ypool%d_%d" % (k, t) for k in range(8)]
        YL = ["ylru%d" % k for k in range(8)]
        it = 0
        for oc in range(8):
            sl = oc % 2
            srcs = [dr["wpp"][oc], dr["win"][24 + oc], dr["wpl"][oc], dr["win"][32 + oc]]
            for i in range(4):
                extra = W_OLD if first_c1[0] else []
                first_c1[0] = False
                P.dma("pool", c1w[sl][i], srcs[i], writes=["c1w%d_%d" % (sl, i)] + extra, slot="c1w%d_%d" % (sl, i))
            for t in range(4):
                bb = [nextbank() for _ in range(4)]
                for k in range(8):
                    mm(ps[bb[0]][:, :], c1w[sl][0][:, k * 128:(k + 1) * 128], ypool3[:, k, 512 * t:512 * t + 512], k == 0, k == 7,
                       ["c1w%d_0" % sl] + YP(t), ["ps%d" % bb[0]])
                for k in range(8):
                    mm(ps[bb[1]][:, :], c1w[sl][1][:, k * 128:(k + 1) * 128], h3[:, k, 256 + 512 * t:256 + 512 * t + 512], k == 0, k == 7,
                       ["c1w%d_1" % sl] + HSEG(1 + t), ["ps%d" % bb[1]])
                for k in range(8):
                    mm(ps[bb[2]][:, :], c1w[sl][2][:, k * 128:(k + 1) * 128], ylru3[:, k, 512 * t:512 * t + 512], k == 0, k == 7,
                       ["c1w%d_2" % sl] + YL, ["ps%d" % bb[2]])
                for k in range(8):
                    mm(ps[bb[3]][:, :], c1w[sl][3][:, k * 128:(k + 1) * 128], h3[:, k, 256 + 512 * t:256 + 512 * t + 512], k == 0, k == 7,
                       ["c1w%d_3" % sl] + HSEG(1 + t), ["ps%d" % bb[3]])
                p = it % 2
                it += 1
                extra = LRUTOK if (oc == 0 and t == 0) else []
                act(sgb[p][0], ps[bb[1]][:, :], AF.Sigmoid, ["ps%d" % bb[1]], ["sg%d_0" % p] + extra)
                act(sgb[p][1], ps[bb[3]][:, :], AF.Sigmoid, ["ps%d" % bb[3]], ["sg%d_1" % p])
                tt("dve", t12[p][0], ps[bb[0]][:, :], sgb[p][0], ALU.mult, ["ps%d" % bb[0], "sg%d_0" % p], ["t12%d_0" % p])
                tt("dve", t12[p][1], ps[bb[2]][:, :], sgb[p][1], ALU.mult, ["ps%d" % bb[2], "sg%d_1" % p], ["t12%d_1" % p])
                tt("dve", m3[:, oc, 512 * t:512 * t + 512], t12[p][0], t12[p][1], ALU.add, ["t12%d_0" % p, "t12%d_1" % p],
                   ["m%d_%d" % (oc, t)])
        MT = lambda tl: ["m%d_%d" % (k, t) for k in range(8) for t in tl]
        if "m" in dbg:
            P.dma("sp", dump("m", mbuf, BF16)[:, :], mbuf, reads=MT(range(4)), slot="dbg_mm")
        if stop == "C1":
            P.emit()
            return nc, dbg_out

        cs = Carve(S_C2, R_S[0] + R_S[1] - S_C2)
        sq2 = [cs.f(640), cs.f(640)]
        tm2 = [cs.f(640), cs.f(640)]
        sd2 = cs.f(640)
        rstd2 = cs.f(640)
        gpad = [cs.f(660) for _ in range(4)]
        mixB = cs.f(8 * 640)
        ssum2 = cs.f(640)
        HY_OLD = [t for si in range(5) for t in HSEG(si)] + [t for tl in range(4) for t in YP(tl)] + YL
        cb_ = Carve(R_H[0], R_H[1] + R_Y[1])
        xt1 = cb_.f(8 * 640)
        mixb = cb_.f(8 * 640)
        hf = cb_.b(8 * 640)
        abuf = cb_.b(24 * 512)
        wupb = [cb_.b(2048) for _ in range(3)]
        wdnb = [cb_.b(3072) for _ in range(2)]
        xt13 = xt1.rearrange("p (c t) -> p c t", t=640)
        mixbufs = [mixb, mixB]
        mix3s = [mb.rearrange("p (c t) -> p c t", t=640) for mb in mixbufs]
        f3s = [mb[:, 0:8 * 512].rearrange("p (c t) -> p c t", t=512) for mb in mixbufs]
        hf3 = hf.rearrange("p (c t) -> p c t", t=640)
        a3 = abuf.rearrange("p (c t) -> p c t", t=512)
        cwd = Carve(W_DYN, R_W[0] + R_W[1] - W_DYN)
        accb = [cwd.f(512) for _ in range(4)]
        wupb.append(cwd.b(2048))
        woutb = [cwd.b(1024) for _ in range(3)]
        wo_ctr = [0]
        C1W_OLD = ["c1w%d_%d" % (s, i) for s in range(2) for i in range(4)]
        oT3 = outT.rearrange("(c p) t -> p c t", p=128)
        memset(gpad[0], 0.0, ["gpad0"] + ["sg%d_%d" % (p, i) for p in range(2) for i in range(2)] + ["t12%d_%d" % (p, i) for p in range(2) for i in range(2)])
        for gi in range(1, 4):
            memset(gpad[gi], 0.0, ["gpad%d" % gi])
        first_hy = [True]
        wup_ctr = [0]
        wdn_ctr = [0]
        dg_ctr = [0]
        gp_ctr = [0]

        def norm_rstd(src_fn, ntok, reads_fn, hook=None):
            subs = [(0, min(512, ntok))] + ([(512, ntok - 512)] if ntok > 512 else [])
            for c in range(8):
                if c == 0:
                    act(ssum2[:, 0:ntok], src_fn(c), AF.Square, reads_fn(c), ["ssum2"])
                else:
                    sq = sq2[c % 2]
                    act(sq[:, 0:ntok], src_fn(c), AF.Square, reads_fn(c), ["sq2_%d" % (c % 2)])
                    tt("dve", ssum2[:, 0:ntok], ssum2[:, 0:ntok], sq[:, 0:ntok], ALU.add, ["ssum2", "sq2_%d" % (c % 2)], ["ssum2"])
                if hook is not None:
                    hook(c)
            for (so, sn) in subs:
                b = nextbank()
                mm(ps[b][:, 0:sn], ones, ssum2[:, so:so + sn], True, True, ["ones", "ssum2"], ["ps%d" % b])
                act(sd2[:, so:so + sn], ps[b][:, 0:sn], AF.Ln, ["ps%d" % b], ["sd2"], scale=1.0 / D, bias=EPS)
            act(rstd2[:, 0:ntok], sd2[:, 0:ntok], AF.Exp, ["sd2"], ["rstd2"], scale=-0.5)

        def geom(tl):
            r0 = max(8 * tl - 1, 0)
            r1 = min(8 * tl + 9, 32)
            ntok = (r1 - r0) * 64
            return r0, ntok, r0 * 64, [(0, 512)] + [(512, ntok - 512)]

        first_mix = [True]

        def mix_units(tl):
            r0, ntok, tok0, subs = geom(tl)
            par = tl % 2
            mtl = sorted({min(3, (tok0 + so) // 512) for so, sn in subs} | {min(3, (tok0 + so + sn - 1) // 512) for so, sn in subs})
            units = []
            for oc in range(8):
                for (so, sn) in subs:
                    st_ = {}

                    def pe_fn(oc=oc, so=so, sn=sn, st_=st_):
                        if so == 0:
                            wi = wo_ctr[0] % 3
                            wo_ctr[0] += 1
                            extra = (LRUTOK + C1W_OLD) if first_mix[0] else []
                            P.dma("sp", woutb[wi], woutS[oc], reads=["wc_all"], writes=["wo%d" % wi] + extra, slot="wo%d" % wi)
                            wo_cur[0] = wi
                        wi = wo_cur[0]
                        b = nextbank()
                        st_["b"] = b
                        for k in range(8):
                            mm(ps[b][:, 0:sn], woutb[wi][:, k * 128:(k + 1) * 128],
                               m3[:, k, tok0 + so:tok0 + so + sn], k == 0, k == 7, ["wo%d" % wi] + MT(mtl), ["ps%d" % b])

                    def act_fn(oc=oc, so=so, sn=sn, st_=st_):
                        b = st_["b"]
                        extra2 = []
                        if oc == 0 and so == 0:
                            extra2 = ["f%d_%d" % (par, q) for q in range(8)]
                            if first_mix[0] or tl == 1:
                                extra2 = extra2 + ["sg%d_%d" % (p_, i) for p_ in range(2) for i in range(2)] + ["t12%d_%d" % (p_, i) for p_ in range(2) for i in range(2)] + HY_OLD
                        first_mix[0] = False
                        act(mix3s[par][:, oc, so:so + sn], ps[b][:, 0:sn], AF.Identity, ["ps%d" % b], ["mix%d_%d" % (par, oc)] + extra2)
                    units.append((pe_fn, act_fn))
            return units

        wo_cur = [0]

        def do_mix(tl):
            for pe_fn, act_fn in mix_units(tl):
                pe_fn()
                act_fn()

        def xloads(tn, extra=()):
            _, ntok_, tok0_, _ = geom(tn)
            for c in range(8):
                P.dma("sp", xt13[:, c, 0:ntok_], xT3[:, c, tok0_:tok0_ + ntok_], writes=["xt1_%d" % c, "x1_%d" % c] + (list(extra) if c == 0 else []),
                      slot="xt1_%d" % c)

        do_mix(0)
        for tl in range(4):
            r0, ntok, tok0, subs = geom(tl)
            par = tl % 2
            mix3 = mix3s[par]
            f3 = f3s[par]
            MIXT = lambda c: "mix%d_%d" % (par, c)
            FT = lambda c: "f%d_%d" % (par, c)
            co = (8 * tl - r0) * 64
            s0 = r0 - (8 * tl - 1)
            extra = HY_OLD if first_hy[0] else []
            first_hy[0] = False
            if tl == 0:
                xloads(0, extra)
            norm_rstd(lambda c: mix3[:, c, 0:ntok], ntok, lambda c: [MIXT(c)])
            pend = mix_units(tl + 1) if tl + 1 < 4 else []
            pend_pe = [u[0] for u in pend]
            pend_act = [u[1] for u in pend]
            inflight = [0]

            def pump(nact):
                for _ in range(nact):
                    if pend_act:
                        pend_act.pop(0)()
                        inflight[0] -= 1
                while pend_pe and inflight[0] < 5:
                    pend_pe.pop(0)()
                    inflight[0] += 1

            pump(0)
            for c in range(8):
                tm = tm2[c % 2]
                tt("dve", tm[:, 0:ntok], mix3[:, c, 0:ntok], rstd2[:, 0:ntok], ALU.mult, [MIXT(c), "rstd2"], ["tm2_%d" % (c % 2)])
                stt(xt13[:, c, 0:ntok], tm[:, 0:ntok], GG1[:, c:c + 1], xt13[:, c, 0:ntok], ALU.mult, ALU.add,
                    ["tm2_%d" % (c % 2), "der", "xt1_%d" % c], ["x1_%d" % c])
            P.dma("pool", oT3[:, :, 512 * tl:512 * tl + 512], xt13[:, :, co:co + 512], reads=["x1_%d" % c for c in range(8)],
                  writes=["outt%d" % tl], slot="out1")
            pump(4)
            norm_rstd(lambda c: xt13[:, c, 0:ntok], ntok, lambda c: ["x1_%d" % c], hook=lambda c: pump(1))
            pump(4)
            for c in range(8):
                tm = tm2[c % 2]
                tt("dve", tm[:, 0:ntok], xt13[:, c, 0:ntok], rstd2[:, 0:ntok], ALU.mult, ["x1_%d" % c, "rstd2"], ["tm2_%d" % (c % 2)])
                act(hf3[:, c, 0:ntok], tm[:, 0:ntok], AF.Identity, ["tm2_%d" % (c % 2), "der", "modfm"], ["hf%d" % c],
                    bias=MOD(3, c, 0), scale=A2[:, c:c + 1])
            while pend_act or pend_pe:
                pump(1)
            HF = ["hf%d" % c for c in range(8)]
            if tl == 3:
                for gi in range(4):
                    memset(gpad[gi].rearrange("p (r c) -> p r c", c=66)[:, 9, :], 0.0, ["gpad%d" % gi])
            def ffn_front(p):
                st = []
                for i in range(2):
                    j = 2 * p + i
                    q = j % 4
                    wi = wup_ctr[0] % 4
                    wup_ctr[0] += 1
                    P.dma("sp", wupb[wi], wupS[j], reads=["wc_all"], writes=["wup%d" % wi], slot="wup%d" % wi)
                    gp3 = gpad[q].rearrange("p (r c) -> p r c", c=66)
                    wg = wupb[wi][:, 0:1024]
                    row = s0
                    for (so, sn) in subs:
                        b = nextbank()
                        for k in range(8):
                            mm(ps[b][:, 0:sn], wg[:, k * 128:(k + 1) * 128], hf3[:, k, so:so + sn], k == 0, k == 7,
                               ["wup%d" % wi] + HF, ["ps%d" % b])
                        nr = sn // 64
                        act(gp3[:, row:row + nr, 1:65], ps[b][:, 0:sn].rearrange("p (r c) -> p r c", c=64), AF.Identity,
                            ["ps%d" % b], ["gpad%d" % q])
                        row += nr
                    acc3 = accb[q].rearrange("p (r c) -> p r c", c=64)
                    act(acc3, gp3[:, 0:8, 0:64], AF.Identity, ["gpad%d" % q, "vecs"], ["acc%d" % q],
                        bias=V("ffn_conv_b", j), scale=V("ffn_conv_w", 0 * 24 + j))
                    st.append((j, q, wi, gp3, acc3))
                return st

            def ffn_taps(st, taps):
                for tap in taps:
                    dy, dx = tap // 3, tap % 3
                    for (j, q, wi, gp3, acc3) in st:
                        stt(acc3, gp3[:, dy:dy + 8, dx:dx + 64], V("ffn_conv_w", tap * 24 + j), acc3, ALU.mult, ALU.add,
                            ["gpad%d" % q, "acc%d" % q, "vecs"], ["acc%d" % q])

            def ffn_u(st):
                out = []
                for (j, q, wi, gp3, acc3) in st:
                    wu = wupb[wi][:, 1024:2048]
                    bu = nextbank()
                    for k in range(8):
                        mm(ps[bu][:, :], wu[:, k * 128:(k + 1) * 128], hf3[:, k, co:co + 512], k == 0, k == 7,
                           ["wup%d" % wi] + HF, ["ps%d" % bu])
                    out.append(bu)
                return out

            def ffn_gelu(st):
                for (j, q, wi, gp3, acc3) in st:
                    act(accb[q], accb[q], AF.Gelu_apprx_tanh, ["acc%d" % q], ["acc%d" % q])

            def ffn_mult(st, bus):
                for (j, q, wi, gp3, acc3), bu in zip(st, bus):
                    tt("dve", a3[:, j, :], accb[q], ps[bu][:, :], ALU.mult, ["acc%d" % q, "ps%d" % bu], ["a%d" % j])

            if tl + 1 < 4:
                xloads(tl + 1)
            prev = None
            for p in range(12):
                st = ffn_front(p)
                if prev is not None:
                    ffn_gelu(prev[0])
                ffn_taps(st, range(1, 2))
                if prev is not None:
                    ffn_mult(*prev)
                ffn_taps(st, range(2, 9))
                bus = ffn_u(st)
                prev = (st, bus)
            ffn_gelu(prev[0])
            ffn_mult(*prev)
            AT = ["a%d" % j for j in range(24)]
            for oc in range(8):
                wi = wdn_ctr[0] % 2
                wdn_ctr[0] += 1
                P.dma("sp", wdnb[wi], wdnS[oc], reads=["wc_all"], writes=["wdn%d" % wi], slot="wdn%d" % wi)
                b = nextbank()
                for k in range(24):
                    mm(ps[b][:, :], wdnb[wi][:, k * 128:(k + 1) * 128], a3[:, k, :], k == 0, k == 23, ["wdn%d" % wi] + AT, ["ps%d" % b])
                act(f3[:, oc, :], ps[b][:, :], AF.Identity, ["ps%d" % b], [FT(oc)] + ([MIXT(q) for q in range(8)] if oc == 0 else []))
                if oc == 0:
                    act(ssum2[:, 0:512], f3[:, oc, :], AF.Square, [FT(oc)], ["ssum2"])
                else:
                    sq = sq2[oc % 2]
                    act(sq[:, 0:512], f3[:, oc, :], AF.Square, [FT(oc)], ["sq2_%d" % (oc % 2)])
                    tt("dve", ssum2[:, 0:512], ssum2[:, 0:512], sq[:, 0:512], ALU.add, ["ssum2", "sq2_%d" % (oc % 2)], ["ssum2"])
            bss = nextbank()
            mm(ps[bss][:, :], ones, ssum2[:, 0:512], True, True, ["ones", "ssum2"], ["ps%d" % bss])
            act(sd2[:, 0:512], ps[bss][:, :], AF.Ln, ["ps%d" % bss], ["sd2"], scale=1.0 / D, bias=EPS)
            act(rstd2[:, 0:512], sd2[:, 0:512], AF.Exp, ["sd2"], ["rstd2"], scale=-0.5)
            for oc in range(8):
                P.op("dve", lambda e, oc=oc, f3=f3: e.scalar_tensor_tensor(out=f3[:, oc, :], in0=f3[:, oc, :], scalar=GG2[:, oc:oc + 1],
                                                                           in1=rstd2[:, 0:512], op0=ALU.mult, op1=ALU.mult),
                     [FT(oc), "rstd2", "der"], [FT(oc)])
            P.dma("pool", oT3[:, :, 512 * tl:512 * tl + 512], f3, reads=[FT(oc) for oc in range(8)], writes=["outt%d" % tl],
                  slot="out2", accum_op=ALU.add)
        P.emit()
    return nc, dbg_out


def _fm(v, nch):
    v = np.asarray(v, np.float32)
    lead = v.shape[:-1]
    r = v.reshape(lead + (nch, 128))
    r = np.moveaxis(r, -1, 0)
    return np.ascontiguousarray(r.reshape(128, -1))


def _wtile(w, kch, och):
    w = np.asarray(w, np.float32)
    r = w.reshape(kch, 128, och, 128).transpose(2, 1, 0, 3)
    return np.ascontiguousarray(r.reshape(och, 128, kch * 128))


def _window_counts():
    out = np.zeros((4, 96), np.float32)
    for gi, w in enumerate(POOL_WINDOWS):
        def cnt1(n):
            pos = np.arange(n)
            lo = np.clip(pos - w // 2, 0, n)
            hi = np.clip(pos + w - w // 2, 0, n)
            return (hi - lo).astype(np.float32)
        out[gi, 0:32] = cnt1(32)
        out[gi, 32:96] = cnt1(64)
    return out.reshape(1, 384)


def prep_inputs(x, c, ctx, c_ctx, w_mod, b_mod, g_pre_mix, g_post_mix, g_pre_ffn, g_post_ffn,
                w_in, pool_w, pool_scale, lru_conv_w, lru_conv_b, lru_wa, lru_ba, lru_wx, lru_bx,
                lru_lambda, w_proj_pool, w_proj_lru, w_out, w_up, ffn_conv_w, ffn_conv_b, w_down):
    f = lambda a: np.asarray(a, np.float32)
    vec_parts = {
        "g_pre_mix": _fm(f(g_pre_mix)[0], 8), "g_post_mix": _fm(f(g_post_mix)[0], 8),
        "g_pre_ffn": _fm(f(g_pre_ffn)[0], 8), "g_post_ffn": _fm(f(g_post_ffn)[0], 8),
        "pool_scale": _fm(f(pool_scale)[0], 8), "lru_conv_w": _fm(f(lru_conv_w)[0], 8),
        "lru_conv_b": _fm(f(lru_conv_b)[0], 8), "lru_ba": _fm(f(lru_ba)[0], 8), "lru_bx": _fm(f(lru_bx)[0], 8),
        "lru_lambda": _fm(f(lru_lambda)[0], 8), "ffn_conv_w": _fm(f(ffn_conv_w)[0].reshape(9, 3072), 24),
        "ffn_conv_b": _fm(f(ffn_conv_b)[0], 24), "b_mod": _fm(f(b_mod)[0], 48),
    }
    vecs = np.ascontiguousarray(np.concatenate([vec_parts[n] for n, _ in _VEC_SPEC], axis=1))
    assert vecs.shape == (128, NV)
    wmod_h = np.ascontiguousarray(f(w_mod)[0].reshape(8, 128, 12, 512).transpose(2, 1, 0, 3).reshape(12, 128, 4096))
    win_h = _wtile(f(w_in)[0], 8, 40)
    wpp_h = _wtile(f(w_proj_pool)[0], 8, 8)
    wpl_h = _wtile(f(w_proj_lru)[0], 8, 8)
    wout_h = _wtile(f(w_out)[0], 8, 8)
    wup_t = _wtile(f(w_up)[0], 8, 48)
    wup_h = np.ascontiguousarray(np.concatenate([wup_t[0:24], wup_t[24:48]], axis=2))
    wdown_h = _wtile(f(w_down)[0], 24, 8)
    pw = f(pool_w)[0].reshape(4, 2, 128, 2, 128).transpose(2, 0, 3, 1, 4)
    poolw_h = np.ascontiguousarray(pw.reshape(128, 2048))
    lw = np.stack([f(lru_wa)[0], f(lru_wx)[0]], axis=0)
    lruw_h = np.ascontiguousarray(lw.transpose(3, 0, 1, 2, 4).reshape(128, 4096))
    shared = {"vecs": vecs, "ident": np.eye(128, dtype=np.float32), "cnt": _window_counts(), "wmod": wmod_h,
              "win": win_h, "wpp": wpp_h, "wpl": wpl_h, "wout": wout_h, "wup": wup_h, "wdown": wdown_h,
              "poolw": poolw_h, "lruw": lruw_h}
    xf, cf, ctxf, ccf = f(x), f(c), f(ctx), f(c_ctx)
    in_maps = []
    for b in range(NCORES):
        m = dict(shared)
        m["xT"] = np.ascontiguousarray(xf[b].T)
        m["ctxT"] = np.ascontiguousarray(ctxf[b].T)
        cc2 = np.stack([cf[b], ccf], axis=0)
        m["cc"] = np.ascontiguousarray(cc2.reshape(2, 8, 128).transpose(2, 1, 0).reshape(128, 16))
        in_maps.append(m)
    return in_maps


_CACHE = {}


def kernel(**inputs):
    in_maps = prep_inputs(**inputs)
    if "nc" not in _CACHE:
        _CACHE["nc"] = build_program()[0]
    nc = _CACHE["nc"]
    res = run_bass_kernel_spmd(nc, in_maps, core_ids=list(range(NCORES)))
    out = np.stack([np.asarray(r["outT"], np.float32).T for r in res.results], axis=0)
    return np.ascontiguousarray(out.astype(np.float32))
```

```python
import numpy as np
from contextlib import ExitStack
import concourse.bass as bass
import concourse.mybir as mybir
from concourse.bass_utils import run_bass_kernel_spmd

F32 = mybir.dt.float32
BF16 = mybir.dt.bfloat16
RELAX_DVE = False
AF = mybir.ActivationFunctionType
ALU = mybir.AluOpType

NCORES = 8
D = 1024
T = 2048
CT = 256
TT = T + CT
NCH = 8
EPS = 1e-6
POOL_WINDOWS = (2, 4, 8, 16)
SEGS = [(0, 256)] + [(256 + 512 * i, 512) for i in range(4)]

_VEC_SPEC = [("g_pre_mix", 8), ("g_post_mix", 8), ("g_pre_ffn", 8), ("g_post_ffn", 8), ("pool_scale", 8),
             ("lru_conv_w", 32), ("lru_conv_b", 8), ("lru_ba", 16), ("lru_bx", 16), ("lru_lambda", 16),
             ("ffn_conv_w", 216), ("ffn_conv_b", 24), ("b_mod", 48)]
VOFF = {}
_o = 0
for _n, _c in _VEC_SPEC:
    VOFF[_n] = _o
    _o += _c
NV = _o


class _Op:
    __slots__ = ("eng", "fn", "deps", "needs_inc", "sem", "val", "is_dma", "slot", "pos")

    def __init__(self, eng, fn, is_dma=False, slot=None):
        self.eng = eng
        self.fn = fn
        self.deps = []
        self.needs_inc = False
        self.sem = None
        self.val = None
        self.is_dma = is_dma
        self.slot = slot
        self.pos = -1


class Prog:
    ENGS = ("pe", "act", "dve", "pool", "sp")

    def __init__(self, nc):
        self.nc = nc
        self.q = {e: [] for e in self.ENGS}
        self.last_w = {}
        self.readers = {}
        self.all_ops = []

    def _track(self, op, reads, writes):
        deps = {}
        for t in reads:
            w = self.last_w.get(t)
            if w is not None:
                deps[id(w)] = w
        for t in writes:
            w = self.last_w.get(t)
            if w is not None:
                deps[id(w)] = w
            for r in self.readers.get(t, {}).values():
                deps[id(r)] = r
        for d in deps.values():
            if d is op:
                continue
            if (not d.is_dma) and (not op.is_dma) and d.eng == "pe" and op.eng == "pe":
                continue
            if RELAX_DVE and (not d.is_dma) and (not op.is_dma) and d.eng == "dve" and op.eng == "dve" and op.pos - d.pos >= 2:
                continue
            d.needs_inc = True
            op.deps.append(d)
        for t in writes:
            self.last_w[t] = op
            self.readers[t] = {}
        for t in reads:
            key = ("dma", op.slot) if op.is_dma else op.eng
            self.readers.setdefault(t, {})[key] = op

    def op(self, eng, fn, reads=(), writes=()):
        o = _Op(eng, fn)
        o.pos = len(self.q[eng])
        self._track(o, reads, writes)
        self.q[eng].append(o)
        self.all_ops.append(o)
        return o

    def dma(self, eng, out, in_, reads=(), writes=(), slot=None, **kw):
        o = _Op(eng, None, is_dma=True, slot=slot)
        o.fn = lambda e, s, o_=out, i_=in_, kw_=kw: e.dma_start(out=o_, in_=i_, **kw_).then_inc(s, 16)
        self._track(o, reads, writes)
        o.needs_inc = True
        self.q[eng].append(o)
        self.all_ops.append(o)
        return o

    def emit(self, final_wait_eng="sp"):
        nc = self.nc
        with ExitStack() as es:
            esem = {e: es.enter_context(nc.semaphore("sem_" + e)) for e in self.ENGS}
            slot_names = sorted({o.slot for o in self.all_ops if o.is_dma})
            ssem = {s: es.enter_context(nc.semaphore("dsem_" + s)) for s in slot_names}
            cnt = {e: 0 for e in self.ENGS}
            for e in self.ENGS:
                for o in self.q[e]:
                    if (not o.is_dma) and o.needs_inc:
                        cnt[e] += 1
                        o.sem, o.val = esem[e], cnt[e]
            scnt = {s: 0 for s in slot_names}
            for o in self.all_ops:
                if o.is_dma:
                    scnt[o.slot] += 16
                    o.sem, o.val = ssem[o.slot], scnt[o.slot]
            block = es.enter_context(nc.Block())
            final = [(ssem[s], scnt[s]) for s in slot_names]

            def run(e):
                def body(eng):
                    waited = {}
                    for o in self.q[e]:
                        for d in o.deps:
                            k = id(d.sem)
                            if waited.get(k, 0) < d.val:
                                eng.wait_ge(d.sem, d.val)
                                waited[k] = d.val
                        if o.is_dma:
                            o.fn(eng, o.sem)
                        else:
                            ins = o.fn(eng)
                            if o.needs_inc:
                                ins.then_inc(o.sem, 1)
                    if e == final_wait_eng:
                        for s, v in final:
                            if v > 0:
                                eng.wait_ge(s, v)
                return body

            block.tensor(run("pe"))
            block.scalar(run("act"))
            block.vector(run("dve"))
            block.gpsimd(run("pool"))
            block.sync(run("sp"))


def build_program(stop=None, dbg=()):
    nc = bass.Bass("TRN2", target_bir_lowering=False)
    dr = {}

    def din(name, shape, dt=F32):
        dr[name] = nc.dram_tensor(name, shape, dt, kind="ExternalInput").ap()

    din("xT", [D, T])
    din("ctxT", [D, CT])
    din("cc", [128, 16])
    din("vecs", [128, NV])
    din("ident", [128, 128])
    din("cnt", [1, 384])
    din("wmod", [12, 128, 4096])
    din("win", [40, 128, 1024])
    din("wpp", [8, 128, 1024])
    din("wpl", [8, 128, 1024])
    din("wout", [8, 128, 1024])
    din("wup", [24, 128, 2048])
    din("wdown", [8, 128, 3072])
    din("poolw", [128, 2048])
    din("lruw", [128, 4096])
    outT = nc.dram_tensor("outT", [D, T], F32, kind="ExternalOutput").ap()
    wupS = nc.dram_tensor("wupS", [24, 128, 2048], BF16, kind="Internal").ap()
    wdnS = nc.dram_tensor("wdnS", [8, 128, 3072], BF16, kind="Internal").ap()
    woutS = nc.dram_tensor("woutS", [8, 128, 1024], BF16, kind="Internal").ap()
    dbg_out = {}

    es = ExitStack()
    with es:
        AW = 53200
        arena = es.enter_context(nc.sbuf_tensor("arena", [128, AW], F32))
        ps = [es.enter_context(nc.psum_tensor("ps%d" % i, [128, 512], F32)) for i in range(8)]
        P = Prog(nc)

        def fv(off, n):
            assert off % 4 == 0 and off + 4 * n <= AW * 4, (off, n)
            return arena[:, off // 4: off // 4 + n]

        def bv(off, n):
            assert off % 4 == 0 and n % 2 == 0 and off + 2 * n <= AW * 4, (off, n)
            return arena[:, off // 4: off // 4 + n // 2].bitcast(BF16)

        class Carve:
            def __init__(self, base, size):
                self.base, self.end, self.cur = base, base + size, base

            def f(self, n):
                a = fv(self.cur, n)
                self.cur += 4 * n
                self.cur = (self.cur + 63) // 64 * 64
                assert self.cur <= self.end, ("carve overflow", self.cur, self.end)
                return a

            def b(self, n):
                a = bv(self.cur, n)
                self.cur += 2 * n
                self.cur = (self.cur + 63) // 64 * 64
                assert self.cur <= self.end, ("carve overflow", self.cur, self.end)
                return a

        KB = 1024
        R_H = (0, 36 * KB)
        R_Y = (36 * KB, 64 * KB)
        R_S = (100 * KB, 84 * KB)
        R_W = (184 * KB, AW * 4 - 184 * KB)

        bank_ctr = [0]

        def nextbank():
            b = bank_ctr[0] % 8
            bank_ctr[0] += 1
            return b

        def mm(out, lhsT, rhs, start, stop, reads, writes):
            P.op("pe", lambda e: e.matmul(out, lhsT, rhs, start=start, stop=stop), reads, writes)

        def act(out, in_, func, reads, writes, bias=None, scale=None):
            kw = {}
            if bias is not None:
                kw["bias"] = bias
            if scale is not None:
                kw["scale"] = scale
            P.op("act", lambda e: e.activation(out=out, in_=in_, func=func, **kw), reads, writes)

        def tt(eng, out, in0, in1, op, reads, writes):
            P.op(eng, lambda e: e.tensor_tensor(out=out, in0=in0, in1=in1, op=op), reads, writes)

        def ts(eng, out, in0, s1, op0, reads, writes, s2=None, op1=None):
            if op1 is None:
                P.op(eng, lambda e: e.tensor_scalar(out=out, in0=in0, scalar1=s1, scalar2=None, op0=op0), reads, writes)
            else:
                P.op(eng, lambda e: e.tensor_scalar(out=out, in0=in0, scalar1=s1, scalar2=s2, op0=op0, op1=op1), reads, writes)

        def stt(out, in0, scalar, in1, op0, op1, reads, writes):
            P.op("dve", lambda e: e.scalar_tensor_tensor(out=out, in0=in0, scalar=scalar, in1=in1, op0=op0, op1=op1), reads, writes)

        def memset(ap, val, writes):
            P.op("pool", lambda e: e.memset(ap, val), (), writes)

        def dump(name, ap, dt=F32):
            shape = [ap.shape[0], int(np.prod(ap.shape[1:]))]
            t = nc.dram_tensor("dbg_" + name, shape, dt, kind="ExternalOutput").ap()
            dbg_out[name] = t
            return t

        cw = Carve(*R_W)
        vecs = cw.f(NV)
        ident = cw.f(128)
        ones = cw.f(128)
        identb = cw.b(128)
        ccs = cw.f(16)
        ssil = cw.f(16)
        modfm = cw.f(96)
        der = cw.f(64)
        lrud = cw.f(64)
        cw2 = cw.f(32)
        cn1 = cw.f(384)
        W_DYN = cw.cur

        def V(name, i=0):
            o = VOFF[name] + i
            return vecs[:, o:o + 1]

        def Vs(name, i0, n):
            o = VOFF[name] + i0
            return vecs[:, o:o + n]

        P.dma("sp", vecs, dr["vecs"][:, :], writes=["vecs"], slot="c_vecs")
        P.dma("sp", ident, dr["ident"][:, :], writes=["ident"], slot="c_ident")
        P.dma("sp", ccs, dr["cc"][:, :], writes=["cc"], slot="c_cc")
        memset(ones, 1.0, ["ones"])
        P.op("dve", lambda e: e.tensor_copy(out=identb, in_=ident), ["ident"], ["identb"])
        act(ssil, ccs, AF.Silu, ["cc"], ["ssil"])
        s3 = ssil.rearrange("p (k r) -> p k r", r=2)

        cs = Carve(*R_S)
        wmb = [cs.f(4096), cs.f(4096)]
        modrow = cs.f(6144)
        xa4 = cs.f(8 * 512)
        rstdA = cs.f(TT)
        cy = Carve(*R_Y)
        xa = [cy.f(8 * 256), cy.f(8 * 512), cy.f(8 * 512), cy.f(8 * 512), xa4]
        sqb = [cy.f(512), cy.f(512)]
        ssumA = cy.f(512)
        sdb = cy.f(512)
        xT3 = dr["xT"].rearrange("(c p) t -> p c t", p=128)
        cT3 = dr["ctxT"].rearrange("(c p) t -> p c t", p=128)
        xa3 = [xa[si].rearrange("p (c t) -> p c t", t=SEGS[si][1]) for si in range(5)]

        def load_x(si):
            o, n = SEGS[si]
            src = cT3[:, :, :] if si == 0 else xT3[:, :, o - 256:o - 256 + n]
            P.dma("sp", xa3[si], src, writes=["xa%d" % si], slot="xa%d" % si)

        def stats(si):
            o, n = SEGS[si]
            tk = "xa%d" % si
            for c in range(8):
                if c == 0:
                    act(ssumA[:, 0:n], xa3[si][:, c, :], AF.Square, [tk], ["ssumA"])
                else:
                    sq = sqb[c % 2]
                    act(sq[:, 0:n], xa3[si][:, c, :], AF.Square, [tk], ["sq%d" % (c % 2)])
                    tt("dve", ssumA[:, 0:n], ssumA[:, 0:n], sq[:, 0:n], ALU.add, ["ssumA", "sq%d" % (c % 2)], ["ssumA"])
            b = nextbank()
            mm(ps[b][:, 0:n], ones, ssumA[:, 0:n], True, True, ["ones", "ssumA"], ["ps%d" % b])
            act(sdb[:, 0:n], ps[b][:, 0:n], AF.Ln, ["ps%d" % b], ["sd"], scale=1.0 / D, bias=EPS)
            act(rstdA[:, o:o + n], sdb[:, 0:n], AF.Exp, ["sd"], ["rstdA%d" % si], scale=-0.5)
            for c in range(8):
                tt("dve", xa3[si][:, c, :], xa3[si][:, c, :], rstdA[:, o:o + n], ALU.mult, [tk, "rstdA%d" % si], ["xn%d_%d" % (si, c)])

        xsched = {1: 0, 2: 1, 4: 2, 6: 3, 8: 4}
        ssched = {3: 0, 5: 1, 7: 2, 9: 3, 11: 4}
        for blk in range(12):
            buf = wmb[blk % 2]
            tk = "wm%d" % (blk % 2)
            P.dma("sp", buf, dr["wmod"][blk], writes=[tk], slot=tk)
            if blk in xsched:
                load_x(xsched[blk])
            b = nextbank()
            for k in range(8):
                mm(ps[b][0:2, 0:512], s3[:, k, :], buf[:, k * 512:(k + 1) * 512], k == 0, k == 7,
                   ["ssil", tk], ["ps%d" % b])
            act(modrow[0:2, blk * 512:(blk + 1) * 512], ps[b][0:2, 0:512], AF.Identity, ["ps%d" % b], ["modrow%d" % blk])
            if blk in ssched:
                stats(ssched[blk])
        b = nextbank()
        for oc in range(48):
            mm(ps[b][:, 2 * oc:2 * oc + 2], modrow[0:2, oc * 128:(oc + 1) * 128], ident[0:2, 0:2], True, True,
               ["modrow%d" % (oc // 4), "ident"], ["ps%d" % b])
        act(modfm, ps[b][:, 0:96], AF.Identity, ["ps%d" % b], ["modfm"])
        mod3 = modfm.rearrange("p (c r) -> p c r", r=2)
        for r in range(2):
            tt("dve", mod3[:, :, r], mod3[:, :, r], Vs("b_mod", 0, 48), ALU.add, ["modfm", "vecs"], ["modfm"])

        def MOD(which, c, r=0):
            return mod3[:, which * 8 + c, r:r + 1]

        A1 = der[:, 0:8]
        A1c = der[:, 8:16]
        GG1 = der[:, 16:24]
        A2 = der[:, 24:32]
        GG2 = der[:, 32:40]
        stt(A1, mod3[:, 8:16, 0], 1.0, Vs("g_pre_mix", 0, 8), ALU.add, ALU.mult, ["modfm", "vecs"], ["der"])
        stt(A1c, mod3[:, 8:16, 1], 1.0, Vs("g_pre_mix", 0, 8), ALU.add, ALU.mult, ["modfm", "vecs"], ["der"])
        tt("dve", GG1, mod3[:, 16:24, 0], Vs("g_post_mix", 0, 8), ALU.mult, ["modfm", "vecs"], ["der"])
        stt(A2, mod3[:, 32:40, 0], 1.0, Vs("g_pre_ffn", 0, 8), ALU.add, ALU.mult, ["modfm", "vecs"], ["der"])
        tt("dve", GG2, mod3[:, 40:48, 0], Vs("g_post_ffn", 0, 8), ALU.mult, ["modfm", "vecs"], ["der"])
        lam = Vs("lru_lambda", 0, 16)
        le = lrud[:, 0:16]
        lsp = lrud[:, 16:32]
        ls1 = lrud[:, 32:48]
        act(le, lam, AF.Exp, ["vecs"], ["le"], scale=-1.0)
        act(lsp, le, AF.Ln, ["le"], ["lsp"], bias=1.0)
        ts("dve", ls1, lsp, -4.0, ALU.mult, ["lsp"], ["hs1"])
        hs1 = ls1
        hbias = cw2
        ts("dve", hbias, Vs("lru_ba", 0, 32), 0.5, ALU.mult, ["vecs"], ["hbias"])

        ch = Carve(*R_H)
        h = ch.b(8 * TT)
        h3 = h.rearrange("p (c t) -> p c t", t=TT)
        for si, (o, n) in enumerate(SEGS):
            for c in range(8):
                if si == 0:
                    sc_, bi_ = A1c[:, c:c + 1], MOD(0, c, 1)
                else:
                    sc_, bi_ = A1[:, c:c + 1], MOD(0, c, 0)
                act(h3[:, c, o:o + n], xa3[si][:, c, :], AF.Identity, ["xn%d_%d" % (si, c), "der", "modfm"], ["h%d_%d" % (c, si)],
                    bias=bi_, scale=sc_)
        HSEG = lambda si: ["h%d_%d" % (c, si) for c in range(8)]

        if "h" in dbg:
            P.dma("sp", dump("h", h, BF16)[:, :], h, reads=[t for si in range(5) for t in HSEG(si)], slot="dbg_h")
            P.dma("sp", dump("modfm", modfm)[:, :], modfm, reads=["modfm"], slot="dbg_m")
        if stop == "A":
            P.emit()
            return nc, dbg_out

        cy = Carve(*R_Y)
        ypool = cy.b(8 * T)
        ylru = cy.b(8 * T)
        ypool3 = ypool.rearrange("p (c t) -> p c t", t=T)
        ylru3 = ylru.rearrange("p (c t) -> p c t", t=T)
        YTOK = ["xa%d" % si for si in range(4)] + ["xn%d_%d" % (si, c) for si in range(4) for c in range(8)] + ["sq0", "sq1", "sd", "ssumA"]
        y_first = [True]

        cwd = Carve(W_DYN, R_W[0] + R_W[1] - W_DYN)
        lruw = cwd.b(4096)
        poolw = cwd.b(2048)
        NWIN = 3
        winb = [cwd.b(1024) for _ in range(NWIN)]
        win_ctr = [0]
        P.dma("pool", lruw[:, 0:2048], dr["lruw"][:, 0:2048], writes=["lruw"], slot="w_lruw")
        P.dma("pool", lruw[:, 2048:4096], dr["lruw"][:, 2048:4096], writes=["lruw"], slot="w_lruw")
        P.dma("pool", poolw, dr["poolw"][:, :], writes=["poolw"], slot="w_poolw")

        def load_win(oc):
            i = win_ctr[0] % NWIN
            win_ctr[0] += 1
            P.dma("pool", winb[i], dr["win"][oc], writes=["win%d" % i], slot="win%d" % i)
            return winb[i], "win%d" % i

        def proj_seg(wt, wtk, si, b):
            o, n = SEGS[si]
            for k in range(8):
                mm(ps[b][:, 0:n], wt[:, k * 128:(k + 1) * 128], h3[:, k, o:o + n], k == 0, k == 7,
                   [wtk] + HSEG(si), ["ps%d" % b])

        cast_plan = []
        for oc in range(8):
            cast_plan.append((woutS[oc], dr["wout"][oc]))
        for j in range(24):
            cast_plan.append((wupS[j], dr["wup"][j]))
        for oc in range(8):
            cast_plan.append((wdnS[oc][:, 0:2048], dr["wdown"][oc][:, 0:2048]))
            cast_plan.append((wdnS[oc][:, 2048:3072], dr["wdown"][oc][:, 2048:3072]))
        cast_ctr = [0]

        def issue_casts(k):
            for _ in range(k):
                i = cast_ctr[0]
                if i >= len(cast_plan):
                    return
                cast_ctr[0] += 1
                dst, src = cast_plan[i]
                wr = ["wc%d" % i] + (["wc_all"] if i == len(cast_plan) - 1 else [])
                P.dma("pool", dst, src, writes=wr, slot="wcast")

        STOK_A = (["wm0", "wm1"] + ["modrow%d" % i for i in range(12)] + ["xa4"] + ["xn4_%d" % c for c in range(8)]
                  + ["rstdA%d" % si for si in range(5)])
        cs = Carve(*R_S)
        cntb = cs.f(T)
        PQ = [[cs.f(47 * 79), cs.f(47 * 79)] for _ in range(2)]
        dbf = [cs.b(T), cs.b(T)]
        PQTOK = ["pq%d_%d" % (cl, i) for cl in range(2) for i in range(2)]
        cnt3 = cntb.rearrange("p (r c) -> p r c", c=64)
        first_S = [True]
        nb_ctr = [0]

        def nextB():
            b = 4 + nb_ctr[0] % 4
            nb_ctr[0] += 1
            return b

        def proj_cl0(g_):
            wt_, wtk_ = load_win(2 * g_)
            for t in range(4):
                proj_seg(wt_, wtk_, 1 + t, t)
            return (wt_, wtk_)

        pre_cl0 = proj_cl0(0)
        P.dma("sp", cn1, dr["cnt"][0:1, :].partition_broadcast(128), writes=["cn1"], slot="c_cn1")
        P.op("dve", lambda e: e.reciprocal(out=cn1, in_=cn1), ["cn1"], ["cn1"])
        for g, w in enumerate(POOL_WINDOWS):
            hw = w // 2
            Hp, Wp = 32 + w - 1, 64 + w - 1
            extra = STOK_A if first_S[0] else []
            first_S[0] = False
            ir = cn1[:, g * 96: g * 96 + 32].unsqueeze(2).broadcast_to([128, 32, 64])
            ic = cn1[:, g * 96 + 32: g * 96 + 96].unsqueeze(1).broadcast_to([128, 32, 64])
            tt("dve", cnt3, ir, ic, ALU.mult, ["cn1"], ["cnt"] + extra)
            views = []
            for cl in range(2):
                v = [PQ[cl][i][:, 0:Hp * Wp].rearrange("p (r c) -> p r c", c=Wp) for i in range(2)]
                views.append(v)
                Pv, Qv = v
                memset(Pv[:, 0:hw, :], 0.0, ["pq%d_0" % cl] + extra)
                if hw > 1:
                    memset(Pv[:, hw + 32:Hp, :], 0.0, ["pq%d_0" % cl])
                memset(Pv[:, hw:hw + 32, 0:hw], 0.0, ["pq%d_0" % cl])
                if hw > 1:
                    memset(Pv[:, hw:hw + 32, hw + 64:Wp], 0.0, ["pq%d_0" % cl])
                if w in (2, 8):
                    memset(Qv[:, 0:hw, 0:64], 0.0, ["pq%d_1" % cl])
                    if hw > 1:
                        memset(Qv[:, hw + 32:Hp, 0:64], 0.0, ["pq%d_1" % cl])
            wts = [pre_cl0]
            for t in range(4):
                act(views[0][0][:, hw + 8 * t: hw + 8 * t + 8, hw:hw + 64], ps[t][:, :].rearrange("p (r c) -> p r c", c=64),
                    AF.Identity, ["ps%d" % t], ["pq0_0"])
            wt, wtk = load_win(2 * g + 1)
            wts.append((wt, wtk))
            for t in range(4):
                b = nextB()
                proj_seg(wt, wtk, 1 + t, b)
                act(views[1][0][:, hw + 8 * t: hw + 8 * t + 8, hw:hw + 64], ps[b][:, :].rearrange("p (r c) -> p r c", c=64),
                    AF.Identity, ["ps%d" % b], ["pq1_0"])
            if g + 1 < 4:
                pre_cl0 = proj_cl0(g + 1)
            state = [dict(cur=0, ln=Wp, rows=Hp) for _ in range(2)]
            k = 1
            while k < w:
                for cl in range(2):
                    st = state[cl]
                    src, dst = views[cl][st["cur"]], views[cl][1 - st["cur"]]
                    nl = st["ln"] - k
                    tt("dve", dst[:, hw:hw + 32, 0:nl], src[:, hw:hw + 32, 0:nl], src[:, hw:hw + 32, k:k + nl], ALU.add,
                       ["pq%d_%d" % (cl, st["cur"])], ["pq%d_%d" % (cl, 1 - st["cur"])])
                    st["cur"], st["ln"] = 1 - st["cur"], nl
                k *= 2
            k = 1
            while k < w:
                for cl in range(2):
                    st = state[cl]
                    src, dst = views[cl][st["cur"]], views[cl][1 - st["cur"]]
                    nr = st["rows"] - k
                    tt("dve", dst[:, 0:nr, 0:64], src[:, 0:nr, 0:64], src[:, k:k + nr, 0:64], ALU.add,
                       ["pq%d_%d" % (cl, st["cur"])], ["pq%d_%d" % (cl, 1 - st["cur"])])
                    st["cur"], st["rows"] = 1 - st["cur"], nr
                k *= 2
            for cl in range(2):
                st = state[cl]
                assert st["ln"] == 64 and st["rows"] == 32
                src, oth = views[cl][st["cur"]], views[cl][1 - st["cur"]]
                tt("dve", oth[:, 0:32, 0:64], src[:, 0:32, 0:64], cnt3, ALU.mult, ["pq%d_%d" % (cl, st["cur"]), "cnt"],
                   ["pq%d_%d" % (cl, 1 - st["cur"])])
                wt, wtk = wts[cl]
                for t in range(4):
                    b = nextB()
                    proj_seg(wt, wtk, 1 + t, b)
                    tt("dve", dbf[cl][:, 512 * t:512 * t + 512].rearrange("p (r c) -> p r c", c=64), oth[:, 8 * t:8 * t + 8, 0:64],
                       ps[b][:, :].rearrange("p (r c) -> p r c", c=64), ALU.subtract, ["pq%d_%d" % (cl, 1 - st["cur"]), "ps%d" % b],
                       ["dbf%d_%d" % (cl, t)])
            for ocl in range(2):
                oc = 2 * g + ocl
                for t in range(4):
                    b = nextB()
                    for k in range(2):
                        idx = ((g * 2 + ocl) * 2 + k) * 128
                        mm(ps[b][:, :], poolw[:, idx:idx + 128], dbf[k][:, 512 * t:512 * t + 512], k == 0, k == 1,
                           ["poolw", "dbf%d_%d" % (k, t)], ["ps%d" % b])
                    extra = YTOK if y_first[0] else []
                    y_first[0] = False
                    act(ypool3[:, oc, 512 * t:512 * t + 512], ps[b][:, :], AF.Identity, ["ps%d" % b, "vecs"],
                        ["ypool%d_%d" % (oc, t)] + extra, scale=V("pool_scale", oc))
            issue_casts(5)
        if "ypool" in dbg:
            P.dma("sp", dump("ypool", ypool, BF16)[:, :], ypool,
                  reads=["ypool%d_%d" % (oc, t) for oc in range(8) for t in range(4)], slot="dbg_yp")
        if stop == "Bp":
            P.emit()
            return nc, dbg_out

        cs = Carve(*R_S)
        UPW = 2320
        LOFF = 264
        upad = cs.b(UPW)
        dgw = cs.b(32 * 128)
        o_m2b = cs.cur
        m2b = cs.f(TT)
        xcb = bv(o_m2b, TT)
        xc = cs.f(TT)
        m2f = cs.f(TT)
        ra = [cs.f(TT), cs.f(TT)]
        ib = [cs.f(TT), cs.f(TT)]
        gel = cs.f(T)
        m2 = [m2f, m2b]
        m2tok = ["m2f", "m2b"]
        POOLTOK = ["cnt"] + PQTOK + ["dbf%d_%d" % (cl, t) for cl in range(2) for t in range(4)]
        memset(upad, 0.0, ["upad"] + POOLTOK)
        for k in range(4):
            for n in range(8):
                i = k * 8 + n
                ts("dve", dgw[:, i * 128:(i + 1) * 128], ident, V("lru_conv_w", i), ALU.mult, ["ident", "vecs"], ["dgw"] + (POOLTOK if i == 0 else []))
        RA = lambda d: ["ra%d_%d" % (d, si) for si in range(5)]
        IB = lambda d: ["ib%d_%d" % (d, si) for si in range(5)]
        win_next = load_win(8 + 0)
        for n in range(8):
            wt, wtk = win_next
            for si, (o, nn) in enumerate(SEGS):
                b = nextbank()
                proj_seg(wt, wtk, si, b)
                po = 2 + o if si == 0 else LOFF + (o - 256)
                act(upad[:, po:po + nn], ps[b][:, 0:nn], AF.Identity, ["ps%d" % b], ["upad"])
            for si, (o, nn) in enumerate(SEGS):
                base = o if si == 0 else LOFF - 2 + (o - 256)
                b = nextbank()
                for k in range(4):
                    i = k * 8 + n
                    mm(ps[b][:, 0:nn], dgw[:, i * 128:(i + 1) * 128], upad[:, base + k:base + k + nn], k == 0, k == 3,
                       ["dgw", "upad"], ["ps%d" % b])
                act(xc[:, o:o + nn], ps[b][:, 0:nn], AF.Identity, ["ps%d" % b, "vecs"], ["xc"], bias=V("lru_conv_b", n))
            P.op("dve", lambda e: e.tensor_copy(out=xcb, in_=xc), ["xc"], ["m2b"])
            for dr_ in range(2):
                for kind, dst, tkf in ((0, ra[dr_], RA(dr_)), (1, ib[dr_], IB(dr_))):
                    widx = ((kind * 2 + dr_) * 8 + n) * 128
                    hb_ = hbias[:, kind * 16 + dr_ * 8 + n: kind * 16 + dr_ * 8 + n + 1]
                    for si, (o, nn) in enumerate(SEGS):
                        b = nextbank()
                        mm(ps[b][:, 0:nn], lruw[:, widx:widx + 128], xcb[:, o:o + nn], True, True, ["lruw", "m2b"], ["ps%d" % b])
                        act(dst[:, o:o + nn], ps[b][:, 0:nn], AF.Tanh, ["ps%d" % b, "hbias"], [tkf[si]], bias=hb_, scale=0.5)
                hs = hs1[:, dr_ * 8 + n: dr_ * 8 + n + 1]
                act(ra[dr_], ra[dr_], AF.Exp, RA(dr_) + ["hs1"], RA(dr_), bias=hs, scale=hs)
                act(m2[dr_], ra[dr_], AF.Square, RA(dr_), [m2tok[dr_]])
                act(m2[dr_], m2[dr_], AF.Sqrt, [m2tok[dr_]], [m2tok[dr_]], scale=-0.25, bias=0.25)
            for dr_ in range(2):
                stt(ib[dr_], ib[dr_], 1.0, xc, ALU.add, ALU.mult, IB(dr_) + ["xc"], IB(dr_))
                tt("dve", ib[dr_], ib[dr_], m2[dr_], ALU.mult, IB(dr_) + [m2tok[dr_]], IB(dr_))
                if dr_ == 0:
                    P.op("dve", lambda e: e.tensor_tensor_scan(out=ib[0], data0=ra[0], data1=ib[0], initial=0.0, op0=ALU.mult, op1=ALU.add),
                         RA(0) + IB(0), IB(0))
                else:
                    P.op("dve", lambda e: e.tensor_tensor_scan(out=ib[1][:, 0:256][:, ::-1], data0=ra[1][:, 0:256][:, ::-1],
                                                                data1=ib[1][:, 0:256][:, ::-1], initial=0.0, op0=ALU.mult, op1=ALU.add),
                         RA(1) + IB(1), IB(1))
                    P.op("dve", lambda e: e.tensor_tensor_scan(out=ib[1][:, 256:TT][:, ::-1], data0=ra[1][:, 256:TT][:, ::-1],
                                                                data1=ib[1][:, 256:TT][:, ::-1], initial=ib[1][:, 0:1], op0=ALU.mult, op1=ALU.add),
                         RA(1) + IB(1), IB(1))
            wt, wtk = load_win(16 + n)
            if n + 1 < 8:
                win_next = load_win(8 + n + 1)
            issue_casts(5)
            for t in range(4):
                b = nextbank()
                proj_seg(wt, wtk, 1 + t, b)
                act(gel[:, 512 * t:512 * t + 512], ps[b][:, :], AF.Gelu_apprx_tanh, ["ps%d" % b], ["gel%d" % t])
            GEL = ["gel%d" % t for t in range(4)]
            tt("dve", ib[0][:, 256:TT], ib[0][:, 256:TT], ib[1][:, 256:TT], ALU.add, IB(0) + IB(1), IB(0))
            tt("dve", ylru3[:, n, :], ib[0][:, 256:TT], gel, ALU.mult, IB(0) + GEL, ["ylru%d" % n] + (YTOK if n == 0 else []))
        if "ylru" in dbg:
            P.dma("sp", dump("ylru", ylru, BF16)[:, :], ylru, reads=["ylru%d" % n for n in range(8)], slot="dbg_yl")
        if stop == "B":
            P.emit()
            return nc, dbg_out

        LRUTOK = ["upad", "xc", "m2f", "m2b", "dgw"] + RA(0) + RA(1) + IB(0) + IB(1) + GEL
        cs = Carve(*R_S)
        mbuf = cs.b(8 * T)
        m3 = mbuf.rearrange("p (c t) -> p c t", t=T)
        S_C2 = cs.cur
        sgb = [[cs.f(512), cs.f(512)] for _ in range(2)]
        t12 = [[cs.f(512), cs.f(512)] for _ in range(2)]
        cwd = Carve(W_DYN, R_W[0] + R_W[1] - W_DYN)
        c1w = [[cwd.b(1024) for _ in range(4)] for _ in range(2)]
        W_OLD = ["lruw", "poolw"] + ["win%d" % i for i in range(NWIN)]
        first_c1 = [True]
        YP = lambda t: ["ypool%d_%d" % (k, t) for k in range(8)]
        YL = ["ylru%d" % k for k in range(8)]
        it = 0
        for oc in range(8):
            sl = oc % 2
            srcs = [dr["wpp"][oc], dr["win"][24 + oc], dr["wpl"][oc], dr["win"][32 + oc]]
            for i in range(4):
                extra = W_OLD if first_c1[0] else []
                first_c1[0] = False
                P.dma("pool", c1w[sl][i], srcs[i], writes=["c1w%d_%d" % (sl, i)] + extra, slot="c1w%d_%d" % (sl, i))
            for t in range(4):
                bb = [nextbank() for _ in range(4)]
                for k in range(8):
                    mm(ps[bb[0]][:, :], c1w[sl][0][:, k * 128:(k + 1) * 128], ypool3[:, k, 512 * t:512 * t + 512], k == 0, k == 7,
                       ["c1w%d_0" % sl] + YP(t), ["ps%d" % bb[0]])
                for k in range(8):
                    mm(ps[bb[1]][:, :], c1w[sl][1][:, k * 128:(k + 1) * 128], h3[:, k, 256 + 512 * t:256 + 512 * t + 512], k == 0, k == 7,
                       ["c1w%d_1" % sl] + HSEG(1 + t), ["ps%d" % bb[1]])
                for k in range(8):
                    mm(ps[bb[2]][:, :], c1w[sl][2][:, k * 128:(k + 1) * 128], ylru3[:, k, 512 * t:512 * t + 512], k == 0, k == 7,
                       ["c1w%d_2" % sl] + YL, ["ps%d" % bb[2]])
                for k in range(8):
                    mm(ps[bb[3]][:, :], c1w[sl][3][:, k * 128:(k + 1) * 128], h3[:, k, 256 + 512 * t:256 + 512 * t + 512], k == 0, k == 7,
                       ["c1w%d_3" % sl] + HSEG(1 + t), ["ps%d" % bb[3]])
                p = it % 2
                it += 1
                extra = LRUTOK if (oc == 0 and t == 0) else []
                act(sgb[p][0], ps[bb[1]][:, :], AF.Sigmoid, ["ps%d" % bb[1]], ["sg%d_0" % p] + extra)
                act(sgb[p][1], ps[bb[3]][:, :], AF.Sigmoid, ["ps%d" % bb[3]], ["sg%d_1" % p])
                tt("dve", t12[p][0], ps[bb[0]][:, :], sgb[p][0], ALU.mult, ["ps%d" % bb[0], "sg%d_0" % p], ["t12%d_0" % p])
                tt("dve", t12[p][1], ps[bb[2]][:, :], sgb[p][1], ALU.mult, ["ps%d" % bb[2], "sg%d_1" % p], ["t12%d_1" % p])
                tt("dve", m3[:, oc, 512 * t:512 * t + 512], t12[p][0], t12[p][1], ALU.add, ["t12%d_0" % p, "t12%d_1" % p],
                   ["m%d_%d" % (oc, t)])
        MT = lambda tl: ["m%d_%d" % (k, t) for k in range(8) for t in tl]
        if "m" in dbg:
            P.dma("sp", dump("m", mbuf, BF16)[:, :], mbuf, reads=MT(range(4)), slot="dbg_mm")
        if stop == "C1":
            P.emit()
            return nc, dbg_out

        cs = Carve(S_C2, R_S[0] + R_S[1] - S_C2)
        sq2 = [cs.f(640), cs.f(640)]
        tm2 = [cs.f(640), cs.f(640)]
        sd2 = cs.f(640)
        rstd2 = cs.f(640)
        gpad = [cs.f(660) for _ in range(4)]
        mixB = cs.f(8 * 640)
        ssum2 = cs.f(640)
        HY_OLD = [t for si in range(5) for t in HSEG(si)] + [t for tl in range(4) for t in YP(tl)] + YL
        cb_ = Carve(R_H[0], R_H[1] + R_Y[1])
        xt1 = cb_.f(8 * 640)
        mixb = cb_.f(8 * 640)
        hf = cb_.b(8 * 640)
        abuf = cb_.b(24 * 512)
        wupb = [cb_.b(2048) for _ in range(3)]
        wdnb = [cb_.b(3072) for _ in range(2)]
        xt13 = xt1.rearrange("p (c t) -> p c t", t=640)
        mixbufs = [mixb, mixB]
        mix3s = [mb.rearrange("p (c t) -> p c t", t=640) for mb in mixbufs]
        f3s = [mb[:, 0:8 * 512].rearrange("p (c t) -> p c t", t=512) for mb in mixbufs]
        hf3 = hf.rearrange("p (c t) -> p c t", t=640)
        a3 = abuf.rearrange("p (c t) -> p c t", t=512)
        cwd = Carve(W_DYN, R_W[0] + R_W[1] - W_DYN)
        accb = [cwd.f(512) for _ in range(4)]
        wupb.append(cwd.b(2048))
        woutb = [cwd.b(1024) for _ in range(3)]
        wo_ctr = [0]
        C1W_OLD = ["c1w%d_%d" % (s, i) for s in range(2) for i in range(4)]
        oT3 = outT.rearrange("(c p) t -> p c t", p=128)
        memset(gpad[0], 0.0, ["gpad0"] + ["sg%d_%d" % (p, i) for p in range(2) for i in range(2)] + ["t12%d_%d" % (p, i) for p in range(2) for i in range(2)])
        for gi in range(1, 4):
            memset(gpad[gi], 0.0, ["gpad%d" % gi])
        first_hy = [True]
        wup_ctr = [0]
        wdn_ctr = [0]
        dg_ctr = [0]
        gp_ctr = [0]

        def norm_rstd(src_fn, ntok, reads_fn, hook=None):
            subs = [(0, min(512, ntok))] + ([(512, ntok - 512)] if ntok > 512 else [])
            for c in range(8):
                if c == 0:
                    act(ssum2[:, 0:ntok], src_fn(c), AF.Square, reads_fn(c), ["ssum2"])
                else:
                    sq = sq2[c % 2]
                    act(sq[:, 0:ntok], src_fn(c), AF.Square, reads_fn(c), ["sq2_%d" % (c % 2)])
                    tt("dve", ssum2[:, 0:ntok], ssum2[:, 0:ntok], sq[:, 0:ntok], ALU.add, ["ssum2", "sq2_%d" % (c % 2)], ["ssum2"])
                if hook is not None:
                    hook(c)
            for (so, sn) in subs:
                b = nextbank()
                mm(ps[b][:, 0:sn], ones, ssum2[:, so:so + sn], True, True, ["ones", "ssum2"], ["ps%d" % b])
                act(sd2[:, so:so + sn], ps[b][:, 0:sn], AF.Ln, ["ps%d" % b], ["sd2"], scale=1.0 / D, bias=EPS)
            act(rstd2[:, 0:ntok], sd2[:, 0:ntok], AF.Exp, ["sd2"], ["rstd2"], scale=-0.5)

        def geom(tl):
            r0 = max(8 * tl - 1, 0)
            r1 = min(8 * tl + 9, 32)
            ntok = (r1 - r0) * 64
            return r0, ntok, r0 * 64, [(0, 512)] + [(512, ntok - 512)]

        first_mix = [True]

        def mix_units(tl):
            r0, ntok, tok0, subs = geom(tl)
            par = tl % 2
            mtl = sorted({min(3, (tok0 + so) // 512) for so, sn in subs} | {min(3, (tok0 + so + sn - 1) // 512) for so, sn in subs})
            units = []
            for oc in range(8):
                for (so, sn) in subs:
                    st_ = {}

                    def pe_fn(oc=oc, so=so, sn=sn, st_=st_):
                        if so == 0:
                            wi = wo_ctr[0] % 3
                            wo_ctr[0] += 1
                            extra = (LRUTOK + C1W_OLD) if first_mix[0] else []
                            P.dma("sp", woutb[wi], woutS[oc], reads=["wc_all"], writes=["wo%d" % wi] + extra, slot="wo%d" % wi)
                            wo_cur[0] = wi
                        wi = wo_cur[0]
                        b = nextbank()
                        st_["b"] = b
                        for k in range(8):
                            mm(ps[b][:, 0:sn], woutb[wi][:, k * 128:(k + 1) * 128],
                               m3[:, k, tok0 + so:tok0 + so + sn], k == 0, k == 7, ["wo%d" % wi] + MT(mtl), ["ps%d" % b])

                    def act_fn(oc=oc, so=so, sn=sn, st_=st_):
                        b = st_["b"]
                        extra2 = []
                        if oc == 0 and so == 0:
                            extra2 = ["f%d_%d" % (par, q) for q in range(8)]
                            if first_mix[0] or tl == 1:
                                extra2 = extra2 + ["sg%d_%d" % (p_, i) for p_ in range(2) for i in range(2)] + ["t12%d_%d" % (p_, i) for p_ in range(2) for i in range(2)] + HY_OLD
                        first_mix[0] = False
                        act(mix3s[par][:, oc, so:so + sn], ps[b][:, 0:sn], AF.Identity, ["ps%d" % b], ["mix%d_%d" % (par, oc)] + extra2)
                    units.append((pe_fn, act_fn))
            return units

        wo_cur = [0]

        def do_mix(tl):
            for pe_fn, act_fn in mix_units(tl):
                pe_fn()
                act_fn()

        do_mix(0)
        for tl in range(4):
            r0, ntok, tok0, subs = geom(tl)
            par = tl % 2
            mix3 = mix3s[par]
            f3 = f3s[par]
            MIXT = lambda c: "mix%d_%d" % (par, c)
            FT = lambda c: "f%d_%d" % (par, c)
            co = (8 * tl - r0) * 64
            s0 = r0 - (8 * tl - 1)
            extra = HY_OLD if first_hy[0] else []
            first_hy[0] = False
            for c in range(8):
                P.dma("pool", xt13[:, c, 0:ntok], xT3[:, c, tok0:tok0 + ntok], writes=["xt1_%d" % c, "x1_%d" % c] + (extra if c == 0 else []),
                      slot="xt1_%d" % c)
            norm_rstd(lambda c: mix3[:, c, 0:ntok], ntok, lambda c: [MIXT(c)])
            pend = mix_units(tl + 1) if tl + 1 < 4 else []
            pend_pe = [u[0] for u in pend]
            pend_act = [u[1] for u in pend]
            inflight = [0]

            def pump(nact):
                for _ in range(nact):
                    if pend_act:
                        pend_act.pop(0)()
                        inflight[0] -= 1
                while pend_pe and inflight[0] < 5:
                    pend_pe.pop(0)()
                    inflight[0] += 1

            pump(0)
            for c in range(8):
                tm = tm2[c % 2]
                tt("dve", tm[:, 0:ntok], mix3[:, c, 0:ntok], rstd2[:, 0:ntok], ALU.mult, [MIXT(c), "rstd2"], ["tm2_%d" % (c % 2)])
                stt(xt13[:, c, 0:ntok], tm[:, 0:ntok], GG1[:, c:c + 1], xt13[:, c, 0:ntok], ALU.mult, ALU.add,
                    ["tm2_%d" % (c % 2), "der", "xt1_%d" % c], ["x1_%d" % c])
            pump(4)
            norm_rstd(lambda c: xt13[:, c, 0:ntok], ntok, lambda c: ["x1_%d" % c], hook=lambda c: pump(1))
            pump(4)
            for c in range(8):
                tm = tm2[c % 2]
                tt("dve", tm[:, 0:ntok], xt13[:, c, 0:ntok], rstd2[:, 0:ntok], ALU.mult, ["x1_%d" % c, "rstd2"], ["tm2_%d" % (c % 2)])
                act(hf3[:, c, 0:ntok], tm[:, 0:ntok], AF.Identity, ["tm2_%d" % (c % 2), "der", "modfm"], ["hf%d" % c],
                    bias=MOD(3, c, 0), scale=A2[:, c:c + 1])
            while pend_act or pend_pe:
                pump(1)
            HF = ["hf%d" % c for c in range(8)]
            if tl == 3:
                for gi in range(4):
                    memset(gpad[gi].rearrange("p (r c) -> p r c", c=66)[:, 9, :], 0.0, ["gpad%d" % gi])
            def ffn_front(p):
                st = []
                for i in range(2):
                    j = 2 * p + i
                    q = j % 4
                    wi = wup_ctr[0] % 4
                    wup_ctr[0] += 1
                    P.dma("sp", wupb[wi], wupS[j], reads=["wc_all"], writes=["wup%d" % wi], slot="wup%d" % wi)
                    gp3 = gpad[q].rearrange("p (r c) -> p r c", c=66)
                    wg = wupb[wi][:, 0:1024]
                    row = s0
                    for (so, sn) in subs:
                        b = nextbank()
                        for k in range(8):
                            mm(ps[b][:, 0:sn], wg[:, k * 128:(k + 1) * 128], hf3[:, k, so:so + sn], k == 0, k == 7,
                               ["wup%d" % wi] + HF, ["ps%d" % b])
                        nr = sn // 64
                        act(gp3[:, row:row + nr, 1:65], ps[b][:, 0:sn].rearrange("p (r c) -> p r c", c=64), AF.Identity,
                            ["ps%d" % b], ["gpad%d" % q])
                        row += nr
                    acc3 = accb[q].rearrange("p (r c) -> p r c", c=64)
                    act(acc3, gp3[:, 0:8, 0:64], AF.Identity, ["gpad%d" % q, "vecs"], ["acc%d" % q],
                        bias=V("ffn_conv_b", j), scale=V("ffn_conv_w", 0 * 24 + j))
                    st.append((j, q, wi, gp3, acc3))
                return st

            def ffn_taps(st, taps):
                for tap in taps:
                    dy, dx = tap // 3, tap % 3
                    for (j, q, wi, gp3, acc3) in st:
                        stt(acc3, gp3[:, dy:dy + 8, dx:dx + 64], V("ffn_conv_w", tap * 24 + j), acc3, ALU.mult, ALU.add,
                            ["gpad%d" % q, "acc%d" % q, "vecs"], ["acc%d" % q])

            def ffn_u(st):
                out = []
                for (j, q, wi, gp3, acc3) in st:
                    wu = wupb[wi][:, 1024:2048]
                    bu = nextbank()
                    for k in range(8):
                        mm(ps[bu][:, :], wu[:, k * 128:(k + 1) * 128], hf3[:, k, co:co + 512], k == 0, k == 7,
                           ["wup%d" % wi] + HF, ["ps%d" % bu])
                    out.append(bu)
                return out

            def ffn_gelu(st):
                for (j, q, wi, gp3, acc3) in st:
                    act(accb[q], accb[q], AF.Gelu_apprx_tanh, ["acc%d" % q], ["acc%d" % q])

            def ffn_mult(st, bus):
                for (j, q, wi, gp3, acc3), bu in zip(st, bus):
                    tt("dve", a3[:, j, :], accb[q], ps[bu][:, :], ALU.mult, ["acc%d" % q, "ps%d" % bu], ["a%d" % j])

            prev = None
            for p in range(12):
                st = ffn_front(p)
                if prev is not None:
                    ffn_gelu(prev[0])
                ffn_taps(st, range(1, 2))
                if prev is not None:
                    ffn_mult(*prev)
                ffn_taps(st, range(2, 9))
                bus = ffn_u(st)
                prev = (st, bus)
            ffn_gelu(prev[0])
            ffn_mult(*prev)
            AT = ["a%d" % j for j in range(24)]
            for oc in range(8):
                wi = wdn_ctr[0] % 2
                wdn_ctr[0] += 1
                P.dma("sp", wdnb[wi], wdnS[oc], reads=["wc_all"], writes=["wdn%d" % wi], slot="wdn%d" % wi)
                b = nextbank()
                for k in range(24):
                    mm(ps[b][:, :], wdnb[wi][:, k * 128:(k + 1) * 128], a3[:, k, :], k == 0, k == 23, ["wdn%d" % wi] + AT, ["ps%d" % b])
                act(f3[:, oc, :], ps[b][:, :], AF.Identity, ["ps%d" % b], [FT(oc)] + ([MIXT(q) for q in range(8)] if oc == 0 else []))
                if oc == 0:
                    act(ssum2[:, 0:512], f3[:, oc, :], AF.Square, [FT(oc)], ["ssum2"])
                else:
                    sq = sq2[oc % 2]
                    act(sq[:, 0:512], f3[:, oc, :], AF.Square, [FT(oc)], ["sq2_%d" % (oc % 2)])
                    tt("dve", ssum2[:, 0:512], ssum2[:, 0:512], sq[:, 0:512], ALU.add, ["ssum2", "sq2_%d" % (oc % 2)], ["ssum2"])
            bss = nextbank()
            mm(ps[bss][:, :], ones, ssum2[:, 0:512], True, True, ["ones", "ssum2"], ["ps%d" % bss])
            act(sd2[:, 0:512], ps[bss][:, :], AF.Ln, ["ps%d" % bss], ["sd2"], scale=1.0 / D, bias=EPS)
            act(rstd2[:, 0:512], sd2[:, 0:512], AF.Exp, ["sd2"], ["rstd2"], scale=-0.5)
            for oc in range(8):
                tm = tm2[oc % 2]
                tt("dve", tm[:, 0:512], f3[:, oc, :], rstd2[:, 0:512], ALU.mult, [FT(oc), "rstd2"], ["tm2_%d" % (oc % 2)])
                stt(f3[:, oc, :], tm[:, 0:512], GG2[:, oc:oc + 1], xt13[:, oc, co:co + 512], ALU.mult, ALU.add,
                    ["tm2_%d" % (oc % 2), "der", "x1_%d" % oc, FT(oc)], [FT(oc)])
            P.dma("pool", oT3[:, :, 512 * tl:512 * tl + 512], f3, reads=[FT(oc) for oc in range(8)], slot="out")
        P.emit()
    return nc, dbg_out


def _fm(v, nch):
    v = np.asarray(v, np.float32)
    lead = v.shape[:-1]
    r = v.reshape(lead + (nch, 128))
    r = np.moveaxis(r, -1, 0)
    return np.ascontiguousarray(r.reshape(128, -1))


def _wtile(w, kch, och):
    w = np.asarray(w, np.float32)
    r = w.reshape(kch, 128, och, 128).transpose(2, 1, 0, 3)
    return np.ascontiguousarray(r.reshape(och, 128, kch * 128))


def _window_counts():
    out = np.zeros((4, 96), np.float32)
    for gi, w in enumerate(POOL_WINDOWS):
        def cnt1(n):
            pos = np.arange(n)
            lo = np.clip(pos - w // 2, 0, n)
            hi = np.clip(pos + w - w // 2, 0, n)
            return (hi - lo).astype(np.float32)
        out[gi, 0:32] = cnt1(32)
        out[gi, 32:96] = cnt1(64)
    return out.reshape(1, 384)


def prep_inputs(x, c, ctx, c_ctx, w_mod, b_mod, g_pre_mix, g_post_mix, g_pre_ffn, g_post_ffn,
                w_in, pool_w, pool_scale, lru_conv_w, lru_conv_b, lru_wa, lru_ba, lru_wx, lru_bx,
                lru_lambda, w_proj_pool, w_proj_lru, w_out, w_up, ffn_conv_w, ffn_conv_b, w_down):
    f = lambda a: np.asarray(a, np.float32)
    vec_parts = {
        "g_pre_mix": _fm(f(g_pre_mix)[0], 8), "g_post_mix": _fm(f(g_post_mix)[0], 8),
        "g_pre_ffn": _fm(f(g_pre_ffn)[0], 8), "g_post_ffn": _fm(f(g_post_ffn)[0], 8),
        "pool_scale": _fm(f(pool_scale)[0], 8), "lru_conv_w": _fm(f(lru_conv_w)[0], 8),
        "lru_conv_b": _fm(f(lru_conv_b)[0], 8), "lru_ba": _fm(f(lru_ba)[0], 8), "lru_bx": _fm(f(lru_bx)[0], 8),
        "lru_lambda": _fm(f(lru_lambda)[0], 8), "ffn_conv_w": _fm(f(ffn_conv_w)[0].reshape(9, 3072), 24),
        "ffn_conv_b": _fm(f(ffn_conv_b)[0], 24), "b_mod": _fm(f(b_mod)[0], 48),
    }
    vecs = np.ascontiguousarray(np.concatenate([vec_parts[n] for n, _ in _VEC_SPEC], axis=1))
    assert vecs.shape == (128, NV)
    wmod_h = np.ascontiguousarray(f(w_mod)[0].reshape(8, 128, 12, 512).transpose(2, 1, 0, 3).reshape(12, 128, 4096))
    win_h = _wtile(f(w_in)[0], 8, 40)
    wpp_h = _wtile(f(w_proj_pool)[0], 8, 8)
    wpl_h = _wtile(f(w_proj_lru)[0], 8, 8)
    wout_h = _wtile(f(w_out)[0], 8, 8)
    wup_t = _wtile(f(w_up)[0], 8, 48)
    wup_h = np.ascontiguousarray(np.concatenate([wup_t[0:24], wup_t[24:48]], axis=2))
    wdown_h = _wtile(f(w_down)[0], 24, 8)
    pw = f(pool_w)[0].reshape(4, 2, 128, 2, 128).transpose(2, 0, 3, 1, 4)
    poolw_h = np.ascontiguousarray(pw.reshape(128, 2048))
    lw = np.stack([f(lru_wa)[0], f(lru_wx)[0]], axis=0)
    lruw_h = np.ascontiguousarray(lw.transpose(3, 0, 1, 2, 4).reshape(128, 4096))
    shared = {"vecs": vecs, "ident": np.eye(128, dtype=np.float32), "cnt": _window_counts(), "wmod": wmod_h,
              "win": win_h, "wpp": wpp_h, "wpl": wpl_h, "wout": wout_h, "wup": wup_h, "wdown": wdown_h,
              "poolw": poolw_h, "lruw": lruw_h}
    xf, cf, ctxf, ccf = f(x), f(c), f(ctx), f(c_ctx)
    in_maps = []
    for b in range(NCORES):
        m = dict(shared)
        m["xT"] = np.ascontiguousarray(xf[b].T)
        m["ctxT"] = np.ascontiguousarray(ctxf[b].T)
        cc2 = np.stack([cf[b], ccf], axis=0)
        m["cc"] = np.ascontiguousarray(cc2.reshape(2, 8, 128).transpose(2, 1, 0).reshape(128, 16))
        in_maps.append(m)
    return in_maps


_CACHE = {}


def kernel(**inputs):
    in_maps = prep_inputs(**inputs)
    if "nc" not in _CACHE:
        _CACHE["nc"] = build_program()[0]
    nc = _CACHE["nc"]
    res = run_bass_kernel_spmd(nc, in_maps, core_ids=list(range(NCORES)))
    out = np.stack([np.asarray(r["outT"], np.float32).T for r in res.results], axis=0)
    return np.ascontiguousarray(out.astype(np.float32))
```

```python
import numpy as np
from contextlib import ExitStack
import concourse.bass as bass
import concourse.mybir as mybir
from concourse.bass_utils import run_bass_kernel_spmd

F32 = mybir.dt.float32
BF16 = mybir.dt.bfloat16
RELAX_DVE = False
AF = mybir.ActivationFunctionType
ALU = mybir.AluOpType

NCORES = 8
D = 1024
T = 2048
CT = 256
TT = T + CT
NCH = 8
EPS = 1e-6
POOL_WINDOWS = (2, 4, 8, 16)
SEGS = [(0, 256)] + [(256 + 512 * i, 512) for i in range(4)]

_VEC_SPEC = [("g_pre_mix", 8), ("g_post_mix", 8), ("g_pre_ffn", 8), ("g_post_ffn", 8), ("pool_scale", 8),
             ("lru_conv_w", 32), ("lru_conv_b", 8), ("lru_ba", 16), ("lru_bx", 16), ("lru_lambda", 16),
             ("ffn_conv_w", 216), ("ffn_conv_b", 24), ("b_mod", 48)]
VOFF = {}
_o = 0
for _n, _c in _VEC_SPEC:
    VOFF[_n] = _o
    _o += _c
NV = _o


class _Op:
    __slots__ = ("eng", "fn", "deps", "needs_inc", "sem", "val", "is_dma", "slot", "pos")

    def __init__(self, eng, fn, is_dma=False, slot=None):
        self.eng = eng
        self.fn = fn
        self.deps = []
        self.needs_inc = False
        self.sem = None
        self.val = None
        self.is_dma = is_dma
        self.slot = slot
        self.pos = -1


class Prog:
    ENGS = ("pe", "act", "dve", "pool", "sp")

    def __init__(self, nc):
        self.nc = nc
        self.q = {e: [] for e in self.ENGS}
        self.last_w = {}
        self.readers = {}
        self.all_ops = []

    def _track(self, op, reads, writes):
        deps = {}
        for t in reads:
            w = self.last_w.get(t)
            if w is not None:
                deps[id(w)] = w
        for t in writes:
            w = self.last_w.get(t)
            if w is not None:
                deps[id(w)] = w
            for r in self.readers.get(t, {}).values():
                deps[id(r)] = r
        for d in deps.values():
            if d is op:
                continue
            if (not d.is_dma) and (not op.is_dma) and d.eng == "pe" and op.eng == "pe":
                continue
            if RELAX_DVE and (not d.is_dma) and (not op.is_dma) and d.eng == "dve" and op.eng == "dve" and op.pos - d.pos >= 2:
                continue
            d.needs_inc = True
            op.deps.append(d)
        for t in writes:
            self.last_w[t] = op
            self.readers[t] = {}
        for t in reads:
            key = ("dma", op.slot) if op.is_dma else op.eng
            self.readers.setdefault(t, {})[key] = op

    def op(self, eng, fn, reads=(), writes=()):
        o = _Op(eng, fn)
        o.pos = len(self.q[eng])
        self._track(o, reads, writes)
        self.q[eng].append(o)
        self.all_ops.append(o)
        return o

    def dma(self, eng, out, in_, reads=(), writes=(), slot=None, **kw):
        o = _Op(eng, None, is_dma=True, slot=slot)
        o.fn = lambda e, s, o_=out, i_=in_, kw_=kw: e.dma_start(out=o_, in_=i_, **kw_).then_inc(s, 16)
        self._track(o, reads, writes)
        o.needs_inc = True
        self.q[eng].append(o)
        self.all_ops.append(o)
        return o

    def emit(self, final_wait_eng="sp"):
        nc = self.nc
        with ExitStack() as es:
            esem = {e: es.enter_context(nc.semaphore("sem_" + e)) for e in self.ENGS}
            slot_names = sorted({o.slot for o in self.all_ops if o.is_dma})
            ssem = {s: es.enter_context(nc.semaphore("dsem_" + s)) for s in slot_names}
            cnt = {e: 0 for e in self.ENGS}
            for e in self.ENGS:
                for o in self.q[e]:
                    if (not o.is_dma) and o.needs_inc:
                        cnt[e] += 1
                        o.sem, o.val = esem[e], cnt[e]
            scnt = {s: 0 for s in slot_names}
            for o in self.all_ops:
                if o.is_dma:
                    scnt[o.slot] += 16
                    o.sem, o.val = ssem[o.slot], scnt[o.slot]
            block = es.enter_context(nc.Block())
            final = [(ssem[s], scnt[s]) for s in slot_names]

            def run(e):
                def body(eng):
                    waited = {}
                    for o in self.q[e]:
                        for d in o.deps:
                            k = id(d.sem)
                            if waited.get(k, 0) < d.val:
                                eng.wait_ge(d.sem, d.val)
                                waited[k] = d.val
                        if o.is_dma:
                            o.fn(eng, o.sem)
                        else:
                            ins = o.fn(eng)
                            if o.needs_inc:
                                ins.then_inc(o.sem, 1)
                    if e == final_wait_eng:
                        for s, v in final:
                            if v > 0:
                                eng.wait_ge(s, v)
                return body

            block.tensor(run("pe"))
            block.scalar(run("act"))
            block.vector(run("dve"))
            block.gpsimd(run("pool"))
            block.sync(run("sp"))


def build_program(stop=None, dbg=()):
    nc = bass.Bass("TRN2", target_bir_lowering=False)
    dr = {}

    def din(name, shape, dt=F32):
        dr[name] = nc.dram_tensor(name, shape, dt, kind="ExternalInput").ap()

    din("xT", [D, T])
    din("ctxT", [D, CT])
    din("cc", [128, 16])
    din("vecs", [128, NV])
    din("ident", [128, 128])
    din("cnt", [1, 384])
    din("wmod", [12, 128, 4096])
    din("win", [40, 128, 1024])
    din("wpp", [8, 128, 1024])
    din("wpl", [8, 128, 1024])
    din("wout", [8, 128, 1024])
    din("wup", [24, 128, 2048])
    din("wdown", [8, 128, 3072])
    din("poolw", [128, 2048])
    din("lruw", [128, 4096])
    outT = nc.dram_tensor("outT", [D, T], F32, kind="ExternalOutput").ap()
    wupS = nc.dram_tensor("wupS", [24, 128, 2048], BF16, kind="Internal").ap()
    wdnS = nc.dram_tensor("wdnS", [8, 128, 3072], BF16, kind="Internal").ap()
    woutS = nc.dram_tensor("woutS", [8, 128, 1024], BF16, kind="Internal").ap()
    dbg_out = {}

    es = ExitStack()
    with es:
        AW = 53200
        arena = es.enter_context(nc.sbuf_tensor("arena", [128, AW], F32))
        psall = es.enter_context(nc.psum_tensor("psall", [128, 4096], F32))
        ps = [psall[:, i * 512:(i + 1) * 512] for i in range(8)]
        P = Prog(nc)

        def fv(off, n):
            assert off % 4 == 0 and off + 4 * n <= AW * 4, (off, n)
            return arena[:, off // 4: off // 4 + n]

        def bv(off, n):
            assert off % 4 == 0 and n % 2 == 0 and off + 2 * n <= AW * 4, (off, n)
            return arena[:, off // 4: off // 4 + n // 2].bitcast(BF16)

        class Carve:
            def __init__(self, base, size):
                self.base, self.end, self.cur = base, base + size, base

            def f(self, n):
                a = fv(self.cur, n)
                self.cur += 4 * n
                self.cur = (self.cur + 63) // 64 * 64
                assert self.cur <= self.end, ("carve overflow", self.cur, self.end)
                return a

            def b(self, n):
                a = bv(self.cur, n)
                self.cur += 2 * n
                self.cur = (self.cur + 63) // 64 * 64
                assert self.cur <= self.end, ("carve overflow", self.cur, self.end)
                return a

        KB = 1024
        R_H = (0, 36 * KB)
        R_Y = (36 * KB, 64 * KB)
        R_S = (100 * KB, 84 * KB)
        R_W = (184 * KB, AW * 4 - 184 * KB)

        bank_ctr = [0]

        def nextbank():
            b = bank_ctr[0] % 8
            bank_ctr[0] += 1
            return b

        def mm(out, lhsT, rhs, start, stop, reads, writes):
            P.op("pe", lambda e: e.matmul(out, lhsT, rhs, start=start, stop=stop), reads, writes)

        def act(out, in_, func, reads, writes, bias=None, scale=None):
            kw = {}
            if bias is not None:
                kw["bias"] = bias
            if scale is not None:
                kw["scale"] = scale
            P.op("act", lambda e: e.activation(out=out, in_=in_, func=func, **kw), reads, writes)

        def evac_latent(banks, dst, o0, func, reads_extra, toks, **kw):
            i = 0
            while i < len(banks):
                j = i
                while j + 1 < len(banks) and banks[j + 1] == banks[j] + 1:
                    j += 1
                r = j - i + 1
                act(dst[:, o0 + 512 * i:o0 + 512 * (i + r)], psall[:, banks[i] * 512:(banks[i] + r) * 512], func,
                    ["ps%d" % b_ for b_ in banks[i:j + 1]] + list(reads_extra), list(toks[i:j + 1]), **kw)
                i = j + 1

        def tt(eng, out, in0, in1, op, reads, writes):
            P.op(eng, lambda e: e.tensor_tensor(out=out, in0=in0, in1=in1, op=op), reads, writes)

        def ts(eng, out, in0, s1, op0, reads, writes, s2=None, op1=None):
            if op1 is None:
                P.op(eng, lambda e: e.tensor_scalar(out=out, in0=in0, scalar1=s1, scalar2=None, op0=op0), reads, writes)
            else:
                P.op(eng, lambda e: e.tensor_scalar(out=out, in0=in0, scalar1=s1, scalar2=s2, op0=op0, op1=op1), reads, writes)

        def stt(out, in0, scalar, in1, op0, op1, reads, writes):
            P.op("dve", lambda e: e.scalar_tensor_tensor(out=out, in0=in0, scalar=scalar, in1=in1, op0=op0, op1=op1), reads, writes)

        def memset(ap, val, writes):
            P.op("pool", lambda e: e.memset(ap, val), (), writes)

        def dump(name, ap, dt=F32):
            shape = [ap.shape[0], int(np.prod(ap.shape[1:]))]
            t = nc.dram_tensor("dbg_" + name, shape, dt, kind="ExternalOutput").ap()
            dbg_out[name] = t
            return t

        cw = Carve(*R_W)
        vecs = cw.f(NV)
        ident = cw.f(128)
        ones = cw.f(128)
        identb = cw.b(128)
        ccs = cw.f(16)
        ssil = cw.f(16)
        modfm = cw.f(96)
        der = cw.f(64)
        lrud = cw.f(64)
        cw2 = cw.f(32)
        cn1 = cw.f(384)
        W_DYN = cw.cur

        def V(name, i=0):
            o = VOFF[name] + i
            return vecs[:, o:o + 1]

        def Vs(name, i0, n):
            o = VOFF[name] + i0
            return vecs[:, o:o + n]

        P.dma("sp", vecs, dr["vecs"][:, :], writes=["vecs"], slot="c_vecs")
        P.dma("sp", ident, dr["ident"][:, :], writes=["ident"], slot="c_ident")
        P.dma("sp", ccs, dr["cc"][:, :], writes=["cc"], slot="c_cc")
        memset(ones, 1.0, ["ones"])
        P.op("dve", lambda e: e.tensor_copy(out=identb, in_=ident), ["ident"], ["identb"])
        act(ssil, ccs, AF.Silu, ["cc"], ["ssil"])
        s3 = ssil.rearrange("p (k r) -> p k r", r=2)

        cs = Carve(*R_S)
        wmb = [cs.f(4096), cs.f(4096)]
        modrow = cs.f(6144)
        xa4 = cs.f(8 * 512)
        rstdA = cs.f(TT)
        cy = Carve(*R_Y)
        xa = [cy.f(8 * 256), cy.f(8 * 512), cy.f(8 * 512), cy.f(8 * 512), xa4]
        sqb = [cy.f(512), cy.f(512)]
        ssumA = cy.f(512)
        sdb = cy.f(512)
        xT3 = dr["xT"].rearrange("(c p) t -> p c t", p=128)
        cT3 = dr["ctxT"].rearrange("(c p) t -> p c t", p=128)
        xa3 = [xa[si].rearrange("p (c t) -> p c t", t=SEGS[si][1]) for si in range(5)]

        def load_x(si):
            o, n = SEGS[si]
            src = cT3[:, :, :] if si == 0 else xT3[:, :, o - 256:o - 256 + n]
            P.dma("sp", xa3[si], src, writes=["xa%d" % si], slot="xa%d" % si)

        def stats(si):
            o, n = SEGS[si]
            tk = "xa%d" % si
            for c in range(8):
                if c == 0:
                    act(ssumA[:, 0:n], xa3[si][:, c, :], AF.Square, [tk], ["ssumA"])
                else:
                    sq = sqb[c % 2]
                    act(sq[:, 0:n], xa3[si][:, c, :], AF.Square, [tk], ["sq%d" % (c % 2)])
                    tt("dve", ssumA[:, 0:n], ssumA[:, 0:n], sq[:, 0:n], ALU.add, ["ssumA", "sq%d" % (c % 2)], ["ssumA"])
            b = nextbank()
            mm(ps[b][:, 0:n], ones, ssumA[:, 0:n], True, True, ["ones", "ssumA"], ["ps%d" % b])
            act(sdb[:, 0:n], ps[b][:, 0:n], AF.Ln, ["ps%d" % b], ["sd"], scale=1.0 / D, bias=EPS)
            act(rstdA[:, o:o + n], sdb[:, 0:n], AF.Exp, ["sd"], ["rstdA%d" % si], scale=-0.5)
            for c in range(8):
                tt("dve", xa3[si][:, c, :], xa3[si][:, c, :], rstdA[:, o:o + n], ALU.mult, [tk, "rstdA%d" % si], ["xn%d_%d" % (si, c)])

        xsched = {1: 0, 2: 1, 4: 2, 6: 3, 8: 4}
        ssched = {3: 0, 5: 1, 7: 2, 9: 3, 11: 4}
        for blk in range(12):
            buf = wmb[blk % 2]
            tk = "wm%d" % (blk % 2)
            P.dma("sp", buf, dr["wmod"][blk], writes=[tk], slot=tk)
            if blk in xsched:
                load_x(xsched[blk])
            b = nextbank()
            for k in range(8):
                mm(ps[b][0:2, 0:512], s3[:, k, :], buf[:, k * 512:(k + 1) * 512], k == 0, k == 7,
                   ["ssil", tk], ["ps%d" % b])
            act(modrow[0:2, blk * 512:(blk + 1) * 512], ps[b][0:2, 0:512], AF.Identity, ["ps%d" % b], ["modrow%d" % blk])
            if blk in ssched:
                stats(ssched[blk])
        b = nextbank()
        for oc in range(48):
            mm(ps[b][:, 2 * oc:2 * oc + 2], modrow[0:2, oc * 128:(oc + 1) * 128], ident[0:2, 0:2], True, True,
               ["modrow%d" % (oc // 4), "ident"], ["ps%d" % b])
        act(modfm, ps[b][:, 0:96], AF.Identity, ["ps%d" % b], ["modfm"])
        mod3 = modfm.rearrange("p (c r) -> p c r", r=2)
        for r in range(2):
            tt("dve", mod3[:, :, r], mod3[:, :, r], Vs("b_mod", 0, 48), ALU.add, ["modfm", "vecs"], ["modfm"])

        def MOD(which, c, r=0):
            return mod3[:, which * 8 + c, r:r + 1]

        A1 = der[:, 0:8]
        A1c = der[:, 8:16]
        GG1 = der[:, 16:24]
        A2 = der[:, 24:32]
        GG2 = der[:, 32:40]
        stt(A1, mod3[:, 8:16, 0], 1.0, Vs("g_pre_mix", 0, 8), ALU.add, ALU.mult, ["modfm", "vecs"], ["der"])
        stt(A1c, mod3[:, 8:16, 1], 1.0, Vs("g_pre_mix", 0, 8), ALU.add, ALU.mult, ["modfm", "vecs"], ["der"])
        tt("dve", GG1, mod3[:, 16:24, 0], Vs("g_post_mix", 0, 8), ALU.mult, ["modfm", "vecs"], ["der"])
        stt(A2, mod3[:, 32:40, 0], 1.0, Vs("g_pre_ffn", 0, 8), ALU.add, ALU.mult, ["modfm", "vecs"], ["der"])
        tt("dve", GG2, mod3[:, 40:48, 0], Vs("g_post_ffn", 0, 8), ALU.mult, ["modfm", "vecs"], ["der"])
        lam = Vs("lru_lambda", 0, 16)
        le = lrud[:, 0:16]
        lsp = lrud[:, 16:32]
        ls1 = lrud[:, 32:48]
        act(le, lam, AF.Exp, ["vecs"], ["le"], scale=-1.0)
        act(lsp, le, AF.Ln, ["le"], ["lsp"], bias=1.0)
        ts("dve", ls1, lsp, -4.0, ALU.mult, ["lsp"], ["hs1"])
        hs1 = ls1
        hbias = cw2
        ts("dve", hbias, Vs("lru_ba", 0, 32), 0.5, ALU.mult, ["vecs"], ["hbias"])

        ch = Carve(*R_H)
        h = ch.b(8 * TT)
        h3 = h.rearrange("p (c t) -> p c t", t=TT)
        for si, (o, n) in enumerate(SEGS):
            for c in range(8):
                if si == 0:
                    sc_, bi_ = A1c[:, c:c + 1], MOD(0, c, 1)
                else:
                    sc_, bi_ = A1[:, c:c + 1], MOD(0, c, 0)
                act(h3[:, c, o:o + n], xa3[si][:, c, :], AF.Identity, ["xn%d_%d" % (si, c), "der", "modfm"], ["h%d_%d" % (c, si)],
                    bias=bi_, scale=sc_)
        HSEG = lambda si: ["h%d_%d" % (c, si) for c in range(8)]

        if "h" in dbg:
            P.dma("sp", dump("h", h, BF16)[:, :], h, reads=[t for si in range(5) for t in HSEG(si)], slot="dbg_h")
            P.dma("sp", dump("modfm", modfm)[:, :], modfm, reads=["modfm"], slot="dbg_m")
        if stop == "A":
            P.emit()
            return nc, dbg_out

        cy = Carve(*R_Y)
        ypool = cy.b(8 * T)
        ylru = cy.b(8 * T)
        ypool3 = ypool.rearrange("p (c t) -> p c t", t=T)
        ylru3 = ylru.rearrange("p (c t) -> p c t", t=T)
        YTOK = ["xa%d" % si for si in range(4)] + ["xn%d_%d" % (si, c) for si in range(4) for c in range(8)] + ["sq0", "sq1", "sd", "ssumA"]
        y_first = [True]

        cwd = Carve(W_DYN, R_W[0] + R_W[1] - W_DYN)
        lruw = cwd.b(4096)
        poolw = cwd.b(2048)
        NWIN = 3
        winb = [cwd.b(1024) for _ in range(NWIN)]
        win_ctr = [0]
        P.dma("pool", lruw[:, 0:2048], dr["lruw"][:, 0:2048], writes=["lruw"], slot="w_lruw")
        P.dma("pool", lruw[:, 2048:4096], dr["lruw"][:, 2048:4096], writes=["lruw"], slot="w_lruw")
        P.dma("pool", poolw, dr["poolw"][:, :], writes=["poolw"], slot="w_poolw")

        def load_win(oc):
            i = win_ctr[0] % NWIN
            win_ctr[0] += 1
            P.dma("pool", winb[i], dr["win"][oc], writes=["win%d" % i], slot="win%d" % i)
            return winb[i], "win%d" % i

        def proj_seg(wt, wtk, si, b):
            o, n = SEGS[si]
            for k in range(8):
                mm(ps[b][:, 0:n], wt[:, k * 128:(k + 1) * 128], h3[:, k, o:o + n], k == 0, k == 7,
                   [wtk] + HSEG(si), ["ps%d" % b])

        cast_plan = []
        for oc in range(8):
            cast_plan.append((woutS[oc], dr["wout"][oc]))
        for j in range(24):
            cast_plan.append((wupS[j], dr["wup"][j]))
        for oc in range(8):
            cast_plan.append((wdnS[oc][:, 0:2048], dr["wdown"][oc][:, 0:2048]))
            cast_plan.append((wdnS[oc][:, 2048:3072], dr["wdown"][oc][:, 2048:3072]))
        cast_ctr = [0]

        def issue_casts(k):
            for _ in range(k):
                i = cast_ctr[0]
                if i >= len(cast_plan):
                    return
                cast_ctr[0] += 1
                dst, src = cast_plan[i]
                wr = ["wc%d" % i] + (["wc_all"] if i == len(cast_plan) - 1 else [])
                P.dma("pool", dst, src, writes=wr, slot="wcast")

        STOK_A = (["wm0", "wm1"] + ["modrow%d" % i for i in range(12)] + ["xa4"] + ["xn4_%d" % c for c in range(8)]
                  + ["rstdA%d" % si for si in range(5)])
        cs = Carve(*R_S)
        cntb = cs.f(T)
        PQ = [[cs.f(47 * 79), cs.f(47 * 79)] for _ in range(2)]
        dbf = [cs.b(T), cs.b(T)]
        PQTOK = ["pq%d_%d" % (cl, i) for cl in range(2) for i in range(2)]
        cnt3 = cntb.rearrange("p (r c) -> p r c", c=64)
        first_S = [True]
        nb_ctr = [0]

        def nextB():
            b = 4 + nb_ctr[0] % 4
            nb_ctr[0] += 1
            return b

        def proj_cl0(g_):
            wt_, wtk_ = load_win(2 * g_)
            for t in range(4):
                proj_seg(wt_, wtk_, 1 + t, t)
            return (wt_, wtk_)

        pre_cl0 = proj_cl0(0)
        P.dma("sp", cn1, dr["cnt"][0:1, :].partition_broadcast(128), writes=["cn1"], slot="c_cn1")
        P.op("dve", lambda e: e.reciprocal(out=cn1, in_=cn1), ["cn1"], ["cn1"])
        for g, w in enumerate(POOL_WINDOWS):
            hw = w // 2
            Hp, Wp = 32 + w - 1, 64 + w - 1
            extra = STOK_A if first_S[0] else []
            first_S[0] = False
            ir = cn1[:, g * 96: g * 96 + 32].unsqueeze(2).broadcast_to([128, 32, 64])
            ic = cn1[:, g * 96 + 32: g * 96 + 96].unsqueeze(1).broadcast_to([128, 32, 64])
            tt("dve", cnt3, ir, ic, ALU.mult, ["cn1"], ["cnt"] + extra)
            views = []
            for cl in range(2):
                v = [PQ[cl][i][:, 0:Hp * Wp].rearrange("p (r c) -> p r c", c=Wp) for i in range(2)]
                views.append(v)
                Pv, Qv = v
                memset(Pv[:, 0:hw, :], 0.0, ["pq%d_0" % cl] + extra)
                if hw > 1:
                    memset(Pv[:, hw + 32:Hp, :], 0.0, ["pq%d_0" % cl])
                memset(Pv[:, hw:hw + 32, 0:hw], 0.0, ["pq%d_0" % cl])
                if hw > 1:
                    memset(Pv[:, hw:hw + 32, hw + 64:Wp], 0.0, ["pq%d_0" % cl])
                if w in (2, 8):
                    memset(Qv[:, 0:hw, 0:64], 0.0, ["pq%d_1" % cl])
                    if hw > 1:
                        memset(Qv[:, hw + 32:Hp, 0:64], 0.0, ["pq%d_1" % cl])
            wts = [pre_cl0]
            for t in range(4):
                act(views[0][0][:, hw + 8 * t: hw + 8 * t + 8, hw:hw + 64], ps[t][:, :].rearrange("p (r c) -> p r c", c=64),
                    AF.Identity, ["ps%d" % t], ["pq0_0"])
            wt, wtk = load_win(2 * g + 1)
            wts.append((wt, wtk))
            for t in range(4):
                b = nextB()
                proj_seg(wt, wtk, 1 + t, b)
                act(views[1][0][:, hw + 8 * t: hw + 8 * t + 8, hw:hw + 64], ps[b][:, :].rearrange("p (r c) -> p r c", c=64),
                    AF.Identity, ["ps%d" % b], ["pq1_0"])
            if g + 1 < 4:
                pre_cl0 = proj_cl0(g + 1)
            state = [dict(cur=0, ln=Wp, rows=Hp) for _ in range(2)]
            k = 1
            while k < w:
                for cl in range(2):
                    st = state[cl]
                    src, dst = views[cl][st["cur"]], views[cl][1 - st["cur"]]
                    nl = st["ln"] - k
                    tt("dve", dst[:, hw:hw + 32, 0:nl], src[:, hw:hw + 32, 0:nl], src[:, hw:hw + 32, k:k + nl], ALU.add,
                       ["pq%d_%d" % (cl, st["cur"])], ["pq%d_%d" % (cl, 1 - st["cur"])])
                    st["cur"], st["ln"] = 1 - st["cur"], nl
                k *= 2
            k = 1
            while k < w:
                for cl in range(2):
                    st = state[cl]
                    src, dst = views[cl][st["cur"]], views[cl][1 - st["cur"]]
                    nr = st["rows"] - k
                    tt("dve", dst[:, 0:nr, 0:64], src[:, 0:nr, 0:64], src[:, k:k + nr, 0:64], ALU.add,
                       ["pq%d_%d" % (cl, st["cur"])], ["pq%d_%d" % (cl, 1 - st["cur"])])
                    st["cur"], st["rows"] = 1 - st["cur"], nr
                k *= 2
            for cl in range(2):
                st = state[cl]
                assert st["ln"] == 64 and st["rows"] == 32
                src, oth = views[cl][st["cur"]], views[cl][1 - st["cur"]]
                tt("dve", oth[:, 0:32, 0:64], src[:, 0:32, 0:64], cnt3, ALU.mult, ["pq%d_%d" % (cl, st["cur"]), "cnt"],
                   ["pq%d_%d" % (cl, 1 - st["cur"])])
                wt, wtk = wts[cl]
                for t in range(4):
                    b = nextB()
                    proj_seg(wt, wtk, 1 + t, b)
                    tt("dve", dbf[cl][:, 512 * t:512 * t + 512].rearrange("p (r c) -> p r c", c=64), oth[:, 8 * t:8 * t + 8, 0:64],
                       ps[b][:, :].rearrange("p (r c) -> p r c", c=64), ALU.subtract, ["pq%d_%d" % (cl, 1 - st["cur"]), "ps%d" % b],
                       ["dbf%d_%d" % (cl, t)])
            for ocl in range(2):
                oc = 2 * g + ocl
                for t in range(4):
                    b = nextB()
                    for k in range(2):
                        idx = ((g * 2 + ocl) * 2 + k) * 128
                        mm(ps[b][:, :], poolw[:, idx:idx + 128], dbf[k][:, 512 * t:512 * t + 512], k == 0, k == 1,
                           ["poolw", "dbf%d_%d" % (k, t)], ["ps%d" % b])
                    extra = YTOK if y_first[0] else []
                    y_first[0] = False
                    act(ypool3[:, oc, 512 * t:512 * t + 512], ps[b][:, :], AF.Identity, ["ps%d" % b, "vecs"],
                        ["ypool%d_%d" % (oc, t)] + extra, scale=V("pool_scale", oc))
            issue_casts(5)
        if "ypool" in dbg:
            P.dma("sp", dump("ypool", ypool, BF16)[:, :], ypool,
                  reads=["ypool%d_%d" % (oc, t) for oc in range(8) for t in range(4)], slot="dbg_yp")
        if stop == "Bp":
            P.emit()
            return nc, dbg_out

        cs = Carve(*R_S)
        UPW = 2320
        LOFF = 264
        upad = cs.b(UPW)
        dgw = cs.b(32 * 128)
        o_m2b = cs.cur
        m2b = cs.f(TT)
        xcb = bv(o_m2b, TT)
        xc = cs.f(TT)
        m2f = cs.f(TT)
        ra = [cs.f(TT), cs.f(TT)]
        ib = [cs.f(TT), cs.f(TT)]
        gel = cs.f(T)
        m2 = [m2f, m2b]
        m2tok = ["m2f", "m2b"]
        POOLTOK = ["cnt"] + PQTOK + ["dbf%d_%d" % (cl, t) for cl in range(2) for t in range(4)]
        memset(upad, 0.0, ["upad"] + POOLTOK)
        for k in range(4):
            for n in range(8):
                i = k * 8 + n
                ts("dve", dgw[:, i * 128:(i + 1) * 128], ident, V("lru_conv_w", i), ALU.mult, ["ident", "vecs"], ["dgw"] + (POOLTOK if i == 0 else []))
        RA = lambda d: ["ra%d_%d" % (d, si) for si in range(5)]
        IB = lambda d: ["ib%d_%d" % (d, si) for si in range(5)]
        win_next = load_win(8 + 0)
        for n in range(8):
            wt, wtk = win_next
            for si, (o, nn) in enumerate(SEGS):
                b = nextbank()
                proj_seg(wt, wtk, si, b)
                po = 2 + o if si == 0 else LOFF + (o - 256)
                act(upad[:, po:po + nn], ps[b][:, 0:nn], AF.Identity, ["ps%d" % b], ["upad"])
            for si, (o, nn) in enumerate(SEGS):
                base = o if si == 0 else LOFF - 2 + (o - 256)
                b = nextbank()
                for k in range(4):
                    i = k * 8 + n
                    mm(ps[b][:, 0:nn], dgw[:, i * 128:(i + 1) * 128], upad[:, base + k:base + k + nn], k == 0, k == 3,
                       ["dgw", "upad"], ["ps%d" % b])
                act(xc[:, o:o + nn], ps[b][:, 0:nn], AF.Identity, ["ps%d" % b, "vecs"], ["xc"], bias=V("lru_conv_b", n))
            P.op("dve", lambda e: e.tensor_copy(out=xcb, in_=xc), ["xc"], ["m2b"])
            for dr_ in range(2):
                for kind, dst, tkf in ((0, ra[dr_], RA(dr_)), (1, ib[dr_], IB(dr_))):
                    widx = ((kind * 2 + dr_) * 8 + n) * 128
                    hb_ = hbias[:, kind * 16 + dr_ * 8 + n: kind * 16 + dr_ * 8 + n + 1]
                    b = nextbank()
                    mm(ps[b][:, 0:256], lruw[:, widx:widx + 128], xcb[:, 0:256], True, True, ["lruw", "m2b"], ["ps%d" % b])
                    act(dst[:, 0:256], ps[b][:, 0:256], AF.Tanh, ["ps%d" % b, "hbias"], [tkf[0]], bias=hb_, scale=0.5)
                    gb = []
                    for si in range(1, 5):
                        o, nn = SEGS[si]
                        b = nextbank()
                        gb.append(b)
                        mm(ps[b][:, 0:nn], lruw[:, widx:widx + 128], xcb[:, o:o + nn], True, True, ["lruw", "m2b"], ["ps%d" % b])
                    evac_latent(gb, dst, 256, AF.Tanh, ["hbias"], tkf[1:5], bias=hb_, scale=0.5)
                hs = hs1[:, dr_ * 8 + n: dr_ * 8 + n + 1]
                act(ra[dr_], ra[dr_], AF.Exp, RA(dr_) + ["hs1"], RA(dr_), bias=hs, scale=hs)
                act(m2[dr_], ra[dr_], AF.Square, RA(dr_), [m2tok[dr_]])
                act(m2[dr_], m2[dr_], AF.Sqrt, [m2tok[dr_]], [m2tok[dr_]], scale=-0.25, bias=0.25)
            for dr_ in range(2):
                stt(ib[dr_], ib[dr_], 1.0, xc, ALU.add, ALU.mult, IB(dr_) + ["xc"], IB(dr_))
                tt("dve", ib[dr_], ib[dr_], m2[dr_], ALU.mult, IB(dr_) + [m2tok[dr_]], IB(dr_))
                if dr_ == 0:
                    P.op("dve", lambda e: e.tensor_tensor_scan(out=ib[0], data0=ra[0], data1=ib[0], initial=0.0, op0=ALU.mult, op1=ALU.add),
                         RA(0) + IB(0), IB(0))
                else:
                    P.op("dve", lambda e: e.tensor_tensor_scan(out=ib[1][:, 0:256][:, ::-1], data0=ra[1][:, 0:256][:, ::-1],
                                                                data1=ib[1][:, 0:256][:, ::-1], initial=0.0, op0=ALU.mult, op1=ALU.add),
                         RA(1) + IB(1), IB(1))
                    P.op("dve", lambda e: e.tensor_tensor_scan(out=ib[1][:, 256:TT][:, ::-1], data0=ra[1][:, 256:TT][:, ::-1],
                                                                data1=ib[1][:, 256:TT][:, ::-1], initial=ib[1][:, 0:1], op0=ALU.mult, op1=ALU.add),
                         RA(1) + IB(1), IB(1))
            wt, wtk = load_win(16 + n)
            if n + 1 < 8:
                win_next = load_win(8 + n + 1)
            issue_casts(5)
            gb = []
            for t in range(4):
                b = nextbank()
                gb.append(b)
                proj_seg(wt, wtk, 1 + t, b)
            evac_latent(gb, gel, 0, AF.Gelu_apprx_tanh, [], ["gel%d" % t for t in range(4)])
            GEL = ["gel%d" % t for t in range(4)]
            tt("dve", ib[0][:, 256:TT], ib[0][:, 256:TT], ib[1][:, 256:TT], ALU.add, IB(0) + IB(1), IB(0))
            tt("dve", ylru3[:, n, :], ib[0][:, 256:TT], gel, ALU.mult, IB(0) + GEL, ["ylru%d" % n] + (YTOK if n == 0 else []))
        if "ylru" in dbg:
            P.dma("sp", dump("ylru", ylru, BF16)[:, :], ylru, reads=["ylru%d" % n for n in range(8)], slot="dbg_yl")
        if stop == "B":
            P.emit()
            return nc, dbg_out

        LRUTOK = ["upad", "xc", "m2f", "m2b", "dgw"] + RA(0) + RA(1) + IB(0) + IB(1) + GEL
        cs = Carve(*R_S)
        mbuf = cs.b(8 * T)
        m3 = mbuf.rearrange("p (c t) -> p c t", t=T)
        S_C2 = cs.cur
        sgb = [[cs.f(512), cs.f(512)] for _ in range(2)]
        t12 = [[cs.f(512), cs.f(512)] for _ in range(2)]
        cwd = Carve(W_DYN, R_W[0] + R_W[1] - W_DYN)
        c1w = [[cwd.b(1024) for _ in range(4)] for _ in range(2)]
        W_OLD = ["lruw", "poolw"] + ["win%d" % i for i in range(NWIN)]
        first_c1 = [True]
        YP = lambda t: ["ypool%d_%d" % (k, t) for k in range(8)]
        YL = ["ylru%d" % k for k in range(8)]
        it = 0
        for oc in range(8):
            sl = oc % 2
            srcs = [dr["wpp"][oc], dr["win"][24 + oc], dr["wpl"][oc], dr["win"][32 + oc]]
            for i in range(4):
                extra = W_OLD if first_c1[0] else []
                first_c1[0] = False
                P.dma("pool", c1w[sl][i], srcs[i], writes=["c1w%d_%d" % (sl, i)] + extra, slot="c1w%d_%d" % (sl, i))
            for t in range(4):
                bb = [nextbank() for _ in range(4)]
                for k in range(8):
                    mm(ps[bb[0]][:, :], c1w[sl][0][:, k * 128:(k + 1) * 128], ypool3[:, k, 512 * t:512 * t + 512], k == 0, k == 7,
                       ["c1w%d_0" % sl] + YP(t), ["ps%d" % bb[0]])
                for k in range(8):
                    mm(ps[bb[1]][:, :], c1w[sl][1][:, k * 128:(k + 1) * 128], h3[:, k, 256 + 512 * t:256 + 512 * t + 512], k == 0, k == 7,
                       ["c1w%d_1" % sl] + HSEG(1 + t), ["ps%d" % bb[1]])
                for k in range(8):
                    mm(ps[bb[2]][:, :], c1w[sl][2][:, k * 128:(k + 1) * 128], ylru3[:, k, 512 * t:512 * t + 512], k == 0, k == 7,
                       ["c1w%d_2" % sl] + YL, ["ps%d" % bb[2]])
                for k in range(8):
                    mm(ps[bb[3]][:, :], c1w[sl][3][:, k * 128:(k + 1) * 128], h3[:, k, 256 + 512 * t:256 + 512 * t + 512], k == 0, k == 7,
                       ["c1w%d_3" % sl] + HSEG(1 + t), ["ps%d" % bb[3]])
                p = it % 2
                it += 1
                extra = LRUTOK if (oc == 0 and t == 0) else []
                act(sgb[p][0], ps[bb[1]][:, :], AF.Sigmoid, ["ps%d" % bb[1]], ["sg%d_0" % p] + extra)
                act(sgb[p][1], ps[bb[3]][:, :], AF.Sigmoid, ["ps%d" % bb[3]], ["sg%d_1" % p])
                tt("dve", t12[p][0], ps[bb[0]][:, :], sgb[p][0], ALU.mult, ["ps%d" % bb[0], "sg%d_0" % p], ["t12%d_0" % p])
                tt("dve", t12[p][1], ps[bb[2]][:, :], sgb[p][1], ALU.mult, ["ps%d" % bb[2], "sg%d_1" % p], ["t12%d_1" % p])
                tt("dve", m3[:, oc, 512 * t:512 * t + 512], t12[p][0], t12[p][1], ALU.add, ["t12%d_0" % p, "t12%d_1" % p],
                   ["m%d_%d" % (oc, t)])
        MT = lambda tl: ["m%d_%d" % (k, t) for k in range(8) for t in tl]
        if "m" in dbg:
            P.dma("sp", dump("m", mbuf, BF16)[:, :], mbuf, reads=MT(range(4)), slot="dbg_mm")
        if stop == "C1":
            P.emit()
            return nc, dbg_out

        cs = Carve(S_C2, R_S[0] + R_S[1] - S_C2)
        sq2 = [cs.f(640), cs.f(640)]
        tm2 = [cs.f(640), cs.f(640)]
        sd2 = cs.f(640)
        rstd2 = cs.f(640)
        gpad = [cs.f(660) for _ in range(4)]
        mixB = cs.f(8 * 640)
        ssum2 = cs.f(640)
        HY_OLD = [t for si in range(5) for t in HSEG(si)] + [t for tl in range(4) for t in YP(tl)] + YL
        cb_ = Carve(R_H[0], R_H[1] + R_Y[1])
        xt1 = cb_.f(8 * 640)
        mixb = cb_.f(8 * 640)
        hf = cb_.b(8 * 640)
        abuf = cb_.b(24 * 512)
        wupb = [cb_.b(2048) for _ in range(3)]
        wdnb = [cb_.b(3072) for _ in range(2)]
        xt13 = xt1.rearrange("p (c t) -> p c t", t=640)
        mixbufs = [mixb, mixB]
        mix3s = [mb.rearrange("p (c t) -> p c t", t=640) for mb in mixbufs]
        f3s = [mb[:, 0:8 * 512].rearrange("p (c t) -> p c t", t=512) for mb in mixbufs]
        hf3 = hf.rearrange("p (c t) -> p c t", t=640)
        a3 = abuf.rearrange("p (c t) -> p c t", t=512)
        cwd = Carve(W_DYN, R_W[0] + R_W[1] - W_DYN)
        accb = [cwd.f(512) for _ in range(4)]
        wupb.append(cwd.b(2048))
        woutb = [cwd.b(1024) for _ in range(3)]
        wo_ctr = [0]
        C1W_OLD = ["c1w%d_%d" % (s, i) for s in range(2) for i in range(4)]
        oT3 = outT.rearrange("(c p) t -> p c t", p=128)
        memset(gpad[0], 0.0, ["gpad0"] + ["sg%d_%d" % (p, i) for p in range(2) for i in range(2)] + ["t12%d_%d" % (p, i) for p in range(2) for i in range(2)])
        for gi in range(1, 4):
            memset(gpad[gi], 0.0, ["gpad%d" % gi])
        first_hy = [True]
        wup_ctr = [0]
        wdn_ctr = [0]
        dg_ctr = [0]
        gp_ctr = [0]

        def norm_rstd(src_fn, ntok, reads_fn, hook=None):
            subs = [(0, min(512, ntok))] + ([(512, ntok - 512)] if ntok > 512 else [])
            for c in range(8):
                if c == 0:
                    act(ssum2[:, 0:ntok], src_fn(c), AF.Square, reads_fn(c), ["ssum2"])
                else:
                    sq = sq2[c % 2]
                    act(sq[:, 0:ntok], src_fn(c), AF.Square, reads_fn(c), ["sq2_%d" % (c % 2)])
                    tt("dve", ssum2[:, 0:ntok], ssum2[:, 0:ntok], sq[:, 0:ntok], ALU.add, ["ssum2", "sq2_%d" % (c % 2)], ["ssum2"])
                if hook is not None:
                    hook(c)
            for (so, sn) in subs:
                b = nextbank()
                mm(ps[b][:, 0:sn], ones, ssum2[:, so:so + sn], True, True, ["ones", "ssum2"], ["ps%d" % b])
                act(sd2[:, so:so + sn], ps[b][:, 0:sn], AF.Ln, ["ps%d" % b], ["sd2"], scale=1.0 / D, bias=EPS)
            act(rstd2[:, 0:ntok], sd2[:, 0:ntok], AF.Exp, ["sd2"], ["rstd2"], scale=-0.5)

        def geom(tl):
            r0 = max(8 * tl - 1, 0)
            r1 = min(8 * tl + 9, 32)
            ntok = (r1 - r0) * 64
            return r0, ntok, r0 * 64, [(0, 512)] + [(512, ntok - 512)]

        first_mix = [True]

        def mix_units(tl):
            r0, ntok, tok0, subs = geom(tl)
            par = tl % 2
            mtl = sorted({min(3, (tok0 + so) // 512) for so, sn in subs} | {min(3, (tok0 + so + sn - 1) // 512) for so, sn in subs})
            units = []
            for oc in range(8):
                for (so, sn) in subs:
                    st_ = {}

                    def pe_fn(oc=oc, so=so, sn=sn, st_=st_):
                        if so == 0:
                            wi = wo_ctr[0] % 3
                            wo_ctr[0] += 1
                            extra = (LRUTOK + C1W_OLD) if first_mix[0] else []
                            P.dma("sp", woutb[wi], woutS[oc], reads=["wc_all"], writes=["wo%d" % wi] + extra, slot="wo%d" % wi)
                            wo_cur[0] = wi
                        wi = wo_cur[0]
                        b = nextbank()
                        st_["b"] = b
                        for k in range(8):
                            mm(ps[b][:, 0:sn], woutb[wi][:, k * 128:(k + 1) * 128],
                               m3[:, k, tok0 + so:tok0 + so + sn], k == 0, k == 7, ["wo%d" % wi] + MT(mtl), ["ps%d" % b])

                    def act_fn(oc=oc, so=so, sn=sn, st_=st_):
                        b = st_["b"]
                        extra2 = []
                        if oc == 0 and so == 0:
                            extra2 = ["f%d_%d" % (par, q) for q in range(8)]
                            if first_mix[0] or tl == 1:
                                extra2 = extra2 + ["sg%d_%d" % (p_, i) for p_ in range(2) for i in range(2)] + ["t12%d_%d" % (p_, i) for p_ in range(2) for i in range(2)] + HY_OLD
                        first_mix[0] = False
                        act(mix3s[par][:, oc, so:so + sn], ps[b][:, 0:sn], AF.Identity, ["ps%d" % b], ["mix%d_%d" % (par, oc)] + extra2)
                    units.append((pe_fn, act_fn))
            return units

        wo_cur = [0]

        def do_mix(tl):
            for pe_fn, act_fn in mix_units(tl):
                pe_fn()
                act_fn()

        do_mix(0)
        for tl in range(4):
            r0, ntok, tok0, subs = geom(tl)
            par = tl % 2
            mix3 = mix3s[par]
            f3 = f3s[par]
            MIXT = lambda c: "mix%d_%d" % (par, c)
            FT = lambda c: "f%d_%d" % (par, c)
            co = (8 * tl - r0) * 64
            s0 = r0 - (8 * tl - 1)
            extra = HY_OLD if first_hy[0] else []
            first_hy[0] = False
            for c in range(8):
                P.dma("sp", xt13[:, c, 0:ntok], xT3[:, c, tok0:tok0 + ntok], writes=["xt1_%d" % c, "x1_%d" % c] + (extra if c == 0 else []),
                      slot="xt1_%d" % c)
            norm_rstd(lambda c: mix3[:, c, 0:ntok], ntok, lambda c: [MIXT(c)])
            pend = mix_units(tl + 1) if tl + 1 < 4 else []
            pend_pe = [u[0] for u in pend]
            pend_act = [u[1] for u in pend]
            inflight = [0]

            def pump(nact):
                for _ in range(nact):
                    if pend_act:
                        pend_act.pop(0)()
                        inflight[0] -= 1
                while pend_pe and inflight[0] < 5:
                    pend_pe.pop(0)()
                    inflight[0] += 1

            pump(0)
            for c in range(8):
                tm = tm2[c % 2]
                tt("dve", tm[:, 0:ntok], mix3[:, c, 0:ntok], rstd2[:, 0:ntok], ALU.mult, [MIXT(c), "rstd2"], ["tm2_%d" % (c % 2)])
                stt(xt13[:, c, 0:ntok], tm[:, 0:ntok], GG1[:, c:c + 1], xt13[:, c, 0:ntok], ALU.mult, ALU.add,
                    ["tm2_%d" % (c % 2), "der", "xt1_%d" % c], ["x1_%d" % c])
            pump(4)
            norm_rstd(lambda c: xt13[:, c, 0:ntok], ntok, lambda c: ["x1_%d" % c], hook=lambda c: pump(1))
            pump(4)
            for c in range(8):
                tm = tm2[c % 2]
                tt("dve", tm[:, 0:ntok], xt13[:, c, 0:ntok], rstd2[:, 0:ntok], ALU.mult, ["x1_%d" % c, "rstd2"], ["tm2_%d" % (c % 2)])
                act(hf3[:, c, 0:ntok], tm[:, 0:ntok], AF.Identity, ["tm2_%d" % (c % 2), "der", "modfm"], ["hf%d" % c],
                    bias=MOD(3, c, 0), scale=A2[:, c:c + 1])
            while pend_act or pend_pe:
                pump(1)
            HF = ["hf%d" % c for c in range(8)]
            if tl == 3:
                for gi in range(4):
                    memset(gpad[gi].rearrange("p (r c) -> p r c", c=66)[:, 9, :], 0.0, ["gpad%d" % gi])
            def ffn_front(p):
                st = []
                for i in range(2):
                    j = 2 * p + i
                    q = j % 4
                    wi = wup_ctr[0] % 4
                    wup_ctr[0] += 1
                    P.dma("sp", wupb[wi], wupS[j], reads=["wc_all"], writes=["wup%d" % wi], slot="wup%d" % wi)
                    gp3 = gpad[q].rearrange("p (r c) -> p r c", c=66)
                    wg = wupb[wi][:, 0:1024]
                    row = s0
                    for (so, sn) in subs:
                        b = nextbank()
                        for k in range(8):
                            mm(ps[b][:, 0:sn], wg[:, k * 128:(k + 1) * 128], hf3[:, k, so:so + sn], k == 0, k == 7,
                               ["wup%d" % wi] + HF, ["ps%d" % b])
                        nr = sn // 64
                        act(gp3[:, row:row + nr, 1:65], ps[b][:, 0:sn].rearrange("p (r c) -> p r c", c=64), AF.Identity,
                            ["ps%d" % b], ["gpad%d" % q])
                        row += nr
                    acc3 = accb[q].rearrange("p (r c) -> p r c", c=64)
                    act(acc3, gp3[:, 0:8, 0:64], AF.Identity, ["gpad%d" % q, "vecs"], ["acc%d" % q],
                        bias=V("ffn_conv_b", j), scale=V("ffn_conv_w", 0 * 24 + j))
                    st.append((j, q, wi, gp3, acc3))
                return st

            def ffn_taps(st, taps):
                for tap in taps:
                    dy, dx = tap // 3, tap % 3
                    for (j, q, wi, gp3, acc3) in st:
                        stt(acc3, gp3[:, dy:dy + 8, dx:dx + 64], V("ffn_conv_w", tap * 24 + j), acc3, ALU.mult, ALU.add,
                            ["gpad%d" % q, "acc%d" % q, "vecs"], ["acc%d" % q])

            def ffn_u(st):
                out = []
                for (j, q, wi, gp3, acc3) in st:
                    wu = wupb[wi][:, 1024:2048]
                    bu = nextbank()
                    for k in range(8):
                        mm(ps[bu][:, :], wu[:, k * 128:(k + 1) * 128], hf3[:, k, co:co + 512], k == 0, k == 7,
                           ["wup%d" % wi] + HF, ["ps%d" % bu])
                    out.append(bu)
                return out

            def ffn_gelu(st):
                for (j, q, wi, gp3, acc3) in st:
                    act(accb[q], accb[q], AF.Gelu_apprx_tanh, ["acc%d" % q], ["acc%d" % q])

            def ffn_mult(st, bus):
                for (j, q, wi, gp3, acc3), bu in zip(st, bus):
                    tt("dve", a3[:, j, :], accb[q], ps[bu][:, :], ALU.mult, ["acc%d" % q, "ps%d" % bu], ["a%d" % j])

            prev = None
            for p in range(12):
                st = ffn_front(p)
                if prev is not None:
                    ffn_gelu(prev[0])
                ffn_taps(st, range(1, 2))
                if prev is not None:
                    ffn_mult(*prev)
                ffn_taps(st, range(2, 9))
                bus = ffn_u(st)
                prev = (st, bus)
            ffn_gelu(prev[0])
            ffn_mult(*prev)
            AT = ["a%d" % j for j in range(24)]
            for oc in range(8):
                wi = wdn_ctr[0] % 2
                wdn_ctr[0] += 1
                P.dma("sp", wdnb[wi], wdnS[oc], reads=["wc_all"], writes=["wdn%d" % wi], slot="wdn%d" % wi)
                b = nextbank()
                for k in range(24):
                    mm(ps[b][:, :], wdnb[wi][:, k * 128:(k + 1) * 128], a3[:, k, :], k == 0, k == 23, ["wdn%d" % wi] + AT, ["ps%d" % b])
                act(f3[:, oc, :], ps[b][:, :], AF.Identity, ["ps%d" % b], [FT(oc)] + ([MIXT(q) for q in range(8)] if oc == 0 else []))
                if oc == 0:
                    act(ssum2[:, 0:512], f3[:, oc, :], AF.Square, [FT(oc)], ["ssum2"])
                else:
                    sq = sq2[oc % 2]
                    act(sq[:, 0:512], f3[:, oc, :], AF.Square, [FT(oc)], ["sq2_%d" % (oc % 2)])
                    tt("dve", ssum2[:, 0:512], ssum2[:, 0:512], sq[:, 0:512], ALU.add, ["ssum2", "sq2_%d" % (oc % 2)], ["ssum2"])
            bss = nextbank()
            mm(ps[bss][:, :], ones, ssum2[:, 0:512], True, True, ["ones", "ssum2"], ["ps%d" % bss])
            act(sd2[:, 0:512], ps[bss][:, :], AF.Ln, ["ps%d" % bss], ["sd2"], scale=1.0 / D, bias=EPS)
            act(rstd2[:, 0:512], sd2[:, 0:512], AF.Exp, ["sd2"], ["rstd2"], scale=-0.5)
            for oc in range(8):
                tm = tm2[oc % 2]
                tt("dve", tm[:, 0:512], f3[:, oc, :], rstd2[:, 0:512], ALU.mult, [FT(oc), "rstd2"], ["tm2_%d" % (oc % 2)])
                stt(f3[:, oc, :], tm[:, 0:512], GG2[:, oc:oc + 1], xt13[:, oc, co:co + 512], ALU.mult, ALU.add,
                    ["tm2_%d" % (oc % 2), "der", "x1_%d" % oc, FT(oc)], [FT(oc)])
            P.dma("sp", oT3[:, :, 512 * tl:512 * tl + 512], f3, reads=[FT(oc) for oc in range(8)], slot="out")
        P.emit()
    return nc, dbg_out


def _fm(v, nch):
    v = np.asarray(v, np.float32)
    lead = v.shape[:-1]
    r = v.reshape(lead + (nch, 128))
    r = np.moveaxis(r, -1, 0)
    return np.ascontiguousarray(r.reshape(128, -1))


def _wtile(w, kch, och):
    w = np.asarray(w, np.float32)
    r = w.reshape(kch, 128, och, 128).transpose(2, 1, 0, 3)
    return np.ascontiguousarray(r.reshape(och, 128, kch * 128))


def _window_counts():
    out = np.zeros((4, 96), np.float32)
    for gi, w in enumerate(POOL_WINDOWS):
        def cnt1(n):
            pos = np.arange(n)
            lo = np.clip(pos - w // 2, 0, n)
            hi = np.clip(pos + w - w // 2, 0, n)
            return (hi - lo).astype(np.float32)
        out[gi, 0:32] = cnt1(32)
        out[gi, 32:96] = cnt1(64)
    return out.reshape(1, 384)


def prep_inputs(x, c, ctx, c_ctx, w_mod, b_mod, g_pre_mix, g_post_mix, g_pre_ffn, g_post_ffn,
                w_in, pool_w, pool_scale, lru_conv_w, lru_conv_b, lru_wa, lru_ba, lru_wx, lru_bx,
                lru_lambda, w_proj_pool, w_proj_lru, w_out, w_up, ffn_conv_w, ffn_conv_b, w_down):
    f = lambda a: np.asarray(a, np.float32)
    vec_parts = {
        "g_pre_mix": _fm(f(g_pre_mix)[0], 8), "g_post_mix": _fm(f(g_post_mix)[0], 8),
        "g_pre_ffn": _fm(f(g_pre_ffn)[0], 8), "g_post_ffn": _fm(f(g_post_ffn)[0], 8),
        "pool_scale": _fm(f(pool_scale)[0], 8), "lru_conv_w": _fm(f(lru_conv_w)[0], 8),
        "lru_conv_b": _fm(f(lru_conv_b)[0], 8), "lru_ba": _fm(f(lru_ba)[0], 8), "lru_bx": _fm(f(lru_bx)[0], 8),
        "lru_lambda": _fm(f(lru_lambda)[0], 8), "ffn_conv_w": _fm(f(ffn_conv_w)[0].reshape(9, 3072), 24),
        "ffn_conv_b": _fm(f(ffn_conv_b)[0], 24), "b_mod": _fm(f(b_mod)[0], 48),
    }
    vecs = np.ascontiguousarray(np.concatenate([vec_parts[n] for n, _ in _VEC_SPEC], axis=1))
    assert vecs.shape == (128, NV)
    wmod_h = np.ascontiguousarray(f(w_mod)[0].reshape(8, 128, 12, 512).transpose(2, 1, 0, 3).reshape(12, 128, 4096))
    win_h = _wtile(f(w_in)[0], 8, 40)
    wpp_h = _wtile(f(w_proj_pool)[0], 8, 8)
    wpl_h = _wtile(f(w_proj_lru)[0], 8, 8)
    wout_h = _wtile(f(w_out)[0], 8, 8)
    wup_t = _wtile(f(w_up)[0], 8, 48)
    wup_h = np.ascontiguousarray(np.concatenate([wup_t[0:24], wup_t[24:48]], axis=2))
    wdown_h = _wtile(f(w_down)[0], 24, 8)
    pw = f(pool_w)[0].reshape(4, 2, 128, 2, 128).transpose(2, 0, 3, 1, 4)
    poolw_h = np.ascontiguousarray(pw.reshape(128, 2048))
    lw = np.stack([f(lru_wa)[0], f(lru_wx)[0]], axis=0)
    lruw_h = np.ascontiguousarray(lw.transpose(3, 0, 1, 2, 4).reshape(128, 4096))
    shared = {"vecs": vecs, "ident": np.eye(128, dtype=np.float32), "cnt": _window_counts(), "wmod": wmod_h,
              "win": win_h, "wpp": wpp_h, "wpl": wpl_h, "wout": wout_h, "wup": wup_h, "wdown": wdown_h,
              "poolw": poolw_h, "lruw": lruw_h}
    xf, cf, ctxf, ccf = f(x), f(c), f(ctx), f(c_ctx)
    in_maps = []
    for b in range(NCORES):
        m = dict(shared)
        m["xT"] = np.ascontiguousarray(xf[b].T)
        m["ctxT"] = np.ascontiguousarray(ctxf[b].T)
        cc2 = np.stack([cf[b], ccf], axis=0)
        m["cc"] = np.ascontiguousarray(cc2.reshape(2, 8, 128).transpose(2, 1, 0).reshape(128, 16))
        in_maps.append(m)
    return in_maps


_CACHE = {}


def kernel(**inputs):
    in_maps = prep_inputs(**inputs)
    if "nc" not in _CACHE:
        _CACHE["nc"] = build_program()[0]
    nc = _CACHE["nc"]
    res = run_bass_kernel_spmd(nc, in_maps, core_ids=list(range(NCORES)))
    out = np.stack([np.asarray(r["outT"], np.float32).T for r in res.results], axis=0)
    return np.ascontiguousarray(out.astype(np.float32))
```

```python
import numpy as np
from contextlib import ExitStack
import concourse.bass as bass
import concourse.mybir as mybir
from concourse.bass_utils import run_bass_kernel_spmd

F32 = mybir.dt.float32
BF16 = mybir.dt.bfloat16
RELAX_DVE = False
AF = mybir.ActivationFunctionType
ALU = mybir.AluOpType

NCORES = 8
D = 1024
T = 2048
CT = 256
TT = T + CT
NCH = 8
EPS = 1e-6
POOL_WINDOWS = (2, 4, 8, 16)
SEGS = [(0, 256)] + [(256 + 512 * i, 512) for i in range(4)]

_VEC_SPEC = [("g_pre_mix", 8), ("g_post_mix", 8), ("g_pre_ffn", 8), ("g_post_ffn", 8), ("pool_scale", 8),
             ("lru_conv_w", 32), ("lru_conv_b", 8), ("lru_ba", 16), ("lru_bx", 16), ("lru_lambda", 16),
             ("ffn_conv_w", 216), ("ffn_conv_b", 24), ("b_mod", 48)]
VOFF = {}
_o = 0
for _n, _c in _VEC_SPEC:
    VOFF[_n] = _o
    _o += _c
NV = _o


class _Op:
    __slots__ = ("eng", "fn", "deps", "needs_inc", "sem", "val", "is_dma", "slot", "pos")

    def __init__(self, eng, fn, is_dma=False, slot=None):
        self.eng = eng
        self.fn = fn
        self.deps = []
        self.needs_inc = False
        self.sem = None
        self.val = None
        self.is_dma = is_dma
        self.slot = slot
        self.pos = -1


class Prog:
    ENGS = ("pe", "act", "dve", "pool", "sp")

    def __init__(self, nc):
        self.nc = nc
        self.q = {e: [] for e in self.ENGS}
        self.last_w = {}
        self.readers = {}
        self.all_ops = []

    def _track(self, op, reads, writes):
        deps = {}
        for t in reads:
            w = self.last_w.get(t)
            if w is not None:
                deps[id(w)] = w
        for t in writes:
            w = self.last_w.get(t)
            if w is not None:
                deps[id(w)] = w
            for r in self.readers.get(t, {}).values():
                deps[id(r)] = r
        for d in deps.values():
            if d is op:
                continue
            if (not d.is_dma) and (not op.is_dma) and d.eng == "pe" and op.eng == "pe":
                continue
            if RELAX_DVE and (not d.is_dma) and (not op.is_dma) and d.eng == "dve" and op.eng == "dve" and op.pos - d.pos >= 2:
                continue
            d.needs_inc = True
            op.deps.append(d)
        for t in writes:
            self.last_w[t] = op
            self.readers[t] = {}
        for t in reads:
            key = ("dma", op.slot) if op.is_dma else op.eng
            self.readers.setdefault(t, {})[key] = op

    def op(self, eng, fn, reads=(), writes=()):
        o = _Op(eng, fn)
        o.pos = len(self.q[eng])
        self._track(o, reads, writes)
        self.q[eng].append(o)
        self.all_ops.append(o)
        return o

    def dma(self, eng, out, in_, reads=(), writes=(), slot=None, **kw):
        o = _Op(eng, None, is_dma=True, slot=slot)
        o.fn = lambda e, s, o_=out, i_=in_, kw_=kw: e.dma_start(out=o_, in_=i_, **kw_).then_inc(s, 16)
        self._track(o, reads, writes)
        o.needs_inc = True
        self.q[eng].append(o)
        self.all_ops.append(o)
        return o

    def emit(self, final_wait_eng="sp"):
        nc = self.nc
        with ExitStack() as es:
            esem = {e: es.enter_context(nc.semaphore("sem_" + e)) for e in self.ENGS}
            slot_names = sorted({o.slot for o in self.all_ops if o.is_dma})
            ssem = {s: es.enter_context(nc.semaphore("dsem_" + s)) for s in slot_names}
            cnt = {e: 0 for e in self.ENGS}
            for e in self.ENGS:
                for o in self.q[e]:
                    if (not o.is_dma) and o.needs_inc:
                        cnt[e] += 1
                        o.sem, o.val = esem[e], cnt[e]
            scnt = {s: 0 for s in slot_names}
            for o in self.all_ops:
                if o.is_dma:
                    scnt[o.slot] += 16
                    o.sem, o.val = ssem[o.slot], scnt[o.slot]
            block = es.enter_context(nc.Block())
            final = [(ssem[s], scnt[s]) for s in slot_names]

            def run(e):
                def body(eng):
                    waited = {}
                    for o in self.q[e]:
                        for d in o.deps:
                            k = id(d.sem)
                            if waited.get(k, 0) < d.val:
                                eng.wait_ge(d.sem, d.val)
                                waited[k] = d.val
                        if o.is_dma:
                            o.fn(eng, o.sem)
                        else:
                            ins = o.fn(eng)
                            if o.needs_inc:
                                ins.then_inc(o.sem, 1)
                    if e == final_wait_eng:
                        for s, v in final:
                            if v > 0:
                                eng.wait_ge(s, v)
                return body

            block.tensor(run("pe"))
            block.scalar(run("act"))
            block.vector(run("dve"))
            block.gpsimd(run("pool"))
            block.sync(run("sp"))


def build_program(stop=None, dbg=()):
    nc = bass.Bass("TRN2", target_bir_lowering=False)
    dr = {}

    def din(name, shape, dt=F32):
        dr[name] = nc.dram_tensor(name, shape, dt, kind="ExternalInput").ap()

    din("xT", [D, T])
    din("ctxT", [D, CT])
    din("cc", [128, 16])
    din("vecs", [128, NV])
    din("ident", [128, 128])
    din("cnt", [1, 384])
    din("wmod", [12, 128, 4096])
    din("win", [40, 128, 1024])
    din("wpp", [8, 128, 1024])
    din("wpl", [8, 128, 1024])
    din("wout", [8, 128, 1024])
    din("wup", [24, 128, 2048])
    din("wdown", [8, 128, 3072])
    din("poolw", [128, 2048])
    din("lruw", [128, 4096])
    outT = nc.dram_tensor("outT", [D, T], F32, kind="ExternalOutput").ap()
    wupS = nc.dram_tensor("wupS", [24, 128, 2048], BF16, kind="Internal").ap()
    wdnS = nc.dram_tensor("wdnS", [8, 128, 3072], BF16, kind="Internal").ap()
    woutS = nc.dram_tensor("woutS", [8, 128, 1024], BF16, kind="Internal").ap()
    dbg_out = {}

    es = ExitStack()
    with es:
        AW = 53200
        arena = es.enter_context(nc.sbuf_tensor("arena", [128, AW], F32))
        psall = es.enter_context(nc.psum_tensor("psall", [128, 4096], F32))
        ps = [psall[:, i * 512:(i + 1) * 512] for i in range(8)]
        P = Prog(nc)

        def fv(off, n):
            assert off % 4 == 0 and off + 4 * n <= AW * 4, (off, n)
            return arena[:, off // 4: off // 4 + n]

        def bv(off, n):
            assert off % 4 == 0 and n % 2 == 0 and off + 2 * n <= AW * 4, (off, n)
            return arena[:, off // 4: off // 4 + n // 2].bitcast(BF16)

        class Carve:
            def __init__(self, base, size):
                self.base, self.end, self.cur = base, base + size, base

            def f(self, n):
                a = fv(self.cur, n)
                self.cur += 4 * n
                self.cur = (self.cur + 63) // 64 * 64
                assert self.cur <= self.end, ("carve overflow", self.cur, self.end)
                return a

            def b(self, n):
                a = bv(self.cur, n)
                self.cur += 2 * n
                self.cur = (self.cur + 63) // 64 * 64
                assert self.cur <= self.end, ("carve overflow", self.cur, self.end)
                return a

        KB = 1024
        R_H = (0, 36 * KB)
        R_Y = (36 * KB, 64 * KB)
        R_S = (100 * KB, 84 * KB)
        R_W = (184 * KB, AW * 4 - 184 * KB)

        bank_ctr = [0]

        def nextbank():
            b = bank_ctr[0] % 8
            bank_ctr[0] += 1
            return b

        def mm(out, lhsT, rhs, start, stop, reads, writes):
            P.op("pe", lambda e: e.matmul(out, lhsT, rhs, start=start, stop=stop), reads, writes)

        def act(out, in_, func, reads, writes, bias=None, scale=None):
            kw = {}
            if bias is not None:
                kw["bias"] = bias
            if scale is not None:
                kw["scale"] = scale
            P.op("act", lambda e: e.activation(out=out, in_=in_, func=func, **kw), reads, writes)

        def evac_latent(banks, dst, o0, func, reads_extra, toks, **kw):
            i = 0
            while i < len(banks):
                j = i
                while j + 1 < len(banks) and banks[j + 1] == banks[j] + 1:
                    j += 1
                r = j - i + 1
                act(dst[:, o0 + 512 * i:o0 + 512 * (i + r)], psall[:, banks[i] * 512:(banks[i] + r) * 512], func,
                    ["ps%d" % b_ for b_ in banks[i:j + 1]] + list(reads_extra), list(toks[i:j + 1]), **kw)
                i = j + 1

        def tt(eng, out, in0, in1, op, reads, writes):
            P.op(eng, lambda e: e.tensor_tensor(out=out, in0=in0, in1=in1, op=op), reads, writes)

        def ts(eng, out, in0, s1, op0, reads, writes, s2=None, op1=None):
            if op1 is None:
                P.op(eng, lambda e: e.tensor_scalar(out=out, in0=in0, scalar1=s1, scalar2=None, op0=op0), reads, writes)
            else:
                P.op(eng, lambda e: e.tensor_scalar(out=out, in0=in0, scalar1=s1, scalar2=s2, op0=op0, op1=op1), reads, writes)

        def stt(out, in0, scalar, in1, op0, op1, reads, writes):
            P.op("dve", lambda e: e.scalar_tensor_tensor(out=out, in0=in0, scalar=scalar, in1=in1, op0=op0, op1=op1), reads, writes)

        def memset(ap, val, writes):
            P.op("pool", lambda e: e.memset(ap, val), (), writes)

        def dump(name, ap, dt=F32):
            shape = [ap.shape[0], int(np.prod(ap.shape[1:]))]
            t = nc.dram_tensor("dbg_" + name, shape, dt, kind="ExternalOutput").ap()
            dbg_out[name] = t
            return t

        cw = Carve(*R_W)
        vecs = cw.f(NV)
        ident = cw.f(128)
        ones = cw.f(128)
        identb = cw.b(128)
        ccs = cw.f(16)
        ssil = cw.f(16)
        modfm = cw.f(96)
        der = cw.f(64)
        lrud = cw.f(64)
        cw2 = cw.f(32)
        cn1 = cw.f(384)
        W_DYN = cw.cur

        def V(name, i=0):
            o = VOFF[name] + i
            return vecs[:, o:o + 1]

        def Vs(name, i0, n):
            o = VOFF[name] + i0
            return vecs[:, o:o + n]

        P.dma("sp", vecs, dr["vecs"][:, :], writes=["vecs"], slot="c_vecs")
        P.dma("sp", ident, dr["ident"][:, :], writes=["ident"], slot="c_ident")
        P.dma("sp", ccs, dr["cc"][:, :], writes=["cc"], slot="c_cc")
        memset(ones, 1.0, ["ones"])
        P.op("dve", lambda e: e.tensor_copy(out=identb, in_=ident), ["ident"], ["identb"])
        act(ssil, ccs, AF.Silu, ["cc"], ["ssil"])
        s3 = ssil.rearrange("p (k r) -> p k r", r=2)

        cs = Carve(*R_S)
        wmb = [cs.f(4096), cs.f(4096)]
        modrow = cs.f(6144)
        xa4 = cs.f(8 * 512)
        rstdA = cs.f(TT)
        cy = Carve(*R_Y)
        xa = [cy.f(8 * 256), cy.f(8 * 512), cy.f(8 * 512), cy.f(8 * 512), xa4]
        sqb = [cy.f(512), cy.f(512)]
        ssumA = cy.f(512)
        sdb = cy.f(512)
        xT3 = dr["xT"].rearrange("(c p) t -> p c t", p=128)
        cT3 = dr["ctxT"].rearrange("(c p) t -> p c t", p=128)
        xa3 = [xa[si].rearrange("p (c t) -> p c t", t=SEGS[si][1]) for si in range(5)]

        def load_x(si):
            o, n = SEGS[si]
            src = cT3[:, :, :] if si == 0 else xT3[:, :, o - 256:o - 256 + n]
            P.dma("sp", xa3[si], src, writes=["xa%d" % si], slot="xa%d" % si)

        def stats(si):
            o, n = SEGS[si]
            tk = "xa%d" % si
            for c in range(8):
                if c == 0:
                    act(ssumA[:, 0:n], xa3[si][:, c, :], AF.Square, [tk], ["ssumA"])
                else:
                    sq = sqb[c % 2]
                    act(sq[:, 0:n], xa3[si][:, c, :], AF.Square, [tk], ["sq%d" % (c % 2)])
                    tt("dve", ssumA[:, 0:n], ssumA[:, 0:n], sq[:, 0:n], ALU.add, ["ssumA", "sq%d" % (c % 2)], ["ssumA"])
            b = nextbank()
            mm(ps[b][:, 0:n], ones, ssumA[:, 0:n], True, True, ["ones", "ssumA"], ["ps%d" % b])
            act(sdb[:, 0:n], ps[b][:, 0:n], AF.Ln, ["ps%d" % b], ["sd"], scale=1.0 / D, bias=EPS)
            act(rstdA[:, o:o + n], sdb[:, 0:n], AF.Exp, ["sd"], ["rstdA%d" % si], scale=-0.5)
            for c in range(8):
                tt("dve", xa3[si][:, c, :], xa3[si][:, c, :], rstdA[:, o:o + n], ALU.mult, [tk, "rstdA%d" % si], ["xn%d_%d" % (si, c)])

        xsched = {1: 0, 2: 1, 4: 2, 6: 3, 8: 4}
        ssched = {3: 0, 5: 1, 7: 2, 9: 3, 11: 4}
        for blk in range(12):
            buf = wmb[blk % 2]
            tk = "wm%d" % (blk % 2)
            P.dma("sp", buf, dr["wmod"][blk], writes=[tk], slot=tk)
            if blk in xsched:
                load_x(xsched[blk])
            b = nextbank()
            for k in range(8):
                mm(ps[b][0:2, 0:512], s3[:, k, :], buf[:, k * 512:(k + 1) * 512], k == 0, k == 7,
                   ["ssil", tk], ["ps%d" % b])
            act(modrow[0:2, blk * 512:(blk + 1) * 512], ps[b][0:2, 0:512], AF.Identity, ["ps%d" % b], ["modrow%d" % blk])
            if blk in ssched:
                stats(ssched[blk])
        b = nextbank()
        for oc in range(48):
            mm(ps[b][:, 2 * oc:2 * oc + 2], modrow[0:2, oc * 128:(oc + 1) * 128], ident[0:2, 0:2], True, True,
               ["modrow%d" % (oc // 4), "ident"], ["ps%d" % b])
        act(modfm, ps[b][:, 0:96], AF.Identity, ["ps%d" % b], ["modfm"])
        mod3 = modfm.rearrange("p (c r) -> p c r", r=2)
        for r in range(2):
            tt("dve", mod3[:, :, r], mod3[:, :, r], Vs("b_mod", 0, 48), ALU.add, ["modfm", "vecs"], ["modfm"])

        def MOD(which, c, r=0):
            return mod3[:, which * 8 + c, r:r + 1]

        A1 = der[:, 0:8]
        A1c = der[:, 8:16]
        GG1 = der[:, 16:24]
        A2 = der[:, 24:32]
        GG2 = der[:, 32:40]
        stt(A1, mod3[:, 8:16, 0], 1.0, Vs("g_pre_mix", 0, 8), ALU.add, ALU.mult, ["modfm", "vecs"], ["der"])
        stt(A1c, mod3[:, 8:16, 1], 1.0, Vs("g_pre_mix", 0, 8), ALU.add, ALU.mult, ["modfm", "vecs"], ["der"])
        tt("dve", GG1, mod3[:, 16:24, 0], Vs("g_post_mix", 0, 8), ALU.mult, ["modfm", "vecs"], ["der"])
        stt(A2, mod3[:, 32:40, 0], 1.0, Vs("g_pre_ffn", 0, 8), ALU.add, ALU.mult, ["modfm", "vecs"], ["der"])
        tt("dve", GG2, mod3[:, 40:48, 0], Vs("g_post_ffn", 0, 8), ALU.mult, ["modfm", "vecs"], ["der"])
        lam = Vs("lru_lambda", 0, 16)
        le = lrud[:, 0:16]
        lsp = lrud[:, 16:32]
        ls1 = lrud[:, 32:48]
        act(le, lam, AF.Exp, ["vecs"], ["le"], scale=-1.0)
        act(lsp, le, AF.Ln, ["le"], ["lsp"], bias=1.0)
        ts("dve", ls1, lsp, -4.0, ALU.mult, ["lsp"], ["hs1"])
        hs1 = ls1
        hbias = cw2
        ts("dve", hbias, Vs("lru_ba", 0, 32), 0.5, ALU.mult, ["vecs"], ["hbias"])

        ch = Carve(*R_H)
        h = ch.b(8 * TT)
        h3 = h.rearrange("p (c t) -> p c t", t=TT)
        for si, (o, n) in enumerate(SEGS):
            for c in range(8):
                if si == 0:
                    sc_, bi_ = A1c[:, c:c + 1], MOD(0, c, 1)
                else:
                    sc_, bi_ = A1[:, c:c + 1], MOD(0, c, 0)
                act(h3[:, c, o:o + n], xa3[si][:, c, :], AF.Identity, ["xn%d_%d" % (si, c), "der", "modfm"], ["h%d_%d" % (c, si)],
                    bias=bi_, scale=sc_)
        HSEG = lambda si: ["h%d_%d" % (c, si) for c in range(8)]

        if "h" in dbg:
            P.dma("sp", dump("h", h, BF16)[:, :], h, reads=[t for si in range(5) for t in HSEG(si)], slot="dbg_h")
            P.dma("sp", dump("modfm", modfm)[:, :], modfm, reads=["modfm"], slot="dbg_m")
        if stop == "A":
            P.emit()
            return nc, dbg_out

        cy = Carve(*R_Y)
        ypool = cy.b(8 * T)
        ylru = cy.b(8 * T)
        ypool3 = ypool.rearrange("p (c t) -> p c t", t=T)
        ylru3 = ylru.rearrange("p (c t) -> p c t", t=T)
        YTOK = ["xa%d" % si for si in range(4)] + ["xn%d_%d" % (si, c) for si in range(4) for c in range(8)] + ["sq0", "sq1", "sd", "ssumA"]
        y_first = [True]

        cwd = Carve(W_DYN, R_W[0] + R_W[1] - W_DYN)
        lruw = cwd.b(4096)
        poolw = cwd.b(2048)
        NWIN = 3
        winb = [cwd.b(1024) for _ in range(NWIN)]
        win_ctr = [0]
        P.dma("pool", lruw[:, 0:2048], dr["lruw"][:, 0:2048], writes=["lruw"], slot="w_lruw")
        P.dma("pool", lruw[:, 2048:4096], dr["lruw"][:, 2048:4096], writes=["lruw"], slot="w_lruw")
        P.dma("pool", poolw, dr["poolw"][:, :], writes=["poolw"], slot="w_poolw")

        def load_win(oc):
            i = win_ctr[0] % NWIN
            win_ctr[0] += 1
            P.dma("pool", winb[i], dr["win"][oc], writes=["win%d" % i], slot="win%d" % i)
            return winb[i], "win%d" % i

        def proj_seg(wt, wtk, si, b):
            o, n = SEGS[si]
            for k in range(8):
                mm(ps[b][:, 0:n], wt[:, k * 128:(k + 1) * 128], h3[:, k, o:o + n], k == 0, k == 7,
                   [wtk] + HSEG(si), ["ps%d" % b])

        cast_plan = []
        for oc in range(8):
            cast_plan.append((woutS[oc], dr["wout"][oc]))
        for j in range(24):
            cast_plan.append((wupS[j], dr["wup"][j]))
        for oc in range(8):
            cast_plan.append((wdnS[oc][:, 0:2048], dr["wdown"][oc][:, 0:2048]))
            cast_plan.append((wdnS[oc][:, 2048:3072], dr["wdown"][oc][:, 2048:3072]))
        cast_ctr = [0]

        def issue_casts(k):
            for _ in range(k):
                i = cast_ctr[0]
                if i >= len(cast_plan):
                    return
                cast_ctr[0] += 1
                dst, src = cast_plan[i]
                wr = ["wc%d" % i] + (["wc_all"] if i == len(cast_plan) - 1 else [])
                P.dma("pool", dst, src, writes=wr, slot="wcast")

        STOK_A = (["wm0", "wm1"] + ["modrow%d" % i for i in range(12)] + ["xa4"] + ["xn4_%d" % c for c in range(8)]
                  + ["rstdA%d" % si for si in range(5)])
        cs = Carve(*R_S)
        cntb = cs.f(T)
        PQ = [[cs.f(47 * 79), cs.f(47 * 79)] for _ in range(2)]
        dbf = [cs.b(T), cs.b(T)]
        PQTOK = ["pq%d_%d" % (cl, i) for cl in range(2) for i in range(2)]
        cnt3 = cntb.rearrange("p (r c) -> p r c", c=64)
        first_S = [True]
        nb_ctr = [0]

        def nextB():
            b = 4 + nb_ctr[0] % 4
            nb_ctr[0] += 1
            return b

        def proj_cl0(g_):
            wt_, wtk_ = load_win(2 * g_)
            for t in range(4):
                proj_seg(wt_, wtk_, 1 + t, t)
            return (wt_, wtk_)

        pre_cl0 = proj_cl0(0)
        P.dma("sp", cn1, dr["cnt"][0:1, :].partition_broadcast(128), writes=["cn1"], slot="c_cn1")
        P.op("dve", lambda e: e.reciprocal(out=cn1, in_=cn1), ["cn1"], ["cn1"])
        for g, w in enumerate(POOL_WINDOWS):
            hw = w // 2
            Hp, Wp = 32 + w - 1, 64 + w - 1
            extra = STOK_A if first_S[0] else []
            first_S[0] = False
            ir = cn1[:, g * 96: g * 96 + 32].unsqueeze(2).broadcast_to([128, 32, 64])
            ic = cn1[:, g * 96 + 32: g * 96 + 96].unsqueeze(1).broadcast_to([128, 32, 64])
            tt("dve", cnt3, ir, ic, ALU.mult, ["cn1"], ["cnt"] + extra)
            views = []
            for cl in range(2):
                v = [PQ[cl][i][:, 0:Hp * Wp].rearrange("p (r c) -> p r c", c=Wp) for i in range(2)]
                views.append(v)
                Pv, Qv = v
                memset(Pv[:, 0:hw, :], 0.0, ["pq%d_0" % cl] + extra)
                if hw > 1:
                    memset(Pv[:, hw + 32:Hp, :], 0.0, ["pq%d_0" % cl])
                memset(Pv[:, hw:hw + 32, 0:hw], 0.0, ["pq%d_0" % cl])
                if hw > 1:
                    memset(Pv[:, hw:hw + 32, hw + 64:Wp], 0.0, ["pq%d_0" % cl])
                if w in (2, 8):
                    memset(Qv[:, 0:hw, 0:64], 0.0, ["pq%d_1" % cl])
                    if hw > 1:
                        memset(Qv[:, hw + 32:Hp, 0:64], 0.0, ["pq%d_1" % cl])
            wts = [pre_cl0]
            for t in range(4):
                act(views[0][0][:, hw + 8 * t: hw + 8 * t + 8, hw:hw + 64], ps[t][:, :].rearrange("p (r c) -> p r c", c=64),
                    AF.Identity, ["ps%d" % t], ["pq0_0"])
            wt, wtk = load_win(2 * g + 1)
            wts.append((wt, wtk))
            for t in range(4):
                b = nextB()
                proj_seg(wt, wtk, 1 + t, b)
                act(views[1][0][:, hw + 8 * t: hw + 8 * t + 8, hw:hw + 64], ps[b][:, :].rearrange("p (r c) -> p r c", c=64),
                    AF.Identity, ["ps%d" % b], ["pq1_0"])
            if g + 1 < 4:
                pre_cl0 = proj_cl0(g + 1)
            state = [dict(cur=0, ln=Wp, rows=Hp) for _ in range(2)]
            k = 1
            while k < w:
                for cl in range(2):
                    st = state[cl]
                    src, dst = views[cl][st["cur"]], views[cl][1 - st["cur"]]
                    nl = st["ln"] - k
                    tt("dve", dst[:, hw:hw + 32, 0:nl], src[:, hw:hw + 32, 0:nl], src[:, hw:hw + 32, k:k + nl], ALU.add,
                       ["pq%d_%d" % (cl, st["cur"])], ["pq%d_%d" % (cl, 1 - st["cur"])])
                    st["cur"], st["ln"] = 1 - st["cur"], nl
                k *= 2
            k = 1
            while k < w:
                for cl in range(2):
                    st = state[cl]
                    src, dst = views[cl][st["cur"]], views[cl][1 - st["cur"]]
                    nr = st["rows"] - k
                    tt("dve", dst[:, 0:nr, 0:64], src[:, 0:nr, 0:64], src[:, k:k + nr, 0:64], ALU.add,
                       ["pq%d_%d" % (cl, st["cur"])], ["pq%d_%d" % (cl, 1 - st["cur"])])
                    st["cur"], st["rows"] = 1 - st["cur"], nr
                k *= 2
            for cl in range(2):
                st = state[cl]
                assert st["ln"] == 64 and st["rows"] == 32
                src, oth = views[cl][st["cur"]], views[cl][1 - st["cur"]]
                tt("dve", oth[:, 0:32, 0:64], src[:, 0:32, 0:64], cnt3, ALU.mult, ["pq%d_%d" % (cl, st["cur"]), "cnt"],
                   ["pq%d_%d" % (cl, 1 - st["cur"])])
                wt, wtk = wts[cl]
                for t in range(4):
                    b = nextB()
                    proj_seg(wt, wtk, 1 + t, b)
                    tt("dve", dbf[cl][:, 512 * t:512 * t + 512].rearrange("p (r c) -> p r c", c=64), oth[:, 8 * t:8 * t + 8, 0:64],
                       ps[b][:, :].rearrange("p (r c) -> p r c", c=64), ALU.subtract, ["pq%d_%d" % (cl, 1 - st["cur"]), "ps%d" % b],
                       ["dbf%d_%d" % (cl, t)])
            for ocl in range(2):
                oc = 2 * g + ocl
                for t in range(4):
                    b = nextB()
                    for k in range(2):
                        idx = ((g * 2 + ocl) * 2 + k) * 128
                        mm(ps[b][:, :], poolw[:, idx:idx + 128], dbf[k][:, 512 * t:512 * t + 512], k == 0, k == 1,
                           ["poolw", "dbf%d_%d" % (k, t)], ["ps%d" % b])
                    extra = YTOK if y_first[0] else []
                    y_first[0] = False
                    act(ypool3[:, oc, 512 * t:512 * t + 512], ps[b][:, :], AF.Identity, ["ps%d" % b, "vecs"],
                        ["ypool%d_%d" % (oc, t)] + extra, scale=V("pool_scale", oc))
            issue_casts(5)
        if "ypool" in dbg:
            P.dma("sp", dump("ypool", ypool, BF16)[:, :], ypool,
                  reads=["ypool%d_%d" % (oc, t) for oc in range(8) for t in range(4)], slot="dbg_yp")
        if stop == "Bp":
            P.emit()
            return nc, dbg_out

        cs = Carve(*R_S)
        UPW = 2320
        LOFF = 264
        upad = cs.b(UPW)
        dgw = cs.b(32 * 128)
        o_m2b = cs.cur
        m2b = cs.f(TT)
        xcb = bv(o_m2b, TT)
        xc = cs.f(TT)
        m2f = cs.f(TT)
        ra = [cs.f(TT), cs.f(TT)]
        ib = [cs.f(TT), cs.f(TT)]
        gel = cs.f(T)
        m2 = [m2f, m2b]
        m2tok = ["m2f", "m2b"]
        POOLTOK = ["cnt"] + PQTOK + ["dbf%d_%d" % (cl, t) for cl in range(2) for t in range(4)]
        memset(upad, 0.0, ["upad"] + POOLTOK)
        for k in range(4):
            for n in range(8):
                i = k * 8 + n
                ts("dve", dgw[:, i * 128:(i + 1) * 128], ident, V("lru_conv_w", i), ALU.mult, ["ident", "vecs"], ["dgw"] + (POOLTOK if i == 0 else []))
        XC = ["xc%d" % si for si in range(5)]
        XCB = ["xcb%d" % si for si in range(5)]
        RA = lambda d: ["ra%d_%d" % (d, si) for si in range(5)]
        IB = lambda d: ["ib%d_%d" % (d, si) for si in range(5)]
        win_next = load_win(8 + 0)
        for n in range(8):
            wt, wtk = win_next
            for si, (o, nn) in enumerate(SEGS):
                b = nextbank()
                proj_seg(wt, wtk, si, b)
                po = 2 + o if si == 0 else LOFF + (o - 256)
                act(upad[:, po:po + nn], ps[b][:, 0:nn], AF.Identity, ["ps%d" % b], ["upad"])
            for si, (o, nn) in enumerate(SEGS):
                base = o if si == 0 else LOFF - 2 + (o - 256)
                b = nextbank()
                for k in range(4):
                    i = k * 8 + n
                    mm(ps[b][:, 0:nn], dgw[:, i * 128:(i + 1) * 128], upad[:, base + k:base + k + nn], k == 0, k == 3,
                       ["dgw", "upad"], ["ps%d" % b])
                act(xcb[:, o:o + nn], ps[b][:, 0:nn], AF.Identity, ["ps%d" % b, "vecs"], ["xcb%d" % si] + (["m2b"] if si == 0 else []),
                    bias=V("lru_conv_b", n))
                act(xc[:, o:o + nn], ps[b][:, 0:nn], AF.Identity, ["ps%d" % b, "vecs"], ["xc%d" % si], bias=V("lru_conv_b", n))
            for dr_ in range(2):
                for kind, dst, tkf in ((0, ra[dr_], RA(dr_)), (1, ib[dr_], IB(dr_))):
                    widx = ((kind * 2 + dr_) * 8 + n) * 128
                    hb_ = hbias[:, kind * 16 + dr_ * 8 + n: kind * 16 + dr_ * 8 + n + 1]
                    b = nextbank()
                    mm(ps[b][:, 0:256], lruw[:, widx:widx + 128], xcb[:, 0:256], True, True, ["lruw", "xcb0"], ["ps%d" % b])
                    act(dst[:, 0:256], ps[b][:, 0:256], AF.Tanh, ["ps%d" % b, "hbias"], [tkf[0]], bias=hb_, scale=0.5)
                    gb = []
                    for si in range(1, 5):
                        o, nn = SEGS[si]
                        b = nextbank()
                        gb.append(b)
                        mm(ps[b][:, 0:nn], lruw[:, widx:widx + 128], xcb[:, o:o + nn], True, True, ["lruw", "xcb%d" % si], ["ps%d" % b])
                    evac_latent(gb, dst, 256, AF.Tanh, ["hbias"], tkf[1:5], bias=hb_, scale=0.5)
                hs = hs1[:, dr_ * 8 + n: dr_ * 8 + n + 1]
                act(ra[dr_], ra[dr_], AF.Exp, RA(dr_) + ["hs1"], RA(dr_), bias=hs, scale=hs)
                act(m2[dr_], ra[dr_], AF.Square, RA(dr_), [m2tok[dr_]] + (XCB if dr_ == 1 else []))
                act(m2[dr_], m2[dr_], AF.Sqrt, [m2tok[dr_]], [m2tok[dr_]], scale=-0.25, bias=0.25)
            for dr_ in range(2):
                stt(ib[dr_], ib[dr_], 1.0, xc, ALU.add, ALU.mult, IB(dr_) + XC, IB(dr_))
                tt("dve", ib[dr_], ib[dr_], m2[dr_], ALU.mult, IB(dr_) + [m2tok[dr_]], IB(dr_))
                if dr_ == 0:
                    P.op("dve", lambda e: e.tensor_tensor_scan(out=ib[0], data0=ra[0], data1=ib[0], initial=0.0, op0=ALU.mult, op1=ALU.add),
                         RA(0) + IB(0), IB(0))
                else:
                    P.op("dve", lambda e: e.tensor_tensor_scan(out=ib[1][:, 0:256][:, ::-1], data0=ra[1][:, 0:256][:, ::-1],
                                                                data1=ib[1][:, 0:256][:, ::-1], initial=0.0, op0=ALU.mult, op1=ALU.add),
                         RA(1) + IB(1), IB(1))
                    P.op("dve", lambda e: e.tensor_tensor_scan(out=ib[1][:, 256:TT][:, ::-1], data0=ra[1][:, 256:TT][:, ::-1],
                                                                data1=ib[1][:, 256:TT][:, ::-1], initial=ib[1][:, 0:1], op0=ALU.mult, op1=ALU.add),
                         RA(1) + IB(1), IB(1))
            wt, wtk = load_win(16 + n)
            if n + 1 < 8:
                win_next = load_win(8 + n + 1)
            issue_casts(5)
            gb = []
            for t in range(4):
                b = nextbank()
                gb.append(b)
                proj_seg(wt, wtk, 1 + t, b)
            evac_latent(gb, gel, 0, AF.Gelu_apprx_tanh, [], ["gel%d" % t for t in range(4)])
            GEL = ["gel%d" % t for t in range(4)]
            tt("dve", ib[0][:, 256:TT], ib[0][:, 256:TT], ib[1][:, 256:TT], ALU.add, IB(0) + IB(1), IB(0))
            tt("dve", ylru3[:, n, :], ib[0][:, 256:TT], gel, ALU.mult, IB(0) + GEL, ["ylru%d" % n] + (YTOK if n == 0 else []))
        if "ylru" in dbg:
            P.dma("sp", dump("ylru", ylru, BF16)[:, :], ylru, reads=["ylru%d" % n for n in range(8)], slot="dbg_yl")
        if stop == "B":
            P.emit()
            return nc, dbg_out

        LRUTOK = ["upad", "m2f", "m2b", "dgw"] + XC + XCB + RA(0) + RA(1) + IB(0) + IB(1) + GEL
        cs = Carve(*R_S)
        mbuf = cs.b(8 * T)
        m3 = mbuf.rearrange("p (c t) -> p c t", t=T)
        S_C2 = cs.cur
        sgb = [[cs.f(512), cs.f(512)] for _ in range(2)]
        t12 = [[cs.f(512), cs.f(512)] for _ in range(2)]
        cwd = Carve(W_DYN, R_W[0] + R_W[1] - W_DYN)
        c1w = [[cwd.b(1024) for _ in range(4)] for _ in range(2)]
        W_OLD = ["lruw", "poolw"] + ["win%d" % i for i in range(NWIN)]
        first_c1 = [True]
        YP = lambda t: ["ypool%d_%d" % (k, t) for k in range(8)]
        YL = ["ylru%d" % k for k in range(8)]
        it = 0
        for oc in range(8):
            sl = oc % 2
            srcs = [dr["wpp"][oc], dr["win"][24 + oc], dr["wpl"][oc], dr["win"][32 + oc]]
            for i in range(4):
                extra = W_OLD if first_c1[0] else []
                first_c1[0] = False
                P.dma("pool", c1w[sl][i], srcs[i], writes=["c1w%d_%d" % (sl, i)] + extra, slot="c1w%d_%d" % (sl, i))
            for t in range(4):
                bb = [nextbank() for _ in range(4)]
                for k in range(8):
                    mm(ps[bb[0]][:, :], c1w[sl][0][:, k * 128:(k + 1) * 128], ypool3[:, k, 512 * t:512 * t + 512], k == 0, k == 7,
                       ["c1w%d_0" % sl] + YP(t), ["ps%d" % bb[0]])
                for k in range(8):
                    mm(ps[bb[1]][:, :], c1w[sl][1][:, k * 128:(k + 1) * 128], h3[:, k, 256 + 512 * t:256 + 512 * t + 512], k == 0, k == 7,
                       ["c1w%d_1" % sl] + HSEG(1 + t), ["ps%d" % bb[1]])
                for k in range(8):
                    mm(ps[bb[2]][:, :], c1w[sl][2][:, k * 128:(k + 1) * 128], ylru3[:, k, 512 * t:512 * t + 512], k == 0, k == 7,
                       ["c1w%d_2" % sl] + YL, ["ps%d" % bb[2]])
                for k in range(8):
                    mm(ps[bb[3]][:, :], c1w[sl][3][:, k * 128:(k + 1) * 128], h3[:, k, 256 + 512 * t:256 + 512 * t + 512], k == 0, k == 7,
                       ["c1w%d_3" % sl] + HSEG(1 + t), ["ps%d" % bb[3]])
                p = it % 2
                it += 1
                extra = LRUTOK if (oc == 0 and t == 0) else []
                act(sgb[p][0], ps[bb[1]][:, :], AF.Sigmoid, ["ps%d" % bb[1]], ["sg%d_0" % p] + extra)
                act(sgb[p][1], ps[bb[3]][:, :], AF.Sigmoid, ["ps%d" % bb[3]], ["sg%d_1" % p])
                tt("dve", t12[p][0], ps[bb[0]][:, :], sgb[p][0], ALU.mult, ["ps%d" % bb[0], "sg%d_0" % p], ["t12%d_0" % p])
                tt("dve", t12[p][1], ps[bb[2]][:, :], sgb[p][1], ALU.mult, ["ps%d" % bb[2], "sg%d_1" % p], ["t12%d_1" % p])
                tt("dve", m3[:, oc, 512 * t:512 * t + 512], t12[p][0], t12[p][1], ALU.add, ["t12%d_0" % p, "t12%d_1" % p],
                   ["m%d_%d" % (oc, t)])
        MT = lambda tl: ["m%d_%d" % (k, t) for k in range(8) for t in tl]
        if "m" in dbg:
            P.dma("sp", dump("m", mbuf, BF16)[:, :], mbuf, reads=MT(range(4)), slot="dbg_mm")
        if stop == "C1":
            P.emit()
            return nc, dbg_out

        cs = Carve(S_C2, R_S[0] + R_S[1] - S_C2)
        sq2 = [cs.f(640), cs.f(640)]
        tm2 = [cs.f(640), cs.f(640)]
        sd2 = cs.f(640)
        rstd2 = cs.f(640)
        gpad = [cs.f(660) for _ in range(4)]
        mixB = cs.f(8 * 640)
        ssum2 = cs.f(640)
        HY_OLD = [t for si in range(5) for t in HSEG(si)] + [t for tl in range(4) for t in YP(tl)] + YL
        cb_ = Carve(R_H[0], R_H[1] + R_Y[1])
        xt1 = cb_.f(8 * 640)
        mixb = cb_.f(8 * 640)
        hf = cb_.b(8 * 640)
        abuf = cb_.b(24 * 512)
        wupb = [cb_.b(2048) for _ in range(3)]
        wdnb = [cb_.b(3072) for _ in range(2)]
        xt13 = xt1.rearrange("p (c t) -> p c t", t=640)
        mixbufs = [mixb, mixB]
        mix3s = [mb.rearrange("p (c t) -> p c t", t=640) for mb in mixbufs]
        f3s = [mb[:, 0:8 * 512].rearrange("p (c t) -> p c t", t=512) for mb in mixbufs]
        hf3 = hf.rearrange("p (c t) -> p c t", t=640)
        a3 = abuf.rearrange("p (c t) -> p c t", t=512)
        cwd = Carve(W_DYN, R_W[0] + R_W[1] - W_DYN)
        accb = [cwd.f(512) for _ in range(4)]
        wupb.append(cwd.b(2048))
        woutb = [cwd.b(1024) for _ in range(3)]
        wo_ctr = [0]
        C1W_OLD = ["c1w%d_%d" % (s, i) for s in range(2) for i in range(4)]
        oT3 = outT.rearrange("(c p) t -> p c t", p=128)
        memset(gpad[0], 0.0, ["gpad0"] + ["sg%d_%d" % (p, i) for p in range(2) for i in range(2)] + ["t12%d_%d" % (p, i) for p in range(2) for i in range(2)])
        for gi in range(1, 4):
            memset(gpad[gi], 0.0, ["gpad%d" % gi])
        first_hy = [True]
        wup_ctr = [0]
        wdn_ctr = [0]
        dg_ctr = [0]
        gp_ctr = [0]

        def norm_rstd(src_fn, ntok, reads_fn, hook=None):
            subs = [(0, min(512, ntok))] + ([(512, ntok - 512)] if ntok > 512 else [])
            for c in range(8):
                if c == 0:
                    act(ssum2[:, 0:ntok], src_fn(c), AF.Square, reads_fn(c), ["ssum2"])
                else:
                    sq = sq2[c % 2]
                    act(sq[:, 0:ntok], src_fn(c), AF.Square, reads_fn(c), ["sq2_%d" % (c % 2)])
                    tt("dve", ssum2[:, 0:ntok], ssum2[:, 0:ntok], sq[:, 0:ntok], ALU.add, ["ssum2", "sq2_%d" % (c % 2)], ["ssum2"])
                if hook is not None:
                    hook(c)
            for (so, sn) in subs:
                b = nextbank()
                mm(ps[b][:, 0:sn], ones, ssum2[:, so:so + sn], True, True, ["ones", "ssum2"], ["ps%d" % b])
                act(sd2[:, so:so + sn], ps[b][:, 0:sn], AF.Ln, ["ps%d" % b], ["sd2"], scale=1.0 / D, bias=EPS)
            act(rstd2[:, 0:ntok], sd2[:, 0:ntok], AF.Exp, ["sd2"], ["rstd2"], scale=-0.5)

        def geom(tl):
            r0 = max(8 * tl - 1, 0)
            r1 = min(8 * tl + 9, 32)
            ntok = (r1 - r0) * 64
            return r0, ntok, r0 * 64, [(0, 512)] + [(512, ntok - 512)]

        first_mix = [True]

        def mix_units(tl):
            r0, ntok, tok0, subs = geom(tl)
            par = tl % 2
            mtl = sorted({min(3, (tok0 + so) // 512) for so, sn in subs} | {min(3, (tok0 + so + sn - 1) // 512) for so, sn in subs})
            units = []
            for oc in range(8):
                for (so, sn) in subs:
                    st_ = {}

                    def pe_fn(oc=oc, so=so, sn=sn, st_=st_):
                        if so == 0:
                            wi = wo_ctr[0] % 3
                            wo_ctr[0] += 1
                            extra = (LRUTOK + C1W_OLD) if first_mix[0] else []
                            P.dma("sp", woutb[wi], woutS[oc], reads=["wc_all"], writes=["wo%d" % wi] + extra, slot="wo%d" % wi)
                            wo_cur[0] = wi
                        wi = wo_cur[0]
                        b = nextbank()
                        st_["b"] = b
                        for k in range(8):
                            mm(ps[b][:, 0:sn], woutb[wi][:, k * 128:(k + 1) * 128],
                               m3[:, k, tok0 + so:tok0 + so + sn], k == 0, k == 7, ["wo%d" % wi] + MT(mtl), ["ps%d" % b])

                    def act_fn(oc=oc, so=so, sn=sn, st_=st_):
                        b = st_["b"]
                        extra2 = []
                        if oc == 0 and so == 0:
                            extra2 = ["f%d_%d" % (par, q) for q in range(8)]
                            if first_mix[0] or tl == 1:
                                extra2 = extra2 + ["sg%d_%d" % (p_, i) for p_ in range(2) for i in range(2)] + ["t12%d_%d" % (p_, i) for p_ in range(2) for i in range(2)] + HY_OLD
                        first_mix[0] = False
                        act(mix3s[par][:, oc, so:so + sn], ps[b][:, 0:sn], AF.Identity, ["ps%d" % b], ["mix%d_%d" % (par, oc)] + extra2)
                    units.append((pe_fn, act_fn))
            return units

        wo_cur = [0]

        def do_mix(tl):
            for pe_fn, act_fn in mix_units(tl):
                pe_fn()
                act_fn()

        do_mix(0)
        for tl in range(4):
            r0, ntok, tok0, subs = geom(tl)
            par = tl % 2
            mix3 = mix3s[par]
            f3 = f3s[par]
            MIXT = lambda c: "mix%d_%d" % (par, c)
            FT = lambda c: "f%d_%d" % (par, c)
            co = (8 * tl - r0) * 64
            s0 = r0 - (8 * tl - 1)
            extra = HY_OLD if first_hy[0] else []
            first_hy[0] = False
            for c in range(8):
                P.dma("sp", xt13[:, c, 0:ntok], xT3[:, c, tok0:tok0 + ntok], writes=["xt1_%d" % c, "x1_%d" % c] + (extra if c == 0 else []),
                      slot="xt1_%d" % c)
            norm_rstd(lambda c: mix3[:, c, 0:ntok], ntok, lambda c: [MIXT(c)])
            pend = mix_units(tl + 1) if tl + 1 < 4 else []
            pend_pe = [u[0] for u in pend]
            pend_act = [u[1] for u in pend]
            inflight = [0]

            def pump(nact):
                for _ in range(nact):
                    if pend_act:
                        pend_act.pop(0)()
                        inflight[0] -= 1
                while pend_pe and inflight[0] < 5:
                    pend_pe.pop(0)()
                    inflight[0] += 1

            pump(0)
            for c in range(8):
                tm = tm2[c % 2]
                tt("dve", tm[:, 0:ntok], mix3[:, c, 0:ntok], rstd2[:, 0:ntok], ALU.mult, [MIXT(c), "rstd2"], ["tm2_%d" % (c % 2)])
                stt(xt13[:, c, 0:ntok], tm[:, 0:ntok], GG1[:, c:c + 1], xt13[:, c, 0:ntok], ALU.mult, ALU.add,
                    ["tm2_%d" % (c % 2), "der", "xt1_%d" % c], ["x1_%d" % c])
            pump(4)
            norm_rstd(lambda c: xt13[:, c, 0:ntok], ntok, lambda c: ["x1_%d" % c], hook=lambda c: pump(1))
            pump(4)
            for c in range(8):
                tm = tm2[c % 2]
                tt("dve", tm[:, 0:ntok], xt13[:, c, 0:ntok], rstd2[:, 0:ntok], ALU.mult, ["x1_%d" % c, "rstd2"], ["tm2_%d" % (c % 2)])
                act(hf3[:, c, 0:ntok], tm[:, 0:ntok], AF.Identity, ["tm2_%d" % (c % 2), "der", "modfm"], ["hf%d" % c],
                    bias=MOD(3, c, 0), scale=A2[:, c:c + 1])
            while pend_act or pend_pe:
                pump(1)
            HF = ["hf%d" % c for c in range(8)]
            if tl == 3:
                for gi in range(4):
                    memset(gpad[gi].rearrange("p (r c) -> p r c", c=66)[:, 9, :], 0.0, ["gpad%d" % gi])
            def ffn_front(p):
                st = []
                for i in range(2):
                    j = 2 * p + i
                    q = j % 4
                    wi = wup_ctr[0] % 4
                    wup_ctr[0] += 1
                    P.dma("sp", wupb[wi], wupS[j], reads=["wc_all"], writes=["wup%d" % wi], slot="wup%d" % wi)
                    gp3 = gpad[q].rearrange("p (r c) -> p r c", c=66)
                    wg = wupb[wi][:, 0:1024]
                    row = s0
                    for (so, sn) in subs:
                        b = nextbank()
                        for k in range(8):
                            mm(ps[b][:, 0:sn], wg[:, k * 128:(k + 1) * 128], hf3[:, k, so:so + sn], k == 0, k == 7,
                               ["wup%d" % wi] + HF, ["ps%d" % b])
                        nr = sn // 64
                        act(gp3[:, row:row + nr, 1:65], ps[b][:, 0:sn].rearrange("p (r c) -> p r c", c=64), AF.Identity,
                            ["ps%d" % b], ["gpad%d" % q])
                        row += nr
                    acc3 = accb[q].rearrange("p (r c) -> p r c", c=64)
                    act(acc3, gp3[:, 0:8, 0:64], AF.Identity, ["gpad%d" % q, "vecs"], ["acc%d" % q],
                        bias=V("ffn_conv_b", j), scale=V("ffn_conv_w", 0 * 24 + j))
                    st.append((j, q, wi, gp3, acc3))
                return st

            def ffn_taps(st, taps):
                for tap in taps:
                    dy, dx = tap // 3, tap % 3
                    for (j, q, wi, gp3, acc3) in st:
                        stt(acc3, gp3[:, dy:dy + 8, dx:dx + 64], V("ffn_conv_w", tap * 24 + j), acc3, ALU.mult, ALU.add,
                            ["gpad%d" % q, "acc%d" % q, "vecs"], ["acc%d" % q])

            def ffn_u(st):
                out = []
                for (j, q, wi, gp3, acc3) in st:
                    wu = wupb[wi][:, 1024:2048]
                    bu = nextbank()
                    for k in range(8):
                        mm(ps[bu][:, :], wu[:, k * 128:(k + 1) * 128], hf3[:, k, co:co + 512], k == 0, k == 7,
                           ["wup%d" % wi] + HF, ["ps%d" % bu])
                    out.append(bu)
                return out

            def ffn_gelu(st):
                for (j, q, wi, gp3, acc3) in st:
                    act(accb[q], accb[q], AF.Gelu_apprx_tanh, ["acc%d" % q], ["acc%d" % q])

            def ffn_mult(st, bus):
                for (j, q, wi, gp3, acc3), bu in zip(st, bus):
                    tt("dve", a3[:, j, :], accb[q], ps[bu][:, :], ALU.mult, ["acc%d" % q, "ps%d" % bu], ["a%d" % j])

            prev = None
            for p in range(12):
                st = ffn_front(p)
                if prev is not None:
                    ffn_gelu(prev[0])
                ffn_taps(st, range(1, 2))
                if prev is not None:
                    ffn_mult(*prev)
                ffn_taps(st, range(2, 9))
                bus = ffn_u(st)
                prev = (st, bus)
            ffn_gelu(prev[0])
            ffn_mult(*prev)
            AT = ["a%d" % j for j in range(24)]
            for oc in range(8):
                wi = wdn_ctr[0] % 2
                wdn_ctr[0] += 1
                P.dma("sp", wdnb[wi], wdnS[oc], reads=["wc_all"], writes=["wdn%d" % wi], slot="wdn%d" % wi)
                b = nextbank()
                for k in range(24):
                    mm(ps[b][:, :], wdnb[wi][:, k * 128:(k + 1) * 128], a3[:, k, :], k == 0, k == 23, ["wdn%d" % wi] + AT, ["ps%d" % b])
                act(f3[:, oc, :], ps[b][:, :], AF.Identity, ["ps%d" % b], [FT(oc)] + ([MIXT(q) for q in range(8)] if oc == 0 else []))
                if oc == 0:
                    act(ssum2[:, 0:512], f3[:, oc, :], AF.Square, [FT(oc)], ["ssum2"])
                else:
                    sq = sq2[oc % 2]
                    act(sq[:, 0:512], f3[:, oc, :], AF.Square, [FT(oc)], ["sq2_%d" % (oc % 2)])
                    tt("dve", ssum2[:, 0:512], ssum2[:, 0:512], sq[:, 0:512], ALU.add, ["ssum2", "sq2_%d" % (oc % 2)], ["ssum2"])
            bss = nextbank()
            mm(ps[bss][:, :], ones, ssum2[:, 0:512], True, True, ["ones", "ssum2"], ["ps%d" % bss])
            act(sd2[:, 0:512], ps[bss][:, :], AF.Ln, ["ps%d" % bss], ["sd2"], scale=1.0 / D, bias=EPS)
            act(rstd2[:, 0:512], sd2[:, 0:512], AF.Exp, ["sd2"], ["rstd2"], scale=-0.5)
            for oc in range(8):
                tm = tm2[oc % 2]
                tt("dve", tm[:, 0:512], f3[:, oc, :], rstd2[:, 0:512], ALU.mult, [FT(oc), "rstd2"], ["tm2_%d" % (oc % 2)])
                stt(f3[:, oc, :], tm[:, 0:512], GG2[:, oc:oc + 1], xt13[:, oc, co:co + 512], ALU.mult, ALU.add,
                    ["tm2_%d" % (oc % 2), "der", "x1_%d" % oc, FT(oc)], [FT(oc)])
            P.dma("sp", oT3[:, :, 512 * tl:512 * tl + 512], f3, reads=[FT(oc) for oc in range(8)], slot="out")
        P.emit()
    return nc, dbg_out


def _fm(v, nch):
    v = np.asarray(v, np.float32)
    lead = v.shape[:-1]
    r = v.reshape(lead + (nch, 128))
    r = np.moveaxis(r, -1, 0)
    return np.ascontiguousarray(r.reshape(128, -1))


def _wtile(w, kch, och):
    w = np.asarray(w, np.float32)
    r = w.reshape(kch, 128, och, 128).transpose(2, 1, 0, 3)
    return np.ascontiguousarray(r.reshape(och, 128, kch * 128))


def _window_counts():
    out = np.zeros((4, 96), np.float32)
    for gi, w in enumerate(POOL_WINDOWS):
        def cnt1(n):
            pos = np.arange(n)
            lo = np.clip(pos - w // 2, 0, n)
            hi = np.clip(pos + w - w // 2, 0, n)
            return (hi - lo).astype(np.float32)
        out[gi, 0:32] = cnt1(32)
        out[gi, 32:96] = cnt1(64)
    return out.reshape(1, 384)


def prep_inputs(x, c, ctx, c_ctx, w_mod, b_mod, g_pre_mix, g_post_mix, g_pre_ffn, g_post_ffn,
                w_in, pool_w, pool_scale, lru_conv_w, lru_conv_b, lru_wa, lru_ba, lru_wx, lru_bx,
                lru_lambda, w_proj_pool, w_proj_lru, w_out, w_up, ffn_conv_w, ffn_conv_b, w_down):
    f = lambda a: np.asarray(a, np.float32)
    vec_parts = {
        "g_pre_mix": _fm(f(g_pre_mix)[0], 8), "g_post_mix": _fm(f(g_post_mix)[0], 8),
        "g_pre_ffn": _fm(f(g_pre_ffn)[0], 8), "g_post_ffn": _fm(f(g_post_ffn)[0], 8),
        "pool_scale": _fm(f(pool_scale)[0], 8), "lru_conv_w": _fm(f(lru_conv_w)[0], 8),
        "lru_conv_b": _fm(f(lru_conv_b)[0], 8), "lru_ba": _fm(f(lru_ba)[0], 8), "lru_bx": _fm(f(lru_bx)[0], 8),
        "lru_lambda": _fm(f(lru_lambda)[0], 8), "ffn_conv_w": _fm(f(ffn_conv_w)[0].reshape(9, 3072), 24),
        "ffn_conv_b": _fm(f(ffn_conv_b)[0], 24), "b_mod": _fm(f(b_mod)[0], 48),
    }
    vecs = np.ascontiguousarray(np.concatenate([vec_parts[n] for n, _ in _VEC_SPEC], axis=1))
    assert vecs.shape == (128, NV)
    wmod_h = np.ascontiguousarray(f(w_mod)[0].reshape(8, 128, 12, 512).transpose(2, 1, 0, 3).reshape(12, 128, 4096))
    win_h = _wtile(f(w_in)[0], 8, 40)
    wpp_h = _wtile(f(w_proj_pool)[0], 8, 8)
    wpl_h = _wtile(f(w_proj_lru)[0], 8, 8)
    wout_h = _wtile(f(w_out)[0], 8, 8)
    wup_t = _wtile(f(w_up)[0], 8, 48)
    wup_h = np.ascontiguousarray(np.concatenate([wup_t[0:24], wup_t[24:48]], axis=2))
    wdown_h = _wtile(f(w_down)[0], 24, 8)
    pw = f(pool_w)[0].reshape(4, 2, 128, 2, 128).transpose(2, 0, 3, 1, 4)
    poolw_h = np.ascontiguousarray(pw.reshape(128, 2048))
    lw = np.stack([f(lru_wa)[0], f(lru_wx)[0]], axis=0)
    lruw_h = np.ascontiguousarray(lw.transpose(3, 0, 1, 2, 4).reshape(128, 4096))
    shared = {"vecs": vecs, "ident": np.eye(128, dtype=np.float32), "cnt": _window_counts(), "wmod": wmod_h,
              "win": win_h, "wpp": wpp_h, "wpl": wpl_h, "wout": wout_h, "wup": wup_h, "wdown": wdown_h,
              "poolw": poolw_h, "lruw": lruw_h}
    xf, cf, ctxf, ccf = f(x), f(c), f(ctx), f(c_ctx)
    in_maps = []
    for b in range(NCORES):
        m = dict(shared)
        m["xT"] = np.ascontiguousarray(xf[b].T)
        m["ctxT"] = np.ascontiguousarray(ctxf[b].T)
        cc2 = np.stack([cf[b], ccf], axis=0)
        m["cc"] = np.ascontiguousarray(cc2.reshape(2, 8, 128).transpose(2, 1, 0).reshape(128, 16))
        in_maps.append(m)
    return in_maps


_CACHE = {}


def kernel(**inputs):
    in_maps = prep_inputs(**inputs)
    if "nc" not in _CACHE:
        _CACHE["nc"] = build_program()[0]
    nc = _CACHE["nc"]
    res = run_bass_kernel_spmd(nc, in_maps, core_ids=list(range(NCORES)))
    out = np.stack([np.asarray(r["outT"], np.float32).T for r in res.results], axis=0)
    return np.ascontiguousarray(out.astype(np.float32))
```

```python
import numpy as np
from contextlib import ExitStack
import concourse.bass as bass
import concourse.mybir as mybir
from concourse.bass_utils import run_bass_kernel_spmd

F32 = mybir.dt.float32
BF16 = mybir.dt.bfloat16
RELAX_DVE = False
AF = mybir.ActivationFunctionType
ALU = mybir.AluOpType

NCORES = 8
D = 1024
T = 2048
CT = 256
TT = T + CT
NCH = 8
EPS = 1e-6
POOL_WINDOWS = (2, 4, 8, 16)
SEGS = [(0, 256)] + [(256 + 512 * i, 512) for i in range(4)]

_VEC_SPEC = [("g_pre_mix", 8), ("g_post_mix", 8), ("g_pre_ffn", 8), ("g_post_ffn", 8), ("pool_scale", 8),
             ("lru_conv_w", 32), ("lru_conv_b", 8), ("lru_ba", 16), ("lru_bx", 16), ("lru_lambda", 16),
             ("ffn_conv_w", 216), ("ffn_conv_b", 24), ("b_mod", 48)]
VOFF = {}
_o = 0
for _n, _c in _VEC_SPEC:
    VOFF[_n] = _o
    _o += _c
NV = _o


class _Op:
    __slots__ = ("eng", "fn", "deps", "needs_inc", "sem", "val", "is_dma", "slot", "pos")

    def __init__(self, eng, fn, is_dma=False, slot=None):
        self.eng = eng
        self.fn = fn
        self.deps = []
        self.needs_inc = False
        self.sem = None
        self.val = None
        self.is_dma = is_dma
        self.slot = slot
        self.pos = -1


class Prog:
    ENGS = ("pe", "act", "dve", "pool", "sp")

    def __init__(self, nc):
        self.nc = nc
        self.q = {e: [] for e in self.ENGS}
        self.last_w = {}
        self.readers = {}
        self.all_ops = []

    def _track(self, op, reads, writes):
        deps = {}
        for t in reads:
            w = self.last_w.get(t)
            if w is not None:
                deps[id(w)] = w
        for t in writes:
            w = self.last_w.get(t)
            if w is not None:
                deps[id(w)] = w
            for r in self.readers.get(t, {}).values():
                deps[id(r)] = r
        for d in deps.values():
            if d is op:
                continue
            if (not d.is_dma) and (not op.is_dma) and d.eng == "pe" and op.eng == "pe":
                continue
            if RELAX_DVE and (not d.is_dma) and (not op.is_dma) and d.eng == "dve" and op.eng == "dve" and op.pos - d.pos >= 2:
                continue
            d.needs_inc = True
            op.deps.append(d)
        for t in writes:
            self.last_w[t] = op
            self.readers[t] = {}
        for t in reads:
            key = ("dma", op.slot) if op.is_dma else op.eng
            self.readers.setdefault(t, {})[key] = op

    def op(self, eng, fn, reads=(), writes=()):
        o = _Op(eng, fn)
        o.pos = len(self.q[eng])
        self._track(o, reads, writes)
        self.q[eng].append(o)
        self.all_ops.append(o)
        return o

    def dma(self, eng, out, in_, reads=(), writes=(), slot=None, **kw):
        o = _Op(eng, None, is_dma=True, slot=slot)
        o.fn = lambda e, s, o_=out, i_=in_, kw_=kw: e.dma_start(out=o_, in_=i_, **kw_).then_inc(s, 16)
        self._track(o, reads, writes)
        o.needs_inc = True
        self.q[eng].append(o)
        self.all_ops.append(o)
        return o

    def emit(self, final_wait_eng="sp"):
        nc = self.nc
        with ExitStack() as es:
            esem = {e: es.enter_context(nc.semaphore("sem_" + e)) for e in self.ENGS}
            slot_names = sorted({o.slot for o in self.all_ops if o.is_dma})
            ssem = {s: es.enter_context(nc.semaphore("dsem_" + s)) for s in slot_names}
            cnt = {e: 0 for e in self.ENGS}
            for e in self.ENGS:
                for o in self.q[e]:
                    if (not o.is_dma) and o.needs_inc:
                        cnt[e] += 1
                        o.sem, o.val = esem[e], cnt[e]
            scnt = {s: 0 for s in slot_names}
            for o in self.all_ops:
                if o.is_dma:
                    scnt[o.slot] += 16
                    o.sem, o.val = ssem[o.slot], scnt[o.slot]
            block = es.enter_context(nc.Block())
            final = [(ssem[s], scnt[s]) for s in slot_names]

            def run(e):
                def body(eng):
                    waited = {}
                    for o in self.q[e]:
                        for d in o.deps:
                            k = id(d.sem)
                            if waited.get(k, 0) < d.val:
                                eng.wait_ge(d.sem, d.val)
                                waited[k] = d.val
                        if o.is_dma:
                            o.fn(eng, o.sem)
                        else:
                            ins = o.fn(eng)
                            if o.needs_inc:
                                ins.then_inc(o.sem, 1)
                    if e == final_wait_eng:
                        for s, v in final:
                            if v > 0:
                                eng.wait_ge(s, v)
                return body

            block.tensor(run("pe"))
            block.scalar(run("act"))
            block.vector(run("dve"))
            block.gpsimd(run("pool"))
            block.sync(run("sp"))


def build_program(stop=None, dbg=()):
    nc = bass.Bass("TRN2", target_bir_lowering=False)
    dr = {}

    def din(name, shape, dt=F32):
        dr[name] = nc.dram_tensor(name, shape, dt, kind="ExternalInput").ap()

    din("xT", [D, T])
    din("ctxT", [D, CT])
    din("cc", [128, 16])
    din("vecs", [128, NV])
    din("ident", [128, 128])
    din("cnt", [1, 384])
    din("wmod", [12, 128, 4096])
    din("win", [40, 128, 1024])
    din("wpp", [8, 128, 1024])
    din("wpl", [8, 128, 1024])
    din("wout", [8, 128, 1024])
    din("wup", [24, 128, 2048])
    din("wdown", [8, 128, 3072])
    din("poolw", [128, 2048])
    din("lruw", [128, 4096])
    outT = nc.dram_tensor("outT", [D, T], F32, kind="ExternalOutput").ap()
    wupS = nc.dram_tensor("wupS", [24, 128, 2048], BF16, kind="Internal").ap()
    wdnS = nc.dram_tensor("wdnS", [8, 128, 3072], BF16, kind="Internal").ap()
    woutS = nc.dram_tensor("woutS", [8, 128, 1024], BF16, kind="Internal").ap()
    dbg_out = {}

    es = ExitStack()
    with es:
        AW = 53200
        arena = es.enter_context(nc.sbuf_tensor("arena", [128, AW], F32))
        psall = es.enter_context(nc.psum_tensor("psall", [128, 4096], F32))
        ps = [psall[:, i * 512:(i + 1) * 512] for i in range(8)]
        P = Prog(nc)

        def fv(off, n):
            assert off % 4 == 0 and off + 4 * n <= AW * 4, (off, n)
            return arena[:, off // 4: off // 4 + n]

        def bv(off, n):
            assert off % 4 == 0 and n % 2 == 0 and off + 2 * n <= AW * 4, (off, n)
            return arena[:, off // 4: off // 4 + n // 2].bitcast(BF16)

        class Carve:
            def __init__(self, base, size):
                self.base, self.end, self.cur = base, base + size, base

            def f(self, n):
                a = fv(self.cur, n)
                self.cur += 4 * n
                self.cur = (self.cur + 63) // 64 * 64
                assert self.cur <= self.end, ("carve overflow", self.cur, self.end)
                return a

            def b(self, n):
                a = bv(self.cur, n)
                self.cur += 2 * n
                self.cur = (self.cur + 63) // 64 * 64
                assert self.cur <= self.end, ("carve overflow", self.cur, self.end)
                return a

        KB = 1024
        R_H = (0, 36 * KB)
        R_Y = (36 * KB, 64 * KB)
        R_S = (100 * KB, 84 * KB)
        R_W = (184 * KB, AW * 4 - 184 * KB)

        bank_ctr = [0]

        def nextbank():
            b = bank_ctr[0] % 8
            bank_ctr[0] += 1
            return b

        def mm(out, lhsT, rhs, start, stop, reads, writes):
            P.op("pe", lambda e: e.matmul(out, lhsT, rhs, start=start, stop=stop), reads, writes)

        def act(out, in_, func, reads, writes, bias=None, scale=None):
            kw = {}
            if bias is not None:
                kw["bias"] = bias
            if scale is not None:
                kw["scale"] = scale
            P.op("act", lambda e: e.activation(out=out, in_=in_, func=func, **kw), reads, writes)

        def evac_latent(banks, dst, o0, func, reads_extra, toks, **kw):
            i = 0
            while i < len(banks):
                j = i
                while j + 1 < len(banks) and banks[j + 1] == banks[j] + 1:
                    j += 1
                r = j - i + 1
                act(dst[:, o0 + 512 * i:o0 + 512 * (i + r)], psall[:, banks[i] * 512:(banks[i] + r) * 512], func,
                    ["ps%d" % b_ for b_ in banks[i:j + 1]] + list(reads_extra), list(toks[i:j + 1]), **kw)
                i = j + 1

        def tt(eng, out, in0, in1, op, reads, writes):
            P.op(eng, lambda e: e.tensor_tensor(out=out, in0=in0, in1=in1, op=op), reads, writes)

        def ts(eng, out, in0, s1, op0, reads, writes, s2=None, op1=None):
            if op1 is None:
                P.op(eng, lambda e: e.tensor_scalar(out=out, in0=in0, scalar1=s1, scalar2=None, op0=op0), reads, writes)
            else:
                P.op(eng, lambda e: e.tensor_scalar(out=out, in0=in0, scalar1=s1, scalar2=s2, op0=op0, op1=op1), reads, writes)

        def stt(out, in0, scalar, in1, op0, op1, reads, writes):
            P.op("dve", lambda e: e.scalar_tensor_tensor(out=out, in0=in0, scalar=scalar, in1=in1, op0=op0, op1=op1), reads, writes)

        def memset(ap, val, writes):
            P.op("pool", lambda e: e.memset(ap, val), (), writes)

        def dump(name, ap, dt=F32):
            shape = [ap.shape[0], int(np.prod(ap.shape[1:]))]
            t = nc.dram_tensor("dbg_" + name, shape, dt, kind="ExternalOutput").ap()
            dbg_out[name] = t
            return t

        cw = Carve(*R_W)
        vecs = cw.f(NV)
        ident = cw.f(128)
        ones = cw.f(128)
        identb = cw.b(128)
        ccs = cw.f(16)
        ssil = cw.f(16)
        modfm = cw.f(96)
        der = cw.f(64)
        lrud = cw.f(64)
        cw2 = cw.f(32)
        cn1 = cw.f(384)
        W_DYN = cw.cur

        def V(name, i=0):
            o = VOFF[name] + i
            return vecs[:, o:o + 1]

        def Vs(name, i0, n):
            o = VOFF[name] + i0
            return vecs[:, o:o + n]

        P.dma("sp", vecs, dr["vecs"][:, :], writes=["vecs"], slot="c_vecs")
        P.dma("sp", ident, dr["ident"][:, :], writes=["ident"], slot="c_ident")
        P.dma("sp", ccs, dr["cc"][:, :], writes=["cc"], slot="c_cc")
        memset(ones, 1.0, ["ones"])
        P.op("dve", lambda e: e.tensor_copy(out=identb, in_=ident), ["ident"], ["identb"])
        act(ssil, ccs, AF.Silu, ["cc"], ["ssil"])
        s3 = ssil.rearrange("p (k r) -> p k r", r=2)

        cs = Carve(*R_S)
        wmb = [cs.f(4096), cs.f(4096)]
        modrow = cs.f(6144)
        xa4 = cs.f(8 * 512)
        rstdA = cs.f(TT)
        cy = Carve(*R_Y)
        xa = [cy.f(8 * 256), cy.f(8 * 512), cy.f(8 * 512), cy.f(8 * 512), xa4]
        sqb = [cy.f(512), cy.f(512)]
        ssumA = cy.f(512)
        sdb = cy.f(512)
        xT3 = dr["xT"].rearrange("(c p) t -> p c t", p=128)
        cT3 = dr["ctxT"].rearrange("(c p) t -> p c t", p=128)
        xa3 = [xa[si].rearrange("p (c t) -> p c t", t=SEGS[si][1]) for si in range(5)]

        def load_x(si):
            o, n = SEGS[si]
            src = cT3[:, :, :] if si == 0 else xT3[:, :, o - 256:o - 256 + n]
            P.dma("sp", xa3[si], src, writes=["xa%d" % si], slot="xa%d" % si)

        def stats(si):
            o, n = SEGS[si]
            tk = "xa%d" % si
            for c in range(8):
                if c == 0:
                    act(ssumA[:, 0:n], xa3[si][:, c, :], AF.Square, [tk], ["ssumA"])
                else:
                    sq = sqb[c % 2]
                    act(sq[:, 0:n], xa3[si][:, c, :], AF.Square, [tk], ["sq%d" % (c % 2)])
                    tt("dve", ssumA[:, 0:n], ssumA[:, 0:n], sq[:, 0:n], ALU.add, ["ssumA", "sq%d" % (c % 2)], ["ssumA"])
            b = nextbank()
            mm(ps[b][:, 0:n], ones, ssumA[:, 0:n], True, True, ["ones", "ssumA"], ["ps%d" % b])
            act(sdb[:, 0:n], ps[b][:, 0:n], AF.Ln, ["ps%d" % b], ["sd"], scale=1.0 / D, bias=EPS)
            act(rstdA[:, o:o + n], sdb[:, 0:n], AF.Exp, ["sd"], ["rstdA%d" % si], scale=-0.5)
            for c in range(8):
                tt("dve", xa3[si][:, c, :], xa3[si][:, c, :], rstdA[:, o:o + n], ALU.mult, [tk, "rstdA%d" % si], ["xn%d_%d" % (si, c)])

        xsched = {1: 0, 2: 1, 4: 2, 6: 3, 8: 4}
        ssched = {3: 0, 5: 1, 7: 2, 9: 3, 11: 4}
        for blk in range(12):
            buf = wmb[blk % 2]
            tk = "wm%d" % (blk % 2)
            P.dma("sp", buf, dr["wmod"][blk], writes=[tk], slot=tk)
            if blk in xsched:
                load_x(xsched[blk])
            b = nextbank()
            for k in range(8):
                mm(ps[b][0:2, 0:512], s3[:, k, :], buf[:, k * 512:(k + 1) * 512], k == 0, k == 7,
                   ["ssil", tk], ["ps%d" % b])
            act(modrow[0:2, blk * 512:(blk + 1) * 512], ps[b][0:2, 0:512], AF.Identity, ["ps%d" % b], ["modrow%d" % blk])
            if blk in ssched:
                stats(ssched[blk])
        b = nextbank()
        for oc in range(48):
            mm(ps[b][:, 2 * oc:2 * oc + 2], modrow[0:2, oc * 128:(oc + 1) * 128], ident[0:2, 0:2], True, True,
               ["modrow%d" % (oc // 4), "ident"], ["ps%d" % b])
        act(modfm, ps[b][:, 0:96], AF.Identity, ["ps%d" % b], ["modfm"])
        mod3 = modfm.rearrange("p (c r) -> p c r", r=2)
        for r in range(2):
            tt("dve", mod3[:, :, r], mod3[:, :, r], Vs("b_mod", 0, 48), ALU.add, ["modfm", "vecs"], ["modfm"])

        def MOD(which, c, r=0):
            return mod3[:, which * 8 + c, r:r + 1]

        A1 = der[:, 0:8]
        A1c = der[:, 8:16]
        GG1 = der[:, 16:24]
        A2 = der[:, 24:32]
        GG2 = der[:, 32:40]
        stt(A1, mod3[:, 8:16, 0], 1.0, Vs("g_pre_mix", 0, 8), ALU.add, ALU.mult, ["modfm", "vecs"], ["der"])
        stt(A1c, mod3[:, 8:16, 1], 1.0, Vs("g_pre_mix", 0, 8), ALU.add, ALU.mult, ["modfm", "vecs"], ["der"])
        tt("dve", GG1, mod3[:, 16:24, 0], Vs("g_post_mix", 0, 8), ALU.mult, ["modfm", "vecs"], ["der"])
        stt(A2, mod3[:, 32:40, 0], 1.0, Vs("g_pre_ffn", 0, 8), ALU.add, ALU.mult, ["modfm", "vecs"], ["der"])
        tt("dve", GG2, mod3[:, 40:48, 0], Vs("g_post_ffn", 0, 8), ALU.mult, ["modfm", "vecs"], ["der"])
        lam = Vs("lru_lambda", 0, 16)
        le = lrud[:, 0:16]
        lsp = lrud[:, 16:32]
        ls1 = lrud[:, 32:48]
        act(le, lam, AF.Exp, ["vecs"], ["le"], scale=-1.0)
        act(lsp, le, AF.Ln, ["le"], ["lsp"], bias=1.0)
        ts("dve", ls1, lsp, -4.0, ALU.mult, ["lsp"], ["hs1"])
        hs1 = ls1
        hbias = cw2
        ts("dve", hbias, Vs("lru_ba", 0, 32), 0.5, ALU.mult, ["vecs"], ["hbias"])

        ch = Carve(*R_H)
        h = ch.b(8 * TT)
        h3 = h.rearrange("p (c t) -> p c t", t=TT)
        for si, (o, n) in enumerate(SEGS):
            for c in range(8):
                if si == 0:
                    sc_, bi_ = A1c[:, c:c + 1], MOD(0, c, 1)
                else:
                    sc_, bi_ = A1[:, c:c + 1], MOD(0, c, 0)
                act(h3[:, c, o:o + n], xa3[si][:, c, :], AF.Identity, ["xn%d_%d" % (si, c), "der", "modfm"], ["h%d_%d" % (c, si)],
                    bias=bi_, scale=sc_)
        HSEG = lambda si: ["h%d_%d" % (c, si) for c in range(8)]

        if "h" in dbg:
            P.dma("sp", dump("h", h, BF16)[:, :], h, reads=[t for si in range(5) for t in HSEG(si)], slot="dbg_h")
            P.dma("sp", dump("modfm", modfm)[:, :], modfm, reads=["modfm"], slot="dbg_m")
        if stop == "A":
            P.emit()
            return nc, dbg_out

        cy = Carve(*R_Y)
        ypool = cy.b(8 * T)
        ylru = cy.b(8 * T)
        ypool3 = ypool.rearrange("p (c t) -> p c t", t=T)
        ylru3 = ylru.rearrange("p (c t) -> p c t", t=T)
        YTOK = ["xa%d" % si for si in range(4)] + ["xn%d_%d" % (si, c) for si in range(4) for c in range(8)] + ["sq0", "sq1", "sd", "ssumA"]
        y_first = [True]

        cwd = Carve(W_DYN, R_W[0] + R_W[1] - W_DYN)
        lruw = cwd.b(4096)
        poolw = cwd.b(2048)
        NWIN = 3
        winb = [cwd.b(1024) for _ in range(NWIN)]
        win_ctr = [0]
        P.dma("pool", lruw[:, 0:2048], dr["lruw"][:, 0:2048], writes=["lruw"], slot="w_lruw")
        P.dma("pool", lruw[:, 2048:4096], dr["lruw"][:, 2048:4096], writes=["lruw"], slot="w_lruw")
        P.dma("pool", poolw, dr["poolw"][:, :], writes=["poolw"], slot="w_poolw")

        def load_win(oc):
            i = win_ctr[0] % NWIN
            win_ctr[0] += 1
            P.dma("pool", winb[i], dr["win"][oc], writes=["win%d" % i], slot="win%d" % i)
            return winb[i], "win%d" % i

        def proj_seg(wt, wtk, si, b):
            o, n = SEGS[si]
            for k in range(8):
                mm(ps[b][:, 0:n], wt[:, k * 128:(k + 1) * 128], h3[:, k, o:o + n], k == 0, k == 7,
                   [wtk] + HSEG(si), ["ps%d" % b])

        cast_plan = []
        for oc in range(8):
            cast_plan.append((woutS[oc], dr["wout"][oc]))
        for j in range(24):
            cast_plan.append((wupS[j], dr["wup"][j]))
        for oc in range(8):
            cast_plan.append((wdnS[oc][:, 0:2048], dr["wdown"][oc][:, 0:2048]))
            cast_plan.append((wdnS[oc][:, 2048:3072], dr["wdown"][oc][:, 2048:3072]))
        cast_ctr = [0]

        def issue_casts(k):
            for _ in range(k):
                i = cast_ctr[0]
                if i >= len(cast_plan):
                    return
                cast_ctr[0] += 1
                dst, src = cast_plan[i]
                wr = ["wc%d" % i] + (["wc_all"] if i == len(cast_plan) - 1 else [])
                P.dma("pool", dst, src, writes=wr, slot="wcast")

        STOK_A = (["wm0", "wm1"] + ["modrow%d" % i for i in range(12)] + ["xa4"] + ["xn4_%d" % c for c in range(8)]
                  + ["rstdA%d" % si for si in range(5)])
        cs = Carve(*R_S)
        cntb = cs.f(T)
        PQ = [[cs.f(47 * 79), cs.f(47 * 79)] for _ in range(2)]
        dbf = [cs.b(T), cs.b(T)]
        PQTOK = ["pq%d_%d" % (cl, i) for cl in range(2) for i in range(2)]
        cnt3 = cntb.rearrange("p (r c) -> p r c", c=64)
        first_S = [True]
        nb_ctr = [0]

        def nextB():
            b = 4 + nb_ctr[0] % 4
            nb_ctr[0] += 1
            return b

        def proj_cl0(g_):
            wt_, wtk_ = load_win(2 * g_)
            for t in range(4):
                proj_seg(wt_, wtk_, 1 + t, t)
            return (wt_, wtk_)

        pre_cl0 = proj_cl0(0)
        P.dma("sp", cn1, dr["cnt"][0:1, :].partition_broadcast(128), writes=["cn1"], slot="c_cn1")
        P.op("dve", lambda e: e.reciprocal(out=cn1, in_=cn1), ["cn1"], ["cn1"])
        for g, w in enumerate(POOL_WINDOWS):
            hw = w // 2
            Hp, Wp = 32 + w - 1, 64 + w - 1
            extra = STOK_A if first_S[0] else []
            first_S[0] = False
            ir = cn1[:, g * 96: g * 96 + 32].unsqueeze(2).broadcast_to([128, 32, 64])
            ic = cn1[:, g * 96 + 32: g * 96 + 96].unsqueeze(1).broadcast_to([128, 32, 64])
            tt("dve", cnt3, ir, ic, ALU.mult, ["cn1"], ["cnt"] + extra)
            views = []
            for cl in range(2):
                v = [PQ[cl][i][:, 0:Hp * Wp].rearrange("p (r c) -> p r c", c=Wp) for i in range(2)]
                views.append(v)
                Pv, Qv = v
                memset(Pv[:, 0:hw, :], 0.0, ["pq%d_0" % cl] + extra)
                if hw > 1:
                    memset(Pv[:, hw + 32:Hp, :], 0.0, ["pq%d_0" % cl])
                memset(Pv[:, hw:hw + 32, 0:hw], 0.0, ["pq%d_0" % cl])
                if hw > 1:
                    memset(Pv[:, hw:hw + 32, hw + 64:Wp], 0.0, ["pq%d_0" % cl])
                if w in (2, 8):
                    memset(Qv[:, 0:hw, 0:64], 0.0, ["pq%d_1" % cl])
                    if hw > 1:
                        memset(Qv[:, hw + 32:Hp, 0:64], 0.0, ["pq%d_1" % cl])
            wts = [pre_cl0]
            for t in range(4):
                act(views[0][0][:, hw + 8 * t: hw + 8 * t + 8, hw:hw + 64], ps[t][:, :].rearrange("p (r c) -> p r c", c=64),
                    AF.Identity, ["ps%d" % t], ["pq0_0"])
            wt, wtk = load_win(2 * g + 1)
            wts.append((wt, wtk))
            for t in range(4):
                b = nextB()
                proj_seg(wt, wtk, 1 + t, b)
                act(views[1][0][:, hw + 8 * t: hw + 8 * t + 8, hw:hw + 64], ps[b][:, :].rearrange("p (r c) -> p r c", c=64),
                    AF.Identity, ["ps%d" % b], ["pq1_0"])
            if g + 1 < 4:
                pre_cl0 = proj_cl0(g + 1)
            state = [dict(cur=0, ln=Wp, rows=Hp) for _ in range(2)]
            k = 1
            while k < w:
                for cl in range(2):
                    st = state[cl]
                    src, dst = views[cl][st["cur"]], views[cl][1 - st["cur"]]
                    nl = st["ln"] - k
                    tt("dve", dst[:, hw:hw + 32, 0:nl], src[:, hw:hw + 32, 0:nl], src[:, hw:hw + 32, k:k + nl], ALU.add,
                       ["pq%d_%d" % (cl, st["cur"])], ["pq%d_%d" % (cl, 1 - st["cur"])])
                    st["cur"], st["ln"] = 1 - st["cur"], nl
                k *= 2
            k = 1
            while k < w:
                for cl in range(2):
                    st = state[cl]
                    src, dst = views[cl][st["cur"]], views[cl][1 - st["cur"]]
                    nr = st["rows"] - k
                    tt("dve", dst[:, 0:nr, 0:64], src[:, 0:nr, 0:64], src[:, k:k + nr, 0:64], ALU.add,
                       ["pq%d_%d" % (cl, st["cur"])], ["pq%d_%d" % (cl, 1 - st["cur"])])
                    st["cur"], st["rows"] = 1 - st["cur"], nr
                k *= 2
            for cl in range(2):
                st = state[cl]
                assert st["ln"] == 64 and st["rows"] == 32
                src, oth = views[cl][st["cur"]], views[cl][1 - st["cur"]]
                tt("dve", oth[:, 0:32, 0:64], src[:, 0:32, 0:64], cnt3, ALU.mult, ["pq%d_%d" % (cl, st["cur"]), "cnt"],
                   ["pq%d_%d" % (cl, 1 - st["cur"])])
                wt, wtk = wts[cl]
                for t in range(4):
                    b = nextB()
                    proj_seg(wt, wtk, 1 + t, b)
                    tt("dve", dbf[cl][:, 512 * t:512 * t + 512].rearrange("p (r c) -> p r c", c=64), oth[:, 8 * t:8 * t + 8, 0:64],
                       ps[b][:, :].rearrange("p (r c) -> p r c", c=64), ALU.subtract, ["pq%d_%d" % (cl, 1 - st["cur"]), "ps%d" % b],
                       ["dbf%d_%d" % (cl, t)])
            for ocl in range(2):
                oc = 2 * g + ocl
                for t in range(4):
                    b = nextB()
                    for k in range(2):
                        idx = ((g * 2 + ocl) * 2 + k) * 128
                        mm(ps[b][:, :], poolw[:, idx:idx + 128], dbf[k][:, 512 * t:512 * t + 512], k == 0, k == 1,
                           ["poolw", "dbf%d_%d" % (k, t)], ["ps%d" % b])
                    extra = YTOK if y_first[0] else []
                    y_first[0] = False
                    act(ypool3[:, oc, 512 * t:512 * t + 512], ps[b][:, :], AF.Identity, ["ps%d" % b, "vecs"],
                        ["ypool%d_%d" % (oc, t)] + extra, scale=V("pool_scale", oc))
            issue_casts(5)
        if "ypool" in dbg:
            P.dma("sp", dump("ypool", ypool, BF16)[:, :], ypool,
                  reads=["ypool%d_%d" % (oc, t) for oc in range(8) for t in range(4)], slot="dbg_yp")
        if stop == "Bp":
            P.emit()
            return nc, dbg_out

        cs = Carve(*R_S)
        UPW = 2320
        LOFF = 264
        upad = cs.b(UPW)
        dgw = cs.b(32 * 128)
        o_m2b = cs.cur
        m2b = cs.f(TT)
        xcb = bv(o_m2b, TT)
        xc = cs.f(TT)
        m2f = cs.f(TT)
        ra = [cs.f(TT), cs.f(TT)]
        ib = [cs.f(TT), cs.f(TT)]
        gel = cs.f(T)
        m2 = [m2f, m2b]
        m2tok = ["m2f", "m2b"]
        POOLTOK = ["cnt"] + PQTOK + ["dbf%d_%d" % (cl, t) for cl in range(2) for t in range(4)]
        UP = ["upad%d" % si for si in range(5)]
        memset(upad, 0.0, UP + POOLTOK)
        for k in range(4):
            for n in range(8):
                i = k * 8 + n
                ts("dve", dgw[:, i * 128:(i + 1) * 128], ident, V("lru_conv_w", i), ALU.mult, ["ident", "vecs"], ["dgw"] + (POOLTOK if i == 0 else []))
        XC = ["xc%d" % si for si in range(5)]
        XCB = ["xcb%d" % si for si in range(5)]
        RA = lambda d: ["ra%d_%d" % (d, si) for si in range(5)]
        IB = lambda d: ["ib%d_%d" % (d, si) for si in range(5)]
        win_next = load_win(8 + 0)
        for n in range(8):
            wt, wtk = win_next
            for si, (o, nn) in enumerate(SEGS):
                b = nextbank()
                proj_seg(wt, wtk, si, b)
                po = 2 + o if si == 0 else LOFF + (o - 256)
                act(upad[:, po:po + nn], ps[b][:, 0:nn], AF.Identity, ["ps%d" % b], ["upad%d" % si])
            for si, (o, nn) in enumerate(SEGS):
                base = o if si == 0 else LOFF - 2 + (o - 256)
                b = nextbank()
                for k in range(4):
                    i = k * 8 + n
                    mm(ps[b][:, 0:nn], dgw[:, i * 128:(i + 1) * 128], upad[:, base + k:base + k + nn], k == 0, k == 3,
                       ["dgw"] + ([UP[0]] if si == 0 else [UP[j_] for j_ in (si - 1, si, si + 1) if 1 <= j_ <= 4]), ["ps%d" % b])
                act(xcb[:, o:o + nn], ps[b][:, 0:nn], AF.Identity, ["ps%d" % b, "vecs"], ["xcb%d" % si] + (["m2b"] if si == 0 else []),
                    bias=V("lru_conv_b", n))
                act(xc[:, o:o + nn], ps[b][:, 0:nn], AF.Identity, ["ps%d" % b, "vecs"], ["xc%d" % si], bias=V("lru_conv_b", n))
            for dr_ in range(2):
                for kind, dst, tkf in ((0, ra[dr_], RA(dr_)), (1, ib[dr_], IB(dr_))):
                    widx = ((kind * 2 + dr_) * 8 + n) * 128
                    hb_ = hbias[:, kind * 16 + dr_ * 8 + n: kind * 16 + dr_ * 8 + n + 1]
                    b = nextbank()
                    mm(ps[b][:, 0:256], lruw[:, widx:widx + 128], xcb[:, 0:256], True, True, ["lruw", "xcb0"], ["ps%d" % b])
                    act(dst[:, 0:256], ps[b][:, 0:256], AF.Tanh, ["ps%d" % b, "hbias"], [tkf[0]], bias=hb_, scale=0.5)
                    gb = []
                    for si in range(1, 5):
                        o, nn = SEGS[si]
                        b = nextbank()
                        gb.append(b)
                        mm(ps[b][:, 0:nn], lruw[:, widx:widx + 128], xcb[:, o:o + nn], True, True, ["lruw", "xcb%d" % si], ["ps%d" % b])
                    evac_latent(gb, dst, 256, AF.Tanh, ["hbias"], tkf[1:5], bias=hb_, scale=0.5)
                hs = hs1[:, dr_ * 8 + n: dr_ * 8 + n + 1]
                act(ra[dr_], ra[dr_], AF.Exp, RA(dr_) + ["hs1"], RA(dr_), bias=hs, scale=hs)
                act(m2[dr_], ra[dr_], AF.Square, RA(dr_), [m2tok[dr_]] + (XCB if dr_ == 1 else []))
                act(m2[dr_], m2[dr_], AF.Sqrt, [m2tok[dr_]], [m2tok[dr_]], scale=-0.25, bias=0.25)
            for dr_ in range(2):
                stt(ib[dr_], ib[dr_], 1.0, xc, ALU.add, ALU.mult, IB(dr_) + XC, IB(dr_))
                tt("dve", ib[dr_], ib[dr_], m2[dr_], ALU.mult, IB(dr_) + [m2tok[dr_]], IB(dr_))
                if dr_ == 0:
                    P.op("dve", lambda e: e.tensor_tensor_scan(out=ib[0], data0=ra[0], data1=ib[0], initial=0.0, op0=ALU.mult, op1=ALU.add),
                         RA(0) + IB(0), IB(0))
                else:
                    P.op("dve", lambda e: e.tensor_tensor_scan(out=ib[1][:, 0:256][:, ::-1], data0=ra[1][:, 0:256][:, ::-1],
                                                                data1=ib[1][:, 0:256][:, ::-1], initial=0.0, op0=ALU.mult, op1=ALU.add),
                         RA(1) + IB(1), IB(1))
                    P.op("dve", lambda e: e.tensor_tensor_scan(out=ib[1][:, 256:TT][:, ::-1], data0=ra[1][:, 256:TT][:, ::-1],
                                                                data1=ib[1][:, 256:TT][:, ::-1], initial=ib[1][:, 0:1], op0=ALU.mult, op1=ALU.add),
                         RA(1) + IB(1), IB(1))
            wt, wtk = load_win(16 + n)
            if n + 1 < 8:
                win_next = load_win(8 + n + 1)
            issue_casts(5)
            gb = []
            for t in range(4):
                b = nextbank()
                gb.append(b)
                proj_seg(wt, wtk, 1 + t, b)
            evac_latent(gb, gel, 0, AF.Gelu_apprx_tanh, [], ["gel%d" % t for t in range(4)])
            GEL = ["gel%d" % t for t in range(4)]
            tt("dve", ib[0][:, 256:TT], ib[0][:, 256:TT], ib[1][:, 256:TT], ALU.add, IB(0) + IB(1), IB(0))
            tt("dve", ylru3[:, n, :], ib[0][:, 256:TT], gel, ALU.mult, IB(0) + GEL, ["ylru%d" % n] + (YTOK if n == 0 else []))
        if "ylru" in dbg:
            P.dma("sp", dump("ylru", ylru, BF16)[:, :], ylru, reads=["ylru%d" % n for n in range(8)], slot="dbg_yl")
        if stop == "B":
            P.emit()
            return nc, dbg_out

        LRUTOK = UP + ["m2f", "m2b", "dgw"] + XC + XCB + RA(0) + RA(1) + IB(0) + IB(1) + GEL
        cs = Carve(*R_S)
        mbuf = cs.b(8 * T)
        m3 = mbuf.rearrange("p (c t) -> p c t", t=T)
        S_C2 = cs.cur
        sgb = [[cs.f(512), cs.f(512)] for _ in range(2)]
        t12 = [[cs.f(512), cs.f(512)] for _ in range(2)]
        cwd = Carve(W_DYN, R_W[0] + R_W[1] - W_DYN)
        c1w = [[cwd.b(1024) for _ in range(4)] for _ in range(2)]
        W_OLD = ["lruw", "poolw"] + ["win%d" % i for i in range(NWIN)]
        first_c1 = [True]
        YP = lambda t: ["ypool%d_%d" % (k, t) for k in range(8)]
        YL = ["ylru%d" % k for k in range(8)]
        it = 0
        for oc in range(8):
            sl = oc % 2
            srcs = [dr["wpp"][oc], dr["win"][24 + oc], dr["wpl"][oc], dr["win"][32 + oc]]
            for i in range(4):
                extra = W_OLD if first_c1[0] else []
                first_c1[0] = False
                P.dma("pool", c1w[sl][i], srcs[i], writes=["c1w%d_%d" % (sl, i)] + extra, slot="c1w%d_%d" % (sl, i))
            for t in range(4):
                bb = [nextbank() for _ in range(4)]
                for k in range(8):
                    mm(ps[bb[0]][:, :], c1w[sl][0][:, k * 128:(k + 1) * 128], ypool3[:, k, 512 * t:512 * t + 512], k == 0, k == 7,
                       ["c1w%d_0" % sl] + YP(t), ["ps%d" % bb[0]])
                for k in range(8):
                    mm(ps[bb[1]][:, :], c1w[sl][1][:, k * 128:(k + 1) * 128], h3[:, k, 256 + 512 * t:256 + 512 * t + 512], k == 0, k == 7,
                       ["c1w%d_1" % sl] + HSEG(1 + t), ["ps%d" % bb[1]])
                for k in range(8):
                    mm(ps[bb[2]][:, :], c1w[sl][2][:, k * 128:(k + 1) * 128], ylru3[:, k, 512 * t:512 * t + 512], k == 0, k == 7,
                       ["c1w%d_2" % sl] + YL, ["ps%d" % bb[2]])
                for k in range(8):
                    mm(ps[bb[3]][:, :], c1w[sl][3][:, k * 128:(k + 1) * 128], h3[:, k, 256 + 512 * t:256 + 512 * t + 512], k == 0, k == 7,
                       ["c1w%d_3" % sl] + HSEG(1 + t), ["ps%d" % bb[3]])
                p = it % 2
                it += 1
                extra = LRUTOK if (oc == 0 and t == 0) else []
                act(sgb[p][0], ps[bb[1]][:, :], AF.Sigmoid, ["ps%d" % bb[1]], ["sg%d_0" % p] + extra)
                act(sgb[p][1], ps[bb[3]][:, :], AF.Sigmoid, ["ps%d" % bb[3]], ["sg%d_1" % p])
                tt("dve", t12[p][0], ps[bb[0]][:, :], sgb[p][0], ALU.mult, ["ps%d" % bb[0], "sg%d_0" % p], ["t12%d_0" % p])
                tt("dve", t12[p][1], ps[bb[2]][:, :], sgb[p][1], ALU.mult, ["ps%d" % bb[2], "sg%d_1" % p], ["t12%d_1" % p])
                tt("dve", m3[:, oc, 512 * t:512 * t + 512], t12[p][0], t12[p][1], ALU.add, ["t12%d_0" % p, "t12%d_1" % p],
                   ["m%d_%d" % (oc, t)])
        MT = lambda tl: ["m%d_%d" % (k, t) for k in range(8) for t in tl]
        if "m" in dbg:
            P.dma("sp", dump("m", mbuf, BF16)[:, :], mbuf, reads=MT(range(4)), slot="dbg_mm")
        if stop == "C1":
            P.emit()
            return nc, dbg_out

        cs = Carve(S_C2, R_S[0] + R_S[1] - S_C2)
        sq2 = [cs.f(640), cs.f(640)]
        tm2 = [cs.f(640), cs.f(640)]
        sd2 = cs.f(640)
        rstd2 = cs.f(640)
        gpad = [cs.f(660) for _ in range(4)]
        mixB = cs.f(8 * 640)
        ssum2 = cs.f(640)
        HY_OLD = [t for si in range(5) for t in HSEG(si)] + [t for tl in range(4) for t in YP(tl)] + YL
        cb_ = Carve(R_H[0], R_H[1] + R_Y[1])
        xt1 = cb_.f(8 * 640)
        mixb = cb_.f(8 * 640)
        hf = cb_.b(8 * 640)
        abuf = cb_.b(24 * 512)
        wupb = [cb_.b(2048) for _ in range(3)]
        wdnb = [cb_.b(3072) for _ in range(2)]
        xt13 = xt1.rearrange("p (c t) -> p c t", t=640)
        mixbufs = [mixb, mixB]
        mix3s = [mb.rearrange("p (c t) -> p c t", t=640) for mb in mixbufs]
        f3s = [mb[:, 0:8 * 512].rearrange("p (c t) -> p c t", t=512) for mb in mixbufs]
        hf3 = hf.rearrange("p (c t) -> p c t", t=640)
        a3 = abuf.rearrange("p (c t) -> p c t", t=512)
        cwd = Carve(W_DYN, R_W[0] + R_W[1] - W_DYN)
        accb = [cwd.f(512) for _ in range(4)]
        wupb.append(cwd.b(2048))
        woutb = [cwd.b(1024) for _ in range(3)]
        wo_ctr = [0]
        C1W_OLD = ["c1w%d_%d" % (s, i) for s in range(2) for i in range(4)]
        oT3 = outT.rearrange("(c p) t -> p c t", p=128)
        memset(gpad[0], 0.0, ["gpad0"] + ["sg%d_%d" % (p, i) for p in range(2) for i in range(2)] + ["t12%d_%d" % (p, i) for p in range(2) for i in range(2)])
        for gi in range(1, 4):
            memset(gpad[gi], 0.0, ["gpad%d" % gi])
        first_hy = [True]
        wup_ctr = [0]
        wdn_ctr = [0]
        dg_ctr = [0]
        gp_ctr = [0]

        def norm_rstd(src_fn, ntok, reads_fn, hook=None):
            subs = [(0, min(512, ntok))] + ([(512, ntok - 512)] if ntok > 512 else [])
            for c in range(8):
                if c == 0:
                    act(ssum2[:, 0:ntok], src_fn(c), AF.Square, reads_fn(c), ["ssum2"])
                else:
                    sq = sq2[c % 2]
                    act(sq[:, 0:ntok], src_fn(c), AF.Square, reads_fn(c), ["sq2_%d" % (c % 2)])
                    tt("dve", ssum2[:, 0:ntok], ssum2[:, 0:ntok], sq[:, 0:ntok], ALU.add, ["ssum2", "sq2_%d" % (c % 2)], ["ssum2"])
                if hook is not None:
                    hook(c)
            for (so, sn) in subs:
                b = nextbank()
                mm(ps[b][:, 0:sn], ones, ssum2[:, so:so + sn], True, True, ["ones", "ssum2"], ["ps%d" % b])
                act(sd2[:, so:so + sn], ps[b][:, 0:sn], AF.Ln, ["ps%d" % b], ["sd2"], scale=1.0 / D, bias=EPS)
            act(rstd2[:, 0:ntok], sd2[:, 0:ntok], AF.Exp, ["sd2"], ["rstd2"], scale=-0.5)

        def geom(tl):
            r0 = max(8 * tl - 1, 0)
            r1 = min(8 * tl + 9, 32)
            ntok = (r1 - r0) * 64
            return r0, ntok, r0 * 64, [(0, 512)] + [(512, ntok - 512)]

        first_mix = [True]

        def mix_units(tl):
            r0, ntok, tok0, subs = geom(tl)
            par = tl % 2
            mtl = sorted({min(3, (tok0 + so) // 512) for so, sn in subs} | {min(3, (tok0 + so + sn - 1) // 512) for so, sn in subs})
            units = []
            for oc in range(8):
                for (so, sn) in subs:
                    st_ = {}

                    def pe_fn(oc=oc, so=so, sn=sn, st_=st_):
                        if so == 0:
                            wi = wo_ctr[0] % 3
                            wo_ctr[0] += 1
                            extra = (LRUTOK + C1W_OLD) if first_mix[0] else []
                            P.dma("sp", woutb[wi], woutS[oc], reads=["wc_all"], writes=["wo%d" % wi] + extra, slot="wo%d" % wi)
                            wo_cur[0] = wi
                        wi = wo_cur[0]
                        b = nextbank()
                        st_["b"] = b
                        for k in range(8):
                            mm(ps[b][:, 0:sn], woutb[wi][:, k * 128:(k + 1) * 128],
                               m3[:, k, tok0 + so:tok0 + so + sn], k == 0, k == 7, ["wo%d" % wi] + MT(mtl), ["ps%d" % b])

                    def act_fn(oc=oc, so=so, sn=sn, st_=st_):
                        b = st_["b"]
                        extra2 = []
                        if oc == 0 and so == 0:
                            extra2 = ["f%d_%d" % (par, q) for q in range(8)]
                            if first_mix[0] or tl == 1:
                                extra2 = extra2 + ["sg%d_%d" % (p_, i) for p_ in range(2) for i in range(2)] + ["t12%d_%d" % (p_, i) for p_ in range(2) for i in range(2)] + HY_OLD
                        first_mix[0] = False
                        act(mix3s[par][:, oc, so:so + sn], ps[b][:, 0:sn], AF.Identity, ["ps%d" % b], ["mix%d_%d" % (par, oc)] + extra2)
                    units.append((pe_fn, act_fn))
            return units

        wo_cur = [0]

        def do_mix(tl):
            for pe_fn, act_fn in mix_units(tl):
                pe_fn()
                act_fn()

        do_mix(0)
        for tl in range(4):
            r0, ntok, tok0, subs = geom(tl)
            par = tl % 2
            mix3 = mix3s[par]
            f3 = f3s[par]
            MIXT = lambda c: "mix%d_%d" % (par, c)
            FT = lambda c: "f%d_%d" % (par, c)
            co = (8 * tl - r0) * 64
            s0 = r0 - (8 * tl - 1)
            extra = HY_OLD if first_hy[0] else []
            first_hy[0] = False
            for c in range(8):
                P.dma("sp", xt13[:, c, 0:ntok], xT3[:, c, tok0:tok0 + ntok], writes=["xt1_%d" % c, "x1_%d" % c] + (extra if c == 0 else []),
                      slot="xt1_%d" % c)
            norm_rstd(lambda c: mix3[:, c, 0:ntok], ntok, lambda c: [MIXT(c)])
            pend = mix_units(tl + 1) if tl + 1 < 4 else []
            pend_pe = [u[0] for u in pend]
            pend_act = [u[1] for u in pend]
            inflight = [0]

            def pump(nact):
                for _ in range(nact):
                    if pend_act:
                        pend_act.pop(0)()
                        inflight[0] -= 1
                while pend_pe and inflight[0] < 5:
                    pend_pe.pop(0)()
                    inflight[0] += 1

            pump(0)
            for c in range(8):
                tm = tm2[c % 2]
                tt("dve", tm[:, 0:ntok], mix3[:, c, 0:ntok], rstd2[:, 0:ntok], ALU.mult, [MIXT(c), "rstd2"], ["tm2_%d" % (c % 2)])
                stt(xt13[:, c, 0:ntok], tm[:, 0:ntok], GG1[:, c:c + 1], xt13[:, c, 0:ntok], ALU.mult, ALU.add,
                    ["tm2_%d" % (c % 2), "der", "xt1_%d" % c], ["x1_%d" % c])
            pump(4)
            norm_rstd(lambda c: xt13[:, c, 0:ntok], ntok, lambda c: ["x1_%d" % c], hook=lambda c: pump(1))
            pump(4)
            for c in range(8):
                tm = tm2[c % 2]
                tt("dve", tm[:, 0:ntok], xt13[:, c, 0:ntok], rstd2[:, 0:ntok], ALU.mult, ["x1_%d" % c, "rstd2"], ["tm2_%d" % (c % 2)])
                act(hf3[:, c, 0:ntok], tm[:, 0:ntok], AF.Identity, ["tm2_%d" % (c % 2), "der", "modfm"], ["hf%d" % c],
                    bias=MOD(3, c, 0), scale=A2[:, c:c + 1])
            while pend_act or pend_pe:
                pump(1)
            HF = ["hf%d" % c for c in range(8)]
            if tl == 3:
                for gi in range(4):
                    memset(gpad[gi].rearrange("p (r c) -> p r c", c=66)[:, 9, :], 0.0, ["gpad%d" % gi])
            def ffn_front(p):
                st = []
                for i in range(2):
                    j = 2 * p + i
                    q = j % 4
                    wi = wup_ctr[0] % 4
                    wup_ctr[0] += 1
                    P.dma("sp", wupb[wi], wupS[j], reads=["wc_all"], writes=["wup%d" % wi], slot="wup%d" % wi)
                    gp3 = gpad[q].rearrange("p (r c) -> p r c", c=66)
                    wg = wupb[wi][:, 0:1024]
                    row = s0
                    for (so, sn) in subs:
                        b = nextbank()
                        for k in range(8):
                            mm(ps[b][:, 0:sn], wg[:, k * 128:(k + 1) * 128], hf3[:, k, so:so + sn], k == 0, k == 7,
                               ["wup%d" % wi] + HF, ["ps%d" % b])
                        nr = sn // 64
                        act(gp3[:, row:row + nr, 1:65], ps[b][:, 0:sn].rearrange("p (r c) -> p r c", c=64), AF.Identity,
                            ["ps%d" % b], ["gpad%d" % q])
                        row += nr
                    acc3 = accb[q].rearrange("p (r c) -> p r c", c=64)
                    act(acc3, gp3[:, 0:8, 0:64], AF.Identity, ["gpad%d" % q, "vecs"], ["acc%d" % q],
                        bias=V("ffn_conv_b", j), scale=V("ffn_conv_w", 0 * 24 + j))
                    st.append((j, q, wi, gp3, acc3))
                return st

            def ffn_taps(st, taps):
                for tap in taps:
                    dy, dx = tap // 3, tap % 3
                    for (j, q, wi, gp3, acc3) in st:
                        stt(acc3, gp3[:, dy:dy + 8, dx:dx + 64], V("ffn_conv_w", tap * 24 + j), acc3, ALU.mult, ALU.add,
                            ["gpad%d" % q, "acc%d" % q, "vecs"], ["acc%d" % q])

            def ffn_u(st):
                out = []
                for (j, q, wi, gp3, acc3) in st:
                    wu = wupb[wi][:, 1024:2048]
                    bu = nextbank()
                    for k in range(8):
                        mm(ps[bu][:, :], wu[:, k * 128:(k + 1) * 128], hf3[:, k, co:co + 512], k == 0, k == 7,
                           ["wup%d" % wi] + HF, ["ps%d" % bu])
                    out.append(bu)
                return out

            def ffn_gelu(st):
                for (j, q, wi, gp3, acc3) in st:
                    act(accb[q], accb[q], AF.Gelu_apprx_tanh, ["acc%d" % q], ["acc%d" % q])

            def ffn_mult(st, bus):
                for (j, q, wi, gp3, acc3), bu in zip(st, bus):
                    tt("dve", a3[:, j, :], accb[q], ps[bu][:, :], ALU.mult, ["acc%d" % q, "ps%d" % bu], ["a%d" % j])

            prev = None
            for p in range(12):
                st = ffn_front(p)
                if prev is not None:
                    ffn_gelu(prev[0])
                ffn_taps(st, range(1, 2))
                if prev is not None:
                    ffn_mult(*prev)
                ffn_taps(st, range(2, 9))
                bus = ffn_u(st)
                prev = (st, bus)
            ffn_gelu(prev[0])
            ffn_mult(*prev)
            AT = ["a%d" % j for j in range(24)]
            for oc in range(8):
                wi = wdn_ctr[0] % 2
                wdn_ctr[0] += 1
                P.dma("sp", wdnb[wi], wdnS[oc], reads=["wc_all"], writes=["wdn%d" % wi], slot="wdn%d" % wi)
                b = nextbank()
                for k in range(24):
                    mm(ps[b][:, :], wdnb[wi][:, k * 128:(k + 1) * 128], a3[:, k, :], k == 0, k == 23, ["wdn%d" % wi] + AT, ["ps%d" % b])
                act(f3[:, oc, :], ps[b][:, :], AF.Identity, ["ps%d" % b], [FT(oc)] + ([MIXT(q) for q in range(8)] if oc == 0 else []))
                if oc == 0:
                    act(ssum2[:, 0:512], f3[:, oc, :], AF.Square, [FT(oc)], ["ssum2"])
                else:
                    sq = sq2[oc % 2]
                    act(sq[:, 0:512], f3[:, oc, :], AF.Square, [FT(oc)], ["sq2_%d" % (oc % 2)])
                    tt("dve", ssum2[:, 0:512], ssum2[:, 0:512], sq[:, 0:512], ALU.add, ["ssum2", "sq2_%d" % (oc % 2)], ["ssum2"])
            bss = nextbank()
            mm(ps[bss][:, :], ones, ssum2[:, 0:512], True, True, ["ones", "ssum2"], ["ps%d" % bss])
            act(sd2[:, 0:512], ps[bss][:, :], AF.Ln, ["ps%d" % bss], ["sd2"], scale=1.0 / D, bias=EPS)
            act(rstd2[:, 0:512], sd2[:, 0:512], AF.Exp, ["sd2"], ["rstd2"], scale=-0.5)
            for oc in range(8):
                tm = tm2[oc % 2]
                tt("dve", tm[:, 0:512], f3[:, oc, :], rstd2[:, 0:512], ALU.mult, [FT(oc), "rstd2"], ["tm2_%d" % (oc % 2)])
                stt(f3[:, oc, :], tm[:, 0:512], GG2[:, oc:oc + 1], xt13[:, oc, co:co + 512], ALU.mult, ALU.add,
                    ["tm2_%d" % (oc % 2), "der", "x1_%d" % oc, FT(oc)], [FT(oc)])
            P.dma("sp", oT3[:, :, 512 * tl:512 * tl + 512], f3, reads=[FT(oc) for oc in range(8)], slot="out")
        P.emit()
    return nc, dbg_out


def _fm(v, nch):
    v = np.asarray(v, np.float32)
    lead = v.shape[:-1]
    r = v.reshape(lead + (nch, 128))
    r = np.moveaxis(r, -1, 0)
    return np.ascontiguousarray(r.reshape(128, -1))


def _wtile(w, kch, och):
    w = np.asarray(w, np.float32)
    r = w.reshape(kch, 128, och, 128).transpose(2, 1, 0, 3)
    return np.ascontiguousarray(r.reshape(och, 128, kch * 128))


def _window_counts():
    out = np.zeros((4, 96), np.float32)
    for gi, w in enumerate(POOL_WINDOWS):
        def cnt1(n):
            pos = np.arange(n)
            lo = np.clip(pos - w // 2, 0, n)
            hi = np.clip(pos + w - w // 2, 0, n)
            return (hi - lo).astype(np.float32)
        out[gi, 0:32] = cnt1(32)
        out[gi, 32:96] = cnt1(64)
    return out.reshape(1, 384)


def prep_inputs(x, c, ctx, c_ctx, w_mod, b_mod, g_pre_mix, g_post_mix, g_pre_ffn, g_post_ffn,
                w_in, pool_w, pool_scale, lru_conv_w, lru_conv_b, lru_wa, lru_ba, lru_wx, lru_bx,
                lru_lambda, w_proj_pool, w_proj_lru, w_out, w_up, ffn_conv_w, ffn_conv_b, w_down):
    f = lambda a: np.asarray(a, np.float32)
    vec_parts = {
        "g_pre_mix": _fm(f(g_pre_mix)[0], 8), "g_post_mix": _fm(f(g_post_mix)[0], 8),
        "g_pre_ffn": _fm(f(g_pre_ffn)[0], 8), "g_post_ffn": _fm(f(g_post_ffn)[0], 8),
        "pool_scale": _fm(f(pool_scale)[0], 8), "lru_conv_w": _fm(f(lru_conv_w)[0], 8),
        "lru_conv_b": _fm(f(lru_conv_b)[0], 8), "lru_ba": _fm(f(lru_ba)[0], 8), "lru_bx": _fm(f(lru_bx)[0], 8),
        "lru_lambda": _fm(f(lru_lambda)[0], 8), "ffn_conv_w": _fm(f(ffn_conv_w)[0].reshape(9, 3072), 24),
        "ffn_conv_b": _fm(f(ffn_conv_b)[0], 24), "b_mod": _fm(f(b_mod)[0], 48),
    }
    vecs = np.ascontiguousarray(np.concatenate([vec_parts[n] for n, _ in _VEC_SPEC], axis=1))
    assert vecs.shape == (128, NV)
    wmod_h = np.ascontiguousarray(f(w_mod)[0].reshape(8, 128, 12, 512).transpose(2, 1, 0, 3).reshape(12, 128, 4096))
    win_h = _wtile(f(w_in)[0], 8, 40)
    wpp_h = _wtile(f(w_proj_pool)[0], 8, 8)
    wpl_h = _wtile(f(w_proj_lru)[0], 8, 8)
    wout_h = _wtile(f(w_out)[0], 8, 8)
    wup_t = _wtile(f(w_up)[0], 8, 48)
    wup_h = np.ascontiguousarray(np.concatenate([wup_t[0:24], wup_t[24:48]], axis=2))
    wdown_h = _wtile(f(w_down)[0], 24, 8)
    pw = f(pool_w)[0].reshape(4, 2, 128, 2, 128).transpose(2, 0, 3, 1, 4)
    poolw_h = np.ascontiguousarray(pw.reshape(128, 2048))
    lw = np.stack([f(lru_wa)[0], f(lru_wx)[0]], axis=0)
    lruw_h = np.ascontiguousarray(lw.transpose(3, 0, 1, 2, 4).reshape(128, 4096))
    shared = {"vecs": vecs, "ident": np.eye(128, dtype=np.float32), "cnt": _window_counts(), "wmod": wmod_h,
              "win": win_h, "wpp": wpp_h, "wpl": wpl_h, "wout": wout_h, "wup": wup_h, "wdown": wdown_h,
              "poolw": poolw_h, "lruw": lruw_h}
    xf, cf, ctxf, ccf = f(x), f(c), f(ctx), f(c_ctx)
    in_maps = []
    for b in range(NCORES):
        m = dict(shared)
        m["xT"] = np.ascontiguousarray(xf[b].T)
        m["ctxT"] = np.ascontiguousarray(ctxf[b].T)
        cc2 = np.stack([cf[b], ccf], axis=0)
        m["cc"] = np.ascontiguousarray(cc2.reshape(2, 8, 128).transpose(2, 1, 0).reshape(128, 16))
        in_maps.append(m)
    return in_maps


_CACHE = {}


def kernel(**inputs):
    in_maps = prep_inputs(**inputs)
    if "nc" not in _CACHE:
        _CACHE["nc"] = build_program()[0]
    nc = _CACHE["nc"]
    res = run_bass_kernel_spmd(nc, in_maps, core_ids=list(range(NCORES)))
    out = np.stack([np.asarray(r["outT"], np.float32).T for r in res.results], axis=0)
    return np.ascontiguousarray(out.astype(np.float32))
```

```python
import numpy as np
from contextlib import ExitStack
import concourse.bass as bass
import concourse.mybir as mybir
from concourse.bass_utils import run_bass_kernel_spmd

F32 = mybir.dt.float32
BF16 = mybir.dt.bfloat16
RELAX_DVE = False
AF = mybir.ActivationFunctionType
ALU = mybir.AluOpType

NCORES = 8
D = 1024
T = 2048
CT = 256
TT = T + CT
NCH = 8
EPS = 1e-6
POOL_WINDOWS = (2, 4, 8, 16)
SEGS = [(0, 256)] + [(256 + 512 * i, 512) for i in range(4)]

_VEC_SPEC = [("g_pre_mix", 8), ("g_post_mix", 8), ("g_pre_ffn", 8), ("g_post_ffn", 8), ("pool_scale", 8),
             ("lru_conv_w", 32), ("lru_conv_b", 8), ("lru_ba", 16), ("lru_bx", 16), ("lru_lambda", 16),
             ("ffn_conv_w", 216), ("ffn_conv_b", 24), ("b_mod", 48)]
VOFF = {}
_o = 0
for _n, _c in _VEC_SPEC:
    VOFF[_n] = _o
    _o += _c
NV = _o


class _Op:
    __slots__ = ("eng", "fn", "deps", "needs_inc", "sem", "val", "is_dma", "slot", "pos")

    def __init__(self, eng, fn, is_dma=False, slot=None):
        self.eng = eng
        self.fn = fn
        self.deps = []
        self.needs_inc = False
        self.sem = None
        self.val = None
        self.is_dma = is_dma
        self.slot = slot
        self.pos = -1


class Prog:
    ENGS = ("pe", "act", "dve", "pool", "sp")

    def __init__(self, nc):
        self.nc = nc
        self.q = {e: [] for e in self.ENGS}
        self.last_w = {}
        self.readers = {}
        self.all_ops = []

    def _track(self, op, reads, writes):
        deps = {}
        for t in reads:
            w = self.last_w.get(t)
            if w is not None:
                deps[id(w)] = w
        for t in writes:
            w = self.last_w.get(t)
            if w is not None:
                deps[id(w)] = w
            for r in self.readers.get(t, {}).values():
                deps[id(r)] = r
        for d in deps.values():
            if d is op:
                continue
            if (not d.is_dma) and (not op.is_dma) and d.eng == "pe" and op.eng == "pe":
                continue
            if RELAX_DVE and (not d.is_dma) and (not op.is_dma) and d.eng == "dve" and op.eng == "dve" and op.pos - d.pos >= 2:
                continue
            d.needs_inc = True
            op.deps.append(d)
        for t in writes:
            self.last_w[t] = op
            self.readers[t] = {}
        for t in reads:
            key = ("dma", op.slot) if op.is_dma else op.eng
            self.readers.setdefault(t, {})[key] = op

    def op(self, eng, fn, reads=(), writes=()):
        o = _Op(eng, fn)
        o.pos = len(self.q[eng])
        self._track(o, reads, writes)
        self.q[eng].append(o)
        self.all_ops.append(o)
        return o

    def dma(self, eng, out, in_, reads=(), writes=(), slot=None, **kw):
        o = _Op(eng, None, is_dma=True, slot=slot)
        o.fn = lambda e, s, o_=out, i_=in_, kw_=kw: e.dma_start(out=o_, in_=i_, **kw_).then_inc(s, 16)
        self._track(o, reads, writes)
        o.needs_inc = True
        self.q[eng].append(o)
        self.all_ops.append(o)
        return o

    def emit(self, final_wait_eng="sp"):
        nc = self.nc
        with ExitStack() as es:
            esem = {e: es.enter_context(nc.semaphore("sem_" + e)) for e in self.ENGS}
            slot_names = sorted({o.slot for o in self.all_ops if o.is_dma})
            ssem = {s: es.enter_context(nc.semaphore("dsem_" + s)) for s in slot_names}
            cnt = {e: 0 for e in self.ENGS}
            for e in self.ENGS:
                for o in self.q[e]:
                    if (not o.is_dma) and o.needs_inc:
                        cnt[e] += 1
                        o.sem, o.val = esem[e], cnt[e]
            scnt = {s: 0 for s in slot_names}
            for o in self.all_ops:
                if o.is_dma:
                    scnt[o.slot] += 16
                    o.sem, o.val = ssem[o.slot], scnt[o.slot]
            block = es.enter_context(nc.Block())
            final = [(ssem[s], scnt[s]) for s in slot_names]

            def run(e):
                def body(eng):
                    waited = {}
                    for o in self.q[e]:
                        for d in o.deps:
                            k = id(d.sem)
                            if waited.get(k, 0) < d.val:
                                eng.wait_ge(d.sem, d.val)
                                waited[k] = d.val
                        if o.is_dma:
                            o.fn(eng, o.sem)
                        else:
                            ins = o.fn(eng)
                            if o.needs_inc:
                                ins.then_inc(o.sem, 1)
                    if e == final_wait_eng:
                        for s, v in final:
                            if v > 0:
                                eng.wait_ge(s, v)
                return body

            block.tensor(run("pe"))
            block.scalar(run("act"))
            block.vector(run("dve"))
            block.gpsimd(run("pool"))
            block.sync(run("sp"))


def build_program(stop=None, dbg=()):
    nc = bass.Bass("TRN2", target_bir_lowering=False)
    dr = {}

    def din(name, shape, dt=F32):
        dr[name] = nc.dram_tensor(name, shape, dt, kind="ExternalInput").ap()

    din("xT", [D, T])
    din("ctxT", [D, CT])
    din("cc", [128, 16])
    din("vecs", [128, NV])
    din("ident", [128, 128])
    din("cnt", [1, 384])
    din("wmod", [12, 128, 4096])
    din("win", [40, 128, 1024])
    din("wpp", [8, 128, 1024])
    din("wpl", [8, 128, 1024])
    din("wout", [8, 128, 1024])
    din("wup", [24, 128, 2048])
    din("wdown", [8, 128, 3072])
    din("poolw", [128, 2048])
    din("lruw", [128, 4096])
    outT = nc.dram_tensor("outT", [D, T], F32, kind="ExternalOutput").ap()
    wupS = nc.dram_tensor("wupS", [24, 128, 2048], BF16, kind="Internal").ap()
    wdnS = nc.dram_tensor("wdnS", [8, 128, 3072], BF16, kind="Internal").ap()
    woutS = nc.dram_tensor("woutS", [8, 128, 1024], BF16, kind="Internal").ap()
    dbg_out = {}

    es = ExitStack()
    with es:
        AW = 53200
        arena = es.enter_context(nc.sbuf_tensor("arena", [128, AW], F32))
        psall = es.enter_context(nc.psum_tensor("psall", [128, 4096], F32))
        ps = [psall[:, i * 512:(i + 1) * 512] for i in range(8)]
        P = Prog(nc)

        def fv(off, n):
            assert off % 4 == 0 and off + 4 * n <= AW * 4, (off, n)
            return arena[:, off // 4: off // 4 + n]

        def bv(off, n):
            assert off % 4 == 0 and n % 2 == 0 and off + 2 * n <= AW * 4, (off, n)
            return arena[:, off // 4: off // 4 + n // 2].bitcast(BF16)

        class Carve:
            def __init__(self, base, size):
                self.base, self.end, self.cur = base, base + size, base

            def f(self, n):
                a = fv(self.cur, n)
                self.cur += 4 * n
                self.cur = (self.cur + 63) // 64 * 64
                assert self.cur <= self.end, ("carve overflow", self.cur, self.end)
                return a

            def b(self, n):
                a = bv(self.cur, n)
                self.cur += 2 * n
                self.cur = (self.cur + 63) // 64 * 64
                assert self.cur <= self.end, ("carve overflow", self.cur, self.end)
                return a

        KB = 1024
        R_H = (0, 36 * KB)
        R_Y = (36 * KB, 64 * KB)
        R_S = (100 * KB, 84 * KB)
        R_W = (184 * KB, AW * 4 - 184 * KB)

        bank_ctr = [0]

        def nextbank():
            b = bank_ctr[0] % 8
            bank_ctr[0] += 1
            return b

        def mm(out, lhsT, rhs, start, stop, reads, writes):
            P.op("pe", lambda e: e.matmul(out, lhsT, rhs, start=start, stop=stop), reads, writes)

        def act(out, in_, func, reads, writes, bias=None, scale=None):
            kw = {}
            if bias is not None:
                kw["bias"] = bias
            if scale is not None:
                kw["scale"] = scale
            P.op("act", lambda e: e.activation(out=out, in_=in_, func=func, **kw), reads, writes)

        def evac_latent(banks, dst, o0, func, reads_extra, toks, **kw):
            i = 0
            while i < len(banks):
                j = i
                while j + 1 < len(banks) and banks[j + 1] == banks[j] + 1:
                    j += 1
                r = j - i + 1
                act(dst[:, o0 + 512 * i:o0 + 512 * (i + r)], psall[:, banks[i] * 512:(banks[i] + r) * 512], func,
                    ["ps%d" % b_ for b_ in banks[i:j + 1]] + list(reads_extra), list(toks[i:j + 1]), **kw)
                i = j + 1

        def tt(eng, out, in0, in1, op, reads, writes):
            P.op(eng, lambda e: e.tensor_tensor(out=out, in0=in0, in1=in1, op=op), reads, writes)

        def ts(eng, out, in0, s1, op0, reads, writes, s2=None, op1=None):
            if op1 is None:
                P.op(eng, lambda e: e.tensor_scalar(out=out, in0=in0, scalar1=s1, scalar2=None, op0=op0), reads, writes)
            else:
                P.op(eng, lambda e: e.tensor_scalar(out=out, in0=in0, scalar1=s1, scalar2=s2, op0=op0, op1=op1), reads, writes)

        def stt(out, in0, scalar, in1, op0, op1, reads, writes):
            P.op("dve", lambda e: e.scalar_tensor_tensor(out=out, in0=in0, scalar=scalar, in1=in1, op0=op0, op1=op1), reads, writes)

        def memset(ap, val, writes):
            P.op("pool", lambda e: e.memset(ap, val), (), writes)

        def dump(name, ap, dt=F32):
            shape = [ap.shape[0], int(np.prod(ap.shape[1:]))]
            t = nc.dram_tensor("dbg_" + name, shape, dt, kind="ExternalOutput").ap()
            dbg_out[name] = t
            return t

        cw = Carve(*R_W)
        vecs = cw.f(NV)
        ident = cw.f(128)
        ones = cw.f(128)
        identb = cw.b(128)
        ccs = cw.f(16)
        ssil = cw.f(16)
        modfm = cw.f(96)
        der = cw.f(64)
        lrud = cw.f(64)
        cw2 = cw.f(32)
        cn1 = cw.f(384)
        W_DYN = cw.cur

        def V(name, i=0):
            o = VOFF[name] + i
            return vecs[:, o:o + 1]

        def Vs(name, i0, n):
            o = VOFF[name] + i0
            return vecs[:, o:o + n]

        P.dma("sp", vecs, dr["vecs"][:, :], writes=["vecs"], slot="c_vecs")
        P.dma("sp", ident, dr["ident"][:, :], writes=["ident"], slot="c_ident")
        P.dma("sp", ccs, dr["cc"][:, :], writes=["cc"], slot="c_cc")
        memset(ones, 1.0, ["ones"])
        P.op("dve", lambda e: e.tensor_copy(out=identb, in_=ident), ["ident"], ["identb"])
        act(ssil, ccs, AF.Silu, ["cc"], ["ssil"])
        s3 = ssil.rearrange("p (k r) -> p k r", r=2)

        cs = Carve(*R_S)
        wmb = [cs.f(4096), cs.f(4096)]
        modrow = cs.f(6144)
        xa4 = cs.f(8 * 512)
        rstdA = cs.f(TT)
        cy = Carve(*R_Y)
        xa = [cy.f(8 * 256), cy.f(8 * 512), cy.f(8 * 512), cy.f(8 * 512), xa4]
        sqb = [cy.f(512), cy.f(512)]
        ssumA = cy.f(512)
        sdb = cy.f(512)
        xT3 = dr["xT"].rearrange("(c p) t -> p c t", p=128)
        cT3 = dr["ctxT"].rearrange("(c p) t -> p c t", p=128)
        xa3 = [xa[si].rearrange("p (c t) -> p c t", t=SEGS[si][1]) for si in range(5)]

        def load_x(si):
            o, n = SEGS[si]
            src = cT3[:, :, :] if si == 0 else xT3[:, :, o - 256:o - 256 + n]
            P.dma("sp", xa3[si], src, writes=["xa%d" % si], slot="xa%d" % si)

        def stats(si):
            o, n = SEGS[si]
            tk = "xa%d" % si
            for c in range(8):
                if c == 0:
                    act(ssumA[:, 0:n], xa3[si][:, c, :], AF.Square, [tk], ["ssumA"])
                else:
                    sq = sqb[c % 2]
                    act(sq[:, 0:n], xa3[si][:, c, :], AF.Square, [tk], ["sq%d" % (c % 2)])
                    tt("dve", ssumA[:, 0:n], ssumA[:, 0:n], sq[:, 0:n], ALU.add, ["ssumA", "sq%d" % (c % 2)], ["ssumA"])
            b = nextbank()
            mm(ps[b][:, 0:n], ones, ssumA[:, 0:n], True, True, ["ones", "ssumA"], ["ps%d" % b])
            act(sdb[:, 0:n], ps[b][:, 0:n], AF.Ln, ["ps%d" % b], ["sd"], scale=1.0 / D, bias=EPS)
            act(rstdA[:, o:o + n], sdb[:, 0:n], AF.Exp, ["sd"], ["rstdA%d" % si], scale=-0.5)
            for c in range(8):
                tt("dve", xa3[si][:, c, :], xa3[si][:, c, :], rstdA[:, o:o + n], ALU.mult, [tk, "rstdA%d" % si], ["xn%d_%d" % (si, c)])

        xsched = {1: 0, 2: 1, 4: 2, 6: 3, 8: 4}
        ssched = {3: 0, 5: 1, 7: 2, 9: 3, 11: 4}
        for blk in range(12):
            buf = wmb[blk % 2]
            tk = "wm%d" % (blk % 2)
            P.dma("sp", buf, dr["wmod"][blk], writes=[tk], slot=tk)
            if blk in xsched:
                load_x(xsched[blk])
            b = nextbank()
            for k in range(8):
                mm(ps[b][0:2, 0:512], s3[:, k, :], buf[:, k * 512:(k + 1) * 512], k == 0, k == 7,
                   ["ssil", tk], ["ps%d" % b])
            act(modrow[0:2, blk * 512:(blk + 1) * 512], ps[b][0:2, 0:512], AF.Identity, ["ps%d" % b], ["modrow%d" % blk])
            if blk in ssched:
                stats(ssched[blk])
        b = nextbank()
        for oc in range(48):
            mm(ps[b][:, 2 * oc:2 * oc + 2], modrow[0:2, oc * 128:(oc + 1) * 128], ident[0:2, 0:2], True, True,
               ["modrow%d" % (oc // 4), "ident"], ["ps%d" % b])
        act(modfm, ps[b][:, 0:96], AF.Identity, ["ps%d" % b], ["modfm"])
        mod3 = modfm.rearrange("p (c r) -> p c r", r=2)
        for r in range(2):
            tt("dve", mod3[:, :, r], mod3[:, :, r], Vs("b_mod", 0, 48), ALU.add, ["modfm", "vecs"], ["modfm"])

        def MOD(which, c, r=0):
            return mod3[:, which * 8 + c, r:r + 1]

        A1 = der[:, 0:8]
        A1c = der[:, 8:16]
        GG1 = der[:, 16:24]
        A2 = der[:, 24:32]
        GG2 = der[:, 32:40]
        stt(A1, mod3[:, 8:16, 0], 1.0, Vs("g_pre_mix", 0, 8), ALU.add, ALU.mult, ["modfm", "vecs"], ["der"])
        stt(A1c, mod3[:, 8:16, 1], 1.0, Vs("g_pre_mix", 0, 8), ALU.add, ALU.mult, ["modfm", "vecs"], ["der"])
        tt("dve", GG1, mod3[:, 16:24, 0], Vs("g_post_mix", 0, 8), ALU.mult, ["modfm", "vecs"], ["der"])
        stt(A2, mod3[:, 32:40, 0], 1.0, Vs("g_pre_ffn", 0, 8), ALU.add, ALU.mult, ["modfm", "vecs"], ["der"])
        tt("dve", GG2, mod3[:, 40:48, 0], Vs("g_post_ffn", 0, 8), ALU.mult, ["modfm", "vecs"], ["der"])
        lam = Vs("lru_lambda", 0, 16)
        le = lrud[:, 0:16]
        lsp = lrud[:, 16:32]
        ls1 = lrud[:, 32:48]
        act(le, lam, AF.Exp, ["vecs"], ["le"], scale=-1.0)
        act(lsp, le, AF.Ln, ["le"], ["lsp"], bias=1.0)
        ts("dve", ls1, lsp, -4.0, ALU.mult, ["lsp"], ["hs1"])
        hs1 = ls1
        hbias = cw2
        ts("dve", hbias, Vs("lru_ba", 0, 32), 0.5, ALU.mult, ["vecs"], ["hbias"])

        ch = Carve(*R_H)
        h = ch.b(8 * TT)
        h3 = h.rearrange("p (c t) -> p c t", t=TT)
        for si, (o, n) in enumerate(SEGS):
            for c in range(8):
                if si == 0:
                    sc_, bi_ = A1c[:, c:c + 1], MOD(0, c, 1)
                else:
                    sc_, bi_ = A1[:, c:c + 1], MOD(0, c, 0)
                act(h3[:, c, o:o + n], xa3[si][:, c, :], AF.Identity, ["xn%d_%d" % (si, c), "der", "modfm"], ["h%d_%d" % (c, si)],
                    bias=bi_, scale=sc_)
        HSEG = lambda si: ["h%d_%d" % (c, si) for c in range(8)]

        if "h" in dbg:
            P.dma("sp", dump("h", h, BF16)[:, :], h, reads=[t for si in range(5) for t in HSEG(si)], slot="dbg_h")
            P.dma("sp", dump("modfm", modfm)[:, :], modfm, reads=["modfm"], slot="dbg_m")
        if stop == "A":
            P.emit()
            return nc, dbg_out

        cy = Carve(*R_Y)
        ypool = cy.b(8 * T)
        ylru = cy.b(8 * T)
        ypool3 = ypool.rearrange("p (c t) -> p c t", t=T)
        ylru3 = ylru.rearrange("p (c t) -> p c t", t=T)
        YTOK = ["xa%d" % si for si in range(4)] + ["xn%d_%d" % (si, c) for si in range(4) for c in range(8)] + ["sq0", "sq1", "sd", "ssumA"]
        y_first = [True]

        cwd = Carve(W_DYN, R_W[0] + R_W[1] - W_DYN)
        lruw = cwd.b(4096)
        poolw = cwd.b(2048)
        NWIN = 3
        winb = [cwd.b(1024) for _ in range(NWIN)]
        win_ctr = [0]
        P.dma("pool", lruw[:, 0:2048], dr["lruw"][:, 0:2048], writes=["lruw"], slot="w_lruw")
        P.dma("pool", lruw[:, 2048:4096], dr["lruw"][:, 2048:4096], writes=["lruw"], slot="w_lruw")
        P.dma("pool", poolw, dr["poolw"][:, :], writes=["poolw"], slot="w_poolw")

        def load_win(oc):
            i = win_ctr[0] % NWIN
            win_ctr[0] += 1
            P.dma("pool", winb[i], dr["win"][oc], writes=["win%d" % i], slot="win%d" % i)
            return winb[i], "win%d" % i

        def proj_seg(wt, wtk, si, b):
            o, n = SEGS[si]
            for k in range(8):
                mm(ps[b][:, 0:n], wt[:, k * 128:(k + 1) * 128], h3[:, k, o:o + n], k == 0, k == 7,
                   [wtk] + HSEG(si), ["ps%d" % b])

        cast_plan = []
        for oc in range(8):
            cast_plan.append((woutS[oc], dr["wout"][oc]))
        for j in range(24):
            cast_plan.append((wupS[j], dr["wup"][j]))
        for oc in range(8):
            cast_plan.append((wdnS[oc][:, 0:2048], dr["wdown"][oc][:, 0:2048]))
            cast_plan.append((wdnS[oc][:, 2048:3072], dr["wdown"][oc][:, 2048:3072]))
        cast_ctr = [0]

        def issue_casts(k):
            for _ in range(k):
                i = cast_ctr[0]
                if i >= len(cast_plan):
                    return
                cast_ctr[0] += 1
                dst, src = cast_plan[i]
                wr = ["wc%d" % i] + (["wc_all"] if i == len(cast_plan) - 1 else [])
                P.dma("pool", dst, src, writes=wr, slot="wcast")

        STOK_A = (["wm0", "wm1"] + ["modrow%d" % i for i in range(12)] + ["xa4"] + ["xn4_%d" % c for c in range(8)]
                  + ["rstdA%d" % si for si in range(5)])
        cs = Carve(*R_S)
        cntb = cs.f(T)
        PQ = [[cs.f(47 * 79), cs.f(47 * 79)] for _ in range(2)]
        dbf = [cs.b(T), cs.b(T)]
        PQTOK = ["pq%d_%d" % (cl, i) for cl in range(2) for i in range(2)]
        cnt3 = cntb.rearrange("p (r c) -> p r c", c=64)
        first_S = [True]
        nb_ctr = [0]

        def nextB():
            b = 4 + nb_ctr[0] % 4
            nb_ctr[0] += 1
            return b

        def proj_cl0(g_):
            wt_, wtk_ = load_win(2 * g_)
            for t in range(4):
                proj_seg(wt_, wtk_, 1 + t, t)
            return (wt_, wtk_)

        pre_cl0 = proj_cl0(0)
        P.dma("sp", cn1, dr["cnt"][0:1, :].partition_broadcast(128), writes=["cn1"], slot="c_cn1")
        P.op("dve", lambda e: e.reciprocal(out=cn1, in_=cn1), ["cn1"], ["cn1"])
        for g, w in enumerate(POOL_WINDOWS):
            hw = w // 2
            Hp, Wp = 32 + w - 1, 64 + w - 1
            extra = STOK_A if first_S[0] else []
            first_S[0] = False
            ir = cn1[:, g * 96: g * 96 + 32].unsqueeze(2).broadcast_to([128, 32, 64])
            ic = cn1[:, g * 96 + 32: g * 96 + 96].unsqueeze(1).broadcast_to([128, 32, 64])
            tt("dve", cnt3, ir, ic, ALU.mult, ["cn1"], ["cnt"] + extra)
            views = []
            for cl in range(2):
                v = [PQ[cl][i][:, 0:Hp * Wp].rearrange("p (r c) -> p r c", c=Wp) for i in range(2)]
                views.append(v)
                Pv, Qv = v
                memset(Pv[:, 0:hw, :], 0.0, ["pq%d_0" % cl] + extra)
                if hw > 1:
                    memset(Pv[:, hw + 32:Hp, :], 0.0, ["pq%d_0" % cl])
                memset(Pv[:, hw:hw + 32, 0:hw], 0.0, ["pq%d_0" % cl])
                if hw > 1:
                    memset(Pv[:, hw:hw + 32, hw + 64:Wp], 0.0, ["pq%d_0" % cl])
                if w in (2, 8):
                    memset(Qv[:, 0:hw, 0:64], 0.0, ["pq%d_1" % cl])
                    if hw > 1:
                        memset(Qv[:, hw + 32:Hp, 0:64], 0.0, ["pq%d_1" % cl])
            wts = [pre_cl0]
            for t in range(4):
                act(views[0][0][:, hw + 8 * t: hw + 8 * t + 8, hw:hw + 64], ps[t][:, :].rearrange("p (r c) -> p r c", c=64),
                    AF.Identity, ["ps%d" % t], ["pq0_0"])
            wt, wtk = load_win(2 * g + 1)
            wts.append((wt, wtk))
            for t in range(4):
                b = nextB()
                proj_seg(wt, wtk, 1 + t, b)
                act(views[1][0][:, hw + 8 * t: hw + 8 * t + 8, hw:hw + 64], ps[b][:, :].rearrange("p (r c) -> p r c", c=64),
                    AF.Identity, ["ps%d" % b], ["pq1_0"])
            if g + 1 < 4:
                pre_cl0 = proj_cl0(g + 1)
            state = [dict(cur=0, ln=Wp, rows=Hp) for _ in range(2)]
            k = 1
            while k < w:
                for cl in range(2):
                    st = state[cl]
                    src, dst = views[cl][st["cur"]], views[cl][1 - st["cur"]]
                    nl = st["ln"] - k
                    tt("dve", dst[:, hw:hw + 32, 0:nl], src[:, hw:hw + 32, 0:nl], src[:, hw:hw + 32, k:k + nl], ALU.add,
                       ["pq%d_%d" % (cl, st["cur"])], ["pq%d_%d" % (cl, 1 - st["cur"])])
                    st["cur"], st["ln"] = 1 - st["cur"], nl
                k *= 2
            k = 1
            while k < w:
                for cl in range(2):
                    st = state[cl]
                    src, dst = views[cl][st["cur"]], views[cl][1 - st["cur"]]
                    nr = st["rows"] - k
                    tt("dve", dst[:, 0:nr, 0:64], src[:, 0:nr, 0:64], src[:, k:k + nr, 0:64], ALU.add,
                       ["pq%d_%d" % (cl, st["cur"])], ["pq%d_%d" % (cl, 1 - st["cur"])])
                    st["cur"], st["rows"] = 1 - st["cur"], nr
                k *= 2
            for cl in range(2):
                st = state[cl]
                assert st["ln"] == 64 and st["rows"] == 32
                src, oth = views[cl][st["cur"]], views[cl][1 - st["cur"]]
                tt("dve", oth[:, 0:32, 0:64], src[:, 0:32, 0:64], cnt3, ALU.mult, ["pq%d_%d" % (cl, st["cur"]), "cnt"],
                   ["pq%d_%d" % (cl, 1 - st["cur"])])
                wt, wtk = wts[cl]
                for t in range(4):
                    b = nextB()
                    proj_seg(wt, wtk, 1 + t, b)
                    tt("dve", dbf[cl][:, 512 * t:512 * t + 512].rearrange("p (r c) -> p r c", c=64), oth[:, 8 * t:8 * t + 8, 0:64],
                       ps[b][:, :].rearrange("p (r c) -> p r c", c=64), ALU.subtract, ["pq%d_%d" % (cl, 1 - st["cur"]), "ps%d" % b],
                       ["dbf%d_%d" % (cl, t)])
            for ocl in range(2):
                oc = 2 * g + ocl
                for t in range(4):
                    b = nextB()
                    for k in range(2):
                        idx = ((g * 2 + ocl) * 2 + k) * 128
                        mm(ps[b][:, :], poolw[:, idx:idx + 128], dbf[k][:, 512 * t:512 * t + 512], k == 0, k == 1,
                           ["poolw", "dbf%d_%d" % (k, t)], ["ps%d" % b])
                    extra = YTOK if y_first[0] else []
                    y_first[0] = False
                    act(ypool3[:, oc, 512 * t:512 * t + 512], ps[b][:, :], AF.Identity, ["ps%d" % b, "vecs"],
                        ["ypool%d_%d" % (oc, t)] + extra, scale=V("pool_scale", oc))
            issue_casts(5)
        if "ypool" in dbg:
            P.dma("sp", dump("ypool", ypool, BF16)[:, :], ypool,
                  reads=["ypool%d_%d" % (oc, t) for oc in range(8) for t in range(4)], slot="dbg_yp")
        if stop == "Bp":
            P.emit()
            return nc, dbg_out

        cs = Carve(*R_S)
        UPW = 2320
        LOFF = 264
        upad = cs.b(UPW)
        dgw = cs.b(32 * 128)
        o_m2b = cs.cur
        m2b = cs.f(TT)
        xcb = bv(o_m2b, TT)
        xc = cs.f(TT)
        m2f = cs.f(TT)
        ra = [cs.f(TT), cs.f(TT)]
        ib = [cs.f(TT), cs.f(TT)]
        gel = cs.f(T)
        m2 = [m2f, m2b]
        m2tok = ["m2f", "m2b"]
        POOLTOK = ["cnt"] + PQTOK + ["dbf%d_%d" % (cl, t) for cl in range(2) for t in range(4)]
        UP = ["upad%d" % si for si in range(5)]
        memset(upad, 0.0, UP + POOLTOK)
        for k in range(4):
            for n in range(8):
                i = k * 8 + n
                ts("dve", dgw[:, i * 128:(i + 1) * 128], ident, V("lru_conv_w", i), ALU.mult, ["ident", "vecs"], ["dgw"] + (POOLTOK if i == 0 else []))
        XC = ["xc%d" % si for si in range(5)]
        XCB = ["xcb%d" % si for si in range(5)]
        RA = lambda d: ["ra%d_%d" % (d, si) for si in range(5)]
        IB = lambda d: ["ib%d_%d" % (d, si) for si in range(5)]
        win_next = load_win(8 + 0)
        for n in range(8):
            wt, wtk = win_next
            for si, (o, nn) in enumerate(SEGS):
                b = nextbank()
                proj_seg(wt, wtk, si, b)
                po = 2 + o if si == 0 else LOFF + (o - 256)
                act(upad[:, po:po + nn], ps[b][:, 0:nn], AF.Identity, ["ps%d" % b], ["upad%d" % si])
            for si, (o, nn) in enumerate(SEGS):
                base = o if si == 0 else LOFF - 2 + (o - 256)
                b = nextbank()
                for k in range(4):
                    i = k * 8 + n
                    mm(ps[b][:, 0:nn], dgw[:, i * 128:(i + 1) * 128], upad[:, base + k:base + k + nn], k == 0, k == 3,
                       ["dgw"] + ([UP[0]] if si == 0 else [UP[j_] for j_ in (si - 1, si, si + 1) if 1 <= j_ <= 4]), ["ps%d" % b])
                act(xcb[:, o:o + nn], ps[b][:, 0:nn], AF.Identity, ["ps%d" % b, "vecs"], ["xcb%d" % si] + (["m2b"] if si == 0 else []),
                    bias=V("lru_conv_b", n))
                act(xc[:, o:o + nn], ps[b][:, 0:nn], AF.Identity, ["ps%d" % b, "vecs"], ["xc%d" % si], bias=V("lru_conv_b", n))
            for dr_ in range(2):
                for kind, dst, tkf in ((0, ra[dr_], RA(dr_)), (1, ib[dr_], IB(dr_))):
                    widx = ((kind * 2 + dr_) * 8 + n) * 128
                    hb_ = hbias[:, kind * 16 + dr_ * 8 + n: kind * 16 + dr_ * 8 + n + 1]
                    b = nextbank()
                    mm(ps[b][:, 0:256], lruw[:, widx:widx + 128], xcb[:, 0:256], True, True, ["lruw", "xcb0"], ["ps%d" % b])
                    act(dst[:, 0:256], ps[b][:, 0:256], AF.Tanh, ["ps%d" % b, "hbias"], [tkf[0]], bias=hb_, scale=0.5)
                    gb = []
                    for si in range(1, 5):
                        o, nn = SEGS[si]
                        b = nextbank()
                        gb.append(b)
                        mm(ps[b][:, 0:nn], lruw[:, widx:widx + 128], xcb[:, o:o + nn], True, True, ["lruw", "xcb%d" % si], ["ps%d" % b])
                    evac_latent(gb, dst, 256, AF.Tanh, ["hbias"], tkf[1:5], bias=hb_, scale=0.5)
                hs = hs1[:, dr_ * 8 + n: dr_ * 8 + n + 1]
                act(ra[dr_], ra[dr_], AF.Exp, RA(dr_) + ["hs1"], RA(dr_), bias=hs, scale=hs)
                act(m2[dr_], ra[dr_], AF.Square, RA(dr_), [m2tok[dr_]] + (XCB if dr_ == 1 else []))
                act(m2[dr_], m2[dr_], AF.Sqrt, [m2tok[dr_]], [m2tok[dr_]], scale=-0.25, bias=0.25)
            for dr_ in range(2):
                stt(ib[dr_], ib[dr_], 1.0, xc, ALU.add, ALU.mult, IB(dr_) + XC, IB(dr_))
                tt("dve", ib[dr_], ib[dr_], m2[dr_], ALU.mult, IB(dr_) + [m2tok[dr_]], IB(dr_))
                if dr_ == 0:
                    P.op("dve", lambda e: e.tensor_tensor_scan(out=ib[0], data0=ra[0], data1=ib[0], initial=0.0, op0=ALU.mult, op1=ALU.add),
                         RA(0) + IB(0), IB(0))
                else:
                    P.op("dve", lambda e: e.tensor_tensor_scan(out=ib[1][:, 0:256][:, ::-1], data0=ra[1][:, 0:256][:, ::-1],
                                                                data1=ib[1][:, 0:256][:, ::-1], initial=0.0, op0=ALU.mult, op1=ALU.add),
                         RA(1) + IB(1), IB(1))
                    P.op("dve", lambda e: e.tensor_tensor_scan(out=ib[1][:, 256:TT][:, ::-1], data0=ra[1][:, 256:TT][:, ::-1],
                                                                data1=ib[1][:, 256:TT][:, ::-1], initial=ib[1][:, 0:1], op0=ALU.mult, op1=ALU.add),
                         RA(1) + IB(1), IB(1))
            wt, wtk = load_win(16 + n)
            if n + 1 < 8:
                win_next = load_win(8 + n + 1)
            issue_casts(5)
            gb = []
            for t in range(4):
                b = nextbank()
                gb.append(b)
                proj_seg(wt, wtk, 1 + t, b)
            evac_latent(gb, gel, 0, AF.Gelu_apprx_tanh, [], ["gel%d" % t for t in range(4)])
            GEL = ["gel%d" % t for t in range(4)]
            tt("dve", ib[0][:, 256:TT], ib[0][:, 256:TT], ib[1][:, 256:TT], ALU.add, IB(0) + IB(1), IB(0))
            tt("dve", ylru3[:, n, :], ib[0][:, 256:TT], gel, ALU.mult, IB(0) + GEL, ["ylru%d" % n] + (YTOK if n == 0 else []))
        if "ylru" in dbg:
            P.dma("sp", dump("ylru", ylru, BF16)[:, :], ylru, reads=["ylru%d" % n for n in range(8)], slot="dbg_yl")
        if stop == "B":
            P.emit()
            return nc, dbg_out

        LRUTOK = UP + ["m2f", "m2b", "dgw"] + XC + XCB + RA(0) + RA(1) + IB(0) + IB(1) + GEL
        cs = Carve(*R_S)
        mbuf = cs.b(8 * T)
        m3 = mbuf.rearrange("p (c t) -> p c t", t=T)
        S_C2 = cs.cur
        sgb = [[cs.f(512), cs.f(512)] for _ in range(2)]
        t12 = [[cs.f(512), cs.f(512)] for _ in range(2)]
        cwd = Carve(W_DYN, R_W[0] + R_W[1] - W_DYN)
        c1w = [[cwd.b(1024) for _ in range(4)] for _ in range(2)]
        W_OLD = ["lruw", "poolw"] + ["win%d" % i for i in range(NWIN)]
        first_c1 = [True]
        YP = lambda t: ["ypool%d_%d" % (k, t) for k in range(8)]
        YL = ["ylru%d" % k for k in range(8)]
        it = 0
        for oc in range(8):
            sl = oc % 2
            srcs = [dr["wpp"][oc], dr["win"][24 + oc], dr["wpl"][oc], dr["win"][32 + oc]]
            for i in range(4):
                extra = W_OLD if first_c1[0] else []
                first_c1[0] = False
                P.dma("pool", c1w[sl][i], srcs[i], writes=["c1w%d_%d" % (sl, i)] + extra, slot="c1w%d_%d" % (sl, i))
            for t in range(4):
                bb = [nextbank() for _ in range(4)]
                for k in range(8):
                    mm(ps[bb[0]][:, :], c1w[sl][0][:, k * 128:(k + 1) * 128], ypool3[:, k, 512 * t:512 * t + 512], k == 0, k == 7,
                       ["c1w%d_0" % sl] + YP(t), ["ps%d" % bb[0]])
                for k in range(8):
                    mm(ps[bb[1]][:, :], c1w[sl][1][:, k * 128:(k + 1) * 128], h3[:, k, 256 + 512 * t:256 + 512 * t + 512], k == 0, k == 7,
                       ["c1w%d_1" % sl] + HSEG(1 + t), ["ps%d" % bb[1]])
                for k in range(8):
                    mm(ps[bb[2]][:, :], c1w[sl][2][:, k * 128:(k + 1) * 128], ylru3[:, k, 512 * t:512 * t + 512], k == 0, k == 7,
                       ["c1w%d_2" % sl] + YL, ["ps%d" % bb[2]])
                for k in range(8):
                    mm(ps[bb[3]][:, :], c1w[sl][3][:, k * 128:(k + 1) * 128], h3[:, k, 256 + 512 * t:256 + 512 * t + 512], k == 0, k == 7,
                       ["c1w%d_3" % sl] + HSEG(1 + t), ["ps%d" % bb[3]])
                p = it % 2
                it += 1
                extra = LRUTOK if (oc == 0 and t == 0) else []
                act(sgb[p][0], ps[bb[1]][:, :], AF.Sigmoid, ["ps%d" % bb[1]], ["sg%d_0" % p] + extra)
                act(sgb[p][1], ps[bb[3]][:, :], AF.Sigmoid, ["ps%d" % bb[3]], ["sg%d_1" % p])
                tt("dve", t12[p][0], ps[bb[0]][:, :], sgb[p][0], ALU.mult, ["ps%d" % bb[0], "sg%d_0" % p], ["t12%d_0" % p])
                tt("dve", t12[p][1], ps[bb[2]][:, :], sgb[p][1], ALU.mult, ["ps%d" % bb[2], "sg%d_1" % p], ["t12%d_1" % p])
                tt("dve", m3[:, oc, 512 * t:512 * t + 512], t12[p][0], t12[p][1], ALU.add, ["t12%d_0" % p, "t12%d_1" % p],
                   ["m%d_%d" % (oc, t)])
        MT = lambda tl: ["m%d_%d" % (k, t) for k in range(8) for t in tl]
        if "m" in dbg:
            P.dma("sp", dump("m", mbuf, BF16)[:, :], mbuf, reads=MT(range(4)), slot="dbg_mm")
        if stop == "C1":
            P.emit()
            return nc, dbg_out

        cs = Carve(S_C2, R_S[0] + R_S[1] - S_C2)
        sq2 = [cs.f(640), cs.f(640)]
        tm2 = [cs.f(640), cs.f(640)]
        sd2 = cs.f(640)
        rstd2 = cs.f(640)
        gpad = [cs.f(660) for _ in range(4)]
        mixB = cs.f(8 * 640)
        ssum2 = cs.f(640)
        HY_OLD = [t for si in range(5) for t in HSEG(si)] + [t for tl in range(4) for t in YP(tl)] + YL
        cb_ = Carve(R_H[0], R_H[1] + R_Y[1])
        xt1 = cb_.f(8 * 640)
        mixb = cb_.f(8 * 640)
        hf = cb_.b(8 * 640)
        abuf = cb_.b(24 * 512)
        wupb = [cb_.b(2048) for _ in range(3)]
        wdnb = [cb_.b(3072) for _ in range(2)]
        xt13 = xt1.rearrange("p (c t) -> p c t", t=640)
        mixbufs = [mixb, mixB]
        mix3s = [mb.rearrange("p (c t) -> p c t", t=640) for mb in mixbufs]
        f3s = [mb[:, 0:8 * 512].rearrange("p (c t) -> p c t", t=512) for mb in mixbufs]
        hf3 = hf.rearrange("p (c t) -> p c t", t=640)
        a3 = abuf.rearrange("p (c t) -> p c t", t=512)
        cwd = Carve(W_DYN, R_W[0] + R_W[1] - W_DYN)
        accb = [cwd.f(512) for _ in range(4)]
        wupb.append(cwd.b(2048))
        woutb = [cwd.b(1024) for _ in range(3)]
        wo_ctr = [0]
        C1W_OLD = ["c1w%d_%d" % (s, i) for s in range(2) for i in range(4)]
        oT3 = outT.rearrange("(c p) t -> p c t", p=128)
        memset(gpad[0], 0.0, ["gpad0"] + ["sg%d_%d" % (p, i) for p in range(2) for i in range(2)] + ["t12%d_%d" % (p, i) for p in range(2) for i in range(2)])
        for gi in range(1, 4):
            memset(gpad[gi], 0.0, ["gpad%d" % gi])
        first_hy = [True]
        wup_ctr = [0]
        wdn_ctr = [0]
        dg_ctr = [0]
        gp_ctr = [0]

        def norm_rstd(src_fn, ntok, reads_fn, hook=None):
            subs = [(0, min(512, ntok))] + ([(512, ntok - 512)] if ntok > 512 else [])
            for c in range(8):
                if c == 0:
                    act(ssum2[:, 0:ntok], src_fn(c), AF.Square, reads_fn(c), ["ssum2"])
                else:
                    sq = sq2[c % 2]
                    act(sq[:, 0:ntok], src_fn(c), AF.Square, reads_fn(c), ["sq2_%d" % (c % 2)])
                    tt("dve", ssum2[:, 0:ntok], ssum2[:, 0:ntok], sq[:, 0:ntok], ALU.add, ["ssum2", "sq2_%d" % (c % 2)], ["ssum2"])
                if hook is not None:
                    hook(c)
            for (so, sn) in subs:
                b = nextbank()
                mm(ps[b][:, 0:sn], ones, ssum2[:, so:so + sn], True, True, ["ones", "ssum2"], ["ps%d" % b])
                act(sd2[:, so:so + sn], ps[b][:, 0:sn], AF.Ln, ["ps%d" % b], ["sd2"], scale=1.0 / D, bias=EPS)
            act(rstd2[:, 0:ntok], sd2[:, 0:ntok], AF.Exp, ["sd2"], ["rstd2"], scale=-0.5)

        def geom(tl):
            r0 = max(8 * tl - 1, 0)
            r1 = min(8 * tl + 9, 32)
            ntok = (r1 - r0) * 64
            return r0, ntok, r0 * 64, [(0, 512)] + [(512, ntok - 512)]

        first_mix = [True]

        def mix_units(tl):
            r0, ntok, tok0, subs = geom(tl)
            par = tl % 2
            mtl = sorted({min(3, (tok0 + so) // 512) for so, sn in subs} | {min(3, (tok0 + so + sn - 1) // 512) for so, sn in subs})
            units = []
            for oc in range(8):
                for (so, sn) in subs:
                    st_ = {}

                    def pe_fn(oc=oc, so=so, sn=sn, st_=st_):
                        if so == 0:
                            wi = wo_ctr[0] % 3
                            wo_ctr[0] += 1
                            extra = (LRUTOK + C1W_OLD) if first_mix[0] else []
                            P.dma("sp", woutb[wi], woutS[oc], reads=["wc_all"], writes=["wo%d" % wi] + extra, slot="wo%d" % wi)
                            wo_cur[0] = wi
                        wi = wo_cur[0]
                        b = nextbank()
                        st_["b"] = b
                        for k in range(8):
                            mm(ps[b][:, 0:sn], woutb[wi][:, k * 128:(k + 1) * 128],
                               m3[:, k, tok0 + so:tok0 + so + sn], k == 0, k == 7, ["wo%d" % wi] + MT(mtl), ["ps%d" % b])

                    def act_fn(oc=oc, so=so, sn=sn, st_=st_):
                        b = st_["b"]
                        extra2 = []
                        if oc == 0 and so == 0:
                            extra2 = ["f%d_%d" % (par, q) for q in range(8)]
                            if first_mix[0] or tl == 1:
                                extra2 = extra2 + ["sg%d_%d" % (p_, i) for p_ in range(2) for i in range(2)] + ["t12%d_%d" % (p_, i) for p_ in range(2) for i in range(2)] + HY_OLD
                        first_mix[0] = False
                        act(mix3s[par][:, oc, so:so + sn], ps[b][:, 0:sn], AF.Identity, ["ps%d" % b], ["mix%d_%d" % (par, oc)] + extra2)
                    units.append((pe_fn, act_fn))
            return units

        wo_cur = [0]

        def do_mix(tl):
            for pe_fn, act_fn in mix_units(tl):
                pe_fn()
                act_fn()

        do_mix(0)
        for tl in range(4):
            r0, ntok, tok0, subs = geom(tl)
            par = tl % 2
            mix3 = mix3s[par]
            f3 = f3s[par]
            MIXT = lambda c: "mix%d_%d" % (par, c)
            FT = lambda c: "f%d_%d" % (par, c)
            co = (8 * tl - r0) * 64
            s0 = r0 - (8 * tl - 1)
            extra = HY_OLD if first_hy[0] else []
            first_hy[0] = False
            for c in range(8):
                P.dma("sp", xt13[:, c, 0:ntok], xT3[:, c, tok0:tok0 + ntok], writes=["xt1_%d" % c, "x1_%d" % c] + (extra if c == 0 else []),
                      slot="xt1_%d" % c)
            norm_rstd(lambda c: mix3[:, c, 0:ntok], ntok, lambda c: [MIXT(c)])
            pend = mix_units(tl + 1) if tl + 1 < 4 else []
            pend_pe = [u[0] for u in pend]
            pend_act = [u[1] for u in pend]
            inflight = [0]

            def pump(nact):
                for _ in range(nact):
                    if pend_act:
                        pend_act.pop(0)()
                        inflight[0] -= 1
                while pend_pe and inflight[0] < 5:
                    pend_pe.pop(0)()
                    inflight[0] += 1

            pump(0)
            for c in range(8):
                tm = tm2[c % 2]
                tt("dve", tm[:, 0:ntok], mix3[:, c, 0:ntok], rstd2[:, 0:ntok], ALU.mult, [MIXT(c), "rstd2"], ["tm2_%d" % (c % 2)])
                stt(xt13[:, c, 0:ntok], tm[:, 0:ntok], GG1[:, c:c + 1], xt13[:, c, 0:ntok], ALU.mult, ALU.add,
                    ["tm2_%d" % (c % 2), "der", "xt1_%d" % c], ["x1_%d" % c])
            pump(4)
            norm_rstd(lambda c: xt13[:, c, 0:ntok], ntok, lambda c: ["x1_%d" % c], hook=lambda c: pump(1))
            pump(4)
            for c in range(8):
                tm = tm2[c % 2]
                tt("dve", tm[:, 0:ntok], xt13[:, c, 0:ntok], rstd2[:, 0:ntok], ALU.mult, ["x1_%d" % c, "rstd2"], ["tm2_%d" % (c % 2)])
                act(hf3[:, c, 0:ntok], tm[:, 0:ntok], AF.Identity, ["tm2_%d" % (c % 2), "der", "modfm"], ["hf%d" % c],
                    bias=MOD(3, c, 0), scale=A2[:, c:c + 1])
            while pend_act or pend_pe:
                pump(1)
            HF = ["hf%d" % c for c in range(8)]
            if tl == 3:
                for gi in range(4):
                    memset(gpad[gi].rearrange("p (r c) -> p r c", c=66)[:, 9, :], 0.0, ["gpad%d" % gi])
            def ffn_front(p):
                st = []
                for i in range(2):
                    j = 2 * p + i
                    q = j % 4
                    wi = wup_ctr[0] % 4
                    wup_ctr[0] += 1
                    P.dma("sp", wupb[wi], wupS[j], reads=["wc_all"], writes=["wup%d" % wi], slot="wup%d" % wi)
                    gp3 = gpad[q].rearrange("p (r c) -> p r c", c=66)
                    wg = wupb[wi][:, 0:1024]
                    row = s0
                    for (so, sn) in subs:
                        b = nextbank()
                        for k in range(8):
                            mm(ps[b][:, 0:sn], wg[:, k * 128:(k + 1) * 128], hf3[:, k, so:so + sn], k == 0, k == 7,
                               ["wup%d" % wi, HF[k]], ["ps%d" % b])
                        nr = sn // 64
                        act(gp3[:, row:row + nr, 1:65], ps[b][:, 0:sn].rearrange("p (r c) -> p r c", c=64), AF.Identity,
                            ["ps%d" % b], ["gpad%d" % q])
                        row += nr
                    acc3 = accb[q].rearrange("p (r c) -> p r c", c=64)
                    act(acc3, gp3[:, 0:8, 0:64], AF.Identity, ["gpad%d" % q, "vecs"], ["acc%d" % q],
                        bias=V("ffn_conv_b", j), scale=V("ffn_conv_w", 0 * 24 + j))
                    st.append((j, q, wi, gp3, acc3))
                return st

            def ffn_taps(st, taps):
                for tap in taps:
                    dy, dx = tap // 3, tap % 3
                    for (j, q, wi, gp3, acc3) in st:
                        stt(acc3, gp3[:, dy:dy + 8, dx:dx + 64], V("ffn_conv_w", tap * 24 + j), acc3, ALU.mult, ALU.add,
                            ["gpad%d" % q, "acc%d" % q, "vecs"], ["acc%d" % q])

            def ffn_u(st):
                out = []
                for (j, q, wi, gp3, acc3) in st:
                    wu = wupb[wi][:, 1024:2048]
                    bu = nextbank()
                    for k in range(8):
                        mm(ps[bu][:, :], wu[:, k * 128:(k + 1) * 128], hf3[:, k, co:co + 512], k == 0, k == 7,
                           ["wup%d" % wi, HF[k]], ["ps%d" % bu])
                    out.append(bu)
                return out

            def ffn_gelu(st):
                for (j, q, wi, gp3, acc3) in st:
                    act(accb[q], accb[q], AF.Gelu_apprx_tanh, ["acc%d" % q], ["acc%d" % q])

            def ffn_mult(st, bus):
                for (j, q, wi, gp3, acc3), bu in zip(st, bus):
                    tt("dve", a3[:, j, :], accb[q], ps[bu][:, :], ALU.mult, ["acc%d" % q, "ps%d" % bu], ["a%d" % j])

            prev = None
            for p in range(12):
                st = ffn_front(p)
                if prev is not None:
                    ffn_gelu(prev[0])
                ffn_taps(st, range(1, 2))
                if prev is not None:
                    ffn_mult(*prev)
                ffn_taps(st, range(2, 9))
                bus = ffn_u(st)
                prev = (st, bus)
            ffn_gelu(prev[0])
            ffn_mult(*prev)
            AT = ["a%d" % j for j in range(24)]
            for oc in range(8):
                wi = wdn_ctr[0] % 2
                wdn_ctr[0] += 1
                P.dma("sp", wdnb[wi], wdnS[oc], reads=["wc_all"], writes=["wdn%d" % wi], slot="wdn%d" % wi)
                b = nextbank()
                for k in range(24):
                    mm(ps[b][:, :], wdnb[wi][:, k * 128:(k + 1) * 128], a3[:, k, :], k == 0, k == 23, ["wdn%d" % wi, AT[k]], ["ps%d" % b])
                act(f3[:, oc, :], ps[b][:, :], AF.Identity, ["ps%d" % b], [FT(oc)] + ([MIXT(q) for q in range(8)] if oc == 0 else []))
                if oc == 0:
                    act(ssum2[:, 0:512], f3[:, oc, :], AF.Square, [FT(oc)], ["ssum2"])
                else:
                    sq = sq2[oc % 2]
                    act(sq[:, 0:512], f3[:, oc, :], AF.Square, [FT(oc)], ["sq2_%d" % (oc % 2)])
                    tt("dve", ssum2[:, 0:512], ssum2[:, 0:512], sq[:, 0:512], ALU.add, ["ssum2", "sq2_%d" % (oc % 2)], ["ssum2"])
            bss = nextbank()
            mm(ps[bss][:, :], ones, ssum2[:, 0:512], True, True, ["ones", "ssum2"], ["ps%d" % bss])
            act(sd2[:, 0:512], ps[bss][:, :], AF.Ln, ["ps%d" % bss], ["sd2"], scale=1.0 / D, bias=EPS)
            act(rstd2[:, 0:512], sd2[:, 0:512], AF.Exp, ["sd2"], ["rstd2"], scale=-0.5)
            for oc in range(8):
                tm = tm2[oc % 2]
                tt("dve", tm[:, 0:512], f3[:, oc, :], rstd2[:, 0:512], ALU.mult, [FT(oc), "rstd2"], ["tm2_%d" % (oc % 2)])
                stt(f3[:, oc, :], tm[:, 0:512], GG2[:, oc:oc + 1], xt13[:, oc, co:co + 512], ALU.mult, ALU.add,
                    ["tm2_%d" % (oc % 2), "der", "x1_%d" % oc, FT(oc)], [FT(oc)])
            P.dma("sp", oT3[:, :, 512 * tl:512 * tl + 512], f3, reads=[FT(oc) for oc in range(8)], slot="out")
        P.emit()
    return nc, dbg_out


def _fm(v, nch):
    v = np.asarray(v, np.float32)
    lead = v.shape[:-1]
    r = v.reshape(lead + (nch, 128))
    r = np.moveaxis(r, -1, 0)
    return np.ascontiguousarray(r.reshape(128, -1))


def _wtile(w, kch, och):
    w = np.asarray(w, np.float32)
    r = w.reshape(kch, 128, och, 128).transpose(2, 1, 0, 3)
    return np.ascontiguousarray(r.reshape(och, 128, kch * 128))


def _window_counts():
    out = np.zeros((4, 96), np.float32)
    for gi, w in enumerate(POOL_WINDOWS):
        def cnt1(n):
            pos = np.arange(n)
            lo = np.clip(pos - w // 2, 0, n)
            hi = np.clip(pos + w - w // 2, 0, n)
            return (hi - lo).astype(np.float32)
        out[gi, 0:32] = cnt1(32)
        out[gi, 32:96] = cnt1(64)
    return out.reshape(1, 384)


def prep_inputs(x, c, ctx, c_ctx, w_mod, b_mod, g_pre_mix, g_post_mix, g_pre_ffn, g_post_ffn,
                w_in, pool_w, pool_scale, lru_conv_w, lru_conv_b, lru_wa, lru_ba, lru_wx, lru_bx,
                lru_lambda, w_proj_pool, w_proj_lru, w_out, w_up, ffn_conv_w, ffn_conv_b, w_down):
    f = lambda a: np.asarray(a, np.float32)
    vec_parts = {
        "g_pre_mix": _fm(f(g_pre_mix)[0], 8), "g_post_mix": _fm(f(g_post_mix)[0], 8),
        "g_pre_ffn": _fm(f(g_pre_ffn)[0], 8), "g_post_ffn": _fm(f(g_post_ffn)[0], 8),
        "pool_scale": _fm(f(pool_scale)[0], 8), "lru_conv_w": _fm(f(lru_conv_w)[0], 8),
        "lru_conv_b": _fm(f(lru_conv_b)[0], 8), "lru_ba": _fm(f(lru_ba)[0], 8), "lru_bx": _fm(f(lru_bx)[0], 8),
        "lru_lambda": _fm(f(lru_lambda)[0], 8), "ffn_conv_w": _fm(f(ffn_conv_w)[0].reshape(9, 3072), 24),
        "ffn_conv_b": _fm(f(ffn_conv_b)[0], 24), "b_mod": _fm(f(b_mod)[0], 48),
    }
    vecs = np.ascontiguousarray(np.concatenate([vec_parts[n] for n, _ in _VEC_SPEC], axis=1))
    assert vecs.shape == (128, NV)
    wmod_h = np.ascontiguousarray(f(w_mod)[0].reshape(8, 128, 12, 512).transpose(2, 1, 0, 3).reshape(12, 128, 4096))
    win_h = _wtile(f(w_in)[0], 8, 40)
    wpp_h = _wtile(f(w_proj_pool)[0], 8, 8)
    wpl_h = _wtile(f(w_proj_lru)[0], 8, 8)
    wout_h = _wtile(f(w_out)[0], 8, 8)
    wup_t = _wtile(f(w_up)[0], 8, 48)
    wup_h = np.ascontiguousarray(np.concatenate([wup_t[0:24], wup_t[24:48]], axis=2))
    wdown_h = _wtile(f(w_down)[0], 24, 8)
    pw = f(pool_w)[0].reshape(4, 2, 128, 2, 128).transpose(2, 0, 3, 1, 4)
    poolw_h = np.ascontiguousarray(pw.reshape(128, 2048))
    lw = np.stack([f(lru_wa)[0], f(lru_wx)[0]], axis=0)
    lruw_h = np.ascontiguousarray(lw.transpose(3, 0, 1, 2, 4).reshape(128, 4096))
    shared = {"vecs": vecs, "ident": np.eye(128, dtype=np.float32), "cnt": _window_counts(), "wmod": wmod_h,
              "win": win_h, "wpp": wpp_h, "wpl": wpl_h, "wout": wout_h, "wup": wup_h, "wdown": wdown_h,
              "poolw": poolw_h, "lruw": lruw_h}
    xf, cf, ctxf, ccf = f(x), f(c), f(ctx), f(c_ctx)
    in_maps = []
    for b in range(NCORES):
        m = dict(shared)
        m["xT"] = np.ascontiguousarray(xf[b].T)
        m["ctxT"] = np.ascontiguousarray(ctxf[b].T)
        cc2 = np.stack([cf[b], ccf], axis=0)
        m["cc"] = np.ascontiguousarray(cc2.reshape(2, 8, 128).transpose(2, 1, 0).reshape(128, 16))
        in_maps.append(m)
    return in_maps


_CACHE = {}


def kernel(**inputs):
    in_maps = prep_inputs(**inputs)
    if "nc" not in _CACHE:
        _CACHE["nc"] = build_program()[0]
    nc = _CACHE["nc"]
    res = run_bass_kernel_spmd(nc, in_maps, core_ids=list(range(NCORES)))
    out = np.stack([np.asarray(r["outT"], np.float32).T for r in res.results], axis=0)
    return np.ascontiguousarray(out.astype(np.float32))
```

```python
import numpy as np
from contextlib import ExitStack
import concourse.bass as bass
import concourse.mybir as mybir
from concourse.bass_utils import run_bass_kernel_spmd

F32 = mybir.dt.float32
BF16 = mybir.dt.bfloat16
RELAX_DVE = False
AF = mybir.ActivationFunctionType
ALU = mybir.AluOpType

NCORES = 8
D = 1024
T = 2048
CT = 256
TT = T + CT
NCH = 8
EPS = 1e-6
POOL_WINDOWS = (2, 4, 8, 16)
SEGS = [(0, 256)] + [(256 + 512 * i, 512) for i in range(4)]

_VEC_SPEC = [("g_pre_mix", 8), ("g_post_mix", 8), ("g_pre_ffn", 8), ("g_post_ffn", 8), ("pool_scale", 8),
             ("lru_conv_w", 32), ("lru_conv_b", 8), ("lru_ba", 16), ("lru_bx", 16), ("lru_lambda", 16),
             ("ffn_conv_w", 216), ("ffn_conv_b", 24), ("b_mod", 48)]
VOFF = {}
_o = 0
for _n, _c in _VEC_SPEC:
    VOFF[_n] = _o
    _o += _c
NV = _o


class _Op:
    __slots__ = ("eng", "fn", "deps", "needs_inc", "sem", "val", "is_dma", "slot", "pos")

    def __init__(self, eng, fn, is_dma=False, slot=None):
        self.eng = eng
        self.fn = fn
        self.deps = []
        self.needs_inc = False
        self.sem = None
        self.val = None
        self.is_dma = is_dma
        self.slot = slot
        self.pos = -1


class Prog:
    ENGS = ("pe", "act", "dve", "pool", "sp")

    def __init__(self, nc):
        self.nc = nc
        self.q = {e: [] for e in self.ENGS}
        self.last_w = {}
        self.readers = {}
        self.all_ops = []

    def _track(self, op, reads, writes):
        deps = {}
        for t in reads:
            w = self.last_w.get(t)
            if w is not None:
                deps[id(w)] = w
        for t in writes:
            w = self.last_w.get(t)
            if w is not None:
                deps[id(w)] = w
            for r in self.readers.get(t, {}).values():
                deps[id(r)] = r
        for d in deps.values():
            if d is op:
                continue
            if (not d.is_dma) and (not op.is_dma) and d.eng == "pe" and op.eng == "pe":
                continue
            if RELAX_DVE and (not d.is_dma) and (not op.is_dma) and d.eng == "dve" and op.eng == "dve" and op.pos - d.pos >= 2:
                continue
            d.needs_inc = True
            op.deps.append(d)
        for t in writes:
            self.last_w[t] = op
            self.readers[t] = {}
        for t in reads:
            key = ("dma", op.slot) if op.is_dma else op.eng
            self.readers.setdefault(t, {})[key] = op

    def op(self, eng, fn, reads=(), writes=()):
        o = _Op(eng, fn)
        o.pos = len(self.q[eng])
        self._track(o, reads, writes)
        self.q[eng].append(o)
        self.all_ops.append(o)
        return o

    def dma(self, eng, out, in_, reads=(), writes=(), slot=None, **kw):
        o = _Op(eng, None, is_dma=True, slot=slot)
        o.fn = lambda e, s, o_=out, i_=in_, kw_=kw: e.dma_start(out=o_, in_=i_, **kw_).then_inc(s, 16)
        self._track(o, reads, writes)
        o.needs_inc = True
        self.q[eng].append(o)
        self.all_ops.append(o)
        return o

    def emit(self, final_wait_eng="sp"):
        nc = self.nc
        with ExitStack() as es:
            esem = {e: es.enter_context(nc.semaphore("sem_" + e)) for e in self.ENGS}
            slot_names = sorted({o.slot for o in self.all_ops if o.is_dma})
            ssem = {s: es.enter_context(nc.semaphore("dsem_" + s)) for s in slot_names}
            cnt = {e: 0 for e in self.ENGS}
            for e in self.ENGS:
                for o in self.q[e]:
                    if (not o.is_dma) and o.needs_inc:
                        cnt[e] += 1
                        o.sem, o.val = esem[e], cnt[e]
            scnt = {s: 0 for s in slot_names}
            for o in self.all_ops:
                if o.is_dma:
                    scnt[o.slot] += 16
                    o.sem, o.val = ssem[o.slot], scnt[o.slot]
            block = es.enter_context(nc.Block())
            final = [(ssem[s], scnt[s]) for s in slot_names]

            def run(e):
                def body(eng):
                    waited = {}
                    for o in self.q[e]:
                        for d in o.deps:
                            k = id(d.sem)
                            if waited.get(k, 0) < d.val:
                                eng.wait_ge(d.sem, d.val)
                                waited[k] = d.val
                        if o.is_dma:
                            o.fn(eng, o.sem)
                        else:
                            ins = o.fn(eng)
                            if o.needs_inc:
                                ins.then_inc(o.sem, 1)
                    if e == final_wait_eng:
                        for s, v in final:
                            if v > 0:
                                eng.wait_ge(s, v)
                return body

            block.tensor(run("pe"))
            block.scalar(run("act"))
            block.vector(run("dve"))
            block.gpsimd(run("pool"))
            block.sync(run("sp"))


def build_program(stop=None, dbg=()):
    nc = bass.Bass("TRN2", target_bir_lowering=False)
    dr = {}

    def din(name, shape, dt=F32):
        dr[name] = nc.dram_tensor(name, shape, dt, kind="ExternalInput").ap()

    din("xT", [D, T])
    din("ctxT", [D, CT])
    din("cc", [128, 16])
    din("vecs", [128, NV])
    din("ident", [128, 128])
    din("cnt", [1, 384])
    din("wmod", [12, 128, 4096])
    din("win", [40, 128, 1024])
    din("wpp", [8, 128, 1024])
    din("wpl", [8, 128, 1024])
    din("wout", [8, 128, 1024])
    din("wup", [24, 128, 2048])
    din("wdown", [8, 128, 3072])
    din("poolw", [128, 2048])
    din("lruw", [128, 4096])
    outT = nc.dram_tensor("outT", [D, T], F32, kind="ExternalOutput").ap()
    wupS = nc.dram_tensor("wupS", [24, 128, 2048], BF16, kind="Internal").ap()
    wdnS = nc.dram_tensor("wdnS", [8, 128, 3072], BF16, kind="Internal").ap()
    woutS = nc.dram_tensor("woutS", [8, 128, 1024], BF16, kind="Internal").ap()
    dbg_out = {}

    es = ExitStack()
    with es:
        AW = 53200
        arena = es.enter_context(nc.sbuf_tensor("arena", [128, AW], F32))
        psall = es.enter_context(nc.psum_tensor("psall", [128, 4096], F32))
        ps = [psall[:, i * 512:(i + 1) * 512] for i in range(8)]
        P = Prog(nc)

        def fv(off, n):
            assert off % 4 == 0 and off + 4 * n <= AW * 4, (off, n)
            return arena[:, off // 4: off // 4 + n]

        def bv(off, n):
            assert off % 4 == 0 and n % 2 == 0 and off + 2 * n <= AW * 4, (off, n)
            return arena[:, off // 4: off // 4 + n // 2].bitcast(BF16)

        class Carve:
            def __init__(self, base, size):
                self.base, self.end, self.cur = base, base + size, base

            def f(self, n):
                a = fv(self.cur, n)
                self.cur += 4 * n
                self.cur = (self.cur + 63) // 64 * 64
                assert self.cur <= self.end, ("carve overflow", self.cur, self.end)
                return a

            def b(self, n):
                a = bv(self.cur, n)
                self.cur += 2 * n
                self.cur = (self.cur + 63) // 64 * 64
                assert self.cur <= self.end, ("carve overflow", self.cur, self.end)
                return a

        KB = 1024
        R_H = (0, 36 * KB)
        R_Y = (36 * KB, 64 * KB)
        R_S = (100 * KB, 84 * KB)
        R_W = (184 * KB, AW * 4 - 184 * KB)

        bank_ctr = [0]

        def nextbank():
            b = bank_ctr[0] % 8
            bank_ctr[0] += 1
            return b

        def mm(out, lhsT, rhs, start, stop, reads, writes):
            P.op("pe", lambda e: e.matmul(out, lhsT, rhs, start=start, stop=stop), reads, writes)

        def act(out, in_, func, reads, writes, bias=None, scale=None):
            kw = {}
            if bias is not None:
                kw["bias"] = bias
            if scale is not None:
                kw["scale"] = scale
            P.op("act", lambda e: e.activation(out=out, in_=in_, func=func, **kw), reads, writes)

        def evac_latent(banks, dst, o0, func, reads_extra, toks, **kw):
            i = 0
            while i < len(banks):
                j = i
                while j + 1 < len(banks) and banks[j + 1] == banks[j] + 1:
                    j += 1
                r = j - i + 1
                act(dst[:, o0 + 512 * i:o0 + 512 * (i + r)], psall[:, banks[i] * 512:(banks[i] + r) * 512], func,
                    ["ps%d" % b_ for b_ in banks[i:j + 1]] + list(reads_extra), list(toks[i:j + 1]), **kw)
                i = j + 1

        def tt(eng, out, in0, in1, op, reads, writes):
            P.op(eng, lambda e: e.tensor_tensor(out=out, in0=in0, in1=in1, op=op), reads, writes)

        def ts(eng, out, in0, s1, op0, reads, writes, s2=None, op1=None):
            if op1 is None:
                P.op(eng, lambda e: e.tensor_scalar(out=out, in0=in0, scalar1=s1, scalar2=None, op0=op0), reads, writes)
            else:
                P.op(eng, lambda e: e.tensor_scalar(out=out, in0=in0, scalar1=s1, scalar2=s2, op0=op0, op1=op1), reads, writes)

        def stt(out, in0, scalar, in1, op0, op1, reads, writes):
            P.op("dve", lambda e: e.scalar_tensor_tensor(out=out, in0=in0, scalar=scalar, in1=in1, op0=op0, op1=op1), reads, writes)

        def memset(ap, val, writes):
            P.op("pool", lambda e: e.memset(ap, val), (), writes)

        def dump(name, ap, dt=F32):
            shape = [ap.shape[0], int(np.prod(ap.shape[1:]))]
            t = nc.dram_tensor("dbg_" + name, shape, dt, kind="ExternalOutput").ap()
            dbg_out[name] = t
            return t

        cw = Carve(*R_W)
        vecs = cw.f(NV)
        ident = cw.f(128)
        ones = cw.f(128)
        identb = cw.b(128)
        ccs = cw.f(16)
        ssil = cw.f(16)
        modfm = cw.f(96)
        der = cw.f(64)
        lrud = cw.f(64)
        cw2 = cw.f(32)
        cn1 = cw.f(384)
        W_DYN = cw.cur

        def V(name, i=0):
            o = VOFF[name] + i
            return vecs[:, o:o + 1]

        def Vs(name, i0, n):
            o = VOFF[name] + i0
            return vecs[:, o:o + n]

        P.dma("sp", vecs, dr["vecs"][:, :], writes=["vecs"], slot="c_vecs")
        P.dma("sp", ident, dr["ident"][:, :], writes=["ident"], slot="c_ident")
        P.dma("sp", ccs, dr["cc"][:, :], writes=["cc"], slot="c_cc")
        memset(ones, 1.0, ["ones"])
        P.op("dve", lambda e: e.tensor_copy(out=identb, in_=ident), ["ident"], ["identb"])
        act(ssil, ccs, AF.Silu, ["cc"], ["ssil"])
        s3 = ssil.rearrange("p (k r) -> p k r", r=2)

        cs = Carve(*R_S)
        wmb = [cs.f(4096), cs.f(4096)]
        modrow = cs.f(6144)
        xa4 = cs.f(8 * 512)
        rstdA = cs.f(TT)
        cy = Carve(*R_Y)
        xa = [cy.f(8 * 256), cy.f(8 * 512), cy.f(8 * 512), cy.f(8 * 512), xa4]
        sqb = [cy.f(512), cy.f(512)]
        ssumA = cy.f(512)
        sdb = cy.f(512)
        xT3 = dr["xT"].rearrange("(c p) t -> p c t", p=128)
        cT3 = dr["ctxT"].rearrange("(c p) t -> p c t", p=128)
        xa3 = [xa[si].rearrange("p (c t) -> p c t", t=SEGS[si][1]) for si in range(5)]

        def load_x(si):
            o, n = SEGS[si]
            src = cT3[:, :, :] if si == 0 else xT3[:, :, o - 256:o - 256 + n]
            P.dma("sp", xa3[si], src, writes=["xa%d" % si], slot="xa%d" % si)

        def stats(si):
            o, n = SEGS[si]
            tk = "xa%d" % si
            for c in range(8):
                if c == 0:
                    act(ssumA[:, 0:n], xa3[si][:, c, :], AF.Square, [tk], ["ssumA"])
                else:
                    sq = sqb[c % 2]
                    act(sq[:, 0:n], xa3[si][:, c, :], AF.Square, [tk], ["sq%d" % (c % 2)])
                    tt("dve", ssumA[:, 0:n], ssumA[:, 0:n], sq[:, 0:n], ALU.add, ["ssumA", "sq%d" % (c % 2)], ["ssumA"])
            b = nextbank()
            mm(ps[b][:, 0:n], ones, ssumA[:, 0:n], True, True, ["ones", "ssumA"], ["ps%d" % b])
            act(sdb[:, 0:n], ps[b][:, 0:n], AF.Ln, ["ps%d" % b], ["sd"], scale=1.0 / D, bias=EPS)
            act(rstdA[:, o:o + n], sdb[:, 0:n], AF.Exp, ["sd"], ["rstdA%d" % si], scale=-0.5)
            for c in range(8):
                tt("dve", xa3[si][:, c, :], xa3[si][:, c, :], rstdA[:, o:o + n], ALU.mult, [tk, "rstdA%d" % si], ["xn%d_%d" % (si, c)])

        xsched = {1: 0, 2: 1, 4: 2, 6: 3, 8: 4}
        ssched = {3: 0, 5: 1, 7: 2, 9: 3, 11: 4}
        for blk in range(12):
            buf = wmb[blk % 2]
            tk = "wm%d" % (blk % 2)
            P.dma("sp", buf, dr["wmod"][blk], writes=[tk], slot=tk)
            if blk in xsched:
                load_x(xsched[blk])
            b = nextbank()
            for k in range(8):
                mm(ps[b][0:2, 0:512], s3[:, k, :], buf[:, k * 512:(k + 1) * 512], k == 0, k == 7,
                   ["ssil", tk], ["ps%d" % b])
            act(modrow[0:2, blk * 512:(blk + 1) * 512], ps[b][0:2, 0:512], AF.Identity, ["ps%d" % b], ["modrow%d" % blk])
            if blk in ssched:
                stats(ssched[blk])
        b = nextbank()
        for oc in range(48):
            mm(ps[b][:, 2 * oc:2 * oc + 2], modrow[0:2, oc * 128:(oc + 1) * 128], ident[0:2, 0:2], True, True,
               ["modrow%d" % (oc // 4), "ident"], ["ps%d" % b])
        act(modfm, ps[b][:, 0:96], AF.Identity, ["ps%d" % b], ["modfm"])
        mod3 = modfm.rearrange("p (c r) -> p c r", r=2)
        for r in range(2):
            tt("dve", mod3[:, :, r], mod3[:, :, r], Vs("b_mod", 0, 48), ALU.add, ["modfm", "vecs"], ["modfm"])

        def MOD(which, c, r=0):
            return mod3[:, which * 8 + c, r:r + 1]

        A1 = der[:, 0:8]
        A1c = der[:, 8:16]
        GG1 = der[:, 16:24]
        A2 = der[:, 24:32]
        GG2 = der[:, 32:40]
        stt(A1, mod3[:, 8:16, 0], 1.0, Vs("g_pre_mix", 0, 8), ALU.add, ALU.mult, ["modfm", "vecs"], ["der"])
        stt(A1c, mod3[:, 8:16, 1], 1.0, Vs("g_pre_mix", 0, 8), ALU.add, ALU.mult, ["modfm", "vecs"], ["der"])
        tt("dve", GG1, mod3[:, 16:24, 0], Vs("g_post_mix", 0, 8), ALU.mult, ["modfm", "vecs"], ["der"])
        stt(A2, mod3[:, 32:40, 0], 1.0, Vs("g_pre_ffn", 0, 8), ALU.add, ALU.mult, ["modfm", "vecs"], ["der"])
        tt("dve", GG2, mod3[:, 40:48, 0], Vs("g_post_ffn", 0, 8), ALU.mult, ["modfm", "vecs"], ["der"])
        lam = Vs("lru_lambda", 0, 16)
        le = lrud[:, 0:16]
        lsp = lrud[:, 16:32]
        ls1 = lrud[:, 32:48]
        act(le, lam, AF.Exp, ["vecs"], ["le"], scale=-1.0)
        act(lsp, le, AF.Ln, ["le"], ["lsp"], bias=1.0)
        ts("dve", ls1, lsp, -4.0, ALU.mult, ["lsp"], ["hs1"])
        hs1 = ls1
        hbias = cw2
        ts("dve", hbias, Vs("lru_ba", 0, 32), 0.5, ALU.mult, ["vecs"], ["hbias"])

        ch = Carve(*R_H)
        h = ch.b(8 * TT)
        h3 = h.rearrange("p (c t) -> p c t", t=TT)
        for si, (o, n) in enumerate(SEGS):
            for c in range(8):
                if si == 0:
                    sc_, bi_ = A1c[:, c:c + 1], MOD(0, c, 1)
                else:
                    sc_, bi_ = A1[:, c:c + 1], MOD(0, c, 0)
                act(h3[:, c, o:o + n], xa3[si][:, c, :], AF.Identity, ["xn%d_%d" % (si, c), "der", "modfm"], ["h%d_%d" % (c, si)],
                    bias=bi_, scale=sc_)
        HSEG = lambda si: ["h%d_%d" % (c, si) for c in range(8)]

        if "h" in dbg:
            P.dma("sp", dump("h", h, BF16)[:, :], h, reads=[t for si in range(5) for t in HSEG(si)], slot="dbg_h")
            P.dma("sp", dump("modfm", modfm)[:, :], modfm, reads=["modfm"], slot="dbg_m")
        if stop == "A":
            P.emit()
            return nc, dbg_out

        cy = Carve(*R_Y)
        ypool = cy.b(8 * T)
        ylru = cy.b(8 * T)
        ypool3 = ypool.rearrange("p (c t) -> p c t", t=T)
        ylru3 = ylru.rearrange("p (c t) -> p c t", t=T)
        YTOK = ["xa%d" % si for si in range(4)] + ["xn%d_%d" % (si, c) for si in range(4) for c in range(8)] + ["sq0", "sq1", "sd", "ssumA"]
        y_first = [True]

        cwd = Carve(W_DYN, R_W[0] + R_W[1] - W_DYN)
        lruw = cwd.b(4096)
        poolw = cwd.b(2048)
        NWIN = 3
        winb = [cwd.b(1024) for _ in range(NWIN)]
        win_ctr = [0]
        P.dma("pool", lruw[:, 0:2048], dr["lruw"][:, 0:2048], writes=["lruw"], slot="w_lruw")
        P.dma("pool", lruw[:, 2048:4096], dr["lruw"][:, 2048:4096], writes=["lruw"], slot="w_lruw")
        P.dma("pool", poolw, dr["poolw"][:, :], writes=["poolw"], slot="w_poolw")

        def load_win(oc):
            i = win_ctr[0] % NWIN
            win_ctr[0] += 1
            P.dma("pool", winb[i], dr["win"][oc], writes=["win%d" % i], slot="win%d" % i)
            return winb[i], "win%d" % i

        def proj_seg(wt, wtk, si, b):
            o, n = SEGS[si]
            for k in range(8):
                mm(ps[b][:, 0:n], wt[:, k * 128:(k + 1) * 128], h3[:, k, o:o + n], k == 0, k == 7,
                   [wtk] + HSEG(si), ["ps%d" % b])

        cast_plan = []
        for oc in range(8):
            cast_plan.append((woutS[oc], dr["wout"][oc]))
        for j in range(24):
            cast_plan.append((wupS[j], dr["wup"][j]))
        for oc in range(8):
            cast_plan.append((wdnS[oc][:, 0:2048], dr["wdown"][oc][:, 0:2048]))
            cast_plan.append((wdnS[oc][:, 2048:3072], dr["wdown"][oc][:, 2048:3072]))
        cast_ctr = [0]

        def issue_casts(k):
            for _ in range(k):
                i = cast_ctr[0]
                if i >= len(cast_plan):
                    return
                cast_ctr[0] += 1
                dst, src = cast_plan[i]
                wr = ["wc%d" % i] + (["wc_all"] if i == len(cast_plan) - 1 else [])
                P.dma("pool", dst, src, writes=wr, slot="wcast")

        STOK_A = (["wm0", "wm1"] + ["modrow%d" % i for i in range(12)] + ["xa4"] + ["xn4_%d" % c for c in range(8)]
                  + ["rstdA%d" % si for si in range(5)])
        cs = Carve(*R_S)
        cntb = cs.f(T)
        PQ = [[cs.f(47 * 79), cs.f(47 * 79)] for _ in range(2)]
        dbf = [cs.b(T), cs.b(T)]
        PQTOK = ["pq%d_%d" % (cl, i) for cl in range(2) for i in range(2)]
        cnt3 = cntb.rearrange("p (r c) -> p r c", c=64)
        first_S = [True]
        nb_ctr = [0]

        def nextB():
            b = 4 + nb_ctr[0] % 4
            nb_ctr[0] += 1
            return b

        def proj_cl0(g_):
            wt_, wtk_ = load_win(2 * g_)
            for t in range(4):
                proj_seg(wt_, wtk_, 1 + t, t)
            return (wt_, wtk_)

        pre_cl0 = proj_cl0(0)
        P.dma("sp", cn1, dr["cnt"][0:1, :].partition_broadcast(128), writes=["cn1"], slot="c_cn1")
        P.op("dve", lambda e: e.reciprocal(out=cn1, in_=cn1), ["cn1"], ["cn1"])
        for g, w in enumerate(POOL_WINDOWS):
            hw = w // 2
            Hp, Wp = 32 + w - 1, 64 + w - 1
            extra = STOK_A if first_S[0] else []
            first_S[0] = False
            ir = cn1[:, g * 96: g * 96 + 32].unsqueeze(2).broadcast_to([128, 32, 64])
            ic = cn1[:, g * 96 + 32: g * 96 + 96].unsqueeze(1).broadcast_to([128, 32, 64])
            tt("dve", cnt3, ir, ic, ALU.mult, ["cn1"], ["cnt"] + extra)
            views = []
            for cl in range(2):
                v = [PQ[cl][i][:, 0:Hp * Wp].rearrange("p (r c) -> p r c", c=Wp) for i in range(2)]
                views.append(v)
                Pv, Qv = v
                memset(Pv[:, 0:hw, :], 0.0, ["pq%d_0" % cl] + extra)
                if hw > 1:
                    memset(Pv[:, hw + 32:Hp, :], 0.0, ["pq%d_0" % cl])
                memset(Pv[:, hw:hw + 32, 0:hw], 0.0, ["pq%d_0" % cl])
                if hw > 1:
                    memset(Pv[:, hw:hw + 32, hw + 64:Wp], 0.0, ["pq%d_0" % cl])
                if w in (2, 8):
                    memset(Qv[:, 0:hw, 0:64], 0.0, ["pq%d_1" % cl])
                    if hw > 1:
                        memset(Qv[:, hw + 32:Hp, 0:64], 0.0, ["pq%d_1" % cl])
            wts = [pre_cl0]
            for t in range(4):
                act(views[0][0][:, hw + 8 * t: hw + 8 * t + 8, hw:hw + 64], ps[t][:, :].rearrange("p (r c) -> p r c", c=64),
                    AF.Identity, ["ps%d" % t], ["pq0_0"])
            wt, wtk = load_win(2 * g + 1)
            wts.append((wt, wtk))
            for t in range(4):
                b = nextB()
                proj_seg(wt, wtk, 1 + t, b)
                act(views[1][0][:, hw + 8 * t: hw + 8 * t + 8, hw:hw + 64], ps[b][:, :].rearrange("p (r c) -> p r c", c=64),
                    AF.Identity, ["ps%d" % b], ["pq1_0"])
            if g + 1 < 4:
                pre_cl0 = proj_cl0(g + 1)
            state = [dict(cur=0, ln=Wp, rows=Hp) for _ in range(2)]
            k = 1
            while k < w:
                for cl in range(2):
                    st = state[cl]
                    src, dst = views[cl][st["cur"]], views[cl][1 - st["cur"]]
                    nl = st["ln"] - k
                    tt("dve", dst[:, hw:hw + 32, 0:nl], src[:, hw:hw + 32, 0:nl], src[:, hw:hw + 32, k:k + nl], ALU.add,
                       ["pq%d_%d" % (cl, st["cur"])], ["pq%d_%d" % (cl, 1 - st["cur"])])
                    st["cur"], st["ln"] = 1 - st["cur"], nl
                k *= 2
            k = 1
            while k < w:
                for cl in range(2):
                    st = state[cl]
                    src, dst = views[cl][st["cur"]], views[cl][1 - st["cur"]]
                    nr = st["rows"] - k
                    tt("dve", dst[:, 0:nr, 0:64], src[:, 0:nr, 0:64], src[:, k:k + nr, 0:64], ALU.add,
                       ["pq%d_%d" % (cl, st["cur"])], ["pq%d_%d" % (cl, 1 - st["cur"])])
                    st["cur"], st["rows"] = 1 - st["cur"], nr
                k *= 2
            for cl in range(2):
                st = state[cl]
                assert st["ln"] == 64 and st["rows"] == 32
                src, oth = views[cl][st["cur"]], views[cl][1 - st["cur"]]
                tt("dve", oth[:, 0:32, 0:64], src[:, 0:32, 0:64], cnt3, ALU.mult, ["pq%d_%d" % (cl, st["cur"]), "cnt"],
                   ["pq%d_%d" % (cl, 1 - st["cur"])])
                wt, wtk = wts[cl]
                for t in range(4):
                    b = nextB()
                    proj_seg(wt, wtk, 1 + t, b)
                    tt("dve", dbf[cl][:, 512 * t:512 * t + 512].rearrange("p (r c) -> p r c", c=64), oth[:, 8 * t:8 * t + 8, 0:64],
                       ps[b][:, :].rearrange("p (r c) -> p r c", c=64), ALU.subtract, ["pq%d_%d" % (cl, 1 - st["cur"]), "ps%d" % b],
                       ["dbf%d_%d" % (cl, t)])
            for ocl in range(2):
                oc = 2 * g + ocl
                for t in range(4):
                    b = nextB()
                    for k in range(2):
                        idx = ((g * 2 + ocl) * 2 + k) * 128
                        mm(ps[b][:, :], poolw[:, idx:idx + 128], dbf[k][:, 512 * t:512 * t + 512], k == 0, k == 1,
                           ["poolw", "dbf%d_%d" % (k, t)], ["ps%d" % b])
                    extra = YTOK if y_first[0] else []
                    y_first[0] = False
                    act(ypool3[:, oc, 512 * t:512 * t + 512], ps[b][:, :], AF.Identity, ["ps%d" % b, "vecs"],
                        ["ypool%d_%d" % (oc, t)] + extra, scale=V("pool_scale", oc))
            issue_casts(5)
        if "ypool" in dbg:
            P.dma("sp", dump("ypool", ypool, BF16)[:, :], ypool,
                  reads=["ypool%d_%d" % (oc, t) for oc in range(8) for t in range(4)], slot="dbg_yp")
        if stop == "Bp":
            P.emit()
            return nc, dbg_out

        cs = Carve(*R_S)
        UPW = 2320
        LOFF = 264
        upad = cs.b(UPW)
        dgw = cs.b(32 * 128)
        o_m2b = cs.cur
        m2b = cs.f(TT)
        xcb = bv(o_m2b, TT)
        xc = cs.f(TT)
        m2f = cs.f(TT)
        ra = [cs.f(TT), cs.f(TT)]
        ib = [cs.f(TT), cs.f(TT)]
        gel = cs.f(T)
        m2 = [m2f, m2b]
        m2tok = ["m2f", "m2b"]
        POOLTOK = ["cnt"] + PQTOK + ["dbf%d_%d" % (cl, t) for cl in range(2) for t in range(4)]
        UP = ["upad%d" % si for si in range(5)]
        memset(upad, 0.0, UP + POOLTOK)
        for k in range(4):
            for n in range(8):
                i = k * 8 + n
                ts("dve", dgw[:, i * 128:(i + 1) * 128], ident, V("lru_conv_w", i), ALU.mult, ["ident", "vecs"], ["dgw"] + (POOLTOK if i == 0 else []))
        XC = ["xc%d" % si for si in range(5)]
        XCB = ["xcb%d" % si for si in range(5)]
        RA = lambda d: ["ra%d_%d" % (d, si) for si in range(5)]
        IB = lambda d: ["ib%d_%d" % (d, si) for si in range(5)]
        win_next = load_win(8 + 0)
        for n in range(8):
            wt, wtk = win_next
            for si, (o, nn) in enumerate(SEGS):
                b = nextbank()
                proj_seg(wt, wtk, si, b)
                po = 2 + o if si == 0 else LOFF + (o - 256)
                act(upad[:, po:po + nn], ps[b][:, 0:nn], AF.Identity, ["ps%d" % b], ["upad%d" % si])
            for si, (o, nn) in enumerate(SEGS):
                base = o if si == 0 else LOFF - 2 + (o - 256)
                b = nextbank()
                for k in range(4):
                    i = k * 8 + n
                    mm(ps[b][:, 0:nn], dgw[:, i * 128:(i + 1) * 128], upad[:, base + k:base + k + nn], k == 0, k == 3,
                       ["dgw"] + ([UP[0]] if si == 0 else [UP[j_] for j_ in (si - 1, si, si + 1) if 1 <= j_ <= 4]), ["ps%d" % b])
                act(xcb[:, o:o + nn], ps[b][:, 0:nn], AF.Identity, ["ps%d" % b, "vecs"], ["xcb%d" % si] + (["m2b"] if si == 0 else []),
                    bias=V("lru_conv_b", n))
                act(xc[:, o:o + nn], ps[b][:, 0:nn], AF.Identity, ["ps%d" % b, "vecs"], ["xc%d" % si], bias=V("lru_conv_b", n))
            for dr_ in range(2):
                for kind, dst, tkf in ((0, ra[dr_], RA(dr_)), (1, ib[dr_], IB(dr_))):
                    widx = ((kind * 2 + dr_) * 8 + n) * 128
                    hb_ = hbias[:, kind * 16 + dr_ * 8 + n: kind * 16 + dr_ * 8 + n + 1]
                    b = nextbank()
                    mm(ps[b][:, 0:256], lruw[:, widx:widx + 128], xcb[:, 0:256], True, True, ["lruw", "xcb0"], ["ps%d" % b])
                    act(dst[:, 0:256], ps[b][:, 0:256], AF.Tanh, ["ps%d" % b, "hbias"], [tkf[0]], bias=hb_, scale=0.5)
                    gb = []
                    for si in range(1, 5):
                        o, nn = SEGS[si]
                        b = nextbank()
                        gb.append(b)
                        mm(ps[b][:, 0:nn], lruw[:, widx:widx + 128], xcb[:, o:o + nn], True, True, ["lruw", "xcb%d" % si], ["ps%d" % b])
                    evac_latent(gb, dst, 256, AF.Tanh, ["hbias"], tkf[1:5], bias=hb_, scale=0.5)
                hs = hs1[:, dr_ * 8 + n: dr_ * 8 + n + 1]
                act(ra[dr_], ra[dr_], AF.Exp, RA(dr_) + ["hs1"], RA(dr_), bias=hs, scale=hs)
                act(m2[dr_], ra[dr_], AF.Square, RA(dr_), [m2tok[dr_]] + (XCB if dr_ == 1 else []))
                act(m2[dr_], m2[dr_], AF.Sqrt, [m2tok[dr_]], [m2tok[dr_]], scale=-0.25, bias=0.25)
            for dr_ in range(2):
                stt(ib[dr_], ib[dr_], 1.0, xc, ALU.add, ALU.mult, IB(dr_) + XC, IB(dr_))
                tt("dve", ib[dr_], ib[dr_], m2[dr_], ALU.mult, IB(dr_) + [m2tok[dr_]], IB(dr_))
                if dr_ == 0:
                    P.op("dve", lambda e: e.tensor_tensor_scan(out=ib[0], data0=ra[0], data1=ib[0], initial=0.0, op0=ALU.mult, op1=ALU.add),
                         RA(0) + IB(0), IB(0))
                else:
                    P.op("dve", lambda e: e.tensor_tensor_scan(out=ib[1][:, 0:256][:, ::-1], data0=ra[1][:, 0:256][:, ::-1],
                                                                data1=ib[1][:, 0:256][:, ::-1], initial=0.0, op0=ALU.mult, op1=ALU.add),
                         RA(1) + IB(1), IB(1))
                    P.op("dve", lambda e: e.tensor_tensor_scan(out=ib[1][:, 256:TT][:, ::-1], data0=ra[1][:, 256:TT][:, ::-1],
                                                                data1=ib[1][:, 256:TT][:, ::-1], initial=ib[1][:, 0:1], op0=ALU.mult, op1=ALU.add),
                         RA(1) + IB(1), IB(1))
            wt, wtk = load_win(16 + n)
            if n + 1 < 8:
                win_next = load_win(8 + n + 1)
            issue_casts(5)
            gb = []
            for t in range(4):
                b = nextbank()
                gb.append(b)
                proj_seg(wt, wtk, 1 + t, b)
            evac_latent(gb, gel, 0, AF.Gelu_apprx_tanh, [], ["gel%d" % t for t in range(4)])
            GEL = ["gel%d" % t for t in range(4)]
            tt("dve", ib[0][:, 256:TT], ib[0][:, 256:TT], ib[1][:, 256:TT], ALU.add, IB(0) + IB(1), IB(0))
            tt("dve", ylru3[:, n, :], ib[0][:, 256:TT], gel, ALU.mult, IB(0) + GEL, ["ylru%d" % n] + (YTOK if n == 0 else []))
        if "ylru" in dbg:
            P.dma("sp", dump("ylru", ylru, BF16)[:, :], ylru, reads=["ylru%d" % n for n in range(8)], slot="dbg_yl")
        if stop == "B":
            P.emit()
            return nc, dbg_out

        LRUTOK = UP + ["m2f", "m2b", "dgw"] + XC + XCB + RA(0) + RA(1) + IB(0) + IB(1) + GEL
        cs = Carve(*R_S)
        mbuf = cs.b(8 * T)
        m3 = mbuf.rearrange("p (c t) -> p c t", t=T)
        S_C2 = cs.cur
        sgb = [[cs.f(512), cs.f(512)] for _ in range(2)]
        t12 = [[cs.f(512), cs.f(512)] for _ in range(2)]
        cwd = Carve(W_DYN, R_W[0] + R_W[1] - W_DYN)
        c1w = [[cwd.b(1024) for _ in range(4)] for _ in range(2)]
        W_OLD = ["lruw", "poolw"] + ["win%d" % i for i in range(NWIN)]
        first_c1 = [True]
        YP = lambda t: ["ypool%d_%d" % (k, t) for k in range(8)]
        YL = ["ylru%d" % k for k in range(8)]
        it = 0
        for oc in range(8):
            sl = oc % 2
            srcs = [dr["wpp"][oc], dr["win"][24 + oc], dr["wpl"][oc], dr["win"][32 + oc]]
            for i in range(4):
                extra = W_OLD if first_c1[0] else []
                first_c1[0] = False
                P.dma("pool", c1w[sl][i], srcs[i], writes=["c1w%d_%d" % (sl, i)] + extra, slot="c1w%d_%d" % (sl, i))
            for t in range(4):
                bb = [nextbank() for _ in range(4)]
                for k in range(8):
                    mm(ps[bb[0]][:, :], c1w[sl][0][:, k * 128:(k + 1) * 128], ypool3[:, k, 512 * t:512 * t + 512], k == 0, k == 7,
                       ["c1w%d_0" % sl, "ypool%d_%d" % (k, t)], ["ps%d" % bb[0]])
                for k in range(8):
                    mm(ps[bb[1]][:, :], c1w[sl][1][:, k * 128:(k + 1) * 128], h3[:, k, 256 + 512 * t:256 + 512 * t + 512], k == 0, k == 7,
                       ["c1w%d_1" % sl, "h%d_%d" % (k, 1 + t)], ["ps%d" % bb[1]])
                for k in range(8):
                    mm(ps[bb[3]][:, :], c1w[sl][3][:, k * 128:(k + 1) * 128], h3[:, k, 256 + 512 * t:256 + 512 * t + 512], k == 0, k == 7,
                       ["c1w%d_3" % sl, "h%d_%d" % (k, 1 + t)], ["ps%d" % bb[3]])
                for k in range(8):
                    mm(ps[bb[2]][:, :], c1w[sl][2][:, k * 128:(k + 1) * 128], ylru3[:, k, 512 * t:512 * t + 512], k == 0, k == 7,
                       ["c1w%d_2" % sl, "ylru%d" % k], ["ps%d" % bb[2]])
                p = it % 2
                it += 1
                extra = LRUTOK if (oc == 0 and t == 0) else []
                act(sgb[p][0], ps[bb[1]][:, :], AF.Sigmoid, ["ps%d" % bb[1]], ["sg%d_0" % p] + extra)
                act(sgb[p][1], ps[bb[3]][:, :], AF.Sigmoid, ["ps%d" % bb[3]], ["sg%d_1" % p])
                tt("dve", t12[p][0], ps[bb[0]][:, :], sgb[p][0], ALU.mult, ["ps%d" % bb[0], "sg%d_0" % p], ["t12%d_0" % p])
                tt("dve", t12[p][1], ps[bb[2]][:, :], sgb[p][1], ALU.mult, ["ps%d" % bb[2], "sg%d_1" % p], ["t12%d_1" % p])
                tt("dve", m3[:, oc, 512 * t:512 * t + 512], t12[p][0], t12[p][1], ALU.add, ["t12%d_0" % p, "t12%d_1" % p],
                   ["m%d_%d" % (oc, t)])
        MT = lambda tl: ["m%d_%d" % (k, t) for k in range(8) for t in tl]
        if "m" in dbg:
            P.dma("sp", dump("m", mbuf, BF16)[:, :], mbuf, reads=MT(range(4)), slot="dbg_mm")
        if stop == "C1":
            P.emit()
            return nc, dbg_out

        cs = Carve(S_C2, R_S[0] + R_S[1] - S_C2)
        sq2 = [cs.f(640), cs.f(640)]
        tm2 = [cs.f(640), cs.f(640)]
        sd2 = cs.f(640)
        rstd2 = cs.f(640)
        gpad = [cs.f(660) for _ in range(4)]
        mixB = cs.f(8 * 640)
        ssum2 = cs.f(640)
        HY_OLD = [t for si in range(5) for t in HSEG(si)] + [t for tl in range(4) for t in YP(tl)] + YL
        cb_ = Carve(R_H[0], R_H[1] + R_Y[1])
        xt1 = cb_.f(8 * 640)
        mixb = cb_.f(8 * 640)
        hf = cb_.b(8 * 640)
        abuf = cb_.b(24 * 512)
        wupb = [cb_.b(2048) for _ in range(3)]
        wdnb = [cb_.b(3072) for _ in range(2)]
        xt13 = xt1.rearrange("p (c t) -> p c t", t=640)
        mixbufs = [mixb, mixB]
        mix3s = [mb.rearrange("p (c t) -> p c t", t=640) for mb in mixbufs]
        f3s = [mb[:, 0:8 * 512].rearrange("p (c t) -> p c t", t=512) for mb in mixbufs]
        hf3 = hf.rearrange("p (c t) -> p c t", t=640)
        a3 = abuf.rearrange("p (c t) -> p c t", t=512)
        cwd = Carve(W_DYN, R_W[0] + R_W[1] - W_DYN)
        accb = [cwd.f(512) for _ in range(4)]
        wupb.append(cwd.b(2048))
        woutb = [cwd.b(1024) for _ in range(3)]
        wo_ctr = [0]
        C1W_OLD = ["c1w%d_%d" % (s, i) for s in range(2) for i in range(4)]
        oT3 = outT.rearrange("(c p) t -> p c t", p=128)
        memset(gpad[0], 0.0, ["gpad0"] + ["sg%d_%d" % (p, i) for p in range(2) for i in range(2)] + ["t12%d_%d" % (p, i) for p in range(2) for i in range(2)])
        for gi in range(1, 4):
            memset(gpad[gi], 0.0, ["gpad%d" % gi])
        first_hy = [True]
        wup_ctr = [0]
        wdn_ctr = [0]
        dg_ctr = [0]
        gp_ctr = [0]

        def norm_rstd(src_fn, ntok, reads_fn, hook=None):
            subs = [(0, min(512, ntok))] + ([(512, ntok - 512)] if ntok > 512 else [])
            for c in range(8):
                if c == 0:
                    act(ssum2[:, 0:ntok], src_fn(c), AF.Square, reads_fn(c), ["ssum2"])
                else:
                    sq = sq2[c % 2]
                    act(sq[:, 0:ntok], src_fn(c), AF.Square, reads_fn(c), ["sq2_%d" % (c % 2)])
                    tt("dve", ssum2[:, 0:ntok], ssum2[:, 0:ntok], sq[:, 0:ntok], ALU.add, ["ssum2", "sq2_%d" % (c % 2)], ["ssum2"])
                if hook is not None:
                    hook(c)
            for (so, sn) in subs:
                b = nextbank()
                mm(ps[b][:, 0:sn], ones, ssum2[:, so:so + sn], True, True, ["ones", "ssum2"], ["ps%d" % b])
                act(sd2[:, so:so + sn], ps[b][:, 0:sn], AF.Ln, ["ps%d" % b], ["sd2"], scale=1.0 / D, bias=EPS)
            act(rstd2[:, 0:ntok], sd2[:, 0:ntok], AF.Exp, ["sd2"], ["rstd2"], scale=-0.5)

        def geom(tl):
            r0 = max(8 * tl - 1, 0)
            r1 = min(8 * tl + 9, 32)
            ntok = (r1 - r0) * 64
            return r0, ntok, r0 * 64, [(0, 512)] + [(512, ntok - 512)]

        first_mix = [True]

        def mix_units(tl):
            r0, ntok, tok0, subs = geom(tl)
            par = tl % 2
            mtl = sorted({min(3, (tok0 + so) // 512) for so, sn in subs} | {min(3, (tok0 + so + sn - 1) // 512) for so, sn in subs})
            units = []
            for oc in range(8):
                for (so, sn) in subs:
                    st_ = {}

                    def pe_fn(oc=oc, so=so, sn=sn, st_=st_):
                        if so == 0:
                            wi = wo_ctr[0] % 3
                            wo_ctr[0] += 1
                            extra = (LRUTOK + C1W_OLD) if first_mix[0] else []
                            P.dma("sp", woutb[wi], woutS[oc], reads=["wc_all"], writes=["wo%d" % wi] + extra, slot="wo%d" % wi)
                            wo_cur[0] = wi
                        wi = wo_cur[0]
                        b = nextbank()
                        st_["b"] = b
                        for k in range(8):
                            mm(ps[b][:, 0:sn], woutb[wi][:, k * 128:(k + 1) * 128],
                               m3[:, k, tok0 + so:tok0 + so + sn], k == 0, k == 7, ["wo%d" % wi] + ["m%d_%d" % (k, t_) for t_ in mtl], ["ps%d" % b])

                    def act_fn(oc=oc, so=so, sn=sn, st_=st_):
                        b = st_["b"]
                        extra2 = []
                        if oc == 0 and so == 0:
                            extra2 = ["f%d_%d" % (par, q) for q in range(8)]
                            if first_mix[0] or tl == 1:
                                extra2 = extra2 + ["sg%d_%d" % (p_, i) for p_ in range(2) for i in range(2)] + ["t12%d_%d" % (p_, i) for p_ in range(2) for i in range(2)] + HY_OLD
                        first_mix[0] = False
                        act(mix3s[par][:, oc, so:so + sn], ps[b][:, 0:sn], AF.Identity, ["ps%d" % b], ["mix%d_%d" % (par, oc)] + extra2)
                    units.append((pe_fn, act_fn))
            return units

        wo_cur = [0]

        def do_mix(tl):
            for pe_fn, act_fn in mix_units(tl):
                pe_fn()
                act_fn()

        do_mix(0)
        for tl in range(4):
            r0, ntok, tok0, subs = geom(tl)
            par = tl % 2
            mix3 = mix3s[par]
            f3 = f3s[par]
            MIXT = lambda c: "mix%d_%d" % (par, c)
            FT = lambda c: "f%d_%d" % (par, c)
            co = (8 * tl - r0) * 64
            s0 = r0 - (8 * tl - 1)
            extra = HY_OLD if first_hy[0] else []
            first_hy[0] = False
            for c in range(8):
                P.dma("sp", xt13[:, c, 0:ntok], xT3[:, c, tok0:tok0 + ntok], writes=["xt1_%d" % c, "x1_%d" % c] + (extra if c == 0 else []),
                      slot="xt1_%d" % c)
            norm_rstd(lambda c: mix3[:, c, 0:ntok], ntok, lambda c: [MIXT(c)])
            pend = mix_units(tl + 1) if tl + 1 < 4 else []
            pend_pe = [u[0] for u in pend]
            pend_act = [u[1] for u in pend]
            inflight = [0]

            def pump(nact):
                for _ in range(nact):
                    if pend_act:
                        pend_act.pop(0)()
                        inflight[0] -= 1
                while pend_pe and inflight[0] < 5:
                    pend_pe.pop(0)()
                    inflight[0] += 1

            pump(0)
            for c in range(8):
                tm = tm2[c % 2]
                tt("dve", tm[:, 0:ntok], mix3[:, c, 0:ntok], rstd2[:, 0:ntok], ALU.mult, [MIXT(c), "rstd2"], ["tm2_%d" % (c % 2)])
                stt(xt13[:, c, 0:ntok], tm[:, 0:ntok], GG1[:, c:c + 1], xt13[:, c, 0:ntok], ALU.mult, ALU.add,
                    ["tm2_%d" % (c % 2), "der", "xt1_%d" % c], ["x1_%d" % c])
            pump(4)
            norm_rstd(lambda c: xt13[:, c, 0:ntok], ntok, lambda c: ["x1_%d" % c], hook=lambda c: pump(1))
            pump(4)
            for c in range(8):
                tm = tm2[c % 2]
                tt("dve", tm[:, 0:ntok], xt13[:, c, 0:ntok], rstd2[:, 0:ntok], ALU.mult, ["x1_%d" % c, "rstd2"], ["tm2_%d" % (c % 2)])
                act(hf3[:, c, 0:ntok], tm[:, 0:ntok], AF.Identity, ["tm2_%d" % (c % 2), "der", "modfm"], ["hf%d" % c],
                    bias=MOD(3, c, 0), scale=A2[:, c:c + 1])
            while pend_act or pend_pe:
                pump(1)
            HF = ["hf%d" % c for c in range(8)]
            if tl == 3:
                for gi in range(4):
                    memset(gpad[gi].rearrange("p (r c) -> p r c", c=66)[:, 9, :], 0.0, ["gpad%d" % gi])
            def ffn_front(p):
                st = []
                for i in range(2):
                    j = 2 * p + i
                    q = j % 4
                    wi = wup_ctr[0] % 4
                    wup_ctr[0] += 1
                    P.dma("sp", wupb[wi], wupS[j], reads=["wc_all"], writes=["wup%d" % wi], slot="wup%d" % wi)
                    gp3 = gpad[q].rearrange("p (r c) -> p r c", c=66)
                    wg = wupb[wi][:, 0:1024]
                    row = s0
                    for (so, sn) in subs:
                        b = nextbank()
                        for k in range(8):
                            mm(ps[b][:, 0:sn], wg[:, k * 128:(k + 1) * 128], hf3[:, k, so:so + sn], k == 0, k == 7,
                               ["wup%d" % wi, HF[k]], ["ps%d" % b])
                        nr = sn // 64
                        act(gp3[:, row:row + nr, 1:65], ps[b][:, 0:sn].rearrange("p (r c) -> p r c", c=64), AF.Identity,
                            ["ps%d" % b], ["gpad%d" % q])
                        row += nr
                    acc3 = accb[q].rearrange("p (r c) -> p r c", c=64)
                    act(acc3, gp3[:, 0:8, 0:64], AF.Identity, ["gpad%d" % q, "vecs"], ["acc%d" % q],
                        bias=V("ffn_conv_b", j), scale=V("ffn_conv_w", 0 * 24 + j))
                    st.append((j, q, wi, gp3, acc3))
                return st

            def ffn_taps(st, taps):
                for tap in taps:
                    dy, dx = tap // 3, tap % 3
                    for (j, q, wi, gp3, acc3) in st:
                        stt(acc3, gp3[:, dy:dy + 8, dx:dx + 64], V("ffn_conv_w", tap * 24 + j), acc3, ALU.mult, ALU.add,
                            ["gpad%d" % q, "acc%d" % q, "vecs"], ["acc%d" % q])

            def ffn_u(st):
                out = []
                for (j, q, wi, gp3, acc3) in st:
                    wu = wupb[wi][:, 1024:2048]
                    bu = nextbank()
                    for k in range(8):
                        mm(ps[bu][:, :], wu[:, k * 128:(k + 1) * 128], hf3[:, k, co:co + 512], k == 0, k == 7,
                           ["wup%d" % wi, HF[k]], ["ps%d" % bu])
                    out.append(bu)
                return out

            def ffn_gelu(st):
                for (j, q, wi, gp3, acc3) in st:
                    act(accb[q], accb[q], AF.Gelu_apprx_tanh, ["acc%d" % q], ["acc%d" % q])

            def ffn_mult(st, bus):
                for (j, q, wi, gp3, acc3), bu in zip(st, bus):
                    tt("dve", a3[:, j, :], accb[q], ps[bu][:, :], ALU.mult, ["acc%d" % q, "ps%d" % bu], ["a%d" % j])

            prev = None
            for p in range(12):
                st = ffn_front(p)
                if prev is not None:
                    ffn_gelu(prev[0])
                ffn_taps(st, range(1, 2))
                if prev is not None:
                    ffn_mult(*prev)
                ffn_taps(st, range(2, 9))
                bus = ffn_u(st)
                prev = (st, bus)
            ffn_gelu(prev[0])
            ffn_mult(*prev)
            AT = ["a%d" % j for j in range(24)]
            for oc in range(8):
                wi = wdn_ctr[0] % 2
                wdn_ctr[0] += 1
                P.dma("sp", wdnb[wi], wdnS[oc], reads=["wc_all"], writes=["wdn%d" % wi], slot="wdn%d" % wi)
                b = nextbank()
                for k in range(24):
                    mm(ps[b][:, :], wdnb[wi][:, k * 128:(k + 1) * 128], a3[:, k, :], k == 0, k == 23, ["wdn%d" % wi, AT[k]], ["ps%d" % b])
                act(f3[:, oc, :], ps[b][:, :], AF.Identity, ["ps%d" % b], [FT(oc)] + ([MIXT(q) for q in range(8)] if oc == 0 else []))
                if oc == 0:
                    act(ssum2[:, 0:512], f3[:, oc, :], AF.Square, [FT(oc)], ["ssum2"])
                else:
                    sq = sq2[oc % 2]
                    act(sq[:, 0:512], f3[:, oc, :], AF.Square, [FT(oc)], ["sq2_%d" % (oc % 2)])
                    tt("dve", ssum2[:, 0:512], ssum2[:, 0:512], sq[:, 0:512], ALU.add, ["ssum2", "sq2_%d" % (oc % 2)], ["ssum2"])
            bss = nextbank()
            mm(ps[bss][:, :], ones, ssum2[:, 0:512], True, True, ["ones", "ssum2"], ["ps%d" % bss])
            act(sd2[:, 0:512], ps[bss][:, :], AF.Ln, ["ps%d" % bss], ["sd2"], scale=1.0 / D, bias=EPS)
            act(rstd2[:, 0:512], sd2[:, 0:512], AF.Exp, ["sd2"], ["rstd2"], scale=-0.5)
            for oc in range(8):
                tm = tm2[oc % 2]
                tt("dve", tm[:, 0:512], f3[:, oc, :], rstd2[:, 0:512], ALU.mult, [FT(oc), "rstd2"], ["tm2_%d" % (oc % 2)])
                stt(f3[:, oc, :], tm[:, 0:512], GG2[:, oc:oc + 1], xt13[:, oc, co:co + 512], ALU.mult, ALU.add,
                    ["tm2_%d" % (oc % 2), "der", "x1_%d" % oc, FT(oc)], [FT(oc)])
            P.dma("sp", oT3[:, :, 512 * tl:512 * tl + 512], f3, reads=[FT(oc) for oc in range(8)], slot="out")
        P.emit()
    return nc, dbg_out


def _fm(v, nch):
    v = np.asarray(v, np.float32)
    lead = v.shape[:-1]
    r = v.reshape(lead + (nch, 128))
    r = np.moveaxis(r, -1, 0)
    return np.ascontiguousarray(r.reshape(128, -1))


def _wtile(w, kch, och):
    w = np.asarray(w, np.float32)
    r = w.reshape(kch, 128, och, 128).transpose(2, 1, 0, 3)
    return np.ascontiguousarray(r.reshape(och, 128, kch * 128))


def _window_counts():
    out = np.zeros((4, 96), np.float32)
    for gi, w in enumerate(POOL_WINDOWS):
        def cnt1(n):
            pos = np.arange(n)
            lo = np.clip(pos - w // 2, 0, n)
            hi = np.clip(pos + w - w // 2, 0, n)
            return (hi - lo).astype(np.float32)
        out[gi, 0:32] = cnt1(32)
        out[gi, 32:96] = cnt1(64)
    return out.reshape(1, 384)


def prep_inputs(x, c, ctx, c_ctx, w_mod, b_mod, g_pre_mix, g_post_mix, g_pre_ffn, g_post_ffn,
                w_in, pool_w, pool_scale, lru_conv_w, lru_conv_b, lru_wa, lru_ba, lru_wx, lru_bx,
                lru_lambda, w_proj_pool, w_proj_lru, w_out, w_up, ffn_conv_w, ffn_conv_b, w_down):
    f = lambda a: np.asarray(a, np.float32)
    vec_parts = {
        "g_pre_mix": _fm(f(g_pre_mix)[0], 8), "g_post_mix": _fm(f(g_post_mix)[0], 8),
        "g_pre_ffn": _fm(f(g_pre_ffn)[0], 8), "g_post_ffn": _fm(f(g_post_ffn)[0], 8),
        "pool_scale": _fm(f(pool_scale)[0], 8), "lru_conv_w": _fm(f(lru_conv_w)[0], 8),
        "lru_conv_b": _fm(f(lru_conv_b)[0], 8), "lru_ba": _fm(f(lru_ba)[0], 8), "lru_bx": _fm(f(lru_bx)[0], 8),
        "lru_lambda": _fm(f(lru_lambda)[0], 8), "ffn_conv_w": _fm(f(ffn_conv_w)[0].reshape(9, 3072), 24),
        "ffn_conv_b": _fm(f(ffn_conv_b)[0], 24), "b_mod": _fm(f(b_mod)[0], 48),
    }
    vecs = np.ascontiguousarray(np.concatenate([vec_parts[n] for n, _ in _VEC_SPEC], axis=1))
    assert vecs.shape == (128, NV)
    wmod_h = np.ascontiguousarray(f(w_mod)[0].reshape(8, 128, 12, 512).transpose(2, 1, 0, 3).reshape(12, 128, 4096))
    win_h = _wtile(f(w_in)[0], 8, 40)
    wpp_h = _wtile(f(w_proj_pool)[0], 8, 8)
    wpl_h = _wtile(f(w_proj_lru)[0], 8, 8)
    wout_h = _wtile(f(w_out)[0], 8, 8)
    wup_t = _wtile(f(w_up)[0], 8, 48)
    wup_h = np.ascontiguousarray(np.concatenate([wup_t[0:24], wup_t[24:48]], axis=2))
    wdown_h = _wtile(f(w_down)[0], 24, 8)
    pw = f(pool_w)[0].reshape(4, 2, 128, 2, 128).transpose(2, 0, 3, 1, 4)
    poolw_h = np.ascontiguousarray(pw.reshape(128, 2048))
    lw = np.stack([f(lru_wa)[0], f(lru_wx)[0]], axis=0)
    lruw_h = np.ascontiguousarray(lw.transpose(3, 0, 1, 2, 4).reshape(128, 4096))
    shared = {"vecs": vecs, "ident": np.eye(128, dtype=np.float32), "cnt": _window_counts(), "wmod": wmod_h,
              "win": win_h, "wpp": wpp_h, "wpl": wpl_h, "wout": wout_h, "wup": wup_h, "wdown": wdown_h,
              "poolw": poolw_h, "lruw": lruw_h}
    xf, cf, ctxf, ccf = f(x), f(c), f(ctx), f(c_ctx)
    in_maps = []
    for b in range(NCORES):
        m = dict(shared)
        m["xT"] = np.ascontiguousarray(xf[b].T)
        m["ctxT"] = np.ascontiguousarray(ctxf[b].T)
        cc2 = np.stack([cf[b], ccf], axis=0)
        m["cc"] = np.ascontiguousarray(cc2.reshape(2, 8, 128).transpose(2, 1, 0).reshape(128, 16))
        in_maps.append(m)
    return in_maps


_CACHE = {}


def kernel(**inputs):
    in_maps = prep_inputs(**inputs)
    if "nc" not in _CACHE:
        _CACHE["nc"] = build_program()[0]
    nc = _CACHE["nc"]
    res = run_bass_kernel_spmd(nc, in_maps, core_ids=list(range(NCORES)))
    out = np.stack([np.asarray(r["outT"], np.float32).T for r in res.results], axis=0)
    return np.ascontiguousarray(out.astype(np.float32))
```
